# Optimizing a Trainium2 kernel written in Bass

```python
import jax, jax.numpy as jnp
from jax import lax
import numpy as np

D_MODEL = 1024
BATCH = 4
SEQ = 8192
DEPTH = 4

HG_HEADS = 4
HG_DK = 128
HG_DV = 128
HG_CHUNK = 64
HG_QK = HG_HEADS * HG_DK
HG_W = HG_HEADS * HG_DV
SG_GROUPS = 4
SG_CHUNK = 128
SG_W = 512
SG_GW = SG_W // SG_GROUPS
AT_QHEADS = 8
AT_KVHEADS = 2
AT_HD = 64
AT_WINDOW = 128
AT_BLOCK = 128
AT_W = AT_QHEADS * AT_HD
AT_KVW = AT_KVHEADS * AT_HD
ROPE_THETA = 10000.0
N_BRANCH = 3
EPS = 1e-6
N_IN = 3 * HG_QK + 2 * HG_W + 3 * SG_W + 2 * AT_W + 2 * AT_KVW + N_BRANCH * D_MODEL

kernel_name = 'hybrid_hgrn2_gmlp_swa_gated_block'


def _split_sizes():
    return [HG_QK, HG_QK, HG_QK, HG_W, HG_W, SG_W, SG_W, SG_W,
            AT_W, AT_KVW, AT_KVW, AT_W, D_MODEL, D_MODEL, D_MODEL]


def rms_norm(x, g):
    xf = x.astype(jnp.float32)
    y = xf * lax.rsqrt(jnp.mean(xf * xf, axis=-1, keepdims=True) + EPS)
    return (y * g.astype(jnp.float32)).astype(x.dtype)


def layer_norm(x, g, b):
    xf = x.astype(jnp.float32)
    mu = jnp.mean(xf, axis=-1, keepdims=True)
    var = jnp.mean(jnp.square(xf - mu), axis=-1, keepdims=True)
    y = (xf - mu) * lax.rsqrt(var + EPS) * g.astype(jnp.float32) + b.astype(jnp.float32)
    return y.astype(x.dtype)


def rope_tables(L):
    half = AT_HD // 2
    inv_freq = ROPE_THETA ** (-jnp.arange(half, dtype=jnp.float32) / half)
    ang = jnp.arange(L, dtype=jnp.float32)[:, None] * inv_freq[None, :]
    return jnp.cos(ang)[None, :, None, :], jnp.sin(ang)[None, :, None, :]


def apply_rope(x, cos, sin):
    x1, x2 = jnp.split(x.astype(jnp.float32), 2, axis=-1)
    y = jnp.concatenate([x1 * cos - x2 * sin, x2 * cos + x1 * sin], axis=-1)
    return y.astype(x.dtype)


def hgrn2_gates(a, lb):
    a = a.astype(jnp.float32)
    log_f = jnp.logaddexp(jnp.log(lb), jnp.log1p(-lb) + jax.nn.log_sigmoid(a))
    k = (1.0 - lb) * jax.nn.sigmoid(-a)
    return log_f, k


def hgrn2_bidirectional(q, a_fwd, a_bwd, i, lb):
    B, L, _ = q.shape
    H, C = HG_HEADS, HG_CHUNK
    n_chunks = L // C
    lf_f, k_f = hgrn2_gates(a_fwd, lb[0])
    lf_b, k_b = hgrn2_gates(a_bwd, lb[1])

    def heads(t, d):
        return t.astype(jnp.float32).reshape(B, L, H, d)

    qh, vh = heads(q, HG_DK), heads(i, HG_DV)
    q2 = jnp.concatenate([qh, qh[:, ::-1]], axis=2)
    v2 = jnp.concatenate([vh, vh[:, ::-1]], axis=2)
    k2 = jnp.concatenate([heads(k_f, HG_DK), heads(k_b, HG_DK)[:, ::-1]], axis=2)
    lf2 = jnp.concatenate([heads(lf_f, HG_DK), heads(lf_b, HG_DK)[:, ::-1]], axis=2)

    def to_chunks(t):
        return t.reshape(B, n_chunks, C, 2 * H, t.shape[-1]).transpose(1, 0, 3, 2, 4)

    tri = jnp.tril(jnp.ones((C, C), dtype=bool))

    def step(S, xs):
        qc, kc, vc, lfc = xs
        b = jnp.cumsum(lfc, axis=2)
        diff = b[:, :, :, None, :] - b[:, :, None, :, :]
        decay = jnp.exp(jnp.where(tri[None, None, :, :, None], diff, -jnp.inf))
        scores = jnp.einsum('bhtd,bhtsd,bhsd->bhts', qc, decay, kc)
        o = jnp.einsum('bhts,bhsv->bhtv', scores, vc) + jnp.einsum('bhtd,bhdv->bhtv', qc * jnp.exp(b), S)
        b_end = b[:, :, -1, :]
        S_new = jnp.exp(b_end)[..., None] * S + jnp.einsum(
            'bhsd,bhsv->bhdv', kc * jnp.exp(b_end[:, :, None, :] - b), vc)
        return S_new, o

    S0 = jnp.zeros((B, 2 * H, HG_DK, HG_DV), jnp.float32)
    _, o = lax.scan(step, S0, (to_chunks(q2), to_chunks(k2), to_chunks(v2), to_chunks(lf2)))
    o = o.transpose(1, 0, 3, 2, 4).reshape(B, L, 2 * H, HG_DV)
    return o[:, :, :H] + o[:, ::-1, H:]


def spatial_gating(u, v, ln_g, ln_b, w_s, b_s):
    B, L, _ = u.shape
    vn = layer_norm(v, ln_g, ln_b).reshape(B, L // SG_CHUNK, SG_CHUNK, SG_GROUPS, SG_GW)
    mixed = jnp.einsum('gts,bnsgc->bntgc', w_s, vn) + b_s.T[None, None, :, :, None]
    return u * mixed.reshape(B, L, SG_W)


def window_attention(q, k, v, sink):
    B, L, _, _ = q.shape
    nb = L // AT_BLOCK
    G = AT_QHEADS // AT_KVHEADS
    qb = q.reshape(B, nb, AT_BLOCK, AT_KVHEADS, G, AT_HD)
    pad = ((0, 0), (AT_BLOCK, AT_BLOCK), (0, 0), (0, 0))

    def band(t):
        tp = jnp.pad(t, pad).reshape(B, nb + 2, AT_BLOCK, AT_KVHEADS, AT_HD)
        return jnp.concatenate([tp[:, :-2], tp[:, 1:-1], tp[:, 2:]], axis=2)

    kw, vw = band(k), band(v)
    blk = jnp.arange(nb)[:, None] * AT_BLOCK
    qpos = blk + jnp.arange(AT_BLOCK)[None, :]
    kpos = blk - AT_BLOCK + jnp.arange(3 * AT_BLOCK)[None, :]
    rel = kpos[:, None, :] - qpos[:, :, None]
    mask = (jnp.abs(rel) <= AT_WINDOW) & ((kpos >= 0) & (kpos < L))[:, None, :]
    s = jnp.einsum('bnqhgd,bnkhd->bnhgqk', qb, kw).astype(jnp.float32) * (AT_HD ** -0.5)
    s = jnp.where(mask[None, :, None, None], s, -jnp.inf)
    sk = sink.astype(jnp.float32).reshape(AT_KVHEADS, G)[None, None, :, :, None, None]
    m = jnp.maximum(jnp.max(s, axis=-1, keepdims=True), sk)
    p = jnp.exp(s - m)
    w = p / (jnp.sum(p, axis=-1, keepdims=True) + jnp.exp(sk - m))
    o = jnp.einsum('bnhgqk,bnkhd->bnqhgd', w.astype(v.dtype), vw)
    return o.reshape(B, L, AT_W)


def setup_inputs(seed: int = 0) -> dict:
    key = jax.random.key(seed)
    ks = jax.random.split(key, 16)
    f32 = jnp.float32
    nrm = lambda k, shape, scale: jax.random.normal(k, shape, f32) * scale
    return {
        'x': jax.random.normal(ks[0], (BATCH, SEQ, D_MODEL), f32),
        'w_in': nrm(ks[1], (DEPTH, D_MODEL, N_IN), D_MODEL ** -0.5),
        'norm_gain': 1.0 + nrm(ks[2], (DEPTH, D_MODEL), 0.02),
        'lb_logits': nrm(ks[3], (DEPTH, 2 * HG_QK), 1.0),
        'hg_norm_gain': 1.0 + nrm(ks[4], (DEPTH, HG_HEADS, HG_DV), 0.02),
        'sg_ln_gain': 1.0 + nrm(ks[5], (DEPTH, SG_W), 0.02),
        'sg_ln_bias': nrm(ks[6], (DEPTH, SG_W), 0.02),
        'w_spatial': nrm(ks[7], (DEPTH, SG_GROUPS, SG_CHUNK, SG_CHUNK), SG_CHUNK ** -0.5),
        'b_spatial': 1.0 + nrm(ks[8], (DEPTH, SG_GROUPS, SG_CHUNK), 0.02),
        'q_norm_gain': 1.0 + nrm(ks[9], (DEPTH, AT_HD), 0.02),
        'k_norm_gain': 1.0 + nrm(ks[10], (DEPTH, AT_HD), 0.02),
        'sink_logits': nrm(ks[11], (DEPTH, AT_QHEADS), 0.5),
        'w_branch_a': nrm(ks[12], (DEPTH, HG_W, D_MODEL), HG_W ** -0.5),
        'w_branch_b': nrm(ks[13], (DEPTH, SG_W, D_MODEL), SG_W ** -0.5),
        'w_branch_c': nrm(ks[14], (DEPTH, AT_W, D_MODEL), AT_W ** -0.5),
        'w_out': nrm(ks[15], (DEPTH, D_MODEL, D_MODEL), D_MODEL ** -0.5),
    }


def reference(x, w_in, norm_gain, lb_logits, hg_norm_gain, sg_ln_gain, sg_ln_bias, w_spatial, b_spatial,
              q_norm_gain, k_norm_gain, sink_logits, w_branch_a, w_branch_b, w_branch_c, w_out):
    B, L, _ = x.shape
    cos, sin = rope_tables(L)
    split_at = np.cumsum(_split_sizes())[:-1].tolist()
    p_lb = jax.nn.softmax(lb_logits.astype(jnp.float32), axis=0)
    lower = jnp.maximum(jnp.cumsum(p_lb, axis=0) - p_lb[0:1], 0.0).reshape(DEPTH, 2, HG_QK)

    for l in range(DEPTH):
        h = rms_norm(x, norm_gain[l])
        proj = jnp.einsum('bld,dn->bln', h, w_in[l])
        (qa, fa_f, fa_b, ia, za, ub, vb, zb, qc, kc, vc, zc,
         g_a, g_b, g_c) = jnp.split(proj, split_at, axis=-1)

        ya = hgrn2_bidirectional(qa, fa_f, fa_b, ia, lower[l]).astype(x.dtype)
        ya = rms_norm(ya, hg_norm_gain[l]).reshape(B, L, HG_W) * jax.nn.silu(za)

        yb = spatial_gating(jax.nn.gelu(ub), jax.nn.gelu(vb), sg_ln_gain[l], sg_ln_bias[l],
                            w_spatial[l], b_spatial[l]) * jax.nn.silu(zb)

        qh = apply_rope(rms_norm(qc.reshape(B, L, AT_QHEADS, AT_HD), q_norm_gain[l]), cos, sin)
        kh = apply_rope(rms_norm(kc.reshape(B, L, AT_KVHEADS, AT_HD), k_norm_gain[l]), cos, sin)
        vh = vc.reshape(B, L, AT_KVHEADS, AT_HD)
        yc = window_attention(qh, kh, vh, sink_logits[l]) * jax.nn.silu(zc)

        merged = (jax.nn.sigmoid(g_a) * jnp.einsum('blw,wd->bld', ya, w_branch_a[l])
                  + jax.nn.sigmoid(g_b) * jnp.einsum('blw,wd->bld', yb, w_branch_b[l])
                  + jax.nn.sigmoid(g_c) * jnp.einsum('blw,wd->bld', yc, w_branch_c[l]))
        x = x + jnp.einsum('bld,de->ble', merged, w_out[l])
    return x
```

```python
import numpy as np
import concourse.bass as bass
import concourse.mybir as mybir
from concourse.bass_utils import run_bass_kernel_spmd

F32 = mybir.dt.float32
BF16 = mybir.dt.bfloat16
ALU = mybir.AluOpType
AF = mybir.ActivationFunctionType

D = 1024
DEPTH = 4
EPS = 1e-6
NCORES = 8
SAME_ENGINE_SYNC = True


def I(name, *args, **kw):
    return lambda e: getattr(e, name)(*args, **kw)


class Buf:
    __slots__ = ("name", "last_w", "readers")

    def __init__(self, name):
        self.name = name
        self.last_w = None
        self.readers = []


class Slot:
    def __init__(self, sem, name):
        self.sem = sem
        self.count = 0
        self.token = Buf("slot_" + name)


class Op:
    __slots__ = ("eng", "fn", "deps", "signal", "val", "sem", "slot", "idx", "epoch")


class Sched:
    ENGS = ("pe", "act", "dve", "pool", "sp")

    def __init__(self, nc, n_epochs):
        self.nc = nc
        self.ops = {e: [] for e in self.ENGS}
        self.epoch = 0
        self.n_epochs = n_epochs
        self.engsem = {}

    def op(self, eng, fn, reads=(), writes=(), slot=None):
        o = Op()
        o.eng = eng
        o.fn = fn
        o.signal = False
        o.val = None
        o.sem = None
        o.slot = slot
        o.epoch = self.epoch
        deps = []
        writes = list(writes)
        if slot is not None:
            writes.append(slot.token)
        for b in reads:
            if b.last_w is not None:
                deps.append(b.last_w)
        for b in writes:
            if b.last_w is not None:
                deps.append(b.last_w)
            deps.extend(b.readers)
        seen = set()
        dd = []
        for d in deps:
            if id(d) not in seen and d is not o:
                seen.add(id(d))
                dd.append(d)
        o.deps = dd
        for b in reads:
            b.readers.append(o)
        for b in writes:
            b.last_w = o
            b.readers = []
        if slot is not None:
            slot.count += 1
            o.sem = slot.sem
            o.val = 16 * slot.count
        o.idx = len(self.ops[eng])
        self.ops[eng].append(o)
        return o

    def _needs_wait(self, cons, prod):
        if prod.slot is not None:
            return True
        if prod.eng == cons.eng and cons.slot is None:
            if prod.eng == "pe":
                return False
            return SAME_ENGINE_SYNC
        return True

    def finalize(self, sems, block):
        for e in self.ENGS:
            for o in self.ops[e]:
                for d in o.deps:
                    if self._needs_wait(o, d):
                        d.signal = True
        for e in self.ENGS:
            cnt = {}
            for o in self.ops[e]:
                if o.slot is not None:
                    pass
                elif o.signal:
                    cnt[o.epoch] = cnt.get(o.epoch, 0) + 1
                    o.sem = sems[(e, o.epoch)]
                    o.val = cnt[o.epoch]
        self.stats = {e: len(self.ops[e]) for e in self.ENGS}

        def emit(e, eng):
            waited = {}
            nwaits = 0
            for o in self.ops[e]:
                for d in o.deps:
                    if not self._needs_wait(o, d):
                        continue
                    key = id(d.sem)
                    if waited.get(key, 0) >= d.val:
                        continue
                    eng.wait_ge(d.sem, d.val)
                    nwaits += 1
                    waited[key] = d.val
                ins = o.fn(eng)
                if o.slot is not None:
                    ins.then_inc(o.sem, 16)
                elif o.signal:
                    ins.then_inc(o.sem, 1)
            self.stats[e + "_waits"] = nwaits

        @block.tensor
        def _(eng):
            emit("pe", eng)

        @block.scalar
        def _(eng):
            emit("act", eng)

        @block.vector
        def _(eng):
            emit("dve", eng)

        @block.gpsimd
        def _(eng):
            emit("pool", eng)

        @block.sync
        def _(eng):
            emit("sp", eng)


OFF = dict(qA=0, fAf=512, fAb=1024, iA=1536, zA=2048, uB=2560, vB=3072, zB=3584, qC=4096, kC=4608,
           vC=4736, zC=4864, gA=5376, gB=6400, gC=7424)

FM2 = (["a2"] * 4 + ["qA"] * 4 + ["zA"] * 4 + ["uB"] * 4 + ["zB"] * 4 + ["qC"] * 4 + ["kC"] * 2 + ["zC"] * 4
       + ["gA"] * 8 + ["gB"] * 8 + ["gC"] * 8)
FM1 = ["a1"] * 4 + ["qA"] * 4


def _fm_cols(kind_list, odd):
    out = []
    cnt = {}
    for kind in kind_list:
        j = cnt.get(kind, 0)
        cnt[kind] = j + 1
        if kind == "a1":
            base = OFF["fAb"] if odd else OFF["fAf"]
            cols = np.arange(base + j * 128, base + (j + 1) * 128)
        elif kind == "a2":
            base = OFF["fAf"] if odd else OFF["fAb"]
            cols = np.arange(base + j * 128, base + (j + 1) * 128)
        elif kind == "kC":
            c = np.arange(OFF["kC"] + j * 64, OFF["kC"] + (j + 1) * 64)
            cols = np.concatenate([c, c])
        else:
            base = OFF[kind]
            cols = np.arange(base + j * 128, base + (j + 1) * 128)
        out.append(cols)
    return out


def _tm_cols(sweep):
    if sweep == 1:
        return np.arange(OFF["iA"], OFF["iA"] + 512)
    return np.concatenate([np.arange(OFF["iA"], OFF["iA"] + 512), np.arange(OFF["vB"], OFF["vB"] + 512),
                           np.arange(OFF["vC"], OFF["vC"] + 128)])


NF1 = len(FM1)
NF2 = len(FM2)
TM1 = 512
TM2 = 1152


class Cfg:
    def __init__(self, n_mt=8, depth=DEPTH, do_a=True, do_b=True, do_c=True):
        self.n_mt = n_mt
        self.T = n_mt * 512
        self.NB = n_mt * 4
        self.depth = depth
        self.do_a = do_a
        self.do_b = do_b
        self.do_c = do_c


def prep_core_inputs(inp, core, cfg):
    T = cfg.T
    L = 2 * T
    b = core // 2
    odd = core % 2
    depth = cfg.depth
    f32 = np.float32
    pos = (np.arange(T) if not odd else (L - 1 - np.arange(T))).astype(np.int64)
    m = {}
    x = np.asarray(inp["x"])[b]
    m["xT"] = np.ascontiguousarray(x[pos, :].T).astype(f32)
    w_in = np.asarray(inp["w_in"])
    w1fm = np.empty((depth, NF1, 128, 8, 128), f32)
    w2fm = np.empty((depth, NF2, 128, 8, 128), f32)
    w1tm = np.empty((depth, 128, 8, TM1), f32)
    w2tm = np.empty((depth, 128, 8, TM2), f32)
    c1 = _fm_cols(FM1, odd)
    c2 = _fm_cols(FM2, odd)
    for l in range(depth):
        wl = w_in[l].reshape(8, 128, -1)
        for j, cols in enumerate(c1):
            w1fm[l, j] = wl[:, :, cols].transpose(1, 0, 2)
        for j, cols in enumerate(c2):
            w2fm[l, j] = wl[:, :, cols].transpose(1, 0, 2)
        w1tm[l] = wl[:, :, _tm_cols(1)].transpose(1, 0, 2)
        w2tm[l] = wl[:, :, _tm_cols(2)].transpose(1, 0, 2)
    m["w1fm"] = w1fm
    m["w2fm"] = w2fm
    m["w1tm"] = w1tm
    m["w2tm"] = w2tm
    wbr = np.empty((depth, 8, 128, 3, 4, 128), f32)
    for bi, key in enumerate(("w_branch_a", "w_branch_b", "w_branch_c")):
        w = np.asarray(inp[key])[:depth].reshape(depth, 4, 128, 8, 128)
        wbr[:, :, :, bi] = w.transpose(0, 3, 2, 1, 4)
    m["wbr"] = wbr
    w = np.asarray(inp["w_out"])[:depth].reshape(depth, 8, 128, 8, 128)
    m["wo"] = np.ascontiguousarray(w.transpose(0, 3, 2, 1, 4)).astype(f32)
    ng = np.asarray(inp["norm_gain"])[:depth]
    m["ngain"] = np.ascontiguousarray(ng.reshape(depth, 8, 128).transpose(2, 0, 1)).astype(f32)
    lb = np.asarray(inp["lb_logits"]).reshape(DEPTH, 2, 4, 128)
    if odd:
        lb = lb[:, ::-1]
    m["lbl"] = np.ascontiguousarray(lb.transpose(3, 0, 1, 2)).astype(f32)
    hg = np.asarray(inp["hg_norm_gain"])[:depth]
    m["hgain"] = np.ascontiguousarray(hg.transpose(2, 0, 1)).astype(f32)
    m["lng"] = np.ascontiguousarray(np.broadcast_to(np.asarray(inp["sg_ln_gain"])[:depth][None], (128, depth, 512))).astype(f32)
    m["lnb"] = np.ascontiguousarray(np.broadcast_to(np.asarray(inp["sg_ln_bias"])[:depth][None], (128, depth, 512))).astype(f32)
    ws = np.asarray(inp["w_spatial"])[:depth]
    bs = np.asarray(inp["b_spatial"])[:depth]
    if odd:
        ws = ws[:, :, ::-1, ::-1]
        bs = bs[:, :, ::-1]
    m["wsT"] = np.ascontiguousarray(ws.transpose(3, 0, 1, 2)).astype(f32)
    m["bsp"] = np.ascontiguousarray(np.broadcast_to(bs[None], (128, depth, 4, 128))).astype(f32)
    qg = np.asarray(inp["q_norm_gain"])[:depth]
    kg = np.asarray(inp["k_norm_gain"])[:depth]
    m["qkg"] = np.ascontiguousarray(np.stack([np.concatenate([qg, qg], 1), np.concatenate([kg, kg], 1)], 1).transpose(2, 0, 1)).astype(f32)
    m["sink"] = np.ascontiguousarray(np.broadcast_to(np.asarray(inp["sink_logits"])[:depth][None], (128, depth, 8))).astype(f32)
    half = 32
    inv_freq = (10000.0 ** (-np.arange(half, dtype=np.float32) / half)).astype(np.float32)
    ang = pos.astype(np.float32)[None, :] * inv_freq[:, None]
    cos = np.cos(ang).astype(f32)
    sin = np.sin(ang).astype(f32)
    m["cosT"] = np.ascontiguousarray(np.concatenate([cos, cos, cos, cos], 0))
    m["sinT"] = np.ascontiguousarray(np.concatenate([sin, sin, sin, sin], 0))
    ident = np.eye(128, dtype=f32)
    m["c_ident"] = ident
    bd = np.zeros((128, 128), f32)
    bd[:64, :64] = 1
    bd[64:, 64:] = 1
    m["c_bd"] = bd
    rot = np.zeros((128, 128), f32)
    for hb in (0, 64):
        for d in range(32):
            rot[hb + d + 32, hb + d] = -1.0
            rot[hb + d, hb + d + 32] = 1.0
    m["c_rot"] = rot
    j = np.arange(128)[:, None]
    i = np.arange(128)[None, :]
    masks = np.stack([(j >= i), (j <= i), (j + i >= 127)], 1).astype(f32)
    m["c_amask"] = np.ascontiguousarray(masks)
    s = np.arange(64)[:, None]
    t = np.arange(64)[None, :]
    h1 = (s <= t).astype(f32)
    h2 = (s >= t).astype(f32)
    m["c_hmask"] = np.ascontiguousarray(np.stack([np.concatenate([h1, h1], 0), np.concatenate([h2, h2], 0)], 1))
    cm = np.ones((128, 512), f32)
    cm[:, ::64] = 0.0
    m["c_cmask"] = cm
    ss_ = np.arange(128)[:, None]
    tt_ = np.arange(128)[None, :]
    same = (ss_ // 64) == (tt_ // 64)
    m["c_hmask2"] = np.ascontiguousarray(np.stack([(same & (ss_ <= tt_)), (same & (ss_ >= tt_))], 1).astype(f32))
    sel = np.zeros((128, 2), f32)
    sel[:, 1 - odd] = 1.0
    m["sel"] = sel
    return m


def build_nc(cfg):
    nc = bass.Bass("TRN2", target_bir_lowering=False)
    T = cfg.T
    depth = cfg.depth
    n_mt = cfg.n_mt

    def din(name, shape, dt=F32):
        return nc.dram_tensor(name, list(shape), dt, kind="ExternalInput")

    xT_d = din("xT", [D, T])
    w1fm_d = din("w1fm", [depth, NF1, 128, 8, 128])
    w2fm_d = din("w2fm", [depth, NF2, 128, 8, 128])
    w1tm_d = din("w1tm", [depth, 128, 8, TM1])
    w2tm_d = din("w2tm", [depth, 128, 8, TM2])
    wbr_d = din("wbr", [depth, 8, 128, 3, 4, 128])
    wo_d = din("wo", [depth, 8, 128, 8, 128])
    ngain_d = din("ngain", [128, depth, 8])
    lbl_d = din("lbl", [128, DEPTH, 2, 4])
    hgain_d = din("hgain", [128, depth, 4])
    lng_d = din("lng", [128, depth, 512])
    lnb_d = din("lnb", [128, depth, 512])
    wsT_d = din("wsT", [128, depth, 4, 128])
    bsp_d = din("bsp", [128, depth, 4, 128])
    qkg_d = din("qkg", [128, depth, 2])
    sink_d = din("sink", [128, depth, 8])
    cosT_d = din("cosT", [128, T])
    sinT_d = din("sinT", [128, T])
    c_ident_d = din("c_ident", [128, 128])
    c_bd_d = din("c_bd", [128, 128])
    c_rot_d = din("c_rot", [128, 128])
    c_amask_d = din("c_amask", [128, 3, 128])
    c_hmask_d = din("c_hmask", [128, 2, 64])
    sel_d = din("sel", [128, 2])
    c_cmask_d = din("c_cmask", [128, 512])
    c_hmask2_d = din("c_hmask2", [128, 2, 128])
    out_d = nc.dram_tensor("out", [D, T], F32, kind="ExternalOutput")

    w1fm_b = nc.dram_tensor("w1fm_b", [depth, NF1, 128, 8, 128], BF16)
    w2fm_b = nc.dram_tensor("w2fm_b", [depth, NF2, 128, 8, 128], BF16)
    w1tm_b = nc.dram_tensor("w1tm_b", [depth, 128, 8, TM1], BF16)
    w2tm_b = nc.dram_tensor("w2tm_b", [depth, 128, 8, TM2], BF16)
    wbr_b = nc.dram_tensor("wbr_b", [depth, 8, 128, 3, 4, 128], BF16)
    wo_b = nc.dram_tensor("wo_b", [depth, 8, 128, 8, 128], BF16)
    o1_d = nc.dram_tensor("o1_spill", [128, cfg.NB, 4, 128], F32)
    CCW = 512 + 256 + 128
    cc_in = [nc.dram_tensor(f"cc_in{l}", [128, CCW], F32) for l in range(depth)]
    cc_out = [nc.dram_tensor(f"cc_out{l}", [256, CCW], F32) for l in range(depth)]

    from contextlib import ExitStack
    es = ExitStack()
    with es:
        S = Sched(nc, depth + 1)

        def sb(name, shape, dt=F32):
            return es.enter_context(nc.sbuf_tensor(name, list(shape), dt))

        def ps(name, shape, dt=F32):
            return es.enter_context(nc.psum_tensor(name, list(shape), dt))

        def sem(name):
            return es.enter_context(nc.semaphore(name))

        engsems = {(e, ep): sem(f"s_{e}_{ep}") for e in ("pe", "act", "dve", "pool") for ep in range(depth + 1)}
        _slot_n = [0]

        def slot(name):
            _slot_n[0] += 1
            return Slot(sem(f"d_{name}_{_slot_n[0]}"), name)

        ident_b = sb("ident_b", [128, 128], BF16)
        ones_b = sb("ones_b", [128, 128], BF16)
        bd_b = sb("bd_b", [128, 128], BF16)
        rot_b = sb("rot_b", [128, 128], BF16)
        amask = sb("amask", [128, 3, 128], BF16)
        hmask = sb("hmask", [128, 2, 64], BF16)
        selt = sb("selt", [128, 2])
        ngain = sb("ngain_s", [128, depth, 8])
        lbl = sb("lbl_s", [128, DEPTH, 2, 4])
        hgain = sb("hgain_s", [128, depth, 4])
        qkg = sb("qkg_s", [128, depth, 2])
        sinkt = sb("sink_s", [128, depth, 8])
        esink = sb("esink", [128, depth, 8])
        lbc1 = sb("lbc1", [128, DEPTH, 2, 4])
        lbc0 = sb("lbc0", [128, DEPTH, 2, 4])
        B_const = Buf("const")
        sl_c = slot("const")

        def dma(eng, out, in_, reads, writes, sl):
            return S.op(eng, I("dma_start", out=out, in_=in_), reads=reads, writes=writes, slot=sl)

        for dst, src in ((ident_b, c_ident_d), (bd_b, c_bd_d), (rot_b, c_rot_d), (amask, c_amask_d), (hmask, c_hmask_d)):
            dma("pool", dst[:], src.ap(), [], [B_const], sl_c)
        for dst, src in ((selt, sel_d), (ngain, ngain_d), (lbl, lbl_d), (hgain, hgain_d), (qkg, qkg_d), (sinkt, sink_d)):
            dma("sp", dst[:], src.ap(), [], [B_const], sl_c)
        S.op("dve", I("memset", ones_b[:], 1.0), [], [B_const])
        S.op("act", I("activation", out=esink[:], in_=sinkt[:], func=AF.Exp), [B_const], [B_const])
        lbe = sb("lbe", [128, DEPTH, 8])
        lbs = sb("lbs", [128, 8])
        lbv = lbl[:].rearrange("p l a b -> p l (a b)")
        S.op("act", I("activation", out=lbe[:], in_=lbv, func=AF.Exp), [B_const], [B_const])
        S.op("dve", I("tensor_tensor", out=lbs[:], in0=lbe[:, 0, :], in1=lbe[:, 1, :], op=ALU.add), [B_const], [B_const])
        S.op("dve", I("tensor_tensor", out=lbs[:], in0=lbs[:], in1=lbe[:, 2, :], op=ALU.add), [B_const], [B_const])
        S.op("dve", I("tensor_tensor", out=lbs[:], in0=lbs[:], in1=lbe[:, 3, :], op=ALU.add), [B_const], [B_const])
        S.op("dve", I("reciprocal", out=lbs[:], in_=lbs[:]), [B_const], [B_const])
        c1v = lbc1[:].rearrange("p l a b -> p l (a b)")
        c0v = lbc0[:].rearrange("p l a b -> p l (a b)")
        S.op("dve", I("memset", c0v[:, 0, :], 0.0), [B_const], [B_const])
        for l in range(1, DEPTH):
            S.op("dve", I("tensor_tensor", out=c1v[:, l, :], in0=lbe[:, l, :], in1=lbs[:], op=ALU.mult), [B_const], [B_const])
            S.op("dve", I("tensor_tensor", out=c0v[:, l, :], in0=c0v[:, l - 1, :], in1=c1v[:, l, :], op=ALU.add), [B_const], [B_const])
        S.op("dve", I("tensor_scalar", out=lbc1[:], in0=lbc0[:], scalar1=-0.5, scalar2=0.5, op0=ALU.mult, op1=ALU.add), [B_const], [B_const])
        S.op("dve", I("tensor_scalar", out=lbc0[:], in0=lbc0[:], scalar1=0.5, scalar2=0.5, op0=ALU.mult, op1=ALU.add), [B_const], [B_const])

        B_wcast = [Buf(f"wcast{l}") for l in range(depth)]
        sl_wc = [slot(f"wc{i}") for i in range(4)]
        _wc_i = [0]

        def wcast(l, dst, src):
            sl = sl_wc[_wc_i[0] % 4]
            _wc_i[0] += 1
            S.op("pool", I("dma_start", out=dst, in_=src), reads=[], writes=[B_wcast[l]], slot=sl)

        import os
        for l in range(depth if not os.environ.get("KSKIPCAST") else 0):
            def v4(t, j0, j1):
                return t[l, j0:j1].rearrange("a p k n -> (a p) (k n)")

            def v3(t):
                return t[l].rearrange("p k n -> p (k n)")

            for j in range(0, NF1, 4):
                wcast(l, v4(w1fm_b, j, j + 4), v4(w1fm_d, j, j + 4))
            wcast(l, v3(w1tm_b), v3(w1tm_d))
            for j in range(0, NF2, 6):
                wcast(l, v4(w2fm_b, j, j + 6), v4(w2fm_d, j, j + 6))
            wcast(l, v3(w2tm_b), v3(w2tm_d))
            wcast(l, wbr_b[l].rearrange("a p b k n -> (a p) (b k n)"), wbr_d[l].rearrange("a p b k n -> (a p) (b k n)"))
            wcast(l, v4(wo_b, 0, 8), v4(wo_d, 0, 8))

        NWB = 4
        wbuf = [sb(f"wbuf{i}", [128, 8, 128], BF16) for i in range(NWB)]
        B_wbuf = [Buf(f"wbuf{i}") for i in range(NWB)]
        sl_wbuf = [slot(f"wb{i}") for i in range(NWB)]
        _wb_i = [0]
        wtm = sb("wtm", [128, 8, TM2], BF16)
        B_wtm = Buf("wtm")
        sl_wtm = slot("wtm")
        wbrc = [sb(f"wbrc{i}", [128, 3, 4, 128], BF16) for i in range(2)]
        B_wbrc = [Buf(f"wbrc{i}") for i in range(2)]
        sl_wbrc = [slot(f"wbrc{i}") for i in range(2)]
        _wbrc_i = [0]
        sl_wl = slot("wl")
        lng = sb("lng_s", [128, 512])
        lnb = sb("lnb_s", [128, 512])
        wsT = sb("wsT_s", [128, 4, 128], BF16)
        bsp = sb("bsp_s", [128, 4, 128])
        B_lw = Buf("layerw")

        xt = sb("xt", [128, 8, 512])
        B_xt = Buf("xt")
        sl_xt = slot("xt")
        hT = sb("hT", [128, 8, 512], BF16)
        B_hT = Buf("hT")
        rstd = sb("rstd", [128, 512])
        B_rstd = Buf("rstd")
        rtmp = sb("rtmp", [128, 512])
        B_rtmp = Buf("rtmp")
        mhalf = sb("mhalf", [128, 512])
        S.op("pool", I("memset", mhalf[:], -0.5), [], [B_const])

        zs = sb("zs", [128, 12, 512], BF16)
        B_zs = [Buf(f"zs{i}") for i in range(12)]
        ub = sb("ub", [128, 4, 512], BF16)
        B_ub = [Buf(f"ub{i}") for i in range(4)]
        yb = sb("yb", [128, 4, 512], BF16)
        B_yb = Buf("yb")
        ya = sb("ya", [128, 4, 512], BF16)
        B_ya = Buf("ya")
        yc = sb("yc", [128, 4, 512], BF16)
        B_yc = Buf("yc")
        mg = sb("mg", [128, 8, 512], BF16)
        B_mg = [Buf(f"mg{i}") for i in range(8)]
        sq = mg
        xo = sb("xo", [128, 8, 512])
        B_xo = Buf("xo")
        sl_xo = slot("xo")
        ya_scr = sb("qkscr", [128, 1024], BF16)
        B_ya_scr = Buf("qkscr")
        NTMP = 8
        tmpf = [sb(f"tmpf{i}", [128, 512]) for i in range(NTMP)]
        B_tmpf = [Buf(f"tmpf{i}") for i in range(NTMP)]
        _tf_i = [0]

        def get_tmp():
            i = _tf_i[0] % NTMP
            _tf_i[0] += 1
            return tmpf[i], B_tmpf[i]

        vn = sb("vn", [128, 512], BF16)
        B_vn = Buf("vn")
        bnst = sb("bnst", [128, 6])
        bnag = sb("bnag", [128, 2])
        B_bn = Buf("bn")
        mhalf1 = sb("mhalf1", [128, 1])
        S.op("pool", I("memset", mhalf1[:], -0.5), [], [B_const])

        NPP = 2
        pp = [ps(f"pp{i}", [128, 512]) for i in range(NPP)]
        B_pp = [Buf(f"pp{i}") for i in range(NPP)]
        _pp_i = [0]

        def get_pp():
            i = _pp_i[0] % NPP
            _pp_i[0] += 1
            return pp[i], B_pp[i]

        pst = ps("pst", [128, 512])
        B_pst = Buf("pst")
        pmx = ps("pmx", [128, 4, 128])
        B_pmx = Buf("pmx")
        pbr = [ps(f"pbr{i}", [128, 512]) for i in range(3)]
        B_pbr = [Buf(f"pbr{i}") for i in range(3)]

        DRAM_x = [Buf(f"dram_x{m}") for m in range(n_mt)]
        qr = sb("qr", [128, 4, 512], BF16)
        B_qr = [Buf(f"qr{i}") for i in range(4)]
        kz = sb("kz", [128, 2, 2, 768], BF16)
        B_kr = Buf("kr")
        S.op("pool", I("memset", kz[:], 0.0), [], [B_kr])
        vaug = sb("vaug", [128, 6, 2, 192], BF16)
        B_vaug = Buf("vaug")
        S.op("dve", I("memset", vaug[:, :, :, 64:128], 1.0), [], [B_vaug])
        pt = sb("pt", [128, 3, 2, 256], BF16)
        B_pt = Buf("pt")
        cs = sb("cs", [128, 2, 640])
        B_cs = Buf("cs")
        sl_cs = slot("cs")
        xh = sb("xh", [128, 8, 128])
        B_xh = Buf("xh")
        sl_xh = slot("xh")
        hTh = sb("hTh", [128, 8, 128], BF16)
        B_hTh = Buf("hTh")
        sqh = sb("sqh", [128, 8, 128], BF16)
        B_sqh = Buf("sqh")
        mone = sb("mone", [128, 256])
        S.op("pool", I("memset", mone[:], -1.0), [], [B_const])
        dtmp = sb("dtmp", [128, 256])
        B_dtmp = Buf("dtmp")
        xo_flat = xo[:].rearrange("p c t -> p (c t)")
        ccg = xo_flat[:, 0:2 * CCW].rearrange("p (r n) -> p r n", r=2)
        ccs = xo_flat[:, 2048:2048 + CCW]
        ccp = xo_flat[:, 3072:3072 + CCW]
        B_ccs = B_ccg = B_ccp = B_xo
        sl_cc = slot("cc")
        DRAM_cc = Buf("dram_cc")
        ccsem = sem("ccsem")
        cmask = sb("cmask", [128, 512])
        hmask2 = sb("hmask2", [128, 2, 128], BF16)
        dma("sp", cmask[:], c_cmask_d.ap(), [], [B_const], sl_c)
        dma("pool", hmask2[:], c_hmask2_d.ap(), [], [B_const], sl_c)
        qraw = sb("qraw", [128, 4, 512])
        B_qraw = Buf("qraw")
        qtT = sb("qtT", [128, 4, 512], BF16)
        B_qtT = Buf("qtT")
        ktT = sb("ktT", [128, 4, 512], BF16)
        B_ktT = Buf("ktT")
        ktA = sb("ktA", [128, 4, 128], BF16)
        ktB = sb("ktB", [128, 4, 128], BF16)
        B_kt = Buf("kt")
        S.op("pool", I("memset", ktA[:], 0.0), [], [B_kt])
        S.op("pool", I("memset", ktB[:], 0.0), [], [B_kt])
        vtok = sb("vtok", [128, 4, 4, 128], BF16)
        B_vtok = [Buf(f"vtok{i}") for i in range(4)]
        sc = sb("sc", [128, 4, 3, 8])
        B_sc = Buf("sc")
        sc8 = sb("sc8", [128, 8])
        B_sc8 = Buf("sc8")
        St = sb("St", [128, 4, 128])
        B_S = Buf("S")
        Sp = sb("Sp", [128, 4, 128], BF16)
        B_Sp = Buf("Sp")
        ATs = sb("ATs", [128, 4, 128], BF16)
        B_ATs = Buf("ATs")
        o1s = sb("o1s", [128, 4, 128])
        B_o1s = Buf("o1s")
        sl_o1 = slot("o1")
        DRAM_o1 = [Buf(f"dram_o1_{i}") for i in range(cfg.NB)]
        ptr = ps("ptr", [128, 4, 128], BF16)
        B_ptr = Buf("ptr")

        def hgrn_gates(l, dirn, srcw, base_a):
            for h in range(4):
                p, Bp = proj_fm(l, srcw, base_a + h)
                tA, BA = get_tmp()
                tK, BK = get_tmp()
                tB, BB = get_tmp()
                tD, BD = get_tmp()
                tE, BE = get_tmp()
                v = lambda t: t[:].rearrange("p (c t) -> p c t", t=64)
                S.op("act", I("activation", out=tA[:], in_=p[:], func=AF.Tanh, scale=0.5), [Bp], [BA])
                S.op("dve", I("tensor_scalar", out=tA[:], in0=tA[:], scalar1=lbc1[:, l, dirn, h:h + 1], scalar2=lbc0[:, l, dirn, h:h + 1], op0=ALU.mult, op1=ALU.add),
                     [BA, B_const], [BA])
                S.op("pool", I("tensor_scalar", out=tK[:], in0=tA[:], scalar1=-1.0, scalar2=1.0, op0=ALU.mult, op1=ALU.add), [BA], [BK])
                S.op("act", I("activation", out=tA[:], in_=tA[:], func=AF.Ln), [BA, BK], [BA])
                S.op("dve", I("tensor_tensor_scan", out=tB[:], data0=cmask[:], data1=tA[:], initial=0.0, op0=ALU.mult, op1=ALU.add), [BA, B_const], [BB])
                if dirn == 0:
                    S.op("dve", I("tensor_tensor", out=v(tD), in0=v(tB), in1=v(tB)[:, :, 31:32].to_broadcast([128, 8, 64]), op=ALU.subtract), [BB], [BD])
                    S.op("act", I("activation", out=sc[:, h, 0, :], in_=v(tB)[:, :, 31], func=AF.Exp), [BB], [B_sc])
                    S.op("act", I("activation", out=sc[:, h, 1, :], in_=v(tB)[:, :, 63], func=AF.Exp), [BB], [B_sc])
                    S.op("act", I("activation", out=sc[:, h, 2, :], in_=v(tD)[:, :, 63], func=AF.Exp), [BD], [B_sc])
                else:
                    S.op("dve", I("tensor_tensor", out=tA[:], in0=tB[:], in1=tA[:], op=ALU.subtract), [BB, BA], [BA])
                    S.op("dve", I("tensor_tensor", out=v(tD), in0=v(tA)[:, :, 32:33].to_broadcast([128, 8, 64]), in1=v(tA), op=ALU.subtract), [BA], [BD])
                    S.op("dve", I("tensor_tensor", out=sc8[:], in0=v(tB)[:, :, 63], in1=v(tA)[:, :, 32], op=ALU.subtract), [BB, BA], [B_sc8])
                    S.op("act", I("activation", out=sc[:, h, 0, :], in_=sc8[:], func=AF.Exp), [B_sc8], [B_sc])
                    S.op("act", I("activation", out=sc[:, h, 1, :], in_=v(tB)[:, :, 63], func=AF.Exp), [BB], [B_sc])
                    S.op("act", I("activation", out=sc[:, h, 2, :], in_=v(tD)[:, :, 0], func=AF.Exp), [BD], [B_sc])
                S.op("act", I("activation", out=tE[:], in_=tD[:], func=AF.Exp), [BD], [BE])
                S.op("dve", I("tensor_tensor", out=qtT[:, h, :], in0=qraw[:, h, :], in1=tE[:], op=ALU.mult), [B_qraw, BE], [B_qtT])
                S.op("act", I("activation", out=tB[:], in_=tD[:], func=AF.Exp, scale=-1.0), [BD, B_sc, B_sc8], [BB])
                S.op("pool", I("tensor_tensor", out=ktT[:, h, :], in0=tK[:], in1=tB[:], op=ALU.mult), [BK, BB], [B_ktT])

        def hgrn_q_and_v(l, srcw, base_q):
            for h in range(4):
                p, Bp = proj_fm(l, srcw, base_q + h)
                S.op("act", I("activation", out=qraw[:, h, :], in_=p[:], func=AF.Copy), [Bp], [B_qraw])

        def hgrn_vtok(bi):
            tsl = slice(bi * 128, (bi + 1) * 128)
            p, Bp = get_pp()
            for k in range(8):
                S.op("pe", I("matmul", p[:], lhsT=hT[:, k, tsl], rhs=wtm[:, k, 0:512], start=(k == 0), stop=(k == 7)), [B_hT, B_wtm], [Bp])
            S.op("act", I("activation", out=vtok[:, bi, :, :], in_=p[:].rearrange("p (h d) -> p h d", h=4), func=AF.Copy), [Bp], [B_vtok[bi]])

        def hgrn_block(l, dirn, bi, m):
            tsl = slice(bi * 128, (bi + 1) * 128)
            psc, B_psc = pbr[0][:].rearrange("p (h t) -> p h t", h=4), B_pbr[0]
            po, B_po = pbr[1][:].rearrange("p (h t) -> p h t", h=4), B_pbr[1]
            pob, B_pob = pbr[2][:].rearrange("p (h t) -> p h t", h=4), B_pbr[2]
            pS, B_pS = pmx[:], B_pmx
            for h in range(4):
                S.op("pe", I("transpose", out=ptr[:, h, :], in_=ktT[:, h, tsl], identity=ident_b[:]), [B_ktT, B_const], [B_ptr])
            S.op("dve", I("tensor_copy", out=ktA[0:64, :, :], in_=ptr[0:64, :, :]), [B_ptr], [B_kt])
            S.op("act", I("activation", out=ktB[64:128, :, :], in_=ptr[64:128, :, :], func=AF.Copy), [B_ptr], [B_kt])
            for h in range(4):
                S.op("pe", I("matmul", psc[:, h, :], lhsT=ktT[:, h, tsl], rhs=qtT[:, h, tsl], start=True, stop=True), [B_ktT, B_qtT], [B_psc])
            S.op("dve", I("tensor_tensor", out=ATs[:], in0=psc, in1=hmask2[:, dirn:dirn + 1, :].to_broadcast([128, 4, 128]), op=ALU.mult), [B_psc, B_const], [B_ATs])
            for h in range(4):
                S.op("pe", I("matmul", po[:, h, :], lhsT=vtok[:, bi, h, :], rhs=ATs[:, h, :], start=True, stop=True), [B_vtok[bi], B_ATs], [B_po])
            chunks = [(0, ktA), (1, ktB)] if dirn == 0 else [(1, ktB), (0, ktA)]
            for ci, (c, kt) in enumerate(chunks):
                gc = bi * 2 + c
                csl = slice(bi * 128 + c * 64, bi * 128 + (c + 1) * 64)
                for h in range(4):
                    S.op("dve", I("tensor_scalar", out=Sp[:, h, :], in0=St[:, h, :], scalar1=sc[:, h, 0, gc:gc + 1], scalar2=None, op0=ALU.mult), [B_S, B_sc], [B_Sp])
                for h in range(4):
                    S.op("pe", I("matmul", pob[:, h, c * 64:(c + 1) * 64], lhsT=Sp[:, h, :], rhs=qtT[:, h, csl], start=True, stop=True), [B_Sp, B_qtT], [B_pob])
                for h in range(4):
                    S.op("pe", I("matmul", pS[:, h, :], lhsT=kt[:, h, :], rhs=vtok[:, bi, h, :], start=True, stop=True), [B_kt, B_vtok[bi]], [B_pS])
                for h in range(4):
                    S.op("dve", I("tensor_scalar", out=St[:, h, :], in0=St[:, h, :], scalar1=sc[:, h, 1, gc:gc + 1], scalar2=None, op0=ALU.mult), [B_S, B_sc], [B_S])
                    S.op("dve", I("scalar_tensor_tensor", out=St[:, h, :], in0=pS[:, h, :], scalar=sc[:, h, 2, gc:gc + 1], in1=St[:, h, :], op0=ALU.mult, op1=ALU.add),
                         [B_pS, B_sc, B_S], [B_S])
            gblk = m * 4 + bi
            if dirn == 0:
                S.op("act", I("activation", out=o1s[:], in_=po, func=AF.Copy), [B_po], [B_o1s])
                S.op("dve", I("tensor_tensor", out=o1s[:], in0=o1s[:], in1=pob, op=ALU.add), [B_o1s, B_pob], [B_o1s])
                dma("sp", o1_d[:, gblk], o1s[:], [B_o1s], [DRAM_o1[gblk]], sl_o1)
            else:
                dma("sp", o1s[:], o1_d[:, gblk], [DRAM_o1[gblk]], [B_o1s], sl_o1)
                osum, Bos = get_tmp()
                rt, Brt = get_tmp()
                osv = osum[:].rearrange("p (h t) -> p h t", h=4)
                S.op("dve", I("tensor_tensor", out=osv, in0=po, in1=o1s[:], op=ALU.add), [B_po, B_o1s], [Bos])
                S.op("dve", I("tensor_tensor", out=osv, in0=osv, in1=pob, op=ALU.add), [Bos, B_pob], [Bos])
                S.op("act", I("activation", out=ya_scr[:, 0:512], in_=osum[:], func=AF.Square), [Bos], [B_ya_scr])
                S.op("pe", I("matmul", pst[:], lhsT=ones_b[:], rhs=ya_scr[:, 0:512], start=True, stop=True), [B_ya_scr, B_const], [B_pst])
                S.op("dve", I("tensor_scalar", out=rt[:], in0=pst[:], scalar1=1.0 / 128, scalar2=EPS, op0=ALU.mult, op1=ALU.add), [B_pst], [Brt])
                S.op("pool", I("tensor_tensor", out=rt[:], in0=rt[:], in1=mhalf[:], op=ALU.pow), [Brt, B_const], [Brt])
                S.op("dve", I("tensor_tensor", out=osum[:], in0=osum[:], in1=rt[:], op=ALU.mult), [Bos, Brt], [Bos])
                for h in range(4):
                    S.op("dve", I("scalar_tensor_tensor", out=ya[:, h, tsl], in0=osv[:, h, :], scalar=hgain[:, l, h:h + 1], in1=zs[:, h, tsl], op0=ALU.mult, op1=ALU.mult),
                         [Bos, B_const, B_zs[h]], [B_ya])

        def sweep1_mt(l, m):
            norm_mt(l, m)
            hgrn_q_and_v(l, w1fm_b, 4)
            hgrn_gates(l, 0, w1fm_b, 0)
            for bi in range(4):
                hgrn_vtok(bi)
                hgrn_block(l, 0, bi, m)
        sl_out = slot("out")

        def xsrc(l):
            return xT_d if l == 0 else out_d

        xview = lambda t, m: t.ap().rearrange("(c p) t -> p c t", p=128)[:, :, m * 512:(m + 1) * 512]

        def load_w_chunk(l, src_b, j):
            i = _wb_i[0] % NWB
            _wb_i[0] += 1
            dma("sp", wbuf[i][:], src_b[l, j], [B_wcast[l]], [B_wbuf[i]], sl_wbuf[i])
            return wbuf[i], B_wbuf[i]

        def tanh_gate(dst, src, scale):
            return I("activation", out=dst, in_=src, func=AF.Tanh, scale=scale)

        def norm_mt(l, m):
            dma("sp", xt[:], xview(xsrc(l), m), [DRAM_x[m]], [B_xt], sl_xt)
            for c in range(8):
                S.op("act", I("activation", out=sq[:, c, :], in_=xt[:, c, :], func=AF.Square), [B_xt], [B_mg[c]])
            for c in range(8):
                S.op("pe", I("matmul", pst[:], lhsT=ones_b[:], rhs=sq[:, c, :], start=(c == 0), stop=(c == 7)),
                     [B_mg[c], B_const], [B_pst])
            S.op("dve", I("tensor_scalar", out=rtmp[:], in0=pst[:], scalar1=1.0 / D, scalar2=EPS, op0=ALU.mult, op1=ALU.add),
                 [B_pst], [B_rtmp])
            S.op("pool", I("tensor_tensor", out=rstd[:], in0=rtmp[:], in1=mhalf[:], op=ALU.pow), [B_rtmp, B_const], [B_rstd])
            for c in range(8):
                S.op("dve", I("scalar_tensor_tensor", out=hT[:, c, :], in0=xt[:, c, :], scalar=ngain[:, l, c:c + 1],
                                                                   in1=rstd[:], op0=ALU.mult, op1=ALU.mult),
                     [B_xt, B_rstd, B_const], [B_hT])

        def proj_fm(l, src_b, j):
            w, Bw = load_w_chunk(l, src_b, j)
            p, Bp = get_pp()
            for k in range(8):
                S.op("pe", I("matmul", p[:], lhsT=w[:, k, :], rhs=hT[:, k, :], start=(k == 0), stop=(k == 7)),
                     [Bw, B_hT], [Bp])
            return p, Bp

        def layer_weights(l):
            dma("sp", wtm[:], w2tm_b[l], [B_wcast[l]], [B_wtm], sl_wtm)
            dma("sp", lng[:], lng_d[:, l, :], [], [B_lw], sl_wl)
            dma("sp", lnb[:], lnb_d[:, l, :], [], [B_lw], sl_wl)
            dma("sp", bsp[:], bsp_d[:, l], [], [B_lw], sl_wl)
            dma("pool", wsT[:], wsT_d[:, l], [], [B_lw], sl_wl)

        GELU_C = 0.7978845608028654

        def gelu_from_psum(p, Bp, dst, Bdst, eng2="pool"):
            t1, B1 = get_tmp()
            t2, B2 = get_tmp()
            S.op("act", I("activation", out=t1[:], in_=p[:], func=AF.Square), [Bp], [B1])
            S.op("dve", I("tensor_scalar", out=t1[:], in0=t1[:], scalar1=0.044715, scalar2=1.0, op0=ALU.mult, op1=ALU.add), [B1], [B1])
            S.op("dve", I("tensor_tensor", out=t2[:], in0=t1[:], in1=p[:], op=ALU.mult), [B1, Bp], [B2])
            S.op("act", I("activation", out=t1[:], in_=t2[:], func=AF.Tanh, scale=GELU_C), [B2], [B1])
            S.op("dve", I("scalar_tensor_tensor", out=dst, in0=t1[:], scalar=1.0, in1=p[:], op0=ALU.add, op1=ALU.mult), [B1, Bp], [Bdst])

        def silu_from_psum(p, Bp, dst, Bdst):
            t1, B1 = get_tmp()
            S.op("act", I("activation", out=t1[:], in_=p[:], func=AF.Tanh, scale=0.5), [Bp], [B1])
            S.op("dve", I("scalar_tensor_tensor", out=dst, in0=t1[:], scalar=1.0, in1=p[:], op0=ALU.add, op1=ALU.mult), [B1, Bp], [Bdst])

        def sweep2_mt(l, m):
            norm_mt(l, m)
            base = {}
            cnt = 0
            for kind in FM2:
                base.setdefault(kind, cnt)
                cnt += 1
            if cfg.do_b:
                for j in range(4):
                    p, Bp = proj_fm(l, w2fm_b, base["uB"] + j)
                    gelu_from_psum(p, Bp, ub[:, j, :], B_ub[j])
                for j in range(4):
                    p, Bp = proj_fm(l, w2fm_b, base["zB"] + j)
                    silu_from_psum(p, Bp, zs[:, 4 + j, :], B_zs[4 + j])
                for blk in range(4):
                    tsl = slice(blk * 128, (blk + 1) * 128)
                    p, Bp = get_pp()
                    for k in range(8):
                        S.op("pe", I("matmul", p[:], lhsT=hT[:, k, tsl], rhs=wtm[:, k, 512:1024],
                                                                          start=(k == 0), stop=(k == 7)), [B_hT, B_wtm], [Bp])
                    vt, B_vt = get_tmp()
                    vt2, B_vt2 = get_tmp()
                    gelu_from_psum(p, Bp, vt[:], B_vt)
                    S.op("dve", I("bn_stats", out=bnst[:], in_=vt[:]), [B_vt], [B_bn])
                    S.op("dve", I("bn_aggr", out=bnag[:], in_=bnst[:]), [B_bn], [B_bn])
                    S.op("dve", I("tensor_scalar", out=bnag[:, 1:2], in0=bnag[:, 1:2], scalar1=0.25, scalar2=EPS, op0=ALU.mult, op1=ALU.add),
                         [B_bn], [B_bn])
                    S.op("pool", I("tensor_tensor", out=bnag[:, 1:2], in0=bnag[:, 1:2], in1=mhalf1[:], op=ALU.pow), [B_bn, B_const], [B_bn])
                    S.op("dve", I("tensor_scalar", out=bnag[:, 1:2], in0=bnag[:, 1:2], scalar1=0.5, scalar2=None, op0=ALU.mult), [B_bn], [B_bn])
                    S.op("dve", I("tensor_scalar", out=vt2[:], in0=vt[:], scalar1=bnag[:, 0:1], scalar2=bnag[:, 1:2],
                                                          op0=ALU.subtract, op1=ALU.mult), [B_vt, B_bn], [B_vt2])
                    S.op("dve", I("tensor_tensor", out=vt2[:], in0=vt2[:], in1=lng[:], op=ALU.mult), [B_vt2, B_lw], [B_vt2])
                    S.op("dve", I("tensor_tensor", out=vn[:], in0=vt2[:], in1=lnb[:], op=ALU.add), [B_vt2, B_lw], [B_vn])
                    for g in range(4):
                        S.op("pe", I("matmul", pmx[:, g, :], lhsT=vn[:, g * 128:(g + 1) * 128], rhs=wsT[:, g, :], start=True, stop=True),
                             [B_vn, B_lw], [B_pmx])
                    t1, B1 = get_tmp()
                    t1v = t1[:].rearrange("p (g t) -> p g t", g=4)
                    S.op("dve", I("tensor_tensor", out=t1v, in0=pmx[:], in1=bsp[:], op=ALU.add), [B_pmx, B_lw], [B1])
                    S.op("pool", I("tensor_tensor", out=t1v, in0=t1v, in1=ub[:, :, tsl], op=ALU.mult), [B1] + B_ub, [B1])
                    S.op("dve", I("tensor_tensor", out=yb[:, :, tsl], in0=t1v, in1=zs[:, 4:8, tsl], op=ALU.mult),
                         [B1] + B_zs[4:8], [B_yb])
            if cfg.do_c:
                top = (m == n_mt - 1)
                has_lo = (m > 0)
                t0 = m * 512 - 128
                if has_lo:
                    dma("sp", cs[:, 0, :], cosT_d[:, t0:t0 + 640], [], [B_cs], sl_cs)
                    dma("sp", cs[:, 1, :], sinT_d[:, t0:t0 + 640], [], [B_cs], sl_cs)
                    dma("sp", xh[:], xsrc(l).ap().rearrange("(c p) t -> p c t", p=128)[:, :, t0:t0 + 128], [DRAM_x[m - 1]], [B_xh], sl_xh)
                    for c in range(8):
                        S.op("act", I("activation", out=sqh[:, c, :], in_=xh[:, c, :], func=AF.Square), [B_xh], [B_sqh])
                    for c in range(8):
                        S.op("pe", I("matmul", pst[:, 0:128], lhsT=ones_b[:], rhs=sqh[:, c, :], start=(c == 0), stop=(c == 7)),
                             [B_sqh, B_const], [B_pst])
                    S.op("dve", I("tensor_scalar", out=rtmp[:, 0:128], in0=pst[:, 0:128], scalar1=1.0 / D, scalar2=EPS, op0=ALU.mult, op1=ALU.add),
                         [B_pst], [B_rtmp])
                    S.op("pool", I("tensor_tensor", out=rtmp[:, 128:256], in0=rtmp[:, 0:128], in1=mhalf[:, 0:128], op=ALU.pow), [B_rtmp, B_const], [B_rtmp])
                    for c in range(8):
                        S.op("dve", I("scalar_tensor_tensor", out=hTh[:, c, :], in0=xh[:, c, :], scalar=ngain[:, l, c:c + 1],
                                                                           in1=rtmp[:, 128:256], op0=ALU.mult, op1=ALU.mult),
                             [B_xh, B_rtmp, B_const], [B_hTh])
                else:
                    dma("sp", cs[:, 0, 128:640], cosT_d[:, 0:512], [], [B_cs], sl_cs)
                    dma("sp", cs[:, 1, 128:640], sinT_d[:, 0:512], [], [B_cs], sl_cs)
                if not top:
                    S.op("pool", I("tensor_copy", out=kz[:, :, :, 640:768], in_=kz[:, :, :, 128:256]), [B_kr], [B_kr])
                    S.op("pool", I("tensor_copy", out=vaug[:, 5, :, :], in_=vaug[:, 1, :, :]), [B_vaug], [B_vaug])

                def qk_post(p, Bp, n, which, dst, Bdst, csl):
                    t1, B1 = get_tmp()
                    t2, B2 = get_tmp()
                    t3, B3 = get_tmp()
                    sqb, Bsqb = ya_scr, B_ya_scr
                    S.op("act", I("activation", out=sqb[:, 0:n], in_=p[:, 0:n], func=AF.Square), [Bp], [Bsqb])
                    S.op("dve", I("tensor_scalar", out=sqb[:, 512:512 + n], in0=p[:, 0:n], scalar1=qkg[:, l, which:which + 1], scalar2=None, op0=ALU.mult),
                         [Bp, B_const], [Bsqb])
                    S.op("pe", I("matmul", pst[:, 0:n], lhsT=bd_b[:], rhs=sqb[:, 0:n], start=True, stop=True), [Bsqb, B_const], [B_pst])
                    pr, Bpr = get_pp()
                    S.op("pe", I("matmul", pr[:, 0:n], lhsT=rot_b[:], rhs=sqb[:, 512:512 + n], start=True, stop=True), [Bsqb, B_const], [Bpr])
                    S.op("dve", I("tensor_scalar", out=t1[:, 0:n], in0=pst[:, 0:n], scalar1=1.0 / 64, scalar2=EPS, op0=ALU.mult, op1=ALU.add), [B_pst], [B1])
                    S.op("pool", I("tensor_tensor", out=t1[:, 0:n], in0=t1[:, 0:n], in1=mhalf[:, 0:n], op=ALU.pow), [B1, B_const], [B1])
                    S.op("pool", I("tensor_tensor", out=t2[:, 0:n], in0=sqb[:, 512:512 + n], in1=cs[:, 0, csl], op=ALU.mult), [Bsqb, B_cs], [B2])
                    S.op("dve", I("tensor_tensor", out=t3[:, 0:n], in0=pr[:, 0:n], in1=cs[:, 1, csl], op=ALU.mult), [Bpr, B_cs], [B3])
                    S.op("dve", I("tensor_tensor", out=t2[:, 0:n], in0=t2[:, 0:n], in1=t3[:, 0:n], op=ALU.add), [B2, B3], [B2])
                    if which == 0:
                        S.op("dve", I("tensor_tensor", out=dst, in0=t2[:, 0:n], in1=t1[:, 0:n], op=ALU.mult), [B2, B1], [Bdst])
                    else:
                        jh, c0 = dst
                        S.op("dve", I("tensor_tensor", out=kz[0:64, 0, jh, c0:c0 + n], in0=t2[0:64, 0:n], in1=t1[0:64, 0:n], op=ALU.mult), [B2, B1], [Bdst])
                        S.op("dve", I("tensor_tensor", out=kz[64:128, 1, jh, c0:c0 + n], in0=t2[64:128, 0:n], in1=t1[64:128, 0:n], op=ALU.mult), [B2, B1], [Bdst])

                for j in range(4):
                    p, Bp = proj_fm(l, w2fm_b, base["qC"] + j)
                    qk_post(p, Bp, 512, 0, qr[:, j, :], B_qr[j], slice(128, 640))
                for j in range(2):
                    w, Bw = load_w_chunk(l, w2fm_b, base["kC"] + j)
                    p, Bp = get_pp()
                    for k in range(8):
                        S.op("pe", I("matmul", p[:], lhsT=w[:, k, :], rhs=hT[:, k, :], start=(k == 0), stop=(k == 7)), [Bw, B_hT], [Bp])
                    qk_post(p, Bp, 512, 1, (j, 128), B_kr, slice(128, 640))
                    if has_lo:
                        p, Bp = get_pp()
                        for k in range(8):
                            S.op("pe", I("matmul", p[:, 0:128], lhsT=w[:, k, :], rhs=hTh[:, k, :], start=(k == 0), stop=(k == 7)), [Bw, B_hTh], [Bp])
                        qk_post(p, Bp, 128, 1, (j, 0), B_kr, slice(0, 128))
                for j in range(4):
                    p, Bp = proj_fm(l, w2fm_b, base["zC"] + j)
                    silu_from_psum(p, Bp, zs[:, 8 + j, :], B_zs[8 + j])
                for sl_i in ([0] if has_lo else []) + [1, 2, 3, 4]:
                    p, Bp = get_pp()
                    for k in range(8):
                        if sl_i == 0:
                            S.op("pe", I("matmul", p[:, 0:128], lhsT=hTh[:, k, :], rhs=wtm[:, k, 1024:1152], start=(k == 0), stop=(k == 7)),
                                 [B_hTh, B_wtm], [Bp])
                        else:
                            tsl = slice((sl_i - 1) * 128, sl_i * 128)
                            S.op("pe", I("matmul", p[:, 0:128], lhsT=hT[:, k, tsl], rhs=wtm[:, k, 1024:1152], start=(k == 0), stop=(k == 7)),
                                 [B_hT, B_wtm], [Bp])
                    pv = p[:, 0:128].rearrange("p (h d) -> p h d", h=2)
                    S.op("act", I("activation", out=vaug[:, sl_i, :, 0:64], in_=pv, func=AF.Copy), [Bp], [B_vaug])
                    S.op("dve", I("tensor_copy", out=vaug[:, sl_i, :, 128:192], in_=pv), [Bp], [B_vaug])
                kc_stage = int(os.environ.get("KC_STAGE", "9"))
                if top and kc_stage >= 2:
                    S.op("dve", I("tensor_copy", out=ccs[0:64, 512:768].rearrange("p (h t) -> p h t", h=2), in_=kz[0:64, 0, :, 512:640]), [B_kr], [B_ccs])
                    S.op("dve", I("tensor_copy", out=ccs[64:128, 512:768].rearrange("p (h t) -> p h t", h=2), in_=kz[64:128, 1, :, 512:640]), [B_kr], [B_ccs])
                    S.op("dve", I("tensor_copy", out=ccs[:, 768:896].rearrange("p (h d) -> p h d", h=2), in_=vaug[:, 4, :, 0:64]), [B_vaug], [B_ccs])
                    if not cfg.do_a:
                        S.op("dve", I("memset", ccs[:, 0:512], 0.0), [], [B_ccs])
                    else:
                        S.op("dve", I("tensor_copy", out=ccs[:, 0:512], in_=St[:].rearrange("p h v -> p (h v)")), [B_S], [B_ccs])
                    dma("pool", cc_in[l].ap(), ccs, [B_ccs], [DRAM_cc], sl_cc)
                    o = S.op("pool", I("collective_compute", "AllGather", ALU.bypass, replica_groups=[[0, 1], [2, 3], [4, 5], [6, 7]],
                                                                  ins=[cc_in[l].ap().opt()], outs=[cc_out[l].ap().opt()]), [DRAM_cc], [DRAM_cc])
                    o.signal = True
                    dma("pool", ccg, cc_out[l].ap().rearrange("(r p) n -> p r n", p=128), [DRAM_cc], [B_ccg], sl_cc)
                    S.op("dve", I("tensor_scalar", out=ccp, in0=ccg[:, 0, :], scalar1=selt[:, 0:1], scalar2=None, op0=ALU.mult), [B_ccg, B_const], [B_ccp])
                    S.op("dve", I("scalar_tensor_tensor", out=ccp, in0=ccg[:, 1, :], scalar=selt[:, 1:2], in1=ccp, op0=ALU.mult, op1=ALU.add),
                         [B_ccg, B_const, B_ccp], [B_ccp])
                    S.op("dve", I("tensor_copy", out=kz[0:64, 0, :, 640:768], in_=ccp[0:64, 512:768].rearrange("p (h t) -> p h t", h=2)), [B_ccp], [B_kr])
                    S.op("dve", I("tensor_copy", out=kz[64:128, 1, :, 640:768], in_=ccp[64:128, 512:768].rearrange("p (h t) -> p h t", h=2)), [B_ccp], [B_kr])
                    S.op("dve", I("tensor_copy", out=vaug[:, 5, :, 0:64], in_=ccp[:, 768:896].rearrange("p (h d) -> p h d", h=2)), [B_ccp], [B_vaug])
                    S.op("dve", I("tensor_copy", out=vaug[:, 5, :, 128:192], in_=ccp[:, 768:896].rearrange("p (h d) -> p h d", h=2)), [B_ccp], [B_vaug])
                    if cfg.do_a:
                        S.op("dve", I("tensor_copy", out=St[:].rearrange("p h v -> p (h v)"), in_=ccp[:, 0:512]), [B_ccp], [B_S])
                for sb_i in ((4, 3, 2, 1) if kc_stage >= 3 else ()):
                    jglob = m * 4 + sb_i - 1
                    tsl = slice((sb_i - 1) * 128, sb_i * 128)
                    kbs = []
                    if jglob > 0:
                        kbs.append((sb_i - 1, 0))
                    kbs.append((sb_i, None))
                    kbs.append((sb_i + 1, 2 if (top and sb_i == 4) else 1))
                    for h in range(2):
                        pssv = [pbr[i][:].rearrange("p (e n) -> p e n", e=2) for i in range(3)]
                        for ki, (slk, mk) in enumerate(kbs):
                            for e_ in range(2):
                                rows = slice(e_ * 64, (e_ + 1) * 64)
                                S.op("pe", I("matmul",
                                    pssv[ki][:, e_, :].rearrange("p (c t) -> p c t", c=2), lhsT=kz[:, e_, h, slk * 128:(slk + 1) * 128],
                                    rhs=qr[:, 2 * h:2 * h + 2, tsl], start=True, stop=True),
                                    [B_kr] + B_qr, [B_pbr[ki]])
                            S.op("act", I("activation", out=pt[:, ki, :, :], in_=pssv[ki], func=AF.Exp, scale=0.125), [B_pbr[ki]], [B_pt])
                            if mk is not None and kc_stage >= 4:
                                ptv = pt[:, ki, :, :].rearrange("p e (c t) -> p (e c) t", c=2)
                                S.op("pool", I("tensor_tensor", out=ptv, in0=ptv, in1=amask[:, mk:mk + 1, :].to_broadcast([128, 4, 128]), op=ALU.mult),
                                     [B_pt, B_const], [B_pt])
                        pso = pmx[:].rearrange("p a b -> p (a b)").rearrange("p (e n) -> p e n", e=2)
                        for e_ in (range(2) if kc_stage >= 5 else ()):
                            for ki, (slk, mk) in enumerate(kbs):
                                S.op("pe", I("matmul", pso[:, e_, :], lhsT=vaug[:, slk, h, e_ * 64:e_ * 64 + 128], rhs=pt[:, ki, e_, :],
                                                                                       start=(ki == 0), stop=(ki == len(kbs) - 1)), [B_vaug, B_pt], [B_pmx])
                        for e_ in (range(2) if kc_stage >= 6 else ()):
                            nr = slice(0, 64) if e_ == 0 else slice(64, 128)
                            dr = slice(64, 128) if e_ == 0 else slice(0, 64)
                            for c in range(2):
                                head = 2 * (2 * h + c) + e_
                                S.op("dve", I("tensor_scalar",
                                    out=dtmp[nr, c * 128:(c + 1) * 128], in0=pso[dr, e_, c * 128:(c + 1) * 128], scalar1=esink[dr, l, head:head + 1], scalar2=None, op0=ALU.add),
                                    [B_pmx, B_const], [B_dtmp])
                            S.op("pool", I("tensor_tensor", out=dtmp[nr, :], in0=dtmp[nr, :], in1=mone[nr, :], op=ALU.pow), [B_dtmp, B_const], [B_dtmp])
                            S.op("dve", I("tensor_tensor", out=dtmp[nr, :], in0=pso[nr, e_, :], in1=dtmp[nr, :], op=ALU.mult), [B_pmx, B_dtmp], [B_dtmp])
                            S.op("pool", I("tensor_tensor", out=yc[nr, 2 * h:2 * h + 2, tsl], in0=dtmp[nr, :].rearrange("p (c t) -> p c t", c=2),
                                                                                    in1=zs[nr, 8 + 2 * h:8 + 2 * h + 2, tsl], op=ALU.mult),
                                 [B_dtmp] + B_zs[8:12], [B_yc])
            if cfg.do_a:
                hgrn_q_and_v(l, w2fm_b, base["qA"])
                hgrn_gates(l, 1, w2fm_b, base["a2"])
                for j in range(4):
                    p, Bp = proj_fm(l, w2fm_b, base["zA"] + j)
                    silu_from_psum(p, Bp, zs[:, j, :], B_zs[j])
                for bi in (3, 2, 1, 0):
                    hgrn_vtok(bi)
                    hgrn_block(l, 1, bi, m)
            scal = {0: 0.25, 1: 0.125, 2: 0.25}
            for dc in range(8):
                dsl = slice(dc * 128, (dc + 1) * 128)
                acc, Bacc = get_tmp()
                first = True
                wi = _wbrc_i[0] % 2
                _wbrc_i[0] += 1
                dma("sp", wbrc[wi][:], wbr_b[l, dc], [B_wcast[l]], [B_wbrc[wi]], sl_wbrc[wi])
                for bi, (on, ysrc, By) in enumerate(((cfg.do_a, ya, B_ya), (cfg.do_b, yb, B_yb), (cfg.do_c, yc, B_yc))):
                    if not on:
                        continue
                    for k in range(4):
                        S.op("pe", I("matmul", pbr[bi][:], lhsT=wbrc[wi][:, bi, k, :], rhs=ysrc[:, k, :],
                                                                                     start=(k == 0), stop=(k == 3)), [B_wbrc[wi], By], [B_pbr[bi]])
                    pg, Bpg = proj_fm(l, w2fm_b, base[("gA", "gB", "gC")[bi]] + dc)
                    gtile, Bg = get_tmp()
                    S.op("act", tanh_gate(gtile[:], pg[:], 0.5), [Bpg], [Bg])
                    g = gtile[:]
                    if first:
                        S.op("dve", I("scalar_tensor_tensor", out=acc[:], in0=g, scalar=1.0, in1=pbr[bi][:], op0=ALU.add, op1=ALU.mult),
                             [Bg, B_pbr[bi]], [Bacc])
                        S.op("pool", I("tensor_scalar", out=acc[:], in0=acc[:], scalar1=scal[bi], scalar2=None, op0=ALU.mult), [Bacc], [Bacc])
                        first = False
                    else:
                        t2, B2 = get_tmp()
                        S.op("dve", I("scalar_tensor_tensor", out=t2[:], in0=g, scalar=1.0, in1=pbr[bi][:], op0=ALU.add, op1=ALU.mult),
                             [Bg, B_pbr[bi]], [B2])
                        S.op("dve", I("scalar_tensor_tensor", out=acc[:], in0=t2[:], scalar=scal[bi], in1=acc[:], op0=ALU.mult, op1=ALU.add),
                             [B2, Bacc], [Bacc])
                S.op("act", I("activation", out=mg[:, dc, :], in_=acc[:], func=AF.Copy), [Bacc], [B_mg[dc]])
            for ec in range(8):
                esl = slice(ec * 128, (ec + 1) * 128)
                w, Bw = load_w_chunk(l, wo_b, ec)
                p, Bp = get_pp()
                for k in range(8):
                    S.op("pe", I("matmul", p[:], lhsT=w[:, k, :], rhs=mg[:, k, :], start=(k == 0), stop=(k == 7)),
                         [Bw, B_mg[k]], [Bp])
                S.op("dve", I("tensor_tensor", out=xo[:, ec, :], in0=p[:], in1=xt[:, ec, :], op=ALU.add), [Bp, B_xt], [B_xo])
            dma("sp", xview(out_d, m), xo[:], [B_xo], [DRAM_x[m]], sl_xo)

        import os
        stop = os.environ.get("KSTOP", "")
        for l in range(depth):
            S.epoch = l
            if stop == "cast":
                for m in range(n_mt):
                    dma("sp", xt[:], xview(xsrc(l), m), [DRAM_x[m]] + B_wcast, [B_xt], sl_xt)
                    dma("sp", xview(out_d, m), xt[:], [B_xt], [DRAM_x[m]], sl_xo)
                continue
            layer_weights(l)
            if stop == "lw":
                for m in range(n_mt):
                    dma("sp", xt[:], xview(xsrc(l), m), [DRAM_x[m]] + B_wcast + [B_wtm, B_lw], [B_xt], sl_xt)
                    dma("sp", xview(out_d, m), xt[:], [B_xt], [DRAM_x[m]], sl_xo)
                continue
            if cfg.do_a:
                S.op("dve", I("memset", St[:], 0.0), [], [B_S])
                for m in range(n_mt):
                    sweep1_mt(l, m)
            for m in reversed(range(n_mt)):
                sweep2_mt(l, m)
        S.epoch = depth
        S.op("sp", I("nop"), reads=[DRAM_x[m] for m in range(n_mt)], writes=[])

        with nc.Block() as block:
            S.finalize(engsems, block)
        build_nc.last_stats = S.stats
    return nc


def run(inputs, cfg, trace=False):
    nc = build_nc(cfg)
    in_maps = [prep_core_inputs(inputs, c, cfg) for c in range(NCORES)]
    res = run_bass_kernel_spmd(nc, in_maps, core_ids=list(range(NCORES)), trace=trace)
    T = cfg.T
    L = 2 * T
    B = NCORES // 2
    out = np.empty((B, L, D), np.float32)
    for c in range(NCORES):
        o = np.asarray(res.results[c]["out"]).T
        if c % 2 == 0:
            out[c // 2, :T] = o
        else:
            out[c // 2, T:] = o[::-1]
    return out, res


def kernel(**inputs):
    cfg = Cfg()
    out, _ = run(inputs, cfg)
    return out
```

```python
import numpy as np
import concourse.bass as bass
import concourse.mybir as mybir
from concourse.bass_utils import run_bass_kernel_spmd

F32 = mybir.dt.float32
BF16 = mybir.dt.bfloat16
ALU = mybir.AluOpType
AF = mybir.ActivationFunctionType

D = 1024
DEPTH = 4
EPS = 1e-6
NCORES = 8
SAME_ENGINE_SYNC = True


def I(name, *args, **kw):
    return lambda e: getattr(e, name)(*args, **kw)


class Buf:
    __slots__ = ("name", "last_w", "readers")

    def __init__(self, name):
        self.name = name
        self.last_w = None
        self.readers = []


class Slot:
    def __init__(self, sem, name):
        self.sem = sem
        self.count = 0
        self.token = Buf("slot_" + name)


class Op:
    __slots__ = ("eng", "fn", "deps", "signal", "val", "sem", "slot", "idx", "epoch")


class Sched:
    ENGS = ("pe", "act", "dve", "pool", "sp")

    def __init__(self, nc, n_epochs):
        self.nc = nc
        self.ops = {e: [] for e in self.ENGS}
        self.epoch = 0
        self.n_epochs = n_epochs
        self.engsem = {}

    def op(self, eng, fn, reads=(), writes=(), slot=None):
        o = Op()
        o.eng = eng
        o.fn = fn
        o.signal = False
        o.val = None
        o.sem = None
        o.slot = slot
        o.epoch = self.epoch
        deps = []
        writes = list(writes)
        if slot is not None:
            writes.append(slot.token)
        for b in reads:
            if b.last_w is not None:
                deps.append(b.last_w)
        for b in writes:
            if b.last_w is not None:
                deps.append(b.last_w)
            deps.extend(b.readers)
        seen = set()
        dd = []
        for d in deps:
            if id(d) not in seen and d is not o:
                seen.add(id(d))
                dd.append(d)
        o.deps = dd
        for b in reads:
            b.readers.append(o)
        for b in writes:
            b.last_w = o
            b.readers = []
        if slot is not None:
            slot.count += 1
            o.sem = slot.sem
            o.val = 16 * slot.count
        o.idx = len(self.ops[eng])
        self.ops[eng].append(o)
        return o

    def _needs_wait(self, cons, prod):
        if prod.slot is not None:
            return True
        if prod.eng == cons.eng and cons.slot is None:
            if prod.eng == "pe":
                return False
            return SAME_ENGINE_SYNC
        return True

    def finalize(self, sems, block):
        for e in self.ENGS:
            for o in self.ops[e]:
                for d in o.deps:
                    if self._needs_wait(o, d):
                        d.signal = True
        for e in self.ENGS:
            cnt = {}
            for o in self.ops[e]:
                if o.slot is not None:
                    pass
                elif o.signal:
                    cnt[o.epoch] = cnt.get(o.epoch, 0) + 1
                    o.sem = sems[(e, o.epoch)]
                    o.val = cnt[o.epoch]
        self.stats = {e: len(self.ops[e]) for e in self.ENGS}

        def emit(e, eng):
            waited = {}
            nwaits = 0
            for o in self.ops[e]:
                for d in o.deps:
                    if not self._needs_wait(o, d):
                        continue
                    key = id(d.sem)
                    if waited.get(key, 0) >= d.val:
                        continue
                    eng.wait_ge(d.sem, d.val)
                    nwaits += 1
                    waited[key] = d.val
                ins = o.fn(eng)
                if o.slot is not None:
                    ins.then_inc(o.sem, 16)
                elif o.signal:
                    ins.then_inc(o.sem, 1)
            self.stats[e + "_waits"] = nwaits

        @block.tensor
        def _(eng):
            emit("pe", eng)

        @block.scalar
        def _(eng):
            emit("act", eng)

        @block.vector
        def _(eng):
            emit("dve", eng)

        @block.gpsimd
        def _(eng):
            emit("pool", eng)

        @block.sync
        def _(eng):
            emit("sp", eng)


OFF = dict(qA=0, fAf=512, fAb=1024, iA=1536, zA=2048, uB=2560, vB=3072, zB=3584, qC=4096, kC=4608,
           vC=4736, zC=4864, gA=5376, gB=6400, gC=7424)

FM2 = (["a2"] * 4 + ["qA"] * 4 + ["zA"] * 4 + ["uB"] * 4 + ["zB"] * 4 + ["qC"] * 4 + ["kC"] * 2 + ["zC"] * 4
       + ["gA"] * 8 + ["gB"] * 8 + ["gC"] * 8)
FM1 = ["a1"] * 4 + ["qA"] * 4


def _fm_cols(kind_list, odd):
    out = []
    cnt = {}
    for kind in kind_list:
        j = cnt.get(kind, 0)
        cnt[kind] = j + 1
        if kind == "a1":
            base = OFF["fAb"] if odd else OFF["fAf"]
            cols = np.arange(base + j * 128, base + (j + 1) * 128)
        elif kind == "a2":
            base = OFF["fAf"] if odd else OFF["fAb"]
            cols = np.arange(base + j * 128, base + (j + 1) * 128)
        elif kind == "kC":
            c = np.arange(OFF["kC"] + j * 64, OFF["kC"] + (j + 1) * 64)
            cols = np.concatenate([c, c])
        else:
            base = OFF[kind]
            cols = np.arange(base + j * 128, base + (j + 1) * 128)
        out.append(cols)
    return out


def _tm_cols(sweep):
    if sweep == 1:
        return np.arange(OFF["iA"], OFF["iA"] + 512)
    return np.concatenate([np.arange(OFF["iA"], OFF["iA"] + 512), np.arange(OFF["vB"], OFF["vB"] + 512),
                           np.arange(OFF["vC"], OFF["vC"] + 128)])


NF1 = len(FM1)
NF2 = len(FM2)
TM1 = 512
TM2 = 1152


class Cfg:
    def __init__(self, n_mt=8, depth=DEPTH, do_a=True, do_b=True, do_c=True):
        self.n_mt = n_mt
        self.T = n_mt * 512
        self.NB = n_mt * 4
        self.depth = depth
        self.do_a = do_a
        self.do_b = do_b
        self.do_c = do_c


def prep_core_inputs(inp, core, cfg):
    T = cfg.T
    L = 2 * T
    b = core // 2
    odd = core % 2
    depth = cfg.depth
    f32 = np.float32
    pos = (np.arange(T) if not odd else (L - 1 - np.arange(T))).astype(np.int64)
    m = {}
    x = np.asarray(inp["x"])[b]
    m["xT"] = np.ascontiguousarray(x[pos, :].T).astype(f32)
    w_in = np.asarray(inp["w_in"])
    w1fm = np.empty((depth, NF1, 128, 8, 128), f32)
    w2fm = np.empty((depth, NF2, 128, 8, 128), f32)
    w1tm = np.empty((depth, 128, 8, TM1), f32)
    w2tm = np.empty((depth, 128, 8, TM2), f32)
    c1 = _fm_cols(FM1, odd)
    c2 = _fm_cols(FM2, odd)
    for l in range(depth):
        wl = w_in[l].reshape(8, 128, -1)
        for j, cols in enumerate(c1):
            w1fm[l, j] = wl[:, :, cols].transpose(1, 0, 2)
        for j, cols in enumerate(c2):
            w2fm[l, j] = wl[:, :, cols].transpose(1, 0, 2)
        w1tm[l] = wl[:, :, _tm_cols(1)].transpose(1, 0, 2)
        w2tm[l] = wl[:, :, _tm_cols(2)].transpose(1, 0, 2)
    m["w1fm"] = w1fm
    m["w2fm"] = w2fm
    m["w1tm"] = w1tm
    m["w2tm"] = w2tm
    wbr = np.empty((depth, 8, 128, 3, 4, 128), f32)
    for bi, key in enumerate(("w_branch_a", "w_branch_b", "w_branch_c")):
        w = np.asarray(inp[key])[:depth].reshape(depth, 4, 128, 8, 128)
        wbr[:, :, :, bi] = w.transpose(0, 3, 2, 1, 4)
    m["wbr"] = wbr
    w = np.asarray(inp["w_out"])[:depth].reshape(depth, 8, 128, 8, 128)
    m["wo"] = np.ascontiguousarray(w.transpose(0, 3, 2, 1, 4)).astype(f32)
    ng = np.asarray(inp["norm_gain"])[:depth]
    m["ngain"] = np.ascontiguousarray(ng.reshape(depth, 8, 128).transpose(2, 0, 1)).astype(f32)
    lb = np.asarray(inp["lb_logits"]).reshape(DEPTH, 2, 4, 128)
    if odd:
        lb = lb[:, ::-1]
    m["lbl"] = np.ascontiguousarray(lb.transpose(3, 0, 1, 2)).astype(f32)
    hg = np.asarray(inp["hg_norm_gain"])[:depth]
    m["hgain"] = np.ascontiguousarray(hg.transpose(2, 0, 1)).astype(f32)
    m["lng"] = np.ascontiguousarray(np.broadcast_to(np.asarray(inp["sg_ln_gain"])[:depth][None], (128, depth, 512))).astype(f32)
    m["lnb"] = np.ascontiguousarray(np.broadcast_to(np.asarray(inp["sg_ln_bias"])[:depth][None], (128, depth, 512))).astype(f32)
    ws = np.asarray(inp["w_spatial"])[:depth]
    bs = np.asarray(inp["b_spatial"])[:depth]
    if odd:
        ws = ws[:, :, ::-1, ::-1]
        bs = bs[:, :, ::-1]
    m["wsT"] = np.ascontiguousarray(ws.transpose(3, 0, 1, 2)).astype(f32)
    m["bsp"] = np.ascontiguousarray(np.broadcast_to(bs[None], (128, depth, 4, 128))).astype(f32)
    qg = np.asarray(inp["q_norm_gain"])[:depth]
    kg = np.asarray(inp["k_norm_gain"])[:depth]
    m["qkg"] = np.ascontiguousarray(np.stack([np.concatenate([qg, qg], 1), np.concatenate([kg, kg], 1)], 1).transpose(2, 0, 1)).astype(f32)
    m["sink"] = np.ascontiguousarray(np.broadcast_to(np.asarray(inp["sink_logits"])[:depth][None], (128, depth, 8))).astype(f32)
    half = 32
    inv_freq = (10000.0 ** (-np.arange(half, dtype=np.float32) / half)).astype(np.float32)
    ang = pos.astype(np.float32)[None, :] * inv_freq[:, None]
    cos = np.cos(ang).astype(f32)
    sin = np.sin(ang).astype(f32)
    m["cosT"] = np.ascontiguousarray(np.concatenate([cos, cos, cos, cos], 0))
    m["sinT"] = np.ascontiguousarray(np.concatenate([sin, sin, sin, sin], 0))
    ident = np.eye(128, dtype=f32)
    m["c_ident"] = ident
    bd = np.zeros((128, 128), f32)
    bd[:64, :64] = 1
    bd[64:, 64:] = 1
    m["c_bd"] = bd
    rot = np.zeros((128, 128), f32)
    for hb in (0, 64):
        for d in range(32):
            rot[hb + d + 32, hb + d] = -1.0
            rot[hb + d, hb + d + 32] = 1.0
    m["c_rot"] = rot
    j = np.arange(128)[:, None]
    i = np.arange(128)[None, :]
    masks = np.stack([(j >= i), (j <= i), (j + i >= 127)], 1).astype(f32)
    m["c_amask"] = np.ascontiguousarray(masks)
    s = np.arange(64)[:, None]
    t = np.arange(64)[None, :]
    h1 = (s <= t).astype(f32)
    h2 = (s >= t).astype(f32)
    m["c_hmask"] = np.ascontiguousarray(np.stack([np.concatenate([h1, h1], 0), np.concatenate([h2, h2], 0)], 1))
    cm = np.ones((128, 512), f32)
    cm[:, ::64] = 0.0
    m["c_cmask"] = cm
    ss_ = np.arange(128)[:, None]
    tt_ = np.arange(128)[None, :]
    same = (ss_ // 64) == (tt_ // 64)
    m["c_hmask2"] = np.ascontiguousarray(np.stack([(same & (ss_ <= tt_)), (same & (ss_ >= tt_))], 1).astype(f32))
    sel = np.zeros((128, 2), f32)
    sel[:, 1 - odd] = 1.0
    m["sel"] = sel
    return m


def build_nc(cfg):
    nc = bass.Bass("TRN2", target_bir_lowering=False)
    T = cfg.T
    depth = cfg.depth
    n_mt = cfg.n_mt

    def din(name, shape, dt=F32):
        return nc.dram_tensor(name, list(shape), dt, kind="ExternalInput")

    xT_d = din("xT", [D, T])
    w1fm_d = din("w1fm", [depth, NF1, 128, 8, 128])
    w2fm_d = din("w2fm", [depth, NF2, 128, 8, 128])
    w1tm_d = din("w1tm", [depth, 128, 8, TM1])
    w2tm_d = din("w2tm", [depth, 128, 8, TM2])
    wbr_d = din("wbr", [depth, 8, 128, 3, 4, 128])
    wo_d = din("wo", [depth, 8, 128, 8, 128])
    ngain_d = din("ngain", [128, depth, 8])
    lbl_d = din("lbl", [128, DEPTH, 2, 4])
    hgain_d = din("hgain", [128, depth, 4])
    lng_d = din("lng", [128, depth, 512])
    lnb_d = din("lnb", [128, depth, 512])
    wsT_d = din("wsT", [128, depth, 4, 128])
    bsp_d = din("bsp", [128, depth, 4, 128])
    qkg_d = din("qkg", [128, depth, 2])
    sink_d = din("sink", [128, depth, 8])
    cosT_d = din("cosT", [128, T])
    sinT_d = din("sinT", [128, T])
    c_ident_d = din("c_ident", [128, 128])
    c_bd_d = din("c_bd", [128, 128])
    c_rot_d = din("c_rot", [128, 128])
    c_amask_d = din("c_amask", [128, 3, 128])
    c_hmask_d = din("c_hmask", [128, 2, 64])
    sel_d = din("sel", [128, 2])
    c_cmask_d = din("c_cmask", [128, 512])
    c_hmask2_d = din("c_hmask2", [128, 2, 128])
    out_d = nc.dram_tensor("out", [D, T], F32, kind="ExternalOutput")

    w1fm_b = nc.dram_tensor("w1fm_b", [depth, NF1, 128, 8, 128], BF16)
    w2fm_b = nc.dram_tensor("w2fm_b", [depth, NF2, 128, 8, 128], BF16)
    w1tm_b = nc.dram_tensor("w1tm_b", [depth, 128, 8, TM1], BF16)
    w2tm_b = nc.dram_tensor("w2tm_b", [depth, 128, 8, TM2], BF16)
    wbr_b = nc.dram_tensor("wbr_b", [depth, 8, 128, 3, 4, 128], BF16)
    wo_b = nc.dram_tensor("wo_b", [depth, 8, 128, 8, 128], BF16)
    o1_d = nc.dram_tensor("o1_spill", [128, cfg.NB, 4, 128], F32)
    CCW = 512 + 256 + 128
    cc_in = [nc.dram_tensor(f"cc_in{l}", [128, CCW], F32) for l in range(depth)]
    cc_out = [nc.dram_tensor(f"cc_out{l}", [256, CCW], F32) for l in range(depth)]

    from contextlib import ExitStack
    es = ExitStack()
    with es:
        S = Sched(nc, depth + 1)

        def sb(name, shape, dt=F32):
            return es.enter_context(nc.sbuf_tensor(name, list(shape), dt))

        def ps(name, shape, dt=F32):
            return es.enter_context(nc.psum_tensor(name, list(shape), dt))

        def sem(name):
            return es.enter_context(nc.semaphore(name))

        engsems = {(e, ep): sem(f"s_{e}_{ep}") for e in ("pe", "act", "dve", "pool") for ep in range(depth + 1)}
        _slot_n = [0]

        def slot(name):
            _slot_n[0] += 1
            return Slot(sem(f"d_{name}_{_slot_n[0]}"), name)

        ident_b = sb("ident_b", [128, 128], BF16)
        ones_b = sb("ones_b", [128, 128], BF16)
        bd_b = sb("bd_b", [128, 128], BF16)
        rot_b = sb("rot_b", [128, 128], BF16)
        amask = sb("amask", [128, 3, 128], BF16)
        hmask = sb("hmask", [128, 2, 64], BF16)
        selt = sb("selt", [128, 2])
        ngain = sb("ngain_s", [128, depth, 8])
        lbl = sb("lbl_s", [128, DEPTH, 2, 4])
        hgain = sb("hgain_s", [128, depth, 4])
        qkg = sb("qkg_s", [128, depth, 2])
        sinkt = sb("sink_s", [128, depth, 8])
        esink = sb("esink", [128, depth, 8])
        lbc1 = sb("lbc1", [128, DEPTH, 2, 4])
        lbc0 = sb("lbc0", [128, DEPTH, 2, 4])
        B_const = Buf("const")
        sl_c = slot("const")

        def dma(eng, out, in_, reads, writes, sl):
            return S.op(eng, I("dma_start", out=out, in_=in_), reads=reads, writes=writes, slot=sl)

        for dst, src in ((ident_b, c_ident_d), (bd_b, c_bd_d), (rot_b, c_rot_d), (amask, c_amask_d), (hmask, c_hmask_d)):
            dma("pool", dst[:], src.ap(), [], [B_const], sl_c)
        for dst, src in ((selt, sel_d), (ngain, ngain_d), (lbl, lbl_d), (hgain, hgain_d), (qkg, qkg_d), (sinkt, sink_d)):
            dma("sp", dst[:], src.ap(), [], [B_const], sl_c)
        S.op("dve", I("memset", ones_b[:], 1.0), [], [B_const])
        S.op("act", I("activation", out=esink[:], in_=sinkt[:], func=AF.Exp), [B_const], [B_const])
        lbe = sb("lbe", [128, DEPTH, 8])
        lbs = sb("lbs", [128, 8])
        lbv = lbl[:].rearrange("p l a b -> p l (a b)")
        S.op("act", I("activation", out=lbe[:], in_=lbv, func=AF.Exp), [B_const], [B_const])
        S.op("dve", I("tensor_tensor", out=lbs[:], in0=lbe[:, 0, :], in1=lbe[:, 1, :], op=ALU.add), [B_const], [B_const])
        S.op("dve", I("tensor_tensor", out=lbs[:], in0=lbs[:], in1=lbe[:, 2, :], op=ALU.add), [B_const], [B_const])
        S.op("dve", I("tensor_tensor", out=lbs[:], in0=lbs[:], in1=lbe[:, 3, :], op=ALU.add), [B_const], [B_const])
        S.op("dve", I("reciprocal", out=lbs[:], in_=lbs[:]), [B_const], [B_const])
        c1v = lbc1[:].rearrange("p l a b -> p l (a b)")
        c0v = lbc0[:].rearrange("p l a b -> p l (a b)")
        S.op("dve", I("memset", c0v[:, 0, :], 0.0), [B_const], [B_const])
        for l in range(1, DEPTH):
            S.op("dve", I("tensor_tensor", out=c1v[:, l, :], in0=lbe[:, l, :], in1=lbs[:], op=ALU.mult), [B_const], [B_const])
            S.op("dve", I("tensor_tensor", out=c0v[:, l, :], in0=c0v[:, l - 1, :], in1=c1v[:, l, :], op=ALU.add), [B_const], [B_const])
        S.op("dve", I("tensor_scalar", out=lbc1[:], in0=lbc0[:], scalar1=-0.5, scalar2=0.5, op0=ALU.mult, op1=ALU.add), [B_const], [B_const])
        S.op("dve", I("tensor_scalar", out=lbc0[:], in0=lbc0[:], scalar1=0.5, scalar2=0.5, op0=ALU.mult, op1=ALU.add), [B_const], [B_const])

        B_wcast = [Buf(f"wcast{l}") for l in range(depth)]
        sl_wc = [slot(f"wc{i}") for i in range(4)]
        _wc_i = [0]

        def wcast(l, dst, src):
            sl = sl_wc[_wc_i[0] % 4]
            _wc_i[0] += 1
            S.op("pool", I("dma_start", out=dst, in_=src), reads=[], writes=[B_wcast[l]], slot=sl)

        import os
        for l in range(depth if not os.environ.get("KSKIPCAST") else 0):
            def v4(t, j0, j1):
                return t[l, j0:j1].rearrange("a p k n -> (a p) (k n)")

            def v3(t):
                return t[l].rearrange("p k n -> p (k n)")

            for j in range(0, NF1, 4):
                wcast(l, v4(w1fm_b, j, j + 4), v4(w1fm_d, j, j + 4))
            wcast(l, v3(w1tm_b), v3(w1tm_d))
            for j in range(0, NF2, 6):
                wcast(l, v4(w2fm_b, j, j + 6), v4(w2fm_d, j, j + 6))
            wcast(l, v3(w2tm_b), v3(w2tm_d))
            wcast(l, wbr_b[l].rearrange("a p b k n -> (a p) (b k n)"), wbr_d[l].rearrange("a p b k n -> (a p) (b k n)"))
            wcast(l, v4(wo_b, 0, 8), v4(wo_d, 0, 8))

        NWB = 4
        wbuf = [sb(f"wbuf{i}", [128, 8, 128], BF16) for i in range(NWB)]
        B_wbuf = [Buf(f"wbuf{i}") for i in range(NWB)]
        sl_wbuf = [slot(f"wb{i}") for i in range(NWB)]
        _wb_i = [0]
        wtm = sb("wtm", [128, 8, TM2], BF16)
        B_wtm = Buf("wtm")
        sl_wtm = slot("wtm")
        wbrc = [sb(f"wbrc{i}", [128, 3, 4, 128], BF16) for i in range(2)]
        B_wbrc = [Buf(f"wbrc{i}") for i in range(2)]
        sl_wbrc = [slot(f"wbrc{i}") for i in range(2)]
        _wbrc_i = [0]
        sl_wl = slot("wl")
        lng = sb("lng_s", [128, 512])
        lnb = sb("lnb_s", [128, 512])
        wsT = sb("wsT_s", [128, 4, 128], BF16)
        bsp = sb("bsp_s", [128, 4, 128])
        B_lw = Buf("layerw")

        xt = sb("xt", [128, 8, 512])
        B_xt = Buf("xt")
        sl_xt = slot("xt")
        hT = sb("hT", [128, 8, 512], BF16)
        B_hT = Buf("hT")
        rstd = sb("rstd", [128, 512])
        B_rstd = Buf("rstd")
        rtmp = sb("rtmp", [128, 512])
        B_rtmp = Buf("rtmp")
        epsb = sb("epsb", [128, 1])
        S.op("dve", I("memset", epsb[:], EPS), [], [B_const])
        mhalf = sb("mhalf", [128, 512])
        S.op("pool", I("memset", mhalf[:], -0.5), [], [B_const])

        zs = sb("zs", [128, 12, 512], BF16)
        B_zs = [Buf(f"zs{i}") for i in range(12)]
        ub = sb("ub", [128, 4, 512], BF16)
        B_ub = [Buf(f"ub{i}") for i in range(4)]
        yb = sb("yb", [128, 4, 512], BF16)
        B_yb = Buf("yb")
        ya = sb("ya", [128, 4, 512], BF16)
        B_ya = Buf("ya")
        yc = sb("yc", [128, 4, 512], BF16)
        B_yc = Buf("yc")
        mg = sb("mg", [128, 8, 512], BF16)
        B_mg = [Buf(f"mg{i}") for i in range(8)]
        sq = mg
        xo = sb("xo", [128, 8, 512])
        B_xo = Buf("xo")
        sl_xo = slot("xo")
        ya_scr = sb("qkscr", [128, 1024], BF16)
        B_ya_scr = Buf("qkscr")
        NTMP = 8
        tmpf = [sb(f"tmpf{i}", [128, 512]) for i in range(NTMP)]
        B_tmpf = [Buf(f"tmpf{i}") for i in range(NTMP)]
        _tf_i = [0]

        def get_tmp():
            i = _tf_i[0] % NTMP
            _tf_i[0] += 1
            return tmpf[i], B_tmpf[i]

        vn = sb("vn", [128, 512], BF16)
        B_vn = Buf("vn")
        bnst = sb("bnst", [128, 6])
        bnag = sb("bnag", [128, 2])
        B_bn = Buf("bn")
        mhalf1 = sb("mhalf1", [128, 1])
        S.op("pool", I("memset", mhalf1[:], -0.5), [], [B_const])

        NPP = 2
        pp = [ps(f"pp{i}", [128, 512]) for i in range(NPP)]
        B_pp = [Buf(f"pp{i}") for i in range(NPP)]
        _pp_i = [0]

        def get_pp():
            i = _pp_i[0] % NPP
            _pp_i[0] += 1
            return pp[i], B_pp[i]

        pst = ps("pst", [128, 512])
        B_pst = Buf("pst")
        pmx = ps("pmx", [128, 4, 128])
        B_pmx = Buf("pmx")
        pbr = [ps(f"pbr{i}", [128, 512]) for i in range(3)]
        B_pbr = [Buf(f"pbr{i}") for i in range(3)]

        DRAM_x = [Buf(f"dram_x{m}") for m in range(n_mt)]
        qr = sb("qr", [128, 4, 512], BF16)
        B_qr = [Buf(f"qr{i}") for i in range(4)]
        kz = sb("kz", [128, 2, 2, 768], BF16)
        B_kr = Buf("kr")
        S.op("pool", I("memset", kz[:], 0.0), [], [B_kr])
        vaug = sb("vaug", [128, 6, 2, 192], BF16)
        B_vaug = Buf("vaug")
        S.op("dve", I("memset", vaug[:, :, :, 64:128], 1.0), [], [B_vaug])
        pt = sb("pt", [128, 3, 2, 256], BF16)
        B_pt = Buf("pt")
        cs = sb("cs", [128, 2, 640])
        B_cs = Buf("cs")
        sl_cs = slot("cs")
        xh = sb("xh", [128, 8, 128])
        B_xh = Buf("xh")
        sl_xh = slot("xh")
        hTh = sb("hTh", [128, 8, 128], BF16)
        B_hTh = Buf("hTh")
        sqh = sb("sqh", [128, 8, 128], BF16)
        B_sqh = Buf("sqh")
        mone = sb("mone", [128, 256])
        S.op("pool", I("memset", mone[:], -1.0), [], [B_const])
        dtmp = sb("dtmp", [128, 256])
        B_dtmp = Buf("dtmp")
        xo_flat = xo[:].rearrange("p c t -> p (c t)")
        ccg = xo_flat[:, 0:2 * CCW].rearrange("p (r n) -> p r n", r=2)
        ccs = xo_flat[:, 2048:2048 + CCW]
        ccp = xo_flat[:, 3072:3072 + CCW]
        B_ccs = B_ccg = B_ccp = B_xo
        sl_cc = slot("cc")
        DRAM_cc = Buf("dram_cc")
        ccsem = sem("ccsem")
        cmask = sb("cmask", [128, 512])
        hmask2 = sb("hmask2", [128, 2, 128], BF16)
        dma("sp", cmask[:], c_cmask_d.ap(), [], [B_const], sl_c)
        dma("pool", hmask2[:], c_hmask2_d.ap(), [], [B_const], sl_c)
        qraw = sb("qraw", [128, 4, 512])
        B_qraw = Buf("qraw")
        qtT = sb("qtT", [128, 4, 512], BF16)
        B_qtT = Buf("qtT")
        ktT = sb("ktT", [128, 4, 512], BF16)
        B_ktT = Buf("ktT")
        ktA = sb("ktA", [128, 4, 128], BF16)
        ktB = sb("ktB", [128, 4, 128], BF16)
        B_kt = Buf("kt")
        S.op("pool", I("memset", ktA[:], 0.0), [], [B_kt])
        S.op("pool", I("memset", ktB[:], 0.0), [], [B_kt])
        vtok = sb("vtok", [128, 4, 4, 128], BF16)
        B_vtok = [Buf(f"vtok{i}") for i in range(4)]
        sc = sb("sc", [128, 4, 3, 8])
        B_sc = Buf("sc")
        sc8 = sb("sc8", [128, 8])
        B_sc8 = Buf("sc8")
        St = sb("St", [128, 4, 128])
        B_S = Buf("S")
        Sp = sb("Sp", [128, 4, 128], BF16)
        B_Sp = Buf("Sp")
        ATs = sb("ATs", [128, 4, 128], BF16)
        B_ATs = Buf("ATs")
        o1s = sb("o1s", [128, 4, 128])
        B_o1s = Buf("o1s")
        sl_o1 = slot("o1")
        DRAM_o1 = [Buf(f"dram_o1_{i}") for i in range(cfg.NB)]
        ptr = ps("ptr", [128, 4, 128], BF16)
        B_ptr = Buf("ptr")

        def hgrn_gates(l, dirn, srcw, base_a):
            for h in range(4):
                p, Bp = proj_fm(l, srcw, base_a + h)
                tA, BA = get_tmp()
                tK, BK = get_tmp()
                tB, BB = get_tmp()
                tD, BD = get_tmp()
                tE, BE = get_tmp()
                v = lambda t: t[:].rearrange("p (c t) -> p c t", t=64)
                S.op("act", I("activation", out=tA[:], in_=p[:], func=AF.Tanh, scale=0.5), [Bp], [BA])
                S.op("dve", I("tensor_scalar", out=tA[:], in0=tA[:], scalar1=lbc1[:, l, dirn, h:h + 1], scalar2=lbc0[:, l, dirn, h:h + 1], op0=ALU.mult, op1=ALU.add),
                     [BA, B_const], [BA])
                S.op("pool", I("tensor_scalar", out=tK[:], in0=tA[:], scalar1=-1.0, scalar2=1.0, op0=ALU.mult, op1=ALU.add), [BA], [BK])
                S.op("act", I("activation", out=tA[:], in_=tA[:], func=AF.Ln), [BA, BK], [BA])
                S.op("dve", I("tensor_tensor_scan", out=tB[:], data0=cmask[:], data1=tA[:], initial=0.0, op0=ALU.mult, op1=ALU.add), [BA, B_const], [BB])
                if dirn == 0:
                    S.op("dve", I("tensor_tensor", out=v(tD), in0=v(tB), in1=v(tB)[:, :, 31:32].to_broadcast([128, 8, 64]), op=ALU.subtract), [BB], [BD])
                    S.op("act", I("activation", out=sc[:, h, 0, :], in_=v(tB)[:, :, 31], func=AF.Exp), [BB], [B_sc])
                    S.op("act", I("activation", out=sc[:, h, 1, :], in_=v(tB)[:, :, 63], func=AF.Exp), [BB], [B_sc])
                    S.op("act", I("activation", out=sc[:, h, 2, :], in_=v(tD)[:, :, 63], func=AF.Exp), [BD], [B_sc])
                else:
                    S.op("dve", I("tensor_tensor", out=tA[:], in0=tB[:], in1=tA[:], op=ALU.subtract), [BB, BA], [BA])
                    S.op("dve", I("tensor_tensor", out=v(tD), in0=v(tA)[:, :, 32:33].to_broadcast([128, 8, 64]), in1=v(tA), op=ALU.subtract), [BA], [BD])
                    S.op("dve", I("tensor_tensor", out=sc8[:], in0=v(tB)[:, :, 63], in1=v(tA)[:, :, 32], op=ALU.subtract), [BB, BA], [B_sc8])
                    S.op("act", I("activation", out=sc[:, h, 0, :], in_=sc8[:], func=AF.Exp), [B_sc8], [B_sc])
                    S.op("act", I("activation", out=sc[:, h, 1, :], in_=v(tB)[:, :, 63], func=AF.Exp), [BB], [B_sc])
                    S.op("act", I("activation", out=sc[:, h, 2, :], in_=v(tD)[:, :, 0], func=AF.Exp), [BD], [B_sc])
                S.op("act", I("activation", out=tE[:], in_=tD[:], func=AF.Exp), [BD], [BE])
                S.op("dve", I("tensor_tensor", out=qtT[:, h, :], in0=qraw[:, h, :], in1=tE[:], op=ALU.mult), [B_qraw, BE], [B_qtT])
                S.op("act", I("activation", out=tB[:], in_=tD[:], func=AF.Exp, scale=-1.0), [BD, B_sc, B_sc8], [BB])
                S.op("pool", I("tensor_tensor", out=ktT[:, h, :], in0=tK[:], in1=tB[:], op=ALU.mult), [BK, BB], [B_ktT])

        def hgrn_q_and_v(l, srcw, base_q):
            for h in range(4):
                p, Bp = proj_fm(l, srcw, base_q + h)
                S.op("act", I("activation", out=qraw[:, h, :], in_=p[:], func=AF.Copy), [Bp], [B_qraw])

        def hgrn_vtok(bi):
            tsl = slice(bi * 128, (bi + 1) * 128)
            p, Bp = get_pp()
            for k in range(8):
                S.op("pe", I("matmul", p[:], lhsT=hT[:, k, tsl], rhs=wtm[:, k, 0:512], start=(k == 0), stop=(k == 7)), [B_hT, B_wtm], [Bp])
            S.op("act", I("activation", out=vtok[:, bi, :, :], in_=p[:].rearrange("p (h d) -> p h d", h=4), func=AF.Copy), [Bp], [B_vtok[bi]])

        def hgrn_block(l, dirn, bi, m):
            tsl = slice(bi * 128, (bi + 1) * 128)
            psc, B_psc = pbr[0][:].rearrange("p (h t) -> p h t", h=4), B_pbr[0]
            po, B_po = pbr[1][:].rearrange("p (h t) -> p h t", h=4), B_pbr[1]
            pob, B_pob = pbr[2][:].rearrange("p (h t) -> p h t", h=4), B_pbr[2]
            pS, B_pS = pmx[:], B_pmx
            for h in range(4):
                S.op("pe", I("transpose", out=ptr[:, h, :], in_=ktT[:, h, tsl], identity=ident_b[:]), [B_ktT, B_const], [B_ptr])
            S.op("dve", I("tensor_copy", out=ktA[0:64, :, :], in_=ptr[0:64, :, :]), [B_ptr], [B_kt])
            S.op("act", I("activation", out=ktB[64:128, :, :], in_=ptr[64:128, :, :], func=AF.Copy), [B_ptr], [B_kt])
            for h in range(4):
                S.op("pe", I("matmul", psc[:, h, :], lhsT=ktT[:, h, tsl], rhs=qtT[:, h, tsl], start=True, stop=True), [B_ktT, B_qtT], [B_psc])
            S.op("dve", I("tensor_tensor", out=ATs[:], in0=psc, in1=hmask2[:, dirn:dirn + 1, :].to_broadcast([128, 4, 128]), op=ALU.mult), [B_psc, B_const], [B_ATs])
            for h in range(4):
                S.op("pe", I("matmul", po[:, h, :], lhsT=vtok[:, bi, h, :], rhs=ATs[:, h, :], start=True, stop=True), [B_vtok[bi], B_ATs], [B_po])
            chunks = [(0, ktA), (1, ktB)] if dirn == 0 else [(1, ktB), (0, ktA)]
            for ci, (c, kt) in enumerate(chunks):
                gc = bi * 2 + c
                csl = slice(bi * 128 + c * 64, bi * 128 + (c + 1) * 64)
                for h in range(4):
                    S.op("dve", I("tensor_scalar", out=Sp[:, h, :], in0=St[:, h, :], scalar1=sc[:, h, 0, gc:gc + 1], scalar2=None, op0=ALU.mult), [B_S, B_sc], [B_Sp])
                for h in range(4):
                    S.op("pe", I("matmul", pob[:, h, c * 64:(c + 1) * 64], lhsT=Sp[:, h, :], rhs=qtT[:, h, csl], start=True, stop=True), [B_Sp, B_qtT], [B_pob])
                for h in range(4):
                    S.op("pe", I("matmul", pS[:, h, :], lhsT=kt[:, h, :], rhs=vtok[:, bi, h, :], start=True, stop=True), [B_kt, B_vtok[bi]], [B_pS])
                for h in range(4):
                    S.op("dve", I("tensor_scalar", out=St[:, h, :], in0=St[:, h, :], scalar1=sc[:, h, 1, gc:gc + 1], scalar2=None, op0=ALU.mult), [B_S, B_sc], [B_S])
                    S.op("dve", I("scalar_tensor_tensor", out=St[:, h, :], in0=pS[:, h, :], scalar=sc[:, h, 2, gc:gc + 1], in1=St[:, h, :], op0=ALU.mult, op1=ALU.add),
                         [B_pS, B_sc, B_S], [B_S])
            gblk = m * 4 + bi
            if dirn == 0:
                S.op("act", I("activation", out=o1s[:], in_=po, func=AF.Copy), [B_po], [B_o1s])
                S.op("dve", I("tensor_tensor", out=o1s[:], in0=o1s[:], in1=pob, op=ALU.add), [B_o1s, B_pob], [B_o1s])
                dma("sp", o1_d[:, gblk], o1s[:], [B_o1s], [DRAM_o1[gblk]], sl_o1)
            else:
                dma("sp", o1s[:], o1_d[:, gblk], [DRAM_o1[gblk]], [B_o1s], sl_o1)
                osum, Bos = get_tmp()
                rt, Brt = get_tmp()
                osv = osum[:].rearrange("p (h t) -> p h t", h=4)
                S.op("dve", I("tensor_tensor", out=osv, in0=po, in1=o1s[:], op=ALU.add), [B_po, B_o1s], [Bos])
                S.op("dve", I("tensor_tensor", out=osv, in0=osv, in1=pob, op=ALU.add), [Bos, B_pob], [Bos])
                S.op("act", I("activation", out=ya_scr[:, 0:512], in_=osum[:], func=AF.Square), [Bos], [B_ya_scr])
                S.op("pe", I("matmul", pst[:], lhsT=ones_b[:], rhs=ya_scr[:, 0:512], start=True, stop=True), [B_ya_scr, B_const], [B_pst])
                S.op("act", I("activation", out=rt[:], in_=pst[:], func=AF.Ln, scale=1.0 / 128, bias=epsb[:]), [B_pst, B_const], [Brt])
                S.op("act", I("activation", out=rt[:], in_=rt[:], func=AF.Exp, scale=-0.5), [Brt], [Brt])
                S.op("dve", I("tensor_tensor", out=osum[:], in0=osum[:], in1=rt[:], op=ALU.mult), [Bos, Brt], [Bos])
                for h in range(4):
                    S.op("dve", I("scalar_tensor_tensor", out=ya[:, h, tsl], in0=osv[:, h, :], scalar=hgain[:, l, h:h + 1], in1=zs[:, h, tsl], op0=ALU.mult, op1=ALU.mult),
                         [Bos, B_const, B_zs[h]], [B_ya])

        def sweep1_mt(l, m):
            norm_mt(l, m)
            hgrn_q_and_v(l, w1fm_b, 4)
            hgrn_gates(l, 0, w1fm_b, 0)
            for bi in range(4):
                hgrn_vtok(bi)
                hgrn_block(l, 0, bi, m)
        sl_out = slot("out")

        def xsrc(l):
            return xT_d if l == 0 else out_d

        xview = lambda t, m: t.ap().rearrange("(c p) t -> p c t", p=128)[:, :, m * 512:(m + 1) * 512]

        def load_w_chunk(l, src_b, j):
            i = _wb_i[0] % NWB
            _wb_i[0] += 1
            dma("sp", wbuf[i][:], src_b[l, j], [B_wcast[l]], [B_wbuf[i]], sl_wbuf[i])
            return wbuf[i], B_wbuf[i]

        def tanh_gate(dst, src, scale):
            return I("activation", out=dst, in_=src, func=AF.Tanh, scale=scale)

        def norm_mt(l, m):
            dma("sp", xt[:], xview(xsrc(l), m), [DRAM_x[m]], [B_xt], sl_xt)
            for c in range(8):
                S.op("act", I("activation", out=sq[:, c, :], in_=xt[:, c, :], func=AF.Square), [B_xt], [B_mg[c]])
            for c in range(8):
                S.op("pe", I("matmul", pst[:], lhsT=ones_b[:], rhs=sq[:, c, :], start=(c == 0), stop=(c == 7)),
                     [B_mg[c], B_const], [B_pst])
            S.op("act", I("activation", out=rtmp[:], in_=pst[:], func=AF.Ln, scale=1.0 / D, bias=epsb[:]), [B_pst, B_const], [B_rtmp])
            S.op("act", I("activation", out=rstd[:], in_=rtmp[:], func=AF.Exp, scale=-0.5), [B_rtmp], [B_rstd])
            for c in range(8):
                S.op("dve", I("scalar_tensor_tensor", out=hT[:, c, :], in0=xt[:, c, :], scalar=ngain[:, l, c:c + 1],
                                                                   in1=rstd[:], op0=ALU.mult, op1=ALU.mult),
                     [B_xt, B_rstd, B_const], [B_hT])

        def proj_fm(l, src_b, j):
            w, Bw = load_w_chunk(l, src_b, j)
            p, Bp = get_pp()
            for k in range(8):
                S.op("pe", I("matmul", p[:], lhsT=w[:, k, :], rhs=hT[:, k, :], start=(k == 0), stop=(k == 7)),
                     [Bw, B_hT], [Bp])
            return p, Bp

        def layer_weights(l):
            dma("sp", wtm[:], w2tm_b[l], [B_wcast[l]], [B_wtm], sl_wtm)
            dma("sp", lng[:], lng_d[:, l, :], [], [B_lw], sl_wl)
            dma("sp", lnb[:], lnb_d[:, l, :], [], [B_lw], sl_wl)
            dma("sp", bsp[:], bsp_d[:, l], [], [B_lw], sl_wl)
            dma("pool", wsT[:], wsT_d[:, l], [], [B_lw], sl_wl)

        GELU_C = 0.7978845608028654

        def gelu_from_psum(p, Bp, dst, Bdst, eng2="pool"):
            t1, B1 = get_tmp()
            t2, B2 = get_tmp()
            S.op("act", I("activation", out=t1[:], in_=p[:], func=AF.Square), [Bp], [B1])
            S.op("dve", I("tensor_scalar", out=t1[:], in0=t1[:], scalar1=0.044715, scalar2=1.0, op0=ALU.mult, op1=ALU.add), [B1], [B1])
            S.op("dve", I("tensor_tensor", out=t2[:], in0=t1[:], in1=p[:], op=ALU.mult), [B1, Bp], [B2])
            S.op("act", I("activation", out=t1[:], in_=t2[:], func=AF.Tanh, scale=GELU_C), [B2], [B1])
            S.op("dve", I("scalar_tensor_tensor", out=dst, in0=t1[:], scalar=1.0, in1=p[:], op0=ALU.add, op1=ALU.mult), [B1, Bp], [Bdst])

        def silu_from_psum(p, Bp, dst, Bdst):
            t1, B1 = get_tmp()
            S.op("act", I("activation", out=t1[:], in_=p[:], func=AF.Tanh, scale=0.5), [Bp], [B1])
            S.op("dve", I("scalar_tensor_tensor", out=dst, in0=t1[:], scalar=1.0, in1=p[:], op0=ALU.add, op1=ALU.mult), [B1, Bp], [Bdst])

        def sweep2_mt(l, m):
            norm_mt(l, m)
            base = {}
            cnt = 0
            for kind in FM2:
                base.setdefault(kind, cnt)
                cnt += 1
            if cfg.do_b:
                for j in range(4):
                    p, Bp = proj_fm(l, w2fm_b, base["uB"] + j)
                    gelu_from_psum(p, Bp, ub[:, j, :], B_ub[j])
                for j in range(4):
                    p, Bp = proj_fm(l, w2fm_b, base["zB"] + j)
                    silu_from_psum(p, Bp, zs[:, 4 + j, :], B_zs[4 + j])
                for blk in range(4):
                    tsl = slice(blk * 128, (blk + 1) * 128)
                    p, Bp = get_pp()
                    for k in range(8):
                        S.op("pe", I("matmul", p[:], lhsT=hT[:, k, tsl], rhs=wtm[:, k, 512:1024],
                                                                          start=(k == 0), stop=(k == 7)), [B_hT, B_wtm], [Bp])
                    vt, B_vt = get_tmp()
                    vt2, B_vt2 = get_tmp()
                    gelu_from_psum(p, Bp, vt[:], B_vt)
                    S.op("dve", I("bn_stats", out=bnst[:], in_=vt[:]), [B_vt], [B_bn])
                    S.op("dve", I("bn_aggr", out=bnag[:], in_=bnst[:]), [B_bn], [B_bn])
                    S.op("dve", I("tensor_scalar", out=bnag[:, 1:2], in0=bnag[:, 1:2], scalar1=0.25, scalar2=EPS, op0=ALU.mult, op1=ALU.add),
                         [B_bn], [B_bn])
                    S.op("pool", I("tensor_tensor", out=bnag[:, 1:2], in0=bnag[:, 1:2], in1=mhalf1[:], op=ALU.pow), [B_bn, B_const], [B_bn])
                    S.op("dve", I("tensor_scalar", out=bnag[:, 1:2], in0=bnag[:, 1:2], scalar1=0.5, scalar2=None, op0=ALU.mult), [B_bn], [B_bn])
                    S.op("dve", I("tensor_scalar", out=vt2[:], in0=vt[:], scalar1=bnag[:, 0:1], scalar2=bnag[:, 1:2],
                                                          op0=ALU.subtract, op1=ALU.mult), [B_vt, B_bn], [B_vt2])
                    S.op("dve", I("tensor_tensor", out=vt2[:], in0=vt2[:], in1=lng[:], op=ALU.mult), [B_vt2, B_lw], [B_vt2])
                    S.op("dve", I("tensor_tensor", out=vn[:], in0=vt2[:], in1=lnb[:], op=ALU.add), [B_vt2, B_lw], [B_vn])
                    for g in range(4):
                        S.op("pe", I("matmul", pmx[:, g, :], lhsT=vn[:, g * 128:(g + 1) * 128], rhs=wsT[:, g, :], start=True, stop=True),
                             [B_vn, B_lw], [B_pmx])
                    t1, B1 = get_tmp()
                    t1v = t1[:].rearrange("p (g t) -> p g t", g=4)
                    S.op("dve", I("tensor_tensor", out=t1v, in0=pmx[:], in1=bsp[:], op=ALU.add), [B_pmx, B_lw], [B1])
                    S.op("pool", I("tensor_tensor", out=t1v, in0=t1v, in1=ub[:, :, tsl], op=ALU.mult), [B1] + B_ub, [B1])
                    S.op("dve", I("tensor_tensor", out=yb[:, :, tsl], in0=t1v, in1=zs[:, 4:8, tsl], op=ALU.mult),
                         [B1] + B_zs[4:8], [B_yb])
            if cfg.do_c:
                top = (m == n_mt - 1)
                has_lo = (m > 0)
                t0 = m * 512 - 128
                if has_lo:
                    dma("sp", cs[:, 0, :], cosT_d[:, t0:t0 + 640], [], [B_cs], sl_cs)
                    dma("sp", cs[:, 1, :], sinT_d[:, t0:t0 + 640], [], [B_cs], sl_cs)
                    dma("sp", xh[:], xsrc(l).ap().rearrange("(c p) t -> p c t", p=128)[:, :, t0:t0 + 128], [DRAM_x[m - 1]], [B_xh], sl_xh)
                    for c in range(8):
                        S.op("act", I("activation", out=sqh[:, c, :], in_=xh[:, c, :], func=AF.Square), [B_xh], [B_sqh])
                    for c in range(8):
                        S.op("pe", I("matmul", pst[:, 0:128], lhsT=ones_b[:], rhs=sqh[:, c, :], start=(c == 0), stop=(c == 7)),
                             [B_sqh, B_const], [B_pst])
                    S.op("act", I("activation", out=rtmp[:, 0:128], in_=pst[:, 0:128], func=AF.Ln, scale=1.0 / D, bias=epsb[:]), [B_pst, B_const], [B_rtmp])
                    S.op("act", I("activation", out=rtmp[:, 128:256], in_=rtmp[:, 0:128], func=AF.Exp, scale=-0.5), [B_rtmp], [B_rtmp])
                    for c in range(8):
                        S.op("dve", I("scalar_tensor_tensor", out=hTh[:, c, :], in0=xh[:, c, :], scalar=ngain[:, l, c:c + 1],
                                                                           in1=rtmp[:, 128:256], op0=ALU.mult, op1=ALU.mult),
                             [B_xh, B_rtmp, B_const], [B_hTh])
                else:
                    dma("sp", cs[:, 0, 128:640], cosT_d[:, 0:512], [], [B_cs], sl_cs)
                    dma("sp", cs[:, 1, 128:640], sinT_d[:, 0:512], [], [B_cs], sl_cs)
                if not top:
                    S.op("pool", I("tensor_copy", out=kz[:, :, :, 640:768], in_=kz[:, :, :, 128:256]), [B_kr], [B_kr])
                    S.op("pool", I("tensor_copy", out=vaug[:, 5, :, :], in_=vaug[:, 1, :, :]), [B_vaug], [B_vaug])

                def qk_post(p, Bp, n, which, dst, Bdst, csl):
                    t1, B1 = get_tmp()
                    t2, B2 = get_tmp()
                    t3, B3 = get_tmp()
                    sqb, Bsqb = ya_scr, B_ya_scr
                    S.op("act", I("activation", out=sqb[:, 0:n], in_=p[:, 0:n], func=AF.Square), [Bp], [Bsqb])
                    S.op("dve", I("tensor_scalar", out=sqb[:, 512:512 + n], in0=p[:, 0:n], scalar1=qkg[:, l, which:which + 1], scalar2=None, op0=ALU.mult),
                         [Bp, B_const], [Bsqb])
                    S.op("pe", I("matmul", pst[:, 0:n], lhsT=bd_b[:], rhs=sqb[:, 0:n], start=True, stop=True), [Bsqb, B_const], [B_pst])
                    pr, Bpr = get_pp()
                    S.op("pe", I("matmul", pr[:, 0:n], lhsT=rot_b[:], rhs=sqb[:, 512:512 + n], start=True, stop=True), [Bsqb, B_const], [Bpr])
                    S.op("act", I("activation", out=t1[:, 0:n], in_=pst[:, 0:n], func=AF.Ln, scale=1.0 / 64, bias=epsb[:]), [B_pst, B_const], [B1])
                    S.op("act", I("activation", out=t1[:, 0:n], in_=t1[:, 0:n], func=AF.Exp, scale=-0.5), [B1], [B1])
                    S.op("pool", I("tensor_tensor", out=t2[:, 0:n], in0=sqb[:, 512:512 + n], in1=cs[:, 0, csl], op=ALU.mult), [Bsqb, B_cs], [B2])
                    S.op("dve", I("tensor_tensor", out=t3[:, 0:n], in0=pr[:, 0:n], in1=cs[:, 1, csl], op=ALU.mult), [Bpr, B_cs], [B3])
                    S.op("dve", I("tensor_tensor", out=t2[:, 0:n], in0=t2[:, 0:n], in1=t3[:, 0:n], op=ALU.add), [B2, B3], [B2])
                    if which == 0:
                        S.op("dve", I("tensor_tensor", out=dst, in0=t2[:, 0:n], in1=t1[:, 0:n], op=ALU.mult), [B2, B1], [Bdst])
                    else:
                        jh, c0 = dst
                        S.op("dve", I("tensor_tensor", out=kz[0:64, 0, jh, c0:c0 + n], in0=t2[0:64, 0:n], in1=t1[0:64, 0:n], op=ALU.mult), [B2, B1], [Bdst])
                        S.op("dve", I("tensor_tensor", out=kz[64:128, 1, jh, c0:c0 + n], in0=t2[64:128, 0:n], in1=t1[64:128, 0:n], op=ALU.mult), [B2, B1], [Bdst])

                for j in range(4):
                    p, Bp = proj_fm(l, w2fm_b, base["qC"] + j)
                    qk_post(p, Bp, 512, 0, qr[:, j, :], B_qr[j], slice(128, 640))
                for j in range(2):
                    w, Bw = load_w_chunk(l, w2fm_b, base["kC"] + j)
                    p, Bp = get_pp()
                    for k in range(8):
                        S.op("pe", I("matmul", p[:], lhsT=w[:, k, :], rhs=hT[:, k, :], start=(k == 0), stop=(k == 7)), [Bw, B_hT], [Bp])
                    qk_post(p, Bp, 512, 1, (j, 128), B_kr, slice(128, 640))
                    if has_lo:
                        p, Bp = get_pp()
                        for k in range(8):
                            S.op("pe", I("matmul", p[:, 0:128], lhsT=w[:, k, :], rhs=hTh[:, k, :], start=(k == 0), stop=(k == 7)), [Bw, B_hTh], [Bp])
                        qk_post(p, Bp, 128, 1, (j, 0), B_kr, slice(0, 128))
                for j in range(4):
                    p, Bp = proj_fm(l, w2fm_b, base["zC"] + j)
                    silu_from_psum(p, Bp, zs[:, 8 + j, :], B_zs[8 + j])
                for sl_i in ([0] if has_lo else []) + [1, 2, 3, 4]:
                    p, Bp = get_pp()
                    for k in range(8):
                        if sl_i == 0:
                            S.op("pe", I("matmul", p[:, 0:128], lhsT=hTh[:, k, :], rhs=wtm[:, k, 1024:1152], start=(k == 0), stop=(k == 7)),
                                 [B_hTh, B_wtm], [Bp])
                        else:
                            tsl = slice((sl_i - 1) * 128, sl_i * 128)
                            S.op("pe", I("matmul", p[:, 0:128], lhsT=hT[:, k, tsl], rhs=wtm[:, k, 1024:1152], start=(k == 0), stop=(k == 7)),
                                 [B_hT, B_wtm], [Bp])
                    pv = p[:, 0:128].rearrange("p (h d) -> p h d", h=2)
                    S.op("act", I("activation", out=vaug[:, sl_i, :, 0:64], in_=pv, func=AF.Copy), [Bp], [B_vaug])
                    S.op("dve", I("tensor_copy", out=vaug[:, sl_i, :, 128:192], in_=pv), [Bp], [B_vaug])
                kc_stage = int(os.environ.get("KC_STAGE", "9"))
                if top and kc_stage >= 2:
                    S.op("dve", I("tensor_copy", out=ccs[0:64, 512:768].rearrange("p (h t) -> p h t", h=2), in_=kz[0:64, 0, :, 512:640]), [B_kr], [B_ccs])
                    S.op("dve", I("tensor_copy", out=ccs[64:128, 512:768].rearrange("p (h t) -> p h t", h=2), in_=kz[64:128, 1, :, 512:640]), [B_kr], [B_ccs])
                    S.op("dve", I("tensor_copy", out=ccs[:, 768:896].rearrange("p (h d) -> p h d", h=2), in_=vaug[:, 4, :, 0:64]), [B_vaug], [B_ccs])
                    if not cfg.do_a:
                        S.op("dve", I("memset", ccs[:, 0:512], 0.0), [], [B_ccs])
                    else:
                        S.op("dve", I("tensor_copy", out=ccs[:, 0:512], in_=St[:].rearrange("p h v -> p (h v)")), [B_S], [B_ccs])
                    dma("pool", cc_in[l].ap(), ccs, [B_ccs], [DRAM_cc], sl_cc)
                    o = S.op("pool", I("collective_compute", "AllGather", ALU.bypass, replica_groups=[[0, 1], [2, 3], [4, 5], [6, 7]],
                                                                  ins=[cc_in[l].ap().opt()], outs=[cc_out[l].ap().opt()]), [DRAM_cc], [DRAM_cc])
                    o.signal = True
                    dma("pool", ccg, cc_out[l].ap().rearrange("(r p) n -> p r n", p=128), [DRAM_cc], [B_ccg], sl_cc)
                    S.op("dve", I("tensor_scalar", out=ccp, in0=ccg[:, 0, :], scalar1=selt[:, 0:1], scalar2=None, op0=ALU.mult), [B_ccg, B_const], [B_ccp])
                    S.op("dve", I("scalar_tensor_tensor", out=ccp, in0=ccg[:, 1, :], scalar=selt[:, 1:2], in1=ccp, op0=ALU.mult, op1=ALU.add),
                         [B_ccg, B_const, B_ccp], [B_ccp])
                    S.op("dve", I("tensor_copy", out=kz[0:64, 0, :, 640:768], in_=ccp[0:64, 512:768].rearrange("p (h t) -> p h t", h=2)), [B_ccp], [B_kr])
                    S.op("dve", I("tensor_copy", out=kz[64:128, 1, :, 640:768], in_=ccp[64:128, 512:768].rearrange("p (h t) -> p h t", h=2)), [B_ccp], [B_kr])
                    S.op("dve", I("tensor_copy", out=vaug[:, 5, :, 0:64], in_=ccp[:, 768:896].rearrange("p (h d) -> p h d", h=2)), [B_ccp], [B_vaug])
                    S.op("dve", I("tensor_copy", out=vaug[:, 5, :, 128:192], in_=ccp[:, 768:896].rearrange("p (h d) -> p h d", h=2)), [B_ccp], [B_vaug])
                    if cfg.do_a:
                        S.op("dve", I("tensor_copy", out=St[:].rearrange("p h v -> p (h v)"), in_=ccp[:, 0:512]), [B_ccp], [B_S])
                for sb_i in ((4, 3, 2, 1) if kc_stage >= 3 else ()):
                    jglob = m * 4 + sb_i - 1
                    tsl = slice((sb_i - 1) * 128, sb_i * 128)
                    kbs = []
                    if jglob > 0:
                        kbs.append((sb_i - 1, 0))
                    kbs.append((sb_i, None))
                    kbs.append((sb_i + 1, 2 if (top and sb_i == 4) else 1))
                    for h in range(2):
                        pssv = [pbr[i][:].rearrange("p (e n) -> p e n", e=2) for i in range(3)]
                        for ki, (slk, mk) in enumerate(kbs):
                            for e_ in range(2):
                                rows = slice(e_ * 64, (e_ + 1) * 64)
                                S.op("pe", I("matmul",
                                    pssv[ki][:, e_, :].rearrange("p (c t) -> p c t", c=2), lhsT=kz[:, e_, h, slk * 128:(slk + 1) * 128],
                                    rhs=qr[:, 2 * h:2 * h + 2, tsl], start=True, stop=True),
                                    [B_kr] + B_qr, [B_pbr[ki]])
                            S.op("act", I("activation", out=pt[:, ki, :, :], in_=pssv[ki], func=AF.Exp, scale=0.125), [B_pbr[ki]], [B_pt])
                            if mk is not None and kc_stage >= 4:
                                ptv = pt[:, ki, :, :].rearrange("p e (c t) -> p (e c) t", c=2)
                                S.op("pool", I("tensor_tensor", out=ptv, in0=ptv, in1=amask[:, mk:mk + 1, :].to_broadcast([128, 4, 128]), op=ALU.mult),
                                     [B_pt, B_const], [B_pt])
                        pso = pmx[:].rearrange("p a b -> p (a b)").rearrange("p (e n) -> p e n", e=2)
                        for e_ in (range(2) if kc_stage >= 5 else ()):
                            for ki, (slk, mk) in enumerate(kbs):
                                S.op("pe", I("matmul", pso[:, e_, :], lhsT=vaug[:, slk, h, e_ * 64:e_ * 64 + 128], rhs=pt[:, ki, e_, :],
                                                                                       start=(ki == 0), stop=(ki == len(kbs) - 1)), [B_vaug, B_pt], [B_pmx])
                        for e_ in (range(2) if kc_stage >= 6 else ()):
                            nr = slice(0, 64) if e_ == 0 else slice(64, 128)
                            dr = slice(64, 128) if e_ == 0 else slice(0, 64)
                            for c in range(2):
                                head = 2 * (2 * h + c) + e_
                                S.op("dve", I("tensor_scalar",
                                    out=dtmp[nr, c * 128:(c + 1) * 128], in0=pso[dr, e_, c * 128:(c + 1) * 128], scalar1=esink[dr, l, head:head + 1], scalar2=None, op0=ALU.add),
                                    [B_pmx, B_const], [B_dtmp])
                            S.op("act", I("activation", out=dtmp[nr, :], in_=dtmp[nr, :], func=AF.Ln), [B_dtmp], [B_dtmp])
                            S.op("act", I("activation", out=dtmp[nr, :], in_=dtmp[nr, :], func=AF.Exp, scale=-1.0), [B_dtmp], [B_dtmp])
                            S.op("dve", I("tensor_tensor", out=dtmp[nr, :], in0=pso[nr, e_, :], in1=dtmp[nr, :], op=ALU.mult), [B_pmx, B_dtmp], [B_dtmp])
                            S.op("pool", I("tensor_tensor", out=yc[nr, 2 * h:2 * h + 2, tsl], in0=dtmp[nr, :].rearrange("p (c t) -> p c t", c=2),
                                                                                    in1=zs[nr, 8 + 2 * h:8 + 2 * h + 2, tsl], op=ALU.mult),
                                 [B_dtmp] + B_zs[8:12], [B_yc])
            if cfg.do_a:
                hgrn_q_and_v(l, w2fm_b, base["qA"])
                hgrn_gates(l, 1, w2fm_b, base["a2"])
                for j in range(4):
                    p, Bp = proj_fm(l, w2fm_b, base["zA"] + j)
                    silu_from_psum(p, Bp, zs[:, j, :], B_zs[j])
                for bi in (3, 2, 1, 0):
                    hgrn_vtok(bi)
                    hgrn_block(l, 1, bi, m)
            scal = {0: 0.25, 1: 0.125, 2: 0.25}
            for dc in range(8):
                dsl = slice(dc * 128, (dc + 1) * 128)
                acc, Bacc = get_tmp()
                first = True
                wi = _wbrc_i[0] % 2
                _wbrc_i[0] += 1
                dma("sp", wbrc[wi][:], wbr_b[l, dc], [B_wcast[l]], [B_wbrc[wi]], sl_wbrc[wi])
                for bi, (on, ysrc, By) in enumerate(((cfg.do_a, ya, B_ya), (cfg.do_b, yb, B_yb), (cfg.do_c, yc, B_yc))):
                    if not on:
                        continue
                    for k in range(4):
                        S.op("pe", I("matmul", pbr[bi][:], lhsT=wbrc[wi][:, bi, k, :], rhs=ysrc[:, k, :],
                                                                                     start=(k == 0), stop=(k == 3)), [B_wbrc[wi], By], [B_pbr[bi]])
                    pg, Bpg = proj_fm(l, w2fm_b, base[("gA", "gB", "gC")[bi]] + dc)
                    gtile, Bg = get_tmp()
                    S.op("act", tanh_gate(gtile[:], pg[:], 0.5), [Bpg], [Bg])
                    g = gtile[:]
                    if first:
                        S.op("dve", I("scalar_tensor_tensor", out=acc[:], in0=g, scalar=1.0, in1=pbr[bi][:], op0=ALU.add, op1=ALU.mult),
                             [Bg, B_pbr[bi]], [Bacc])
                        S.op("pool", I("tensor_scalar", out=acc[:], in0=acc[:], scalar1=scal[bi], scalar2=None, op0=ALU.mult), [Bacc], [Bacc])
                        first = False
                    else:
                        t2, B2 = get_tmp()
                        S.op("dve", I("scalar_tensor_tensor", out=t2[:], in0=g, scalar=1.0, in1=pbr[bi][:], op0=ALU.add, op1=ALU.mult),
                             [Bg, B_pbr[bi]], [B2])
                        S.op("dve", I("scalar_tensor_tensor", out=acc[:], in0=t2[:], scalar=scal[bi], in1=acc[:], op0=ALU.mult, op1=ALU.add),
                             [B2, Bacc], [Bacc])
                S.op("act", I("activation", out=mg[:, dc, :], in_=acc[:], func=AF.Copy), [Bacc], [B_mg[dc]])
            for ec in range(8):
                esl = slice(ec * 128, (ec + 1) * 128)
                w, Bw = load_w_chunk(l, wo_b, ec)
                p, Bp = get_pp()
                for k in range(8):
                    S.op("pe", I("matmul", p[:], lhsT=w[:, k, :], rhs=mg[:, k, :], start=(k == 0), stop=(k == 7)),
                         [Bw, B_mg[k]], [Bp])
                S.op("dve", I("tensor_tensor", out=xo[:, ec, :], in0=p[:], in1=xt[:, ec, :], op=ALU.add), [Bp, B_xt], [B_xo])
            dma("sp", xview(out_d, m), xo[:], [B_xo], [DRAM_x[m]], sl_xo)

        import os
        stop = os.environ.get("KSTOP", "")
        for l in range(depth):
            S.epoch = l
            if stop == "cast":
                for m in range(n_mt):
                    dma("sp", xt[:], xview(xsrc(l), m), [DRAM_x[m]] + B_wcast, [B_xt], sl_xt)
                    dma("sp", xview(out_d, m), xt[:], [B_xt], [DRAM_x[m]], sl_xo)
                continue
            layer_weights(l)
            if stop == "lw":
                for m in range(n_mt):
                    dma("sp", xt[:], xview(xsrc(l), m), [DRAM_x[m]] + B_wcast + [B_wtm, B_lw], [B_xt], sl_xt)
                    dma("sp", xview(out_d, m), xt[:], [B_xt], [DRAM_x[m]], sl_xo)
                continue
            if cfg.do_a:
                S.op("dve", I("memset", St[:], 0.0), [], [B_S])
                for m in range(n_mt):
                    sweep1_mt(l, m)
            for m in reversed(range(n_mt)):
                sweep2_mt(l, m)
        S.epoch = depth
        S.op("sp", I("nop"), reads=[DRAM_x[m] for m in range(n_mt)], writes=[])

        with nc.Block() as block:
            S.finalize(engsems, block)
        build_nc.last_stats = S.stats
    return nc


def run(inputs, cfg, trace=False):
    nc = build_nc(cfg)
    in_maps = [prep_core_inputs(inputs, c, cfg) for c in range(NCORES)]
    res = run_bass_kernel_spmd(nc, in_maps, core_ids=list(range(NCORES)), trace=trace)
    T = cfg.T
    L = 2 * T
    B = NCORES // 2
    out = np.empty((B, L, D), np.float32)
    for c in range(NCORES):
        o = np.asarray(res.results[c]["out"]).T
        if c % 2 == 0:
            out[c // 2, :T] = o
        else:
            out[c // 2, T:] = o[::-1]
    return out, res


def kernel(**inputs):
    cfg = Cfg()
    out, _ = run(inputs, cfg)
    return out
```

```python
import numpy as np
import concourse.bass as bass
import concourse.mybir as mybir
from concourse.bass_utils import run_bass_kernel_spmd

F32 = mybir.dt.float32
BF16 = mybir.dt.bfloat16
ALU = mybir.AluOpType
AF = mybir.ActivationFunctionType

D = 1024
DEPTH = 4
EPS = 1e-6
NCORES = 8
SAME_ENGINE_SYNC = True


def I(name, *args, **kw):
    return lambda e: getattr(e, name)(*args, **kw)


class Buf:
    __slots__ = ("name", "last_w", "readers")

    def __init__(self, name):
        self.name = name
        self.last_w = None
        self.readers = []


class Slot:
    def __init__(self, sem, name):
        self.sem = sem
        self.count = 0
        self.token = Buf("slot_" + name)


class Op:
    __slots__ = ("eng", "fn", "deps", "signal", "val", "sem", "slot", "idx", "epoch", "raw")


class Sched:
    ENGS = ("pe", "act", "dve", "pool", "sp")

    def __init__(self, nc, n_epochs):
        self.nc = nc
        self.ops = {e: [] for e in self.ENGS}
        self.epoch = 0
        self.n_epochs = n_epochs
        self.engsem = {}

    def op(self, eng, fn, reads=(), writes=(), slot=None):
        o = Op()
        o.eng = eng
        o.fn = fn
        o.signal = False
        o.val = None
        o.sem = None
        o.slot = slot
        o.epoch = self.epoch
        deps = []
        raw = set()
        writes = list(writes)
        if slot is not None:
            writes.append(slot.token)
        for b in reads:
            if b.last_w is not None:
                deps.append(b.last_w)
                raw.add(id(b.last_w))
        for b in writes:
            if b.last_w is not None:
                deps.append(b.last_w)
            deps.extend(b.readers)
        seen = set()
        dd = []
        for d in deps:
            if id(d) not in seen and d is not o:
                seen.add(id(d))
                dd.append(d)
        o.deps = dd
        o.raw = raw
        for b in reads:
            b.readers.append(o)
        for b in writes:
            b.last_w = o
            b.readers = []
        if slot is not None:
            slot.count += 1
            o.sem = slot.sem
            o.val = 16 * slot.count
        o.idx = len(self.ops[eng])
        self.ops[eng].append(o)
        return o

    def _needs_wait(self, cons, prod):
        if prod.slot is not None:
            return True
        if prod.eng == cons.eng and cons.slot is None:
            if prod.eng == "pe":
                return False
            return SAME_ENGINE_SYNC and (id(prod) in cons.raw)
        return True

    def finalize(self, sems, block):
        for e in self.ENGS:
            for o in self.ops[e]:
                for d in o.deps:
                    if self._needs_wait(o, d):
                        d.signal = True
        for e in self.ENGS:
            cnt = {}
            for o in self.ops[e]:
                if o.slot is not None:
                    pass
                elif o.signal:
                    cnt[o.epoch] = cnt.get(o.epoch, 0) + 1
                    o.sem = sems[(e, o.epoch)]
                    o.val = cnt[o.epoch]
        self.stats = {e: len(self.ops[e]) for e in self.ENGS}

        def emit(e, eng):
            waited = {}
            nwaits = 0
            for o in self.ops[e]:
                for d in o.deps:
                    if not self._needs_wait(o, d):
                        continue
                    key = id(d.sem)
                    if waited.get(key, 0) >= d.val:
                        continue
                    eng.wait_ge(d.sem, d.val)
                    nwaits += 1
                    waited[key] = d.val
                ins = o.fn(eng)
                if o.slot is not None:
                    ins.then_inc(o.sem, 16)
                elif o.signal:
                    ins.then_inc(o.sem, 1)
            self.stats[e + "_waits"] = nwaits

        @block.tensor
        def _(eng):
            emit("pe", eng)

        @block.scalar
        def _(eng):
            emit("act", eng)

        @block.vector
        def _(eng):
            emit("dve", eng)

        @block.gpsimd
        def _(eng):
            emit("pool", eng)

        @block.sync
        def _(eng):
            emit("sp", eng)


OFF = dict(qA=0, fAf=512, fAb=1024, iA=1536, zA=2048, uB=2560, vB=3072, zB=3584, qC=4096, kC=4608,
           vC=4736, zC=4864, gA=5376, gB=6400, gC=7424)

FM2 = (["a2"] * 4 + ["qA"] * 4 + ["zA"] * 4 + ["uB"] * 4 + ["zB"] * 4 + ["qC"] * 4 + ["kC"] * 2 + ["zC"] * 4
       + ["gA"] * 8 + ["gB"] * 8 + ["gC"] * 8)
FM1 = ["a1"] * 4 + ["qA"] * 4


def _fm_cols(kind_list, odd):
    out = []
    cnt = {}
    for kind in kind_list:
        j = cnt.get(kind, 0)
        cnt[kind] = j + 1
        if kind == "a1":
            base = OFF["fAb"] if odd else OFF["fAf"]
            cols = np.arange(base + j * 128, base + (j + 1) * 128)
        elif kind == "a2":
            base = OFF["fAf"] if odd else OFF["fAb"]
            cols = np.arange(base + j * 128, base + (j + 1) * 128)
        elif kind == "kC":
            c = np.arange(OFF["kC"] + j * 64, OFF["kC"] + (j + 1) * 64)
            cols = np.concatenate([c, c])
        else:
            base = OFF[kind]
            cols = np.arange(base + j * 128, base + (j + 1) * 128)
        out.append(cols)
    return out


def _tm_cols(sweep):
    if sweep == 1:
        return np.arange(OFF["iA"], OFF["iA"] + 512)
    return np.concatenate([np.arange(OFF["iA"], OFF["iA"] + 512), np.arange(OFF["vB"], OFF["vB"] + 512),
                           np.arange(OFF["vC"], OFF["vC"] + 128)])


NF1 = len(FM1)
NF2 = len(FM2)
TM1 = 512
TM2 = 1152


class Cfg:
    def __init__(self, n_mt=8, depth=DEPTH, do_a=True, do_b=True, do_c=True):
        self.n_mt = n_mt
        self.T = n_mt * 512
        self.NB = n_mt * 4
        self.depth = depth
        self.do_a = do_a
        self.do_b = do_b
        self.do_c = do_c


def prep_core_inputs(inp, core, cfg):
    T = cfg.T
    L = 2 * T
    b = core // 2
    odd = core % 2
    depth = cfg.depth
    f32 = np.float32
    pos = (np.arange(T) if not odd else (L - 1 - np.arange(T))).astype(np.int64)
    m = {}
    x = np.asarray(inp["x"])[b]
    m["xT"] = np.ascontiguousarray(x[pos, :].T).astype(f32)
    w_in = np.asarray(inp["w_in"])
    w1fm = np.empty((depth, NF1, 128, 8, 128), f32)
    w2fm = np.empty((depth, NF2, 128, 8, 128), f32)
    w1tm = np.empty((depth, 128, 8, TM1), f32)
    w2tm = np.empty((depth, 128, 8, TM2), f32)
    c1 = _fm_cols(FM1, odd)
    c2 = _fm_cols(FM2, odd)
    for l in range(depth):
        wl = w_in[l].reshape(8, 128, -1)
        for j, cols in enumerate(c1):
            w1fm[l, j] = wl[:, :, cols].transpose(1, 0, 2)
        for j, cols in enumerate(c2):
            w2fm[l, j] = wl[:, :, cols].transpose(1, 0, 2)
        w1tm[l] = wl[:, :, _tm_cols(1)].transpose(1, 0, 2)
        w2tm[l] = wl[:, :, _tm_cols(2)].transpose(1, 0, 2)
    m["w1fm"] = w1fm
    m["w2fm"] = w2fm
    m["w1tm"] = w1tm
    m["w2tm"] = w2tm
    wbr = np.empty((depth, 8, 128, 3, 4, 128), f32)
    for bi, key in enumerate(("w_branch_a", "w_branch_b", "w_branch_c")):
        w = np.asarray(inp[key])[:depth].reshape(depth, 4, 128, 8, 128)
        wbr[:, :, :, bi] = w.transpose(0, 3, 2, 1, 4)
    m["wbr"] = wbr
    w = np.asarray(inp["w_out"])[:depth].reshape(depth, 8, 128, 8, 128)
    m["wo"] = np.ascontiguousarray(w.transpose(0, 3, 2, 1, 4)).astype(f32)
    ng = np.asarray(inp["norm_gain"])[:depth]
    m["ngain"] = np.ascontiguousarray(ng.reshape(depth, 8, 128).transpose(2, 0, 1)).astype(f32)
    lb = np.asarray(inp["lb_logits"]).reshape(DEPTH, 2, 4, 128)
    if odd:
        lb = lb[:, ::-1]
    m["lbl"] = np.ascontiguousarray(lb.transpose(3, 0, 1, 2)).astype(f32)
    hg = np.asarray(inp["hg_norm_gain"])[:depth]
    m["hgain"] = np.ascontiguousarray(hg.transpose(2, 0, 1)).astype(f32)
    m["lng"] = np.ascontiguousarray(np.broadcast_to(np.asarray(inp["sg_ln_gain"])[:depth][None], (128, depth, 512))).astype(f32)
    m["lnb"] = np.ascontiguousarray(np.broadcast_to(np.asarray(inp["sg_ln_bias"])[:depth][None], (128, depth, 512))).astype(f32)
    ws = np.asarray(inp["w_spatial"])[:depth]
    bs = np.asarray(inp["b_spatial"])[:depth]
    if odd:
        ws = ws[:, :, ::-1, ::-1]
        bs = bs[:, :, ::-1]
    m["wsT"] = np.ascontiguousarray(ws.transpose(3, 0, 1, 2)).astype(f32)
    m["bsp"] = np.ascontiguousarray(np.broadcast_to(bs[None], (128, depth, 4, 128))).astype(f32)
    qg = np.asarray(inp["q_norm_gain"])[:depth]
    kg = np.asarray(inp["k_norm_gain"])[:depth]
    m["qkg"] = np.ascontiguousarray(np.stack([np.concatenate([qg, qg], 1), np.concatenate([kg, kg], 1)], 1).transpose(2, 0, 1)).astype(f32)
    m["sink"] = np.ascontiguousarray(np.broadcast_to(np.asarray(inp["sink_logits"])[:depth][None], (128, depth, 8))).astype(f32)
    half = 32
    inv_freq = (10000.0 ** (-np.arange(half, dtype=np.float32) / half)).astype(np.float32)
    ang = pos.astype(np.float32)[None, :] * inv_freq[:, None]
    cos = np.cos(ang).astype(f32)
    sin = np.sin(ang).astype(f32)
    m["cosT"] = np.ascontiguousarray(np.concatenate([cos, cos, cos, cos], 0))
    m["sinT"] = np.ascontiguousarray(np.concatenate([sin, sin, sin, sin], 0))
    ident = np.eye(128, dtype=f32)
    m["c_ident"] = ident
    bd = np.zeros((128, 128), f32)
    bd[:64, :64] = 1
    bd[64:, 64:] = 1
    m["c_bd"] = bd
    rot = np.zeros((128, 128), f32)
    for hb in (0, 64):
        for d in range(32):
            rot[hb + d + 32, hb + d] = -1.0
            rot[hb + d, hb + d + 32] = 1.0
    m["c_rot"] = rot
    j = np.arange(128)[:, None]
    i = np.arange(128)[None, :]
    masks = np.stack([(j >= i), (j <= i), (j + i >= 127)], 1).astype(f32)
    m["c_amask"] = np.ascontiguousarray(masks)
    s = np.arange(64)[:, None]
    t = np.arange(64)[None, :]
    h1 = (s <= t).astype(f32)
    h2 = (s >= t).astype(f32)
    m["c_hmask"] = np.ascontiguousarray(np.stack([np.concatenate([h1, h1], 0), np.concatenate([h2, h2], 0)], 1))
    cm = np.ones((128, 512), f32)
    cm[:, ::64] = 0.0
    m["c_cmask"] = cm
    ss_ = np.arange(128)[:, None]
    tt_ = np.arange(128)[None, :]
    same = (ss_ // 64) == (tt_ // 64)
    m["c_hmask2"] = np.ascontiguousarray(np.stack([(same & (ss_ <= tt_)), (same & (ss_ >= tt_))], 1).astype(f32))
    sel = np.zeros((128, 2), f32)
    sel[:, 1 - odd] = 1.0
    m["sel"] = sel
    return m


def build_nc(cfg):
    nc = bass.Bass("TRN2", target_bir_lowering=False)
    T = cfg.T
    depth = cfg.depth
    n_mt = cfg.n_mt

    def din(name, shape, dt=F32):
        return nc.dram_tensor(name, list(shape), dt, kind="ExternalInput")

    xT_d = din("xT", [D, T])
    w1fm_d = din("w1fm", [depth, NF1, 128, 8, 128])
    w2fm_d = din("w2fm", [depth, NF2, 128, 8, 128])
    w1tm_d = din("w1tm", [depth, 128, 8, TM1])
    w2tm_d = din("w2tm", [depth, 128, 8, TM2])
    wbr_d = din("wbr", [depth, 8, 128, 3, 4, 128])
    wo_d = din("wo", [depth, 8, 128, 8, 128])
    ngain_d = din("ngain", [128, depth, 8])
    lbl_d = din("lbl", [128, DEPTH, 2, 4])
    hgain_d = din("hgain", [128, depth, 4])
    lng_d = din("lng", [128, depth, 512])
    lnb_d = din("lnb", [128, depth, 512])
    wsT_d = din("wsT", [128, depth, 4, 128])
    bsp_d = din("bsp", [128, depth, 4, 128])
    qkg_d = din("qkg", [128, depth, 2])
    sink_d = din("sink", [128, depth, 8])
    cosT_d = din("cosT", [128, T])
    sinT_d = din("sinT", [128, T])
    c_ident_d = din("c_ident", [128, 128])
    c_bd_d = din("c_bd", [128, 128])
    c_rot_d = din("c_rot", [128, 128])
    c_amask_d = din("c_amask", [128, 3, 128])
    c_hmask_d = din("c_hmask", [128, 2, 64])
    sel_d = din("sel", [128, 2])
    c_cmask_d = din("c_cmask", [128, 512])
    c_hmask2_d = din("c_hmask2", [128, 2, 128])
    out_d = nc.dram_tensor("out", [D, T], F32, kind="ExternalOutput")

    w1fm_b = nc.dram_tensor("w1fm_b", [depth, NF1, 128, 8, 128], BF16)
    w2fm_b = nc.dram_tensor("w2fm_b", [depth, NF2, 128, 8, 128], BF16)
    w1tm_b = nc.dram_tensor("w1tm_b", [depth, 128, 8, TM1], BF16)
    w2tm_b = nc.dram_tensor("w2tm_b", [depth, 128, 8, TM2], BF16)
    wbr_b = nc.dram_tensor("wbr_b", [depth, 8, 128, 3, 4, 128], BF16)
    wo_b = nc.dram_tensor("wo_b", [depth, 8, 128, 8, 128], BF16)
    o1_d = nc.dram_tensor("o1_spill", [128, cfg.NB, 4, 128], F32)
    CCW = 512 + 256 + 128
    cc_in = [nc.dram_tensor(f"cc_in{l}", [128, CCW], F32) for l in range(depth)]
    cc_out = [nc.dram_tensor(f"cc_out{l}", [256, CCW], F32) for l in range(depth)]

    from contextlib import ExitStack
    es = ExitStack()
    with es:
        S = Sched(nc, depth + 1)

        def sb(name, shape, dt=F32):
            return es.enter_context(nc.sbuf_tensor(name, list(shape), dt))

        def ps(name, shape, dt=F32):
            return es.enter_context(nc.psum_tensor(name, list(shape), dt))

        def sem(name):
            return es.enter_context(nc.semaphore(name))

        engsems = {(e, ep): sem(f"s_{e}_{ep}") for e in ("pe", "act", "dve", "pool") for ep in range(depth + 1)}
        _slot_n = [0]

        def slot(name):
            _slot_n[0] += 1
            return Slot(sem(f"d_{name}_{_slot_n[0]}"), name)

        ident_b = sb("ident_b", [128, 128], BF16)
        ones_b = sb("ones_b", [128, 128], BF16)
        bd_b = sb("bd_b", [128, 128], BF16)
        rot_b = sb("rot_b", [128, 128], BF16)
        amask = sb("amask", [128, 3, 128], BF16)
        hmask = sb("hmask", [128, 2, 64], BF16)
        selt = sb("selt", [128, 2])
        ngain = sb("ngain_s", [128, depth, 8])
        lbl = sb("lbl_s", [128, DEPTH, 2, 4])
        hgain = sb("hgain_s", [128, depth, 4])
        qkg = sb("qkg_s", [128, depth, 2])
        sinkt = sb("sink_s", [128, depth, 8])
        esink = sb("esink", [128, depth, 8])
        lbc1 = sb("lbc1", [128, DEPTH, 2, 4])
        lbc0 = sb("lbc0", [128, DEPTH, 2, 4])
        B_const = Buf("const")
        sl_c = slot("const")

        def dma(eng, out, in_, reads, writes, sl):
            return S.op(eng, I("dma_start", out=out, in_=in_), reads=reads, writes=writes, slot=sl)

        for dst, src in ((ident_b, c_ident_d), (bd_b, c_bd_d), (rot_b, c_rot_d), (amask, c_amask_d), (hmask, c_hmask_d)):
            dma("pool", dst[:], src.ap(), [], [B_const], sl_c)
        for dst, src in ((selt, sel_d), (ngain, ngain_d), (lbl, lbl_d), (hgain, hgain_d), (qkg, qkg_d), (sinkt, sink_d)):
            dma("sp", dst[:], src.ap(), [], [B_const], sl_c)
        S.op("dve", I("memset", ones_b[:], 1.0), [], [B_const])
        S.op("act", I("activation", out=esink[:], in_=sinkt[:], func=AF.Exp), [B_const], [B_const])
        lbe = sb("lbe", [128, DEPTH, 8])
        lbs = sb("lbs", [128, 8])
        lbv = lbl[:].rearrange("p l a b -> p l (a b)")
        S.op("act", I("activation", out=lbe[:], in_=lbv, func=AF.Exp), [B_const], [B_const])
        S.op("dve", I("tensor_tensor", out=lbs[:], in0=lbe[:, 0, :], in1=lbe[:, 1, :], op=ALU.add), [B_const], [B_const])
        S.op("dve", I("tensor_tensor", out=lbs[:], in0=lbs[:], in1=lbe[:, 2, :], op=ALU.add), [B_const], [B_const])
        S.op("dve", I("tensor_tensor", out=lbs[:], in0=lbs[:], in1=lbe[:, 3, :], op=ALU.add), [B_const], [B_const])
        S.op("dve", I("reciprocal", out=lbs[:], in_=lbs[:]), [B_const], [B_const])
        c1v = lbc1[:].rearrange("p l a b -> p l (a b)")
        c0v = lbc0[:].rearrange("p l a b -> p l (a b)")
        S.op("dve", I("memset", c0v[:, 0, :], 0.0), [B_const], [B_const])
        for l in range(1, DEPTH):
            S.op("dve", I("tensor_tensor", out=c1v[:, l, :], in0=lbe[:, l, :], in1=lbs[:], op=ALU.mult), [B_const], [B_const])
            S.op("dve", I("tensor_tensor", out=c0v[:, l, :], in0=c0v[:, l - 1, :], in1=c1v[:, l, :], op=ALU.add), [B_const], [B_const])
        S.op("dve", I("tensor_scalar", out=lbc1[:], in0=lbc0[:], scalar1=-0.5, scalar2=0.5, op0=ALU.mult, op1=ALU.add), [B_const], [B_const])
        S.op("dve", I("tensor_scalar", out=lbc0[:], in0=lbc0[:], scalar1=0.5, scalar2=0.5, op0=ALU.mult, op1=ALU.add), [B_const], [B_const])

        B_wcast = [Buf(f"wcast{l}") for l in range(depth)]
        sl_wc = [slot(f"wc{i}") for i in range(4)]
        _wc_i = [0]

        def wcast(l, dst, src):
            sl = sl_wc[_wc_i[0] % 4]
            _wc_i[0] += 1
            S.op("pool", I("dma_start", out=dst, in_=src), reads=[], writes=[B_wcast[l]], slot=sl)

        import os
        for l in range(depth if not os.environ.get("KSKIPCAST") else 0):
            def v4(t, j0, j1):
                return t[l, j0:j1].rearrange("a p k n -> (a p) (k n)")

            def v3(t):
                return t[l].rearrange("p k n -> p (k n)")

            for j in range(0, NF1, 4):
                wcast(l, v4(w1fm_b, j, j + 4), v4(w1fm_d, j, j + 4))
            wcast(l, v3(w1tm_b), v3(w1tm_d))
            for j in range(0, NF2, 6):
                wcast(l, v4(w2fm_b, j, j + 6), v4(w2fm_d, j, j + 6))
            wcast(l, v3(w2tm_b), v3(w2tm_d))
            wcast(l, wbr_b[l].rearrange("a p b k n -> (a p) (b k n)"), wbr_d[l].rearrange("a p b k n -> (a p) (b k n)"))
            wcast(l, v4(wo_b, 0, 8), v4(wo_d, 0, 8))

        NWB = 4
        wbuf = [sb(f"wbuf{i}", [128, 8, 128], BF16) for i in range(NWB)]
        B_wbuf = [Buf(f"wbuf{i}") for i in range(NWB)]
        sl_wbuf = [slot(f"wb{i}") for i in range(NWB)]
        _wb_i = [0]
        wtm = sb("wtm", [128, 8, TM2], BF16)
        B_wtm = Buf("wtm")
        sl_wtm = slot("wtm")
        wbrc = [sb(f"wbrc{i}", [128, 3, 4, 128], BF16) for i in range(2)]
        B_wbrc = [Buf(f"wbrc{i}") for i in range(2)]
        sl_wbrc = [slot(f"wbrc{i}") for i in range(2)]
        _wbrc_i = [0]
        sl_wl = slot("wl")
        lng = sb("lng_s", [128, 512])
        lnb = sb("lnb_s", [128, 512])
        wsT = sb("wsT_s", [128, 4, 128], BF16)
        bsp = sb("bsp_s", [128, 4, 128])
        B_lw = Buf("layerw")

        xt = sb("xt", [128, 8, 512])
        B_xt = Buf("xt")
        sl_xt = slot("xt")
        hT = sb("hT", [128, 8, 512], BF16)
        B_hT = Buf("hT")
        rstd = sb("rstd", [128, 512])
        B_rstd = Buf("rstd")
        rtmp = sb("rtmp", [128, 512])
        B_rtmp = Buf("rtmp")
        epsb = sb("epsb", [128, 1])
        S.op("dve", I("memset", epsb[:], EPS), [], [B_const])
        mhalf = sb("mhalf", [128, 512])
        S.op("pool", I("memset", mhalf[:], -0.5), [], [B_const])

        zs = sb("zs", [128, 12, 512], BF16)
        B_zs = [Buf(f"zs{i}") for i in range(12)]
        ub = sb("ub", [128, 4, 512], BF16)
        B_ub = [Buf(f"ub{i}") for i in range(4)]
        yb = sb("yb", [128, 4, 512], BF16)
        B_yb = Buf("yb")
        ya = sb("ya", [128, 4, 512], BF16)
        B_ya = Buf("ya")
        yc = sb("yc", [128, 4, 512], BF16)
        B_yc = Buf("yc")
        mg = sb("mg", [128, 8, 512], BF16)
        B_mg = [Buf(f"mg{i}") for i in range(8)]
        sq = mg
        xo = sb("xo", [128, 8, 512])
        B_xo = Buf("xo")
        sl_xo = slot("xo")
        ya_scr = sb("qkscr", [128, 1024], BF16)
        B_ya_scr = Buf("qkscr")
        NTMP = 8
        tmpf = [sb(f"tmpf{i}", [128, 512]) for i in range(NTMP)]
        B_tmpf = [Buf(f"tmpf{i}") for i in range(NTMP)]
        _tf_i = [0]

        def get_tmp():
            i = _tf_i[0] % NTMP
            _tf_i[0] += 1
            return tmpf[i], B_tmpf[i]

        vn = sb("vn", [128, 512], BF16)
        B_vn = Buf("vn")
        bnst = sb("bnst", [128, 6])
        bnag = sb("bnag", [128, 2])
        B_bn = Buf("bn")
        mhalf1 = sb("mhalf1", [128, 1])
        S.op("pool", I("memset", mhalf1[:], -0.5), [], [B_const])

        NPP = 2
        pp = [ps(f"pp{i}", [128, 512]) for i in range(NPP)]
        B_pp = [Buf(f"pp{i}") for i in range(NPP)]
        _pp_i = [0]
        pp_mode = ["narrow"]

        def get_pp():
            if pp_mode[0] == "wide":
                banks = [(pp[0], B_pp[0]), (pp[1], B_pp[1]), (pbr[0], B_pbr[0]), (pbr[1], B_pbr[1]), (pbr[2], B_pbr[2]), (pmx_flat, B_pmx)]
            elif pp_mode[0] == "nopmx":
                banks = [(pp[0], B_pp[0]), (pp[1], B_pp[1]), (pbr[0], B_pbr[0]), (pbr[1], B_pbr[1]), (pbr[2], B_pbr[2])]
            else:
                banks = [(pp[0], B_pp[0]), (pp[1], B_pp[1])]
            i = _pp_i[0] % len(banks)
            _pp_i[0] += 1
            return banks[i]

        pst = ps("pst", [128, 512])
        B_pst = Buf("pst")
        pmx = ps("pmx", [128, 4, 128])
        B_pmx = Buf("pmx")
        pbr = [ps(f"pbr{i}", [128, 512]) for i in range(3)]
        B_pbr = [Buf(f"pbr{i}") for i in range(3)]

        class _Flat:
            def __getitem__(self, idx):
                return pmx[:].rearrange("p a b -> p (a b)")[idx]
        pmx_flat = _Flat()

        DRAM_x = [Buf(f"dram_x{m}") for m in range(n_mt)]
        qr = sb("qr", [128, 4, 512], BF16)
        B_qr = [Buf(f"qr{i}") for i in range(4)]
        kz = sb("kz", [128, 2, 2, 768], BF16)
        B_kr = Buf("kr")
        S.op("pool", I("memset", kz[:], 0.0), [], [B_kr])
        vaug = sb("vaug", [128, 6, 2, 192], BF16)
        B_vaug = Buf("vaug")
        S.op("dve", I("memset", vaug[:, :, :, 64:128], 1.0), [], [B_vaug])
        pt = sb("pt", [128, 3, 2, 256], BF16)
        B_pt = Buf("pt")
        cs = sb("cs", [128, 2, 640])
        B_cs = Buf("cs")
        sl_cs = slot("cs")
        xh = sb("xh", [128, 8, 128])
        B_xh = Buf("xh")
        sl_xh = slot("xh")
        hTh = sb("hTh", [128, 8, 128], BF16)
        B_hTh = Buf("hTh")
        sqh = sb("sqh", [128, 8, 128], BF16)
        B_sqh = Buf("sqh")
        mone = sb("mone", [128, 256])
        S.op("pool", I("memset", mone[:], -1.0), [], [B_const])
        dtmp = sb("dtmp", [128, 256])
        B_dtmp = Buf("dtmp")
        xo_flat = xo[:].rearrange("p c t -> p (c t)")
        ccg = xo_flat[:, 0:2 * CCW].rearrange("p (r n) -> p r n", r=2)
        ccs = xo_flat[:, 2048:2048 + CCW]
        ccp = xo_flat[:, 3072:3072 + CCW]
        B_ccs = B_ccg = B_ccp = B_xo
        sl_cc = slot("cc")
        DRAM_cc = Buf("dram_cc")
        ccsem = sem("ccsem")
        cmask = sb("cmask", [128, 512])
        hmask2 = sb("hmask2", [128, 2, 128], BF16)
        dma("sp", cmask[:], c_cmask_d.ap(), [], [B_const], sl_c)
        dma("pool", hmask2[:], c_hmask2_d.ap(), [], [B_const], sl_c)
        qraw = sb("qraw", [128, 4, 512])
        B_qraw = Buf("qraw")
        qtT = sb("qtT", [128, 4, 512], BF16)
        B_qtT = Buf("qtT")
        ktT = sb("ktT", [128, 4, 512], BF16)
        B_ktT = Buf("ktT")
        ktA = sb("ktA", [128, 4, 128], BF16)
        ktB = sb("ktB", [128, 4, 128], BF16)
        B_kt = Buf("kt")
        S.op("pool", I("memset", ktA[:], 0.0), [], [B_kt])
        S.op("pool", I("memset", ktB[:], 0.0), [], [B_kt])
        vtok = sb("vtok", [128, 4, 4, 128], BF16)
        B_vtok = [Buf(f"vtok{i}") for i in range(4)]
        sc = sb("sc", [128, 4, 3, 8])
        B_sc = Buf("sc")
        sc8 = sb("sc8", [128, 8])
        B_sc8 = Buf("sc8")
        St = sb("St", [128, 4, 128])
        B_S = Buf("S")
        Sp = sb("Sp", [128, 4, 128], BF16)
        B_Sp = Buf("Sp")
        ATs = sb("ATs", [128, 4, 128], BF16)
        B_ATs = Buf("ATs")
        o1s = sb("o1s", [128, 4, 128])
        B_o1s = Buf("o1s")
        sl_o1 = slot("o1")
        DRAM_o1 = [Buf(f"dram_o1_{i}") for i in range(cfg.NB)]
        ptr = ps("ptr", [128, 4, 128], BF16)
        B_ptr = Buf("ptr")

        def hgrn_gates(l, dirn, srcw, base_a):
            for h in range(4):
                p, Bp = proj_fm(l, srcw, base_a + h)
                tA, BA = get_tmp()
                tK, BK = get_tmp()
                tB, BB = get_tmp()
                tD, BD = get_tmp()
                tE, BE = get_tmp()
                v = lambda t: t[:].rearrange("p (c t) -> p c t", t=64)
                S.op("act", I("activation", out=tA[:], in_=p[:], func=AF.Tanh, scale=0.5), [Bp], [BA])
                S.op("dve", I("tensor_scalar", out=tA[:], in0=tA[:], scalar1=lbc1[:, l, dirn, h:h + 1], scalar2=lbc0[:, l, dirn, h:h + 1], op0=ALU.mult, op1=ALU.add),
                     [BA, B_const], [BA])
                S.op("pool", I("tensor_scalar", out=tK[:], in0=tA[:], scalar1=-1.0, scalar2=1.0, op0=ALU.mult, op1=ALU.add), [BA], [BK])
                S.op("act", I("activation", out=tA[:], in_=tA[:], func=AF.Ln), [BA, BK], [BA])
                S.op("dve", I("tensor_tensor_scan", out=tB[:], data0=cmask[:], data1=tA[:], initial=0.0, op0=ALU.mult, op1=ALU.add), [BA, B_const], [BB])
                if dirn == 0:
                    S.op("dve", I("tensor_tensor", out=v(tD), in0=v(tB), in1=v(tB)[:, :, 31:32].to_broadcast([128, 8, 64]), op=ALU.subtract), [BB], [BD])
                    S.op("act", I("activation", out=sc[:, h, 0, :], in_=v(tB)[:, :, 31], func=AF.Exp), [BB], [B_sc])
                    S.op("act", I("activation", out=sc[:, h, 1, :], in_=v(tB)[:, :, 63], func=AF.Exp), [BB], [B_sc])
                    S.op("act", I("activation", out=sc[:, h, 2, :], in_=v(tD)[:, :, 63], func=AF.Exp), [BD], [B_sc])
                else:
                    S.op("dve", I("tensor_tensor", out=tA[:], in0=tB[:], in1=tA[:], op=ALU.subtract), [BB, BA], [BA])
                    S.op("dve", I("tensor_tensor", out=v(tD), in0=v(tA)[:, :, 32:33].to_broadcast([128, 8, 64]), in1=v(tA), op=ALU.subtract), [BA], [BD])
                    S.op("dve", I("tensor_tensor", out=sc8[:], in0=v(tB)[:, :, 63], in1=v(tA)[:, :, 32], op=ALU.subtract), [BB, BA], [B_sc8])
                    S.op("act", I("activation", out=sc[:, h, 0, :], in_=sc8[:], func=AF.Exp), [B_sc8], [B_sc])
                    S.op("act", I("activation", out=sc[:, h, 1, :], in_=v(tB)[:, :, 63], func=AF.Exp), [BB], [B_sc])
                    S.op("act", I("activation", out=sc[:, h, 2, :], in_=v(tD)[:, :, 0], func=AF.Exp), [BD], [B_sc])
                S.op("act", I("activation", out=tE[:], in_=tD[:], func=AF.Exp), [BD], [BE])
                S.op("dve", I("tensor_tensor", out=qtT[:, h, :], in0=qraw[:, h, :], in1=tE[:], op=ALU.mult), [B_qraw, BE], [B_qtT])
                S.op("act", I("activation", out=tB[:], in_=tD[:], func=AF.Exp, scale=-1.0), [BD, B_sc, B_sc8], [BB])
                S.op("pool", I("tensor_tensor", out=ktT[:, h, :], in0=tK[:], in1=tB[:], op=ALU.mult), [BK, BB], [B_ktT])

        def hgrn_q_and_v(l, srcw, base_q):
            for h in range(4):
                p, Bp = proj_fm(l, srcw, base_q + h)
                S.op("act", I("activation", out=qraw[:, h, :], in_=p[:], func=AF.Copy), [Bp], [B_qraw])

        def hgrn_vtok(bi):
            tsl = slice(bi * 128, (bi + 1) * 128)
            p, Bp = get_pp()
            for k in range(8):
                S.op("pe", I("matmul", p[:], lhsT=hT[:, k, tsl], rhs=wtm[:, k, 0:512], start=(k == 0), stop=(k == 7)), [B_hT, B_wtm], [Bp])
            S.op("act", I("activation", out=vtok[:, bi, :, :], in_=p[:].rearrange("p (h d) -> p h d", h=4), func=AF.Copy), [Bp], [B_vtok[bi]])

        def hgrn_block(l, dirn, bi, m):
            tsl = slice(bi * 128, (bi + 1) * 128)
            psc, B_psc = pbr[0][:].rearrange("p (h t) -> p h t", h=4), B_pbr[0]
            po, B_po = pbr[1][:].rearrange("p (h t) -> p h t", h=4), B_pbr[1]
            pob, B_pob = pbr[2][:].rearrange("p (h t) -> p h t", h=4), B_pbr[2]
            pS, B_pS = pmx[:], B_pmx
            for h in range(4):
                S.op("pe", I("transpose", out=ptr[:, h, :], in_=ktT[:, h, tsl], identity=ident_b[:]), [B_ktT, B_const], [B_ptr])
            S.op("dve", I("tensor_copy", out=ktA[0:64, :, :], in_=ptr[0:64, :, :]), [B_ptr], [B_kt])
            S.op("act", I("activation", out=ktB[64:128, :, :], in_=ptr[64:128, :, :], func=AF.Copy), [B_ptr], [B_kt])
            for h in range(4):
                S.op("pe", I("matmul", psc[:, h, :], lhsT=ktT[:, h, tsl], rhs=qtT[:, h, tsl], start=True, stop=True), [B_ktT, B_qtT], [B_psc])
            S.op("dve", I("tensor_tensor", out=ATs[:], in0=psc, in1=hmask2[:, dirn:dirn + 1, :].to_broadcast([128, 4, 128]), op=ALU.mult), [B_psc, B_const], [B_ATs])
            for h in range(4):
                S.op("pe", I("matmul", po[:, h, :], lhsT=vtok[:, bi, h, :], rhs=ATs[:, h, :], start=True, stop=True), [B_vtok[bi], B_ATs], [B_po])
            chunks = [(0, ktA), (1, ktB)] if dirn == 0 else [(1, ktB), (0, ktA)]
            for ci, (c, kt) in enumerate(chunks):
                gc = bi * 2 + c
                csl = slice(bi * 128 + c * 64, bi * 128 + (c + 1) * 64)
                for h in range(4):
                    S.op("dve", I("tensor_scalar", out=Sp[:, h, :], in0=St[:, h, :], scalar1=sc[:, h, 0, gc:gc + 1], scalar2=None, op0=ALU.mult), [B_S, B_sc], [B_Sp])
                for h in range(4):
                    S.op("pe", I("matmul", pob[:, h, c * 64:(c + 1) * 64], lhsT=Sp[:, h, :], rhs=qtT[:, h, csl], start=True, stop=True), [B_Sp, B_qtT], [B_pob])
                for h in range(4):
                    S.op("pe", I("matmul", pS[:, h, :], lhsT=kt[:, h, :], rhs=vtok[:, bi, h, :], start=True, stop=True), [B_kt, B_vtok[bi]], [B_pS])
                for h in range(4):
                    S.op("dve", I("tensor_scalar", out=St[:, h, :], in0=St[:, h, :], scalar1=sc[:, h, 1, gc:gc + 1], scalar2=None, op0=ALU.mult), [B_S, B_sc], [B_S])
                    S.op("dve", I("scalar_tensor_tensor", out=St[:, h, :], in0=pS[:, h, :], scalar=sc[:, h, 2, gc:gc + 1], in1=St[:, h, :], op0=ALU.mult, op1=ALU.add),
                         [B_pS, B_sc, B_S], [B_S])
            gblk = m * 4 + bi
            if dirn == 0:
                S.op("act", I("activation", out=o1s[:], in_=po, func=AF.Copy), [B_po], [B_o1s])
                S.op("dve", I("tensor_tensor", out=o1s[:], in0=o1s[:], in1=pob, op=ALU.add), [B_o1s, B_pob], [B_o1s])
                dma("sp", o1_d[:, gblk], o1s[:], [B_o1s], [DRAM_o1[gblk]], sl_o1)
            else:
                dma("sp", o1s[:], o1_d[:, gblk], [DRAM_o1[gblk]], [B_o1s], sl_o1)
                osum, Bos = get_tmp()
                rt, Brt = get_tmp()
                osv = osum[:].rearrange("p (h t) -> p h t", h=4)
                S.op("dve", I("tensor_tensor", out=osv, in0=po, in1=o1s[:], op=ALU.add), [B_po, B_o1s], [Bos])
                S.op("dve", I("tensor_tensor", out=osv, in0=osv, in1=pob, op=ALU.add), [Bos, B_pob], [Bos])
                S.op("act", I("activation", out=ya_scr[:, 0:512], in_=osum[:], func=AF.Square), [Bos], [B_ya_scr])
                S.op("pe", I("matmul", pst[:], lhsT=ones_b[:], rhs=ya_scr[:, 0:512], start=True, stop=True), [B_ya_scr, B_const], [B_pst])
                S.op("act", I("activation", out=rt[:], in_=pst[:], func=AF.Ln, scale=1.0 / 128, bias=epsb[:]), [B_pst, B_const], [Brt])
                S.op("act", I("activation", out=rt[:], in_=rt[:], func=AF.Exp, scale=-0.5), [Brt], [Brt])
                S.op("dve", I("tensor_tensor", out=osum[:], in0=osum[:], in1=rt[:], op=ALU.mult), [Bos, Brt], [Bos])
                for h in range(4):
                    S.op("dve", I("scalar_tensor_tensor", out=ya[:, h, tsl], in0=osv[:, h, :], scalar=hgain[:, l, h:h + 1], in1=zs[:, h, tsl], op0=ALU.mult, op1=ALU.mult),
                         [Bos, B_const, B_zs[h]], [B_ya])

        def sweep1_mt(l, m):
            norm_mt(l, m)
            pp_mode[0] = "wide"
            hgrn_q_and_v(l, w1fm_b, 4)
            hgrn_gates(l, 0, w1fm_b, 0)
            for bi in range(4):
                hgrn_vtok(bi)
                hgrn_block(l, 0, bi, m)
        sl_out = slot("out")

        def xsrc(l):
            return xT_d if l == 0 else out_d

        xview = lambda t, m: t.ap().rearrange("(c p) t -> p c t", p=128)[:, :, m * 512:(m + 1) * 512]

        def load_w_chunk(l, src_b, j):
            i = _wb_i[0] % NWB
            _wb_i[0] += 1
            dma("sp", wbuf[i][:], src_b[l, j], [B_wcast[l]], [B_wbuf[i]], sl_wbuf[i])
            return wbuf[i], B_wbuf[i]

        def tanh_gate(dst, src, scale):
            return I("activation", out=dst, in_=src, func=AF.Tanh, scale=scale)

        def norm_mt(l, m):
            dma("sp", xt[:], xview(xsrc(l), m), [DRAM_x[m]], [B_xt], sl_xt)
            for c in range(8):
                S.op("act", I("activation", out=sq[:, c, :], in_=xt[:, c, :], func=AF.Square), [B_xt], [B_mg[c]])
            for c in range(8):
                S.op("pe", I("matmul", pst[:], lhsT=ones_b[:], rhs=sq[:, c, :], start=(c == 0), stop=(c == 7)),
                     [B_mg[c], B_const], [B_pst])
            S.op("act", I("activation", out=rtmp[:], in_=pst[:], func=AF.Ln, scale=1.0 / D, bias=epsb[:]), [B_pst, B_const], [B_rtmp])
            S.op("act", I("activation", out=rstd[:], in_=rtmp[:], func=AF.Exp, scale=-0.5), [B_rtmp], [B_rstd])
            for c in range(8):
                S.op("dve", I("scalar_tensor_tensor", out=hT[:, c, :], in0=xt[:, c, :], scalar=ngain[:, l, c:c + 1],
                                                                   in1=rstd[:], op0=ALU.mult, op1=ALU.mult),
                     [B_xt, B_rstd, B_const], [B_hT])

        def proj_fm(l, src_b, j):
            w, Bw = load_w_chunk(l, src_b, j)
            p, Bp = get_pp()
            for k in range(8):
                S.op("pe", I("matmul", p[:], lhsT=w[:, k, :], rhs=hT[:, k, :], start=(k == 0), stop=(k == 7)),
                     [Bw, B_hT], [Bp])
            return p, Bp

        def layer_weights(l):
            dma("sp", wtm[:], w2tm_b[l], [B_wcast[l]], [B_wtm], sl_wtm)
            dma("sp", lng[:], lng_d[:, l, :], [], [B_lw], sl_wl)
            dma("sp", lnb[:], lnb_d[:, l, :], [], [B_lw], sl_wl)
            dma("sp", bsp[:], bsp_d[:, l], [], [B_lw], sl_wl)
            dma("pool", wsT[:], wsT_d[:, l], [], [B_lw], sl_wl)

        GELU_C = 0.7978845608028654

        def gelu_from_psum(p, Bp, dst, Bdst, eng2="pool"):
            t1, B1 = get_tmp()
            t2, B2 = get_tmp()
            S.op("act", I("activation", out=t1[:], in_=p[:], func=AF.Square), [Bp], [B1])
            S.op("dve", I("tensor_scalar", out=t1[:], in0=t1[:], scalar1=0.044715, scalar2=1.0, op0=ALU.mult, op1=ALU.add), [B1], [B1])
            S.op("dve", I("tensor_tensor", out=t2[:], in0=t1[:], in1=p[:], op=ALU.mult), [B1, Bp], [B2])
            S.op("act", I("activation", out=t1[:], in_=t2[:], func=AF.Tanh, scale=GELU_C), [B2], [B1])
            S.op("dve", I("scalar_tensor_tensor", out=dst, in0=t1[:], scalar=1.0, in1=p[:], op0=ALU.add, op1=ALU.mult), [B1, Bp], [Bdst])

        def silu_from_psum(p, Bp, dst, Bdst):
            t1, B1 = get_tmp()
            S.op("act", I("activation", out=t1[:], in_=p[:], func=AF.Tanh, scale=0.5), [Bp], [B1])
            S.op("dve", I("scalar_tensor_tensor", out=dst, in0=t1[:], scalar=1.0, in1=p[:], op0=ALU.add, op1=ALU.mult), [B1, Bp], [Bdst])

        def sweep2_mt(l, m):
            norm_mt(l, m)
            base = {}
            cnt = 0
            for kind in FM2:
                base.setdefault(kind, cnt)
                cnt += 1
            if cfg.do_b:
                pp_mode[0] = "wide"
                for j in range(4):
                    p, Bp = proj_fm(l, w2fm_b, base["uB"] + j)
                    gelu_from_psum(p, Bp, ub[:, j, :], B_ub[j])
                for j in range(4):
                    p, Bp = proj_fm(l, w2fm_b, base["zB"] + j)
                    silu_from_psum(p, Bp, zs[:, 4 + j, :], B_zs[4 + j])
                pp_mode[0] = "nopmx"
                for blk in range(4):
                    tsl = slice(blk * 128, (blk + 1) * 128)
                    p, Bp = get_pp()
                    for k in range(8):
                        S.op("pe", I("matmul", p[:], lhsT=hT[:, k, tsl], rhs=wtm[:, k, 512:1024],
                                                                          start=(k == 0), stop=(k == 7)), [B_hT, B_wtm], [Bp])
                    vt, B_vt = get_tmp()
                    vt2, B_vt2 = get_tmp()
                    gelu_from_psum(p, Bp, vt[:], B_vt)
                    S.op("dve", I("bn_stats", out=bnst[:], in_=vt[:]), [B_vt], [B_bn])
                    S.op("dve", I("bn_aggr", out=bnag[:], in_=bnst[:]), [B_bn], [B_bn])
                    S.op("dve", I("tensor_scalar", out=bnag[:, 1:2], in0=bnag[:, 1:2], scalar1=0.25, scalar2=EPS, op0=ALU.mult, op1=ALU.add),
                         [B_bn], [B_bn])
                    S.op("pool", I("tensor_tensor", out=bnag[:, 1:2], in0=bnag[:, 1:2], in1=mhalf1[:], op=ALU.pow), [B_bn, B_const], [B_bn])
                    S.op("dve", I("tensor_scalar", out=bnag[:, 1:2], in0=bnag[:, 1:2], scalar1=0.5, scalar2=None, op0=ALU.mult), [B_bn], [B_bn])
                    S.op("dve", I("tensor_scalar", out=vt2[:], in0=vt[:], scalar1=bnag[:, 0:1], scalar2=bnag[:, 1:2],
                                                          op0=ALU.subtract, op1=ALU.mult), [B_vt, B_bn], [B_vt2])
                    S.op("dve", I("tensor_tensor", out=vt2[:], in0=vt2[:], in1=lng[:], op=ALU.mult), [B_vt2, B_lw], [B_vt2])
                    S.op("dve", I("tensor_tensor", out=vn[:], in0=vt2[:], in1=lnb[:], op=ALU.add), [B_vt2, B_lw], [B_vn])
                    for g in range(4):
                        S.op("pe", I("matmul", pmx[:, g, :], lhsT=vn[:, g * 128:(g + 1) * 128], rhs=wsT[:, g, :], start=True, stop=True),
                             [B_vn, B_lw], [B_pmx])
                    t1, B1 = get_tmp()
                    t1v = t1[:].rearrange("p (g t) -> p g t", g=4)
                    S.op("dve", I("tensor_tensor", out=t1v, in0=pmx[:], in1=bsp[:], op=ALU.add), [B_pmx, B_lw], [B1])
                    S.op("pool", I("tensor_tensor", out=t1v, in0=t1v, in1=ub[:, :, tsl], op=ALU.mult), [B1] + B_ub, [B1])
                    S.op("dve", I("tensor_tensor", out=yb[:, :, tsl], in0=t1v, in1=zs[:, 4:8, tsl], op=ALU.mult),
                         [B1] + B_zs[4:8], [B_yb])
            if cfg.do_c:
                pp_mode[0] = "wide"
                top = (m == n_mt - 1)
                has_lo = (m > 0)
                t0 = m * 512 - 128
                if has_lo:
                    dma("sp", cs[:, 0, :], cosT_d[:, t0:t0 + 640], [], [B_cs], sl_cs)
                    dma("sp", cs[:, 1, :], sinT_d[:, t0:t0 + 640], [], [B_cs], sl_cs)
                    dma("sp", xh[:], xsrc(l).ap().rearrange("(c p) t -> p c t", p=128)[:, :, t0:t0 + 128], [DRAM_x[m - 1]], [B_xh], sl_xh)
                    for c in range(8):
                        S.op("act", I("activation", out=sqh[:, c, :], in_=xh[:, c, :], func=AF.Square), [B_xh], [B_sqh])
                    for c in range(8):
                        S.op("pe", I("matmul", pst[:, 0:128], lhsT=ones_b[:], rhs=sqh[:, c, :], start=(c == 0), stop=(c == 7)),
                             [B_sqh, B_const], [B_pst])
                    S.op("act", I("activation", out=rtmp[:, 0:128], in_=pst[:, 0:128], func=AF.Ln, scale=1.0 / D, bias=epsb[:]), [B_pst, B_const], [B_rtmp])
                    S.op("act", I("activation", out=rtmp[:, 128:256], in_=rtmp[:, 0:128], func=AF.Exp, scale=-0.5), [B_rtmp], [B_rtmp])
                    for c in range(8):
                        S.op("dve", I("scalar_tensor_tensor", out=hTh[:, c, :], in0=xh[:, c, :], scalar=ngain[:, l, c:c + 1],
                                                                           in1=rtmp[:, 128:256], op0=ALU.mult, op1=ALU.mult),
                             [B_xh, B_rtmp, B_const], [B_hTh])
                else:
                    dma("sp", cs[:, 0, 128:640], cosT_d[:, 0:512], [], [B_cs], sl_cs)
                    dma("sp", cs[:, 1, 128:640], sinT_d[:, 0:512], [], [B_cs], sl_cs)
                if not top:
                    S.op("pool", I("tensor_copy", out=kz[:, :, :, 640:768], in_=kz[:, :, :, 128:256]), [B_kr], [B_kr])
                    S.op("pool", I("tensor_copy", out=vaug[:, 5, :, :], in_=vaug[:, 1, :, :]), [B_vaug], [B_vaug])

                def qk_post(p, Bp, n, which, dst, Bdst, csl):
                    t1, B1 = get_tmp()
                    t2, B2 = get_tmp()
                    t3, B3 = get_tmp()
                    sqb, Bsqb = ya_scr, B_ya_scr
                    S.op("act", I("activation", out=sqb[:, 0:n], in_=p[:, 0:n], func=AF.Square), [Bp], [Bsqb])
                    S.op("dve", I("tensor_scalar", out=sqb[:, 512:512 + n], in0=p[:, 0:n], scalar1=qkg[:, l, which:which + 1], scalar2=None, op0=ALU.mult),
                         [Bp, B_const], [Bsqb])
                    S.op("pe", I("matmul", pst[:, 0:n], lhsT=bd_b[:], rhs=sqb[:, 0:n], start=True, stop=True), [Bsqb, B_const], [B_pst])
                    pr, Bpr = get_pp()
                    S.op("pe", I("matmul", pr[:, 0:n], lhsT=rot_b[:], rhs=sqb[:, 512:512 + n], start=True, stop=True), [Bsqb, B_const], [Bpr])
                    S.op("act", I("activation", out=t1[:, 0:n], in_=pst[:, 0:n], func=AF.Ln, scale=1.0 / 64, bias=epsb[:]), [B_pst, B_const], [B1])
                    S.op("act", I("activation", out=t1[:, 0:n], in_=t1[:, 0:n], func=AF.Exp, scale=-0.5), [B1], [B1])
                    S.op("pool", I("tensor_tensor", out=t2[:, 0:n], in0=sqb[:, 512:512 + n], in1=cs[:, 0, csl], op=ALU.mult), [Bsqb, B_cs], [B2])
                    S.op("dve", I("tensor_tensor", out=t3[:, 0:n], in0=pr[:, 0:n], in1=cs[:, 1, csl], op=ALU.mult), [Bpr, B_cs], [B3])
                    S.op("dve", I("tensor_tensor", out=t2[:, 0:n], in0=t2[:, 0:n], in1=t3[:, 0:n], op=ALU.add), [B2, B3], [B2])
                    if which == 0:
                        S.op("dve", I("tensor_tensor", out=dst, in0=t2[:, 0:n], in1=t1[:, 0:n], op=ALU.mult), [B2, B1], [Bdst])
                    else:
                        jh, c0 = dst
                        S.op("dve", I("tensor_tensor", out=kz[0:64, 0, jh, c0:c0 + n], in0=t2[0:64, 0:n], in1=t1[0:64, 0:n], op=ALU.mult), [B2, B1], [Bdst])
                        S.op("dve", I("tensor_tensor", out=kz[64:128, 1, jh, c0:c0 + n], in0=t2[64:128, 0:n], in1=t1[64:128, 0:n], op=ALU.mult), [B2, B1], [Bdst])

                for j in range(4):
                    p, Bp = proj_fm(l, w2fm_b, base["qC"] + j)
                    qk_post(p, Bp, 512, 0, qr[:, j, :], B_qr[j], slice(128, 640))
                for j in range(2):
                    w, Bw = load_w_chunk(l, w2fm_b, base["kC"] + j)
                    p, Bp = get_pp()
                    for k in range(8):
                        S.op("pe", I("matmul", p[:], lhsT=w[:, k, :], rhs=hT[:, k, :], start=(k == 0), stop=(k == 7)), [Bw, B_hT], [Bp])
                    qk_post(p, Bp, 512, 1, (j, 128), B_kr, slice(128, 640))
                    if has_lo:
                        p, Bp = get_pp()
                        for k in range(8):
                            S.op("pe", I("matmul", p[:, 0:128], lhsT=w[:, k, :], rhs=hTh[:, k, :], start=(k == 0), stop=(k == 7)), [Bw, B_hTh], [Bp])
                        qk_post(p, Bp, 128, 1, (j, 0), B_kr, slice(0, 128))
                for j in range(4):
                    p, Bp = proj_fm(l, w2fm_b, base["zC"] + j)
                    silu_from_psum(p, Bp, zs[:, 8 + j, :], B_zs[8 + j])
                for sl_i in ([0] if has_lo else []) + [1, 2, 3, 4]:
                    p, Bp = get_pp()
                    for k in range(8):
                        if sl_i == 0:
                            S.op("pe", I("matmul", p[:, 0:128], lhsT=hTh[:, k, :], rhs=wtm[:, k, 1024:1152], start=(k == 0), stop=(k == 7)),
                                 [B_hTh, B_wtm], [Bp])
                        else:
                            tsl = slice((sl_i - 1) * 128, sl_i * 128)
                            S.op("pe", I("matmul", p[:, 0:128], lhsT=hT[:, k, tsl], rhs=wtm[:, k, 1024:1152], start=(k == 0), stop=(k == 7)),
                                 [B_hT, B_wtm], [Bp])
                    pv = p[:, 0:128].rearrange("p (h d) -> p h d", h=2)
                    S.op("act", I("activation", out=vaug[:, sl_i, :, 0:64], in_=pv, func=AF.Copy), [Bp], [B_vaug])
                    S.op("dve", I("tensor_copy", out=vaug[:, sl_i, :, 128:192], in_=pv), [Bp], [B_vaug])
                kc_stage = int(os.environ.get("KC_STAGE", "9"))
                if top and kc_stage >= 2:
                    S.op("dve", I("tensor_copy", out=ccs[0:64, 512:768].rearrange("p (h t) -> p h t", h=2), in_=kz[0:64, 0, :, 512:640]), [B_kr], [B_ccs])
                    S.op("dve", I("tensor_copy", out=ccs[64:128, 512:768].rearrange("p (h t) -> p h t", h=2), in_=kz[64:128, 1, :, 512:640]), [B_kr], [B_ccs])
                    S.op("dve", I("tensor_copy", out=ccs[:, 768:896].rearrange("p (h d) -> p h d", h=2), in_=vaug[:, 4, :, 0:64]), [B_vaug], [B_ccs])
                    if not cfg.do_a:
                        S.op("dve", I("memset", ccs[:, 0:512], 0.0), [], [B_ccs])
                    else:
                        S.op("dve", I("tensor_copy", out=ccs[:, 0:512], in_=St[:].rearrange("p h v -> p (h v)")), [B_S], [B_ccs])
                    dma("pool", cc_in[l].ap(), ccs, [B_ccs], [DRAM_cc], sl_cc)
                    o = S.op("pool", I("collective_compute", "AllGather", ALU.bypass, replica_groups=[[0, 1], [2, 3], [4, 5], [6, 7]],
                                                                  ins=[cc_in[l].ap().opt()], outs=[cc_out[l].ap().opt()]), [DRAM_cc], [DRAM_cc])
                    o.signal = True
                    dma("pool", ccg, cc_out[l].ap().rearrange("(r p) n -> p r n", p=128), [DRAM_cc], [B_ccg], sl_cc)
                    S.op("dve", I("tensor_scalar", out=ccp, in0=ccg[:, 0, :], scalar1=selt[:, 0:1], scalar2=None, op0=ALU.mult), [B_ccg, B_const], [B_ccp])
                    S.op("dve", I("scalar_tensor_tensor", out=ccp, in0=ccg[:, 1, :], scalar=selt[:, 1:2], in1=ccp, op0=ALU.mult, op1=ALU.add),
                         [B_ccg, B_const, B_ccp], [B_ccp])
                    S.op("dve", I("tensor_copy", out=kz[0:64, 0, :, 640:768], in_=ccp[0:64, 512:768].rearrange("p (h t) -> p h t", h=2)), [B_ccp], [B_kr])
                    S.op("dve", I("tensor_copy", out=kz[64:128, 1, :, 640:768], in_=ccp[64:128, 512:768].rearrange("p (h t) -> p h t", h=2)), [B_ccp], [B_kr])
                    S.op("dve", I("tensor_copy", out=vaug[:, 5, :, 0:64], in_=ccp[:, 768:896].rearrange("p (h d) -> p h d", h=2)), [B_ccp], [B_vaug])
                    S.op("dve", I("tensor_copy", out=vaug[:, 5, :, 128:192], in_=ccp[:, 768:896].rearrange("p (h d) -> p h d", h=2)), [B_ccp], [B_vaug])
                    if cfg.do_a:
                        S.op("dve", I("tensor_copy", out=St[:].rearrange("p h v -> p (h v)"), in_=ccp[:, 0:512]), [B_ccp], [B_S])
                for sb_i in ((4, 3, 2, 1) if kc_stage >= 3 else ()):
                    jglob = m * 4 + sb_i - 1
                    tsl = slice((sb_i - 1) * 128, sb_i * 128)
                    kbs = []
                    if jglob > 0:
                        kbs.append((sb_i - 1, 0))
                    kbs.append((sb_i, None))
                    kbs.append((sb_i + 1, 2 if (top and sb_i == 4) else 1))
                    for h in range(2):
                        pssv = [pbr[i][:].rearrange("p (e n) -> p e n", e=2) for i in range(3)]
                        for ki, (slk, mk) in enumerate(kbs):
                            for e_ in range(2):
                                rows = slice(e_ * 64, (e_ + 1) * 64)
                                S.op("pe", I("matmul",
                                    pssv[ki][:, e_, :].rearrange("p (c t) -> p c t", c=2), lhsT=kz[:, e_, h, slk * 128:(slk + 1) * 128],
                                    rhs=qr[:, 2 * h:2 * h + 2, tsl], start=True, stop=True),
                                    [B_kr] + B_qr, [B_pbr[ki]])
                            S.op("act", I("activation", out=pt[:, ki, :, :], in_=pssv[ki], func=AF.Exp, scale=0.125), [B_pbr[ki]], [B_pt])
                            if mk is not None and kc_stage >= 4:
                                ptv = pt[:, ki, :, :].rearrange("p e (c t) -> p (e c) t", c=2)
                                S.op("pool", I("tensor_tensor", out=ptv, in0=ptv, in1=amask[:, mk:mk + 1, :].to_broadcast([128, 4, 128]), op=ALU.mult),
                                     [B_pt, B_const], [B_pt])
                        pso = pmx[:].rearrange("p a b -> p (a b)").rearrange("p (e n) -> p e n", e=2)
                        for e_ in (range(2) if kc_stage >= 5 else ()):
                            for ki, (slk, mk) in enumerate(kbs):
                                S.op("pe", I("matmul", pso[:, e_, :], lhsT=vaug[:, slk, h, e_ * 64:e_ * 64 + 128], rhs=pt[:, ki, e_, :],
                                                                                       start=(ki == 0), stop=(ki == len(kbs) - 1)), [B_vaug, B_pt], [B_pmx])
                        for e_ in (range(2) if kc_stage >= 6 else ()):
                            nr = slice(0, 64) if e_ == 0 else slice(64, 128)
                            dr = slice(64, 128) if e_ == 0 else slice(0, 64)
                            for c in range(2):
                                head = 2 * (2 * h + c) + e_
                                S.op("dve", I("tensor_scalar",
                                    out=dtmp[nr, c * 128:(c + 1) * 128], in0=pso[dr, e_, c * 128:(c + 1) * 128], scalar1=esink[dr, l, head:head + 1], scalar2=None, op0=ALU.add),
                                    [B_pmx, B_const], [B_dtmp])
                            S.op("act", I("activation", out=dtmp[nr, :], in_=dtmp[nr, :], func=AF.Ln), [B_dtmp], [B_dtmp])
                            S.op("act", I("activation", out=dtmp[nr, :], in_=dtmp[nr, :], func=AF.Exp, scale=-1.0), [B_dtmp], [B_dtmp])
                            S.op("dve", I("tensor_tensor", out=dtmp[nr, :], in0=pso[nr, e_, :], in1=dtmp[nr, :], op=ALU.mult), [B_pmx, B_dtmp], [B_dtmp])
                            S.op("pool", I("tensor_tensor", out=yc[nr, 2 * h:2 * h + 2, tsl], in0=dtmp[nr, :].rearrange("p (c t) -> p c t", c=2),
                                                                                    in1=zs[nr, 8 + 2 * h:8 + 2 * h + 2, tsl], op=ALU.mult),
                                 [B_dtmp] + B_zs[8:12], [B_yc])
            if cfg.do_a:
                pp_mode[0] = "wide"
                hgrn_q_and_v(l, w2fm_b, base["qA"])
                hgrn_gates(l, 1, w2fm_b, base["a2"])
                for j in range(4):
                    p, Bp = proj_fm(l, w2fm_b, base["zA"] + j)
                    silu_from_psum(p, Bp, zs[:, j, :], B_zs[j])
                for bi in (3, 2, 1, 0):
                    hgrn_vtok(bi)
                    hgrn_block(l, 1, bi, m)
            pp_mode[0] = "narrow"
            scal = {0: 0.25, 1: 0.125, 2: 0.25}
            for dc in range(8):
                dsl = slice(dc * 128, (dc + 1) * 128)
                acc, Bacc = get_tmp()
                first = True
                wi = _wbrc_i[0] % 2
                _wbrc_i[0] += 1
                dma("sp", wbrc[wi][:], wbr_b[l, dc], [B_wcast[l]], [B_wbrc[wi]], sl_wbrc[wi])
                for bi, (on, ysrc, By) in enumerate(((cfg.do_a, ya, B_ya), (cfg.do_b, yb, B_yb), (cfg.do_c, yc, B_yc))):
                    if not on:
                        continue
                    for k in range(4):
                        S.op("pe", I("matmul", pbr[bi][:], lhsT=wbrc[wi][:, bi, k, :], rhs=ysrc[:, k, :],
                                                                                     start=(k == 0), stop=(k == 3)), [B_wbrc[wi], By], [B_pbr[bi]])
                    pg, Bpg = proj_fm(l, w2fm_b, base[("gA", "gB", "gC")[bi]] + dc)
                    gtile, Bg = get_tmp()
                    S.op("act", tanh_gate(gtile[:], pg[:], 0.5), [Bpg], [Bg])
                    g = gtile[:]
                    if first:
                        S.op("dve", I("scalar_tensor_tensor", out=acc[:], in0=g, scalar=1.0, in1=pbr[bi][:], op0=ALU.add, op1=ALU.mult),
                             [Bg, B_pbr[bi]], [Bacc])
                        S.op("pool", I("tensor_scalar", out=acc[:], in0=acc[:], scalar1=scal[bi], scalar2=None, op0=ALU.mult), [Bacc], [Bacc])
                        first = False
                    else:
                        t2, B2 = get_tmp()
                        S.op("dve", I("scalar_tensor_tensor", out=t2[:], in0=g, scalar=1.0, in1=pbr[bi][:], op0=ALU.add, op1=ALU.mult),
                             [Bg, B_pbr[bi]], [B2])
                        S.op("dve", I("scalar_tensor_tensor", out=acc[:], in0=t2[:], scalar=scal[bi], in1=acc[:], op0=ALU.mult, op1=ALU.add),
                             [B2, Bacc], [Bacc])
                S.op("act", I("activation", out=mg[:, dc, :], in_=acc[:], func=AF.Copy), [Bacc], [B_mg[dc]])
            for ec in range(8):
                esl = slice(ec * 128, (ec + 1) * 128)
                w, Bw = load_w_chunk(l, wo_b, ec)
                p, Bp = get_pp()
                for k in range(8):
                    S.op("pe", I("matmul", p[:], lhsT=w[:, k, :], rhs=mg[:, k, :], start=(k == 0), stop=(k == 7)),
                         [Bw, B_mg[k]], [Bp])
                S.op("dve", I("tensor_tensor", out=xo[:, ec, :], in0=p[:], in1=xt[:, ec, :], op=ALU.add), [Bp, B_xt], [B_xo])
            dma("sp", xview(out_d, m), xo[:], [B_xo], [DRAM_x[m]], sl_xo)

        import os
        stop = os.environ.get("KSTOP", "")
        for l in range(depth):
            S.epoch = l
            if stop == "cast":
                for m in range(n_mt):
                    dma("sp", xt[:], xview(xsrc(l), m), [DRAM_x[m]] + B_wcast, [B_xt], sl_xt)
                    dma("sp", xview(out_d, m), xt[:], [B_xt], [DRAM_x[m]], sl_xo)
                continue
            layer_weights(l)
            if stop == "lw":
                for m in range(n_mt):
                    dma("sp", xt[:], xview(xsrc(l), m), [DRAM_x[m]] + B_wcast + [B_wtm, B_lw], [B_xt], sl_xt)
                    dma("sp", xview(out_d, m), xt[:], [B_xt], [DRAM_x[m]], sl_xo)
                continue
            if cfg.do_a:
                S.op("dve", I("memset", St[:], 0.0), [], [B_S])
                for m in range(n_mt):
                    sweep1_mt(l, m)
            for m in reversed(range(n_mt)):
                sweep2_mt(l, m)
        S.epoch = depth
        S.op("sp", I("nop"), reads=[DRAM_x[m] for m in range(n_mt)], writes=[])

        with nc.Block() as block:
            S.finalize(engsems, block)
        build_nc.last_stats = S.stats
    return nc


def run(inputs, cfg, trace=False):
    nc = build_nc(cfg)
    in_maps = [prep_core_inputs(inputs, c, cfg) for c in range(NCORES)]
    res = run_bass_kernel_spmd(nc, in_maps, core_ids=list(range(NCORES)), trace=trace)
    T = cfg.T
    L = 2 * T
    B = NCORES // 2
    out = np.empty((B, L, D), np.float32)
    for c in range(NCORES):
        o = np.asarray(res.results[c]["out"]).T
        if c % 2 == 0:
            out[c // 2, :T] = o
        else:
            out[c // 2, T:] = o[::-1]
    return out, res


def kernel(**inputs):
    cfg = Cfg()
    out, _ = run(inputs, cfg)
    return out
```

```python
import numpy as np
import concourse.bass as bass
import concourse.mybir as mybir
from concourse.bass_utils import run_bass_kernel_spmd

F32 = mybir.dt.float32
BF16 = mybir.dt.bfloat16
ALU = mybir.AluOpType
AF = mybir.ActivationFunctionType

D = 1024
DEPTH = 4
EPS = 1e-6
NCORES = 8
SAME_ENGINE_SYNC = True


def I(name, *args, **kw):
    return lambda e: getattr(e, name)(*args, **kw)


class Buf:
    __slots__ = ("name", "last_w", "readers")

    def __init__(self, name):
        self.name = name
        self.last_w = None
        self.readers = []


class Slot:
    def __init__(self, sem, name):
        self.sem = sem
        self.count = 0
        self.token = Buf("slot_" + name)


class Op:
    __slots__ = ("eng", "fn", "deps", "signal", "val", "sem", "slot", "idx", "epoch", "raw")


class Sched:
    ENGS = ("pe", "act", "dve", "pool", "sp")

    def __init__(self, nc, n_epochs):
        self.nc = nc
        self.ops = {e: [] for e in self.ENGS}
        self.epoch = 0
        self.n_epochs = n_epochs
        self.engsem = {}

    def op(self, eng, fn, reads=(), writes=(), slot=None):
        o = Op()
        o.eng = eng
        o.fn = fn
        o.signal = False
        o.val = None
        o.sem = None
        o.slot = slot
        o.epoch = self.epoch
        deps = []
        raw = set()
        writes = list(writes)
        if slot is not None:
            writes.append(slot.token)
        for b in reads:
            if b.last_w is not None:
                deps.append(b.last_w)
                raw.add(id(b.last_w))
        for b in writes:
            if b.last_w is not None:
                deps.append(b.last_w)
            deps.extend(b.readers)
        seen = set()
        dd = []
        for d in deps:
            if id(d) not in seen and d is not o:
                seen.add(id(d))
                dd.append(d)
        o.deps = dd
        o.raw = raw
        for b in reads:
            b.readers.append(o)
        for b in writes:
            b.last_w = o
            b.readers = []
        if slot is not None:
            slot.count += 1
            o.sem = slot.sem
            o.val = 16 * slot.count
        o.idx = len(self.ops[eng])
        self.ops[eng].append(o)
        return o

    def _needs_wait(self, cons, prod):
        if prod.slot is not None:
            return True
        if prod.eng == cons.eng and cons.slot is None:
            if prod.eng == "pe":
                return False
            return SAME_ENGINE_SYNC and (id(prod) in cons.raw)
        return True

    def finalize(self, sems, block):
        for e in self.ENGS:
            for o in self.ops[e]:
                for d in o.deps:
                    if self._needs_wait(o, d):
                        d.signal = True
        for e in self.ENGS:
            cnt = {}
            for o in self.ops[e]:
                if o.slot is not None:
                    pass
                elif o.signal:
                    cnt[o.epoch] = cnt.get(o.epoch, 0) + 1
                    o.sem = sems[(e, o.epoch)]
                    o.val = cnt[o.epoch]
        self.stats = {e: len(self.ops[e]) for e in self.ENGS}

        def emit(e, eng):
            waited = {}
            nwaits = 0
            for o in self.ops[e]:
                for d in o.deps:
                    if not self._needs_wait(o, d):
                        continue
                    key = id(d.sem)
                    if waited.get(key, 0) >= d.val:
                        continue
                    eng.wait_ge(d.sem, d.val)
                    nwaits += 1
                    waited[key] = d.val
                ins = o.fn(eng)
                if o.slot is not None:
                    ins.then_inc(o.sem, 16)
                elif o.signal:
                    ins.then_inc(o.sem, 1)
            self.stats[e + "_waits"] = nwaits

        @block.tensor
        def _(eng):
            emit("pe", eng)

        @block.scalar
        def _(eng):
            emit("act", eng)

        @block.vector
        def _(eng):
            emit("dve", eng)

        @block.gpsimd
        def _(eng):
            emit("pool", eng)

        @block.sync
        def _(eng):
            emit("sp", eng)


OFF = dict(qA=0, fAf=512, fAb=1024, iA=1536, zA=2048, uB=2560, vB=3072, zB=3584, qC=4096, kC=4608,
           vC=4736, zC=4864, gA=5376, gB=6400, gC=7424)

FM2 = (["a2"] * 4 + ["qA"] * 4 + ["zA"] * 4 + ["uB"] * 4 + ["zB"] * 4 + ["qC"] * 4 + ["kC"] * 2 + ["zC"] * 4
       + ["gA"] * 8 + ["gB"] * 8 + ["gC"] * 8)
FM1 = ["a1"] * 4 + ["qA"] * 4


def _fm_cols(kind_list, odd):
    out = []
    cnt = {}
    for kind in kind_list:
        j = cnt.get(kind, 0)
        cnt[kind] = j + 1
        if kind == "a1":
            base = OFF["fAb"] if odd else OFF["fAf"]
            cols = np.arange(base + j * 128, base + (j + 1) * 128)
        elif kind == "a2":
            base = OFF["fAf"] if odd else OFF["fAb"]
            cols = np.arange(base + j * 128, base + (j + 1) * 128)
        elif kind == "kC":
            c = np.arange(OFF["kC"] + j * 64, OFF["kC"] + (j + 1) * 64)
            cols = np.concatenate([c, c])
        else:
            base = OFF[kind]
            cols = np.arange(base + j * 128, base + (j + 1) * 128)
        out.append(cols)
    return out


def _tm_cols(sweep):
    if sweep == 1:
        return np.arange(OFF["iA"], OFF["iA"] + 512)
    return np.concatenate([np.arange(OFF["iA"], OFF["iA"] + 512), np.arange(OFF["vB"], OFF["vB"] + 512),
                           np.arange(OFF["vC"], OFF["vC"] + 128)])


NF1 = len(FM1)
NF2 = len(FM2)
TM1 = 512
TM2 = 1152


class Cfg:
    def __init__(self, n_mt=8, depth=DEPTH, do_a=True, do_b=True, do_c=True):
        self.n_mt = n_mt
        self.T = n_mt * 512
        self.NB = n_mt * 4
        self.depth = depth
        self.do_a = do_a
        self.do_b = do_b
        self.do_c = do_c


def prep_core_inputs(inp, core, cfg):
    T = cfg.T
    L = 2 * T
    b = core // 2
    odd = core % 2
    depth = cfg.depth
    f32 = np.float32
    pos = (np.arange(T) if not odd else (L - 1 - np.arange(T))).astype(np.int64)
    m = {}
    x = np.asarray(inp["x"])[b]
    m["xT"] = np.ascontiguousarray(x[pos, :].T).astype(f32)
    w_in = np.asarray(inp["w_in"])
    w1fm = np.empty((depth, NF1, 128, 8, 128), f32)
    w2fm = np.empty((depth, NF2, 128, 8, 128), f32)
    w1tm = np.empty((depth, 128, 8, TM1), f32)
    w2tm = np.empty((depth, 128, 8, TM2), f32)
    c1 = _fm_cols(FM1, odd)
    c2 = _fm_cols(FM2, odd)
    for l in range(depth):
        wl = w_in[l].reshape(8, 128, -1)
        for j, cols in enumerate(c1):
            w1fm[l, j] = wl[:, :, cols].transpose(1, 0, 2)
        for j, cols in enumerate(c2):
            w2fm[l, j] = wl[:, :, cols].transpose(1, 0, 2)
        w1tm[l] = wl[:, :, _tm_cols(1)].transpose(1, 0, 2)
        w2tm[l] = wl[:, :, _tm_cols(2)].transpose(1, 0, 2)
    m["w1fm"] = w1fm
    m["w2fm"] = w2fm
    m["w1tm"] = w1tm
    m["w2tm"] = w2tm
    wbr = np.empty((depth, 8, 128, 3, 4, 128), f32)
    for bi, key in enumerate(("w_branch_a", "w_branch_b", "w_branch_c")):
        w = np.asarray(inp[key])[:depth].reshape(depth, 4, 128, 8, 128)
        wbr[:, :, :, bi] = w.transpose(0, 3, 2, 1, 4)
    m["wbr"] = wbr
    w = np.asarray(inp["w_out"])[:depth].reshape(depth, 8, 128, 8, 128)
    m["wo"] = np.ascontiguousarray(w.transpose(0, 3, 2, 1, 4)).astype(f32)
    ng = np.asarray(inp["norm_gain"])[:depth]
    m["ngain"] = np.ascontiguousarray(ng.reshape(depth, 8, 128).transpose(2, 0, 1)).astype(f32)
    lb = np.asarray(inp["lb_logits"]).reshape(DEPTH, 2, 4, 128)
    if odd:
        lb = lb[:, ::-1]
    m["lbl"] = np.ascontiguousarray(lb.transpose(3, 0, 1, 2)).astype(f32)
    hg = np.asarray(inp["hg_norm_gain"])[:depth]
    m["hgain"] = np.ascontiguousarray(hg.transpose(2, 0, 1)).astype(f32)
    m["lng"] = np.ascontiguousarray(np.broadcast_to(np.asarray(inp["sg_ln_gain"])[:depth][None], (128, depth, 512))).astype(f32)
    m["lnb"] = np.ascontiguousarray(np.broadcast_to(np.asarray(inp["sg_ln_bias"])[:depth][None], (128, depth, 512))).astype(f32)
    ws = np.asarray(inp["w_spatial"])[:depth]
    bs = np.asarray(inp["b_spatial"])[:depth]
    if odd:
        ws = ws[:, :, ::-1, ::-1]
        bs = bs[:, :, ::-1]
    m["wsT"] = np.ascontiguousarray(ws.transpose(3, 0, 1, 2)).astype(f32)
    m["bsp"] = np.ascontiguousarray(np.broadcast_to(bs[None], (128, depth, 4, 128))).astype(f32)
    qg = np.asarray(inp["q_norm_gain"])[:depth]
    kg = np.asarray(inp["k_norm_gain"])[:depth]
    m["qkg"] = np.ascontiguousarray(np.stack([np.concatenate([qg, qg], 1), np.concatenate([kg, kg], 1)], 1).transpose(2, 0, 1)).astype(f32)
    m["sink"] = np.ascontiguousarray(np.broadcast_to(np.asarray(inp["sink_logits"])[:depth][None], (128, depth, 8))).astype(f32)
    half = 32
    inv_freq = (10000.0 ** (-np.arange(half, dtype=np.float32) / half)).astype(np.float32)
    ang = pos.astype(np.float32)[None, :] * inv_freq[:, None]
    cos = np.cos(ang).astype(f32)
    sin = np.sin(ang).astype(f32)
    m["cosT"] = np.ascontiguousarray(np.concatenate([cos, cos, cos, cos], 0))
    m["sinT"] = np.ascontiguousarray(np.concatenate([sin, sin, sin, sin], 0))
    ident = np.eye(128, dtype=f32)
    m["c_ident"] = ident
    bd = np.zeros((128, 128), f32)
    bd[:64, :64] = 1
    bd[64:, 64:] = 1
    m["c_bd"] = bd
    rot = np.zeros((128, 128), f32)
    for hb in (0, 64):
        for d in range(32):
            rot[hb + d + 32, hb + d] = -1.0
            rot[hb + d, hb + d + 32] = 1.0
    m["c_rot"] = rot
    j = np.arange(128)[:, None]
    i = np.arange(128)[None, :]
    masks = np.stack([(j >= i), (j <= i), (j + i >= 127)], 1).astype(f32)
    m["c_amask"] = np.ascontiguousarray(masks)
    s = np.arange(64)[:, None]
    t = np.arange(64)[None, :]
    h1 = (s <= t).astype(f32)
    h2 = (s >= t).astype(f32)
    m["c_hmask"] = np.ascontiguousarray(np.stack([np.concatenate([h1, h1], 0), np.concatenate([h2, h2], 0)], 1))
    cm = np.ones((128, 512), f32)
    cm[:, ::64] = 0.0
    m["c_cmask"] = cm
    ss_ = np.arange(128)[:, None]
    tt_ = np.arange(128)[None, :]
    same = (ss_ // 64) == (tt_ // 64)
    m["c_hmask2"] = np.ascontiguousarray(np.stack([(same & (ss_ <= tt_)), (same & (ss_ >= tt_))], 1).astype(f32))
    sel = np.zeros((128, 2), f32)
    sel[:, 1 - odd] = 1.0
    m["sel"] = sel
    return m


def build_nc(cfg):
    nc = bass.Bass("TRN2", target_bir_lowering=False)
    T = cfg.T
    depth = cfg.depth
    n_mt = cfg.n_mt

    def din(name, shape, dt=F32):
        return nc.dram_tensor(name, list(shape), dt, kind="ExternalInput")

    xT_d = din("xT", [D, T])
    w1fm_d = din("w1fm", [depth, NF1, 128, 8, 128])
    w2fm_d = din("w2fm", [depth, NF2, 128, 8, 128])
    w1tm_d = din("w1tm", [depth, 128, 8, TM1])
    w2tm_d = din("w2tm", [depth, 128, 8, TM2])
    wbr_d = din("wbr", [depth, 8, 128, 3, 4, 128])
    wo_d = din("wo", [depth, 8, 128, 8, 128])
    ngain_d = din("ngain", [128, depth, 8])
    lbl_d = din("lbl", [128, DEPTH, 2, 4])
    hgain_d = din("hgain", [128, depth, 4])
    lng_d = din("lng", [128, depth, 512])
    lnb_d = din("lnb", [128, depth, 512])
    wsT_d = din("wsT", [128, depth, 4, 128])
    bsp_d = din("bsp", [128, depth, 4, 128])
    qkg_d = din("qkg", [128, depth, 2])
    sink_d = din("sink", [128, depth, 8])
    cosT_d = din("cosT", [128, T])
    sinT_d = din("sinT", [128, T])
    c_ident_d = din("c_ident", [128, 128])
    c_bd_d = din("c_bd", [128, 128])
    c_rot_d = din("c_rot", [128, 128])
    c_amask_d = din("c_amask", [128, 3, 128])
    c_hmask_d = din("c_hmask", [128, 2, 64])
    sel_d = din("sel", [128, 2])
    c_cmask_d = din("c_cmask", [128, 512])
    c_hmask2_d = din("c_hmask2", [128, 2, 128])
    out_d = nc.dram_tensor("out", [D, T], F32, kind="ExternalOutput")

    w1fm_b = nc.dram_tensor("w1fm_b", [depth, NF1, 128, 8, 128], BF16)
    w2fm_b = nc.dram_tensor("w2fm_b", [depth, NF2, 128, 8, 128], BF16)
    w1tm_b = nc.dram_tensor("w1tm_b", [depth, 128, 8, TM1], BF16)
    w2tm_b = nc.dram_tensor("w2tm_b", [depth, 128, 8, TM2], BF16)
    wbr_b = nc.dram_tensor("wbr_b", [depth, 8, 128, 3, 4, 128], BF16)
    wo_b = nc.dram_tensor("wo_b", [depth, 8, 128, 8, 128], BF16)
    o1_d = nc.dram_tensor("o1_spill", [128, cfg.NB, 4, 128], F32)
    CCW = 512 + 256 + 128
    cc_in = [nc.dram_tensor(f"cc_in{l}", [128, CCW], F32) for l in range(depth)]
    cc_out = [nc.dram_tensor(f"cc_out{l}", [256, CCW], F32) for l in range(depth)]

    from contextlib import ExitStack
    es = ExitStack()
    with es:
        S = Sched(nc, depth + 1)

        def sb(name, shape, dt=F32):
            return es.enter_context(nc.sbuf_tensor(name, list(shape), dt))

        def ps(name, shape, dt=F32):
            return es.enter_context(nc.psum_tensor(name, list(shape), dt))

        def sem(name):
            return es.enter_context(nc.semaphore(name))

        engsems = {(e, ep): sem(f"s_{e}_{ep}") for e in ("pe", "act", "dve", "pool") for ep in range(depth + 1)}
        _slot_n = [0]

        def slot(name):
            _slot_n[0] += 1
            return Slot(sem(f"d_{name}_{_slot_n[0]}"), name)

        ident_b = sb("ident_b", [128, 128], BF16)
        ones_b = sb("ones_b", [128, 128], BF16)
        bd_b = sb("bd_b", [128, 128], BF16)
        rot_b = sb("rot_b", [128, 128], BF16)
        amask = sb("amask", [128, 3, 128], BF16)
        hmask = sb("hmask", [128, 2, 64], BF16)
        selt = sb("selt", [128, 2])
        ngain = sb("ngain_s", [128, depth, 8])
        lbl = sb("lbl_s", [128, DEPTH, 2, 4])
        hgain = sb("hgain_s", [128, depth, 4])
        qkg = sb("qkg_s", [128, depth, 2])
        sinkt = sb("sink_s", [128, depth, 8])
        esink = sb("esink", [128, depth, 8])
        lbc1 = sb("lbc1", [128, DEPTH, 2, 4])
        lbc0 = sb("lbc0", [128, DEPTH, 2, 4])
        B_const = Buf("const")
        sl_c = slot("const")

        def dma(eng, out, in_, reads, writes, sl):
            return S.op(eng, I("dma_start", out=out, in_=in_), reads=reads, writes=writes, slot=sl)

        for dst, src in ((ident_b, c_ident_d), (bd_b, c_bd_d), (rot_b, c_rot_d), (amask, c_amask_d), (hmask, c_hmask_d)):
            dma("pool", dst[:], src.ap(), [], [B_const], sl_c)
        for dst, src in ((selt, sel_d), (ngain, ngain_d), (lbl, lbl_d), (hgain, hgain_d), (qkg, qkg_d), (sinkt, sink_d)):
            dma("sp", dst[:], src.ap(), [], [B_const], sl_c)
        S.op("dve", I("memset", ones_b[:], 1.0), [], [B_const])
        S.op("act", I("activation", out=esink[:], in_=sinkt[:], func=AF.Exp), [B_const], [B_const])
        lbe = sb("lbe", [128, DEPTH, 8])
        lbs = sb("lbs", [128, 8])
        lbv = lbl[:].rearrange("p l a b -> p l (a b)")
        S.op("act", I("activation", out=lbe[:], in_=lbv, func=AF.Exp), [B_const], [B_const])
        S.op("dve", I("tensor_tensor", out=lbs[:], in0=lbe[:, 0, :], in1=lbe[:, 1, :], op=ALU.add), [B_const], [B_const])
        S.op("dve", I("tensor_tensor", out=lbs[:], in0=lbs[:], in1=lbe[:, 2, :], op=ALU.add), [B_const], [B_const])
        S.op("dve", I("tensor_tensor", out=lbs[:], in0=lbs[:], in1=lbe[:, 3, :], op=ALU.add), [B_const], [B_const])
        S.op("dve", I("reciprocal", out=lbs[:], in_=lbs[:]), [B_const], [B_const])
        c1v = lbc1[:].rearrange("p l a b -> p l (a b)")
        c0v = lbc0[:].rearrange("p l a b -> p l (a b)")
        S.op("dve", I("memset", c0v[:, 0, :], 0.0), [B_const], [B_const])
        for l in range(1, DEPTH):
            S.op("dve", I("tensor_tensor", out=c1v[:, l, :], in0=lbe[:, l, :], in1=lbs[:], op=ALU.mult), [B_const], [B_const])
            S.op("dve", I("tensor_tensor", out=c0v[:, l, :], in0=c0v[:, l - 1, :], in1=c1v[:, l, :], op=ALU.add), [B_const], [B_const])
        S.op("dve", I("tensor_scalar", out=lbc1[:], in0=lbc0[:], scalar1=-0.5, scalar2=0.5, op0=ALU.mult, op1=ALU.add), [B_const], [B_const])
        S.op("dve", I("tensor_scalar", out=lbc0[:], in0=lbc0[:], scalar1=0.5, scalar2=0.5, op0=ALU.mult, op1=ALU.add), [B_const], [B_const])

        B_wcast = [Buf(f"wcast{l}") for l in range(depth)]
        sl_wc = [slot(f"wc{i}") for i in range(4)]
        _wc_i = [0]

        def wcast(l, dst, src):
            sl = sl_wc[_wc_i[0] % 4]
            _wc_i[0] += 1
            S.op("pool", I("dma_start", out=dst, in_=src), reads=[], writes=[B_wcast[l]], slot=sl)

        import os
        for l in range(depth if not os.environ.get("KSKIPCAST") else 0):
            def v4(t, j0, j1):
                return t[l, j0:j1].rearrange("a p k n -> (a p) (k n)")

            def v3(t):
                return t[l].rearrange("p k n -> p (k n)")

            for j in range(0, NF1, 4):
                wcast(l, v4(w1fm_b, j, j + 4), v4(w1fm_d, j, j + 4))
            wcast(l, v3(w1tm_b), v3(w1tm_d))
            for j in range(0, NF2, 6):
                wcast(l, v4(w2fm_b, j, j + 6), v4(w2fm_d, j, j + 6))
            wcast(l, v3(w2tm_b), v3(w2tm_d))
            wcast(l, wbr_b[l].rearrange("a p b k n -> (a p) (b k n)"), wbr_d[l].rearrange("a p b k n -> (a p) (b k n)"))
            wcast(l, v4(wo_b, 0, 8), v4(wo_d, 0, 8))

        NWB = 4
        wbuf = [sb(f"wbuf{i}", [128, 8, 128], BF16) for i in range(NWB)]
        B_wbuf = [Buf(f"wbuf{i}") for i in range(NWB)]
        sl_wbuf = [slot(f"wb{i}") for i in range(NWB)]
        _wb_i = [0]
        wtm = sb("wtm", [128, 8, TM2], BF16)
        B_wtm = Buf("wtm")
        sl_wtm = slot("wtm")
        wbrc = [sb(f"wbrc{i}", [128, 3, 4, 128], BF16) for i in range(2)]
        B_wbrc = [Buf(f"wbrc{i}") for i in range(2)]
        sl_wbrc = [slot(f"wbrc{i}") for i in range(2)]
        _wbrc_i = [0]
        sl_wl = slot("wl")
        lng = sb("lng_s", [128, 512])
        lnb = sb("lnb_s", [128, 512])
        wsT = sb("wsT_s", [128, 4, 128], BF16)
        bsp = sb("bsp_s", [128, 4, 128])
        B_lw = Buf("layerw")

        xt = sb("xt", [128, 8, 512])
        B_xt = Buf("xt")
        sl_xt = slot("xt")
        hT = sb("hT", [128, 8, 512], BF16)
        B_hT = Buf("hT")
        rstd = sb("rstd", [128, 512])
        B_rstd = Buf("rstd")
        rtmp = sb("rtmp", [128, 512])
        B_rtmp = Buf("rtmp")
        epsb = sb("epsb", [128, 1])
        S.op("dve", I("memset", epsb[:], EPS), [], [B_const])
        mhalf = sb("mhalf", [128, 512])
        S.op("pool", I("memset", mhalf[:], -0.5), [], [B_const])

        zs = sb("zs", [128, 12, 512], BF16)
        B_zs = [Buf(f"zs{i}") for i in range(12)]
        ub = sb("ub", [128, 4, 512], BF16)
        B_ub = [Buf(f"ub{i}") for i in range(4)]
        yb = sb("yb", [128, 4, 512], BF16)
        B_yb = Buf("yb")
        ya = sb("ya", [128, 4, 512], BF16)
        B_ya = Buf("ya")
        yc = sb("yc", [128, 4, 512], BF16)
        B_yc = Buf("yc")
        mg = sb("mg", [128, 8, 512], BF16)
        B_mg = [Buf(f"mg{i}") for i in range(8)]
        sq = mg
        xo = sb("xo", [128, 8, 512])
        B_xo = Buf("xo")
        sl_xo = slot("xo")
        ya_scr = sb("qkscr", [128, 1024], BF16)
        B_ya_scr = Buf("qkscr")
        NTMP = 8
        tmpf = [sb(f"tmpf{i}", [128, 512]) for i in range(NTMP)]
        B_tmpf = [Buf(f"tmpf{i}") for i in range(NTMP)]
        _tf_i = [0]

        def get_tmp():
            i = _tf_i[0] % NTMP
            _tf_i[0] += 1
            return tmpf[i], B_tmpf[i]

        vn = sb("vn", [128, 512], BF16)
        B_vn = Buf("vn")
        bnst = sb("bnst", [128, 6])
        bnag = sb("bnag", [128, 2])
        B_bn = Buf("bn")
        mhalf1 = sb("mhalf1", [128, 1])
        S.op("pool", I("memset", mhalf1[:], -0.5), [], [B_const])

        NPP = 2
        pp = [ps(f"pp{i}", [128, 512]) for i in range(NPP)]
        B_pp = [Buf(f"pp{i}") for i in range(NPP)]
        _pp_i = [0]
        pp_mode = ["narrow"]

        def get_pp():
            if pp_mode[0] == "wide":
                banks = [(pp[0], B_pp[0]), (pp[1], B_pp[1]), (pbr[0], B_pbr[0]), (pbr[1], B_pbr[1]), (pbr[2], B_pbr[2]), (pmx_flat, B_pmx)]
            elif pp_mode[0] == "nopmx":
                banks = [(pp[0], B_pp[0]), (pp[1], B_pp[1]), (pbr[0], B_pbr[0]), (pbr[1], B_pbr[1]), (pbr[2], B_pbr[2])]
            else:
                banks = [(pp[0], B_pp[0]), (pp[1], B_pp[1])]
            i = _pp_i[0] % len(banks)
            _pp_i[0] += 1
            return banks[i]

        pst = ps("pst", [128, 512])
        B_pst = Buf("pst")
        pmx = ps("pmx", [128, 4, 128])
        B_pmx = Buf("pmx")
        pbr = [ps(f"pbr{i}", [128, 512]) for i in range(3)]
        B_pbr = [Buf(f"pbr{i}") for i in range(3)]

        class _Flat:
            def __getitem__(self, idx):
                return pmx[:].rearrange("p a b -> p (a b)")[idx]
        pmx_flat = _Flat()

        DRAM_x = [Buf(f"dram_x{m}") for m in range(n_mt)]
        qr = sb("qr", [128, 4, 512], BF16)
        B_qr = [Buf(f"qr{i}") for i in range(4)]
        kz = sb("kz", [128, 2, 2, 768], BF16)
        B_kr = Buf("kr")
        S.op("pool", I("memset", kz[:], 0.0), [], [B_kr])
        vaug = sb("vaug", [128, 6, 2, 192], BF16)
        B_vaug = Buf("vaug")
        S.op("dve", I("memset", vaug[:, :, :, 64:128], 1.0), [], [B_vaug])
        pt = sb("pt", [128, 3, 2, 256], BF16)
        B_pt = Buf("pt")
        B_ptk = [Buf(f"pt{i}") for i in range(3)]
        cs = sb("cs", [128, 2, 640])
        B_cs = Buf("cs")
        sl_cs = slot("cs")
        xh = sb("xh", [128, 8, 128])
        B_xh = Buf("xh")
        sl_xh = slot("xh")
        hTh = sb("hTh", [128, 8, 128], BF16)
        B_hTh = Buf("hTh")
        sqh = sb("sqh", [128, 8, 128], BF16)
        B_sqh = Buf("sqh")
        mone = sb("mone", [128, 256])
        S.op("pool", I("memset", mone[:], -1.0), [], [B_const])
        dtmp = sb("dtmp", [128, 256])
        B_dtmp = Buf("dtmp")
        xo_flat = xo[:].rearrange("p c t -> p (c t)")
        ccg = xo_flat[:, 0:2 * CCW].rearrange("p (r n) -> p r n", r=2)
        ccs = xo_flat[:, 2048:2048 + CCW]
        ccp = xo_flat[:, 3072:3072 + CCW]
        B_ccs = B_ccg = B_ccp = B_xo
        sl_cc = slot("cc")
        DRAM_cc = Buf("dram_cc")
        ccsem = sem("ccsem")
        cmask = sb("cmask", [128, 512])
        hmask2 = sb("hmask2", [128, 2, 128], BF16)
        dma("sp", cmask[:], c_cmask_d.ap(), [], [B_const], sl_c)
        dma("pool", hmask2[:], c_hmask2_d.ap(), [], [B_const], sl_c)
        qraw = sb("qraw", [128, 4, 512])
        B_qraw = Buf("qraw")
        qtT = sb("qtT", [128, 4, 512], BF16)
        B_qtT = Buf("qtT")
        ktT = sb("ktT", [128, 4, 512], BF16)
        B_ktT = Buf("ktT")
        ktA = sb("ktA", [128, 4, 128], BF16)
        ktB = sb("ktB", [128, 4, 128], BF16)
        B_kt = Buf("kt")
        S.op("pool", I("memset", ktA[:], 0.0), [], [B_kt])
        S.op("pool", I("memset", ktB[:], 0.0), [], [B_kt])
        vtok = sb("vtok", [128, 4, 4, 128], BF16)
        B_vtok = [Buf(f"vtok{i}") for i in range(4)]
        sc = sb("sc", [128, 4, 3, 8])
        B_sc = Buf("sc")
        sc8 = sb("sc8", [128, 8])
        B_sc8 = Buf("sc8")
        St = sb("St", [128, 4, 128])
        B_S = Buf("S")
        Sp = sb("Sp", [128, 4, 128], BF16)
        B_Sp = Buf("Sp")
        ATs = sb("ATs", [128, 4, 128], BF16)
        B_ATs = Buf("ATs")
        o1s = sb("o1s", [128, 4, 128])
        B_o1s = Buf("o1s")
        sl_o1 = slot("o1")
        DRAM_o1 = [Buf(f"dram_o1_{i}") for i in range(cfg.NB)]
        ptr = ps("ptr", [128, 4, 128], BF16)
        B_ptr = Buf("ptr")

        def hgrn_gates(l, dirn, srcw, base_a):
            for h in range(4):
                p, Bp = proj_fm(l, srcw, base_a + h)
                tA, BA = get_tmp()
                tK, BK = get_tmp()
                tB, BB = get_tmp()
                tD, BD = get_tmp()
                tE, BE = get_tmp()
                v = lambda t: t[:].rearrange("p (c t) -> p c t", t=64)
                S.op("act", I("activation", out=tA[:], in_=p[:], func=AF.Tanh, scale=0.5), [Bp], [BA])
                S.op("dve", I("tensor_scalar", out=tA[:], in0=tA[:], scalar1=lbc1[:, l, dirn, h:h + 1], scalar2=lbc0[:, l, dirn, h:h + 1], op0=ALU.mult, op1=ALU.add),
                     [BA, B_const], [BA])
                S.op("pool", I("tensor_scalar", out=tK[:], in0=tA[:], scalar1=-1.0, scalar2=1.0, op0=ALU.mult, op1=ALU.add), [BA], [BK])
                S.op("act", I("activation", out=tA[:], in_=tA[:], func=AF.Ln), [BA, BK], [BA])
                S.op("dve", I("tensor_tensor_scan", out=tB[:], data0=cmask[:], data1=tA[:], initial=0.0, op0=ALU.mult, op1=ALU.add), [BA, B_const], [BB])
                if dirn == 0:
                    S.op("dve", I("tensor_tensor", out=v(tD), in0=v(tB), in1=v(tB)[:, :, 31:32].to_broadcast([128, 8, 64]), op=ALU.subtract), [BB], [BD])
                    S.op("act", I("activation", out=sc[:, h, 0, :], in_=v(tB)[:, :, 31], func=AF.Exp), [BB], [B_sc])
                    S.op("act", I("activation", out=sc[:, h, 1, :], in_=v(tB)[:, :, 63], func=AF.Exp), [BB], [B_sc])
                    S.op("act", I("activation", out=sc[:, h, 2, :], in_=v(tD)[:, :, 63], func=AF.Exp), [BD], [B_sc])
                else:
                    S.op("dve", I("tensor_tensor", out=tA[:], in0=tB[:], in1=tA[:], op=ALU.subtract), [BB, BA], [BA])
                    S.op("dve", I("tensor_tensor", out=v(tD), in0=v(tA)[:, :, 32:33].to_broadcast([128, 8, 64]), in1=v(tA), op=ALU.subtract), [BA], [BD])
                    S.op("dve", I("tensor_tensor", out=sc8[:], in0=v(tB)[:, :, 63], in1=v(tA)[:, :, 32], op=ALU.subtract), [BB, BA], [B_sc8])
                    S.op("act", I("activation", out=sc[:, h, 0, :], in_=sc8[:], func=AF.Exp), [B_sc8], [B_sc])
                    S.op("act", I("activation", out=sc[:, h, 1, :], in_=v(tB)[:, :, 63], func=AF.Exp), [BB], [B_sc])
                    S.op("act", I("activation", out=sc[:, h, 2, :], in_=v(tD)[:, :, 0], func=AF.Exp), [BD], [B_sc])
                S.op("act", I("activation", out=tE[:], in_=tD[:], func=AF.Exp), [BD], [BE])
                S.op("dve", I("tensor_tensor", out=qtT[:, h, :], in0=qraw[:, h, :], in1=tE[:], op=ALU.mult), [B_qraw, BE], [B_qtT])
                S.op("act", I("activation", out=tB[:], in_=tD[:], func=AF.Exp, scale=-1.0), [BD, B_sc, B_sc8], [BB])
                S.op("pool", I("tensor_tensor", out=ktT[:, h, :], in0=tK[:], in1=tB[:], op=ALU.mult), [BK, BB], [B_ktT])

        def hgrn_q_and_v(l, srcw, base_q):
            for h in range(4):
                p, Bp = proj_fm(l, srcw, base_q + h)
                S.op("act", I("activation", out=qraw[:, h, :], in_=p[:], func=AF.Copy), [Bp], [B_qraw])

        def hgrn_vtok(bi):
            tsl = slice(bi * 128, (bi + 1) * 128)
            p, Bp = get_pp()
            for k in range(8):
                S.op("pe", I("matmul", p[:], lhsT=hT[:, k, tsl], rhs=wtm[:, k, 0:512], start=(k == 0), stop=(k == 7)), [B_hT, B_wtm], [Bp])
            S.op("act", I("activation", out=vtok[:, bi, :, :], in_=p[:].rearrange("p (h d) -> p h d", h=4), func=AF.Copy), [Bp], [B_vtok[bi]])

        def hgrn_block(l, dirn, bi, m):
            tsl = slice(bi * 128, (bi + 1) * 128)
            psc, B_psc = pbr[0][:].rearrange("p (h t) -> p h t", h=4), B_pbr[0]
            po, B_po = pbr[1][:].rearrange("p (h t) -> p h t", h=4), B_pbr[1]
            pob, B_pob = pbr[2][:].rearrange("p (h t) -> p h t", h=4), B_pbr[2]
            pS, B_pS = pmx[:], B_pmx
            for h in range(4):
                S.op("pe", I("transpose", out=ptr[:, h, :], in_=ktT[:, h, tsl], identity=ident_b[:]), [B_ktT, B_const], [B_ptr])
            S.op("dve", I("tensor_copy", out=ktA[0:64, :, :], in_=ptr[0:64, :, :]), [B_ptr], [B_kt])
            S.op("act", I("activation", out=ktB[64:128, :, :], in_=ptr[64:128, :, :], func=AF.Copy), [B_ptr], [B_kt])
            for h in range(4):
                S.op("pe", I("matmul", psc[:, h, :], lhsT=ktT[:, h, tsl], rhs=qtT[:, h, tsl], start=True, stop=True), [B_ktT, B_qtT], [B_psc])
            S.op("dve", I("tensor_tensor", out=ATs[:], in0=psc, in1=hmask2[:, dirn:dirn + 1, :].to_broadcast([128, 4, 128]), op=ALU.mult), [B_psc, B_const], [B_ATs])
            for h in range(4):
                S.op("pe", I("matmul", po[:, h, :], lhsT=vtok[:, bi, h, :], rhs=ATs[:, h, :], start=True, stop=True), [B_vtok[bi], B_ATs], [B_po])
            chunks = [(0, ktA), (1, ktB)] if dirn == 0 else [(1, ktB), (0, ktA)]
            for ci, (c, kt) in enumerate(chunks):
                gc = bi * 2 + c
                csl = slice(bi * 128 + c * 64, bi * 128 + (c + 1) * 64)
                for h in range(4):
                    S.op("dve", I("tensor_scalar", out=Sp[:, h, :], in0=St[:, h, :], scalar1=sc[:, h, 0, gc:gc + 1], scalar2=None, op0=ALU.mult), [B_S, B_sc], [B_Sp])
                for h in range(4):
                    S.op("pe", I("matmul", pob[:, h, c * 64:(c + 1) * 64], lhsT=Sp[:, h, :], rhs=qtT[:, h, csl], start=True, stop=True), [B_Sp, B_qtT], [B_pob])
                for h in range(4):
                    S.op("pe", I("matmul", pS[:, h, :], lhsT=kt[:, h, :], rhs=vtok[:, bi, h, :], start=True, stop=True), [B_kt, B_vtok[bi]], [B_pS])
                for h in range(4):
                    S.op("dve", I("tensor_scalar", out=St[:, h, :], in0=St[:, h, :], scalar1=sc[:, h, 1, gc:gc + 1], scalar2=None, op0=ALU.mult), [B_S, B_sc], [B_S])
                    S.op("dve", I("scalar_tensor_tensor", out=St[:, h, :], in0=pS[:, h, :], scalar=sc[:, h, 2, gc:gc + 1], in1=St[:, h, :], op0=ALU.mult, op1=ALU.add),
                         [B_pS, B_sc, B_S], [B_S])
            gblk = m * 4 + bi
            if dirn == 0:
                S.op("act", I("activation", out=o1s[:], in_=po, func=AF.Copy), [B_po], [B_o1s])
                S.op("dve", I("tensor_tensor", out=o1s[:], in0=o1s[:], in1=pob, op=ALU.add), [B_o1s, B_pob], [B_o1s])
                dma("sp", o1_d[:, gblk], o1s[:], [B_o1s], [DRAM_o1[gblk]], sl_o1)
            else:
                dma("sp", o1s[:], o1_d[:, gblk], [DRAM_o1[gblk]], [B_o1s], sl_o1)
                osum, Bos = get_tmp()
                rt, Brt = get_tmp()
                osv = osum[:].rearrange("p (h t) -> p h t", h=4)
                S.op("dve", I("tensor_tensor", out=osv, in0=po, in1=o1s[:], op=ALU.add), [B_po, B_o1s], [Bos])
                S.op("dve", I("tensor_tensor", out=osv, in0=osv, in1=pob, op=ALU.add), [Bos, B_pob], [Bos])
                S.op("act", I("activation", out=ya_scr[:, 0:512], in_=osum[:], func=AF.Square), [Bos], [B_ya_scr])
                S.op("pe", I("matmul", pst[:], lhsT=ones_b[:], rhs=ya_scr[:, 0:512], start=True, stop=True), [B_ya_scr, B_const], [B_pst])
                S.op("act", I("activation", out=rt[:], in_=pst[:], func=AF.Ln, scale=1.0 / 128, bias=epsb[:]), [B_pst, B_const], [Brt])
                S.op("act", I("activation", out=rt[:], in_=rt[:], func=AF.Exp, scale=-0.5), [Brt], [Brt])
                S.op("dve", I("tensor_tensor", out=osum[:], in0=osum[:], in1=rt[:], op=ALU.mult), [Bos, Brt], [Bos])
                for h in range(4):
                    S.op("dve", I("scalar_tensor_tensor", out=ya[:, h, tsl], in0=osv[:, h, :], scalar=hgain[:, l, h:h + 1], in1=zs[:, h, tsl], op0=ALU.mult, op1=ALU.mult),
                         [Bos, B_const, B_zs[h]], [B_ya])

        def sweep1_mt(l, m):
            norm_mt(l, m)
            pp_mode[0] = "wide"
            hgrn_q_and_v(l, w1fm_b, 4)
            hgrn_gates(l, 0, w1fm_b, 0)
            for bi in range(4):
                hgrn_vtok(bi)
                hgrn_block(l, 0, bi, m)
        sl_out = slot("out")

        def xsrc(l):
            return xT_d if l == 0 else out_d

        xview = lambda t, m: t.ap().rearrange("(c p) t -> p c t", p=128)[:, :, m * 512:(m + 1) * 512]

        def load_w_chunk(l, src_b, j):
            i = _wb_i[0] % NWB
            _wb_i[0] += 1
            dma("sp", wbuf[i][:], src_b[l, j], [B_wcast[l]], [B_wbuf[i]], sl_wbuf[i])
            return wbuf[i], B_wbuf[i]

        def tanh_gate(dst, src, scale):
            return I("activation", out=dst, in_=src, func=AF.Tanh, scale=scale)

        def norm_mt(l, m):
            dma("sp", xt[:], xview(xsrc(l), m), [DRAM_x[m]], [B_xt], sl_xt)
            for c in range(8):
                S.op("act", I("activation", out=sq[:, c, :], in_=xt[:, c, :], func=AF.Square), [B_xt], [B_mg[c]])
            for c in range(8):
                S.op("pe", I("matmul", pst[:], lhsT=ones_b[:], rhs=sq[:, c, :], start=(c == 0), stop=(c == 7)),
                     [B_mg[c], B_const], [B_pst])
            S.op("act", I("activation", out=rtmp[:], in_=pst[:], func=AF.Ln, scale=1.0 / D, bias=epsb[:]), [B_pst, B_const], [B_rtmp])
            S.op("act", I("activation", out=rstd[:], in_=rtmp[:], func=AF.Exp, scale=-0.5), [B_rtmp], [B_rstd])
            for c in range(8):
                S.op("dve", I("scalar_tensor_tensor", out=hT[:, c, :], in0=xt[:, c, :], scalar=ngain[:, l, c:c + 1],
                                                                   in1=rstd[:], op0=ALU.mult, op1=ALU.mult),
                     [B_xt, B_rstd, B_const], [B_hT])

        def proj_fm(l, src_b, j):
            w, Bw = load_w_chunk(l, src_b, j)
            p, Bp = get_pp()
            for k in range(8):
                S.op("pe", I("matmul", p[:], lhsT=w[:, k, :], rhs=hT[:, k, :], start=(k == 0), stop=(k == 7)),
                     [Bw, B_hT], [Bp])
            return p, Bp

        def layer_weights(l):
            dma("sp", wtm[:], w2tm_b[l], [B_wcast[l]], [B_wtm], sl_wtm)
            dma("sp", lng[:], lng_d[:, l, :], [], [B_lw], sl_wl)
            dma("sp", lnb[:], lnb_d[:, l, :], [], [B_lw], sl_wl)
            dma("sp", bsp[:], bsp_d[:, l], [], [B_lw], sl_wl)
            dma("pool", wsT[:], wsT_d[:, l], [], [B_lw], sl_wl)

        GELU_C = 0.7978845608028654

        def gelu_from_psum(p, Bp, dst, Bdst, eng2="pool"):
            t1, B1 = get_tmp()
            t2, B2 = get_tmp()
            S.op("act", I("activation", out=t1[:], in_=p[:], func=AF.Square), [Bp], [B1])
            S.op("dve", I("tensor_scalar", out=t1[:], in0=t1[:], scalar1=0.044715, scalar2=1.0, op0=ALU.mult, op1=ALU.add), [B1], [B1])
            S.op("dve", I("tensor_tensor", out=t2[:], in0=t1[:], in1=p[:], op=ALU.mult), [B1, Bp], [B2])
            S.op("act", I("activation", out=t1[:], in_=t2[:], func=AF.Tanh, scale=GELU_C), [B2], [B1])
            S.op("dve", I("scalar_tensor_tensor", out=dst, in0=t1[:], scalar=1.0, in1=p[:], op0=ALU.add, op1=ALU.mult), [B1, Bp], [Bdst])

        def silu_from_psum(p, Bp, dst, Bdst):
            t1, B1 = get_tmp()
            S.op("act", I("activation", out=t1[:], in_=p[:], func=AF.Tanh, scale=0.5), [Bp], [B1])
            S.op("dve", I("scalar_tensor_tensor", out=dst, in0=t1[:], scalar=1.0, in1=p[:], op0=ALU.add, op1=ALU.mult), [B1, Bp], [Bdst])

        def sweep2_mt(l, m):
            norm_mt(l, m)
            base = {}
            cnt = 0
            for kind in FM2:
                base.setdefault(kind, cnt)
                cnt += 1
            if cfg.do_b:
                pp_mode[0] = "wide"
                for j in range(4):
                    p, Bp = proj_fm(l, w2fm_b, base["uB"] + j)
                    gelu_from_psum(p, Bp, ub[:, j, :], B_ub[j])
                for j in range(4):
                    p, Bp = proj_fm(l, w2fm_b, base["zB"] + j)
                    silu_from_psum(p, Bp, zs[:, 4 + j, :], B_zs[4 + j])
                pp_mode[0] = "nopmx"
                for blk in range(4):
                    tsl = slice(blk * 128, (blk + 1) * 128)
                    p, Bp = get_pp()
                    for k in range(8):
                        S.op("pe", I("matmul", p[:], lhsT=hT[:, k, tsl], rhs=wtm[:, k, 512:1024],
                                                                          start=(k == 0), stop=(k == 7)), [B_hT, B_wtm], [Bp])
                    vt, B_vt = get_tmp()
                    vt2, B_vt2 = get_tmp()
                    gelu_from_psum(p, Bp, vt[:], B_vt)
                    S.op("dve", I("bn_stats", out=bnst[:], in_=vt[:]), [B_vt], [B_bn])
                    S.op("dve", I("bn_aggr", out=bnag[:], in_=bnst[:]), [B_bn], [B_bn])
                    S.op("dve", I("tensor_scalar", out=bnag[:, 1:2], in0=bnag[:, 1:2], scalar1=0.25, scalar2=EPS, op0=ALU.mult, op1=ALU.add),
                         [B_bn], [B_bn])
                    S.op("pool", I("tensor_tensor", out=bnag[:, 1:2], in0=bnag[:, 1:2], in1=mhalf1[:], op=ALU.pow), [B_bn, B_const], [B_bn])
                    S.op("dve", I("tensor_scalar", out=bnag[:, 1:2], in0=bnag[:, 1:2], scalar1=0.5, scalar2=None, op0=ALU.mult), [B_bn], [B_bn])
                    S.op("dve", I("tensor_scalar", out=vt2[:], in0=vt[:], scalar1=bnag[:, 0:1], scalar2=bnag[:, 1:2],
                                                          op0=ALU.subtract, op1=ALU.mult), [B_vt, B_bn], [B_vt2])
                    S.op("dve", I("tensor_tensor", out=vt2[:], in0=vt2[:], in1=lng[:], op=ALU.mult), [B_vt2, B_lw], [B_vt2])
                    S.op("dve", I("tensor_tensor", out=vn[:], in0=vt2[:], in1=lnb[:], op=ALU.add), [B_vt2, B_lw], [B_vn])
                    for g in range(4):
                        S.op("pe", I("matmul", pmx[:, g, :], lhsT=vn[:, g * 128:(g + 1) * 128], rhs=wsT[:, g, :], start=True, stop=True),
                             [B_vn, B_lw], [B_pmx])
                    t1, B1 = get_tmp()
                    t1v = t1[:].rearrange("p (g t) -> p g t", g=4)
                    S.op("dve", I("tensor_tensor", out=t1v, in0=pmx[:], in1=bsp[:], op=ALU.add), [B_pmx, B_lw], [B1])
                    S.op("pool", I("tensor_tensor", out=t1v, in0=t1v, in1=ub[:, :, tsl], op=ALU.mult), [B1] + B_ub, [B1])
                    S.op("dve", I("tensor_tensor", out=yb[:, :, tsl], in0=t1v, in1=zs[:, 4:8, tsl], op=ALU.mult),
                         [B1] + B_zs[4:8], [B_yb])
            if cfg.do_c:
                pp_mode[0] = "wide"
                top = (m == n_mt - 1)
                has_lo = (m > 0)
                t0 = m * 512 - 128
                if has_lo:
                    dma("sp", cs[:, 0, :], cosT_d[:, t0:t0 + 640], [], [B_cs], sl_cs)
                    dma("sp", cs[:, 1, :], sinT_d[:, t0:t0 + 640], [], [B_cs], sl_cs)
                    dma("sp", xh[:], xsrc(l).ap().rearrange("(c p) t -> p c t", p=128)[:, :, t0:t0 + 128], [DRAM_x[m - 1]], [B_xh], sl_xh)
                    for c in range(8):
                        S.op("act", I("activation", out=sqh[:, c, :], in_=xh[:, c, :], func=AF.Square), [B_xh], [B_sqh])
                    for c in range(8):
                        S.op("pe", I("matmul", pst[:, 0:128], lhsT=ones_b[:], rhs=sqh[:, c, :], start=(c == 0), stop=(c == 7)),
                             [B_sqh, B_const], [B_pst])
                    S.op("act", I("activation", out=rtmp[:, 0:128], in_=pst[:, 0:128], func=AF.Ln, scale=1.0 / D, bias=epsb[:]), [B_pst, B_const], [B_rtmp])
                    S.op("act", I("activation", out=rtmp[:, 128:256], in_=rtmp[:, 0:128], func=AF.Exp, scale=-0.5), [B_rtmp], [B_rtmp])
                    for c in range(8):
                        S.op("dve", I("scalar_tensor_tensor", out=hTh[:, c, :], in0=xh[:, c, :], scalar=ngain[:, l, c:c + 1],
                                                                           in1=rtmp[:, 128:256], op0=ALU.mult, op1=ALU.mult),
                             [B_xh, B_rtmp, B_const], [B_hTh])
                else:
                    dma("sp", cs[:, 0, 128:640], cosT_d[:, 0:512], [], [B_cs], sl_cs)
                    dma("sp", cs[:, 1, 128:640], sinT_d[:, 0:512], [], [B_cs], sl_cs)
                if not top:
                    S.op("pool", I("tensor_copy", out=kz[:, :, :, 640:768], in_=kz[:, :, :, 128:256]), [B_kr], [B_kr])
                    S.op("pool", I("tensor_copy", out=vaug[:, 5, :, :], in_=vaug[:, 1, :, :]), [B_vaug], [B_vaug])

                def qk_post(p, Bp, n, which, dst, Bdst, csl):
                    t1, B1 = get_tmp()
                    t2, B2 = get_tmp()
                    t3, B3 = get_tmp()
                    sqb, Bsqb = ya_scr, B_ya_scr
                    S.op("act", I("activation", out=sqb[:, 0:n], in_=p[:, 0:n], func=AF.Square), [Bp], [Bsqb])
                    S.op("dve", I("tensor_scalar", out=sqb[:, 512:512 + n], in0=p[:, 0:n], scalar1=qkg[:, l, which:which + 1], scalar2=None, op0=ALU.mult),
                         [Bp, B_const], [Bsqb])
                    S.op("pe", I("matmul", pst[:, 0:n], lhsT=bd_b[:], rhs=sqb[:, 0:n], start=True, stop=True), [Bsqb, B_const], [B_pst])
                    pr, Bpr = get_pp()
                    S.op("pe", I("matmul", pr[:, 0:n], lhsT=rot_b[:], rhs=sqb[:, 512:512 + n], start=True, stop=True), [Bsqb, B_const], [Bpr])
                    S.op("act", I("activation", out=t1[:, 0:n], in_=pst[:, 0:n], func=AF.Ln, scale=1.0 / 64, bias=epsb[:]), [B_pst, B_const], [B1])
                    S.op("act", I("activation", out=t1[:, 0:n], in_=t1[:, 0:n], func=AF.Exp, scale=-0.5), [B1], [B1])
                    S.op("pool", I("tensor_tensor", out=t2[:, 0:n], in0=sqb[:, 512:512 + n], in1=cs[:, 0, csl], op=ALU.mult), [Bsqb, B_cs], [B2])
                    S.op("dve", I("tensor_tensor", out=t3[:, 0:n], in0=pr[:, 0:n], in1=cs[:, 1, csl], op=ALU.mult), [Bpr, B_cs], [B3])
                    S.op("dve", I("tensor_tensor", out=t2[:, 0:n], in0=t2[:, 0:n], in1=t3[:, 0:n], op=ALU.add), [B2, B3], [B2])
                    if which == 0:
                        S.op("dve", I("tensor_tensor", out=dst, in0=t2[:, 0:n], in1=t1[:, 0:n], op=ALU.mult), [B2, B1], [Bdst])
                    else:
                        jh, c0 = dst
                        S.op("dve", I("tensor_tensor", out=kz[0:64, 0, jh, c0:c0 + n], in0=t2[0:64, 0:n], in1=t1[0:64, 0:n], op=ALU.mult), [B2, B1], [Bdst])
                        S.op("dve", I("tensor_tensor", out=kz[64:128, 1, jh, c0:c0 + n], in0=t2[64:128, 0:n], in1=t1[64:128, 0:n], op=ALU.mult), [B2, B1], [Bdst])

                for j in range(4):
                    p, Bp = proj_fm(l, w2fm_b, base["qC"] + j)
                    qk_post(p, Bp, 512, 0, qr[:, j, :], B_qr[j], slice(128, 640))
                for j in range(2):
                    w, Bw = load_w_chunk(l, w2fm_b, base["kC"] + j)
                    p, Bp = get_pp()
                    for k in range(8):
                        S.op("pe", I("matmul", p[:], lhsT=w[:, k, :], rhs=hT[:, k, :], start=(k == 0), stop=(k == 7)), [Bw, B_hT], [Bp])
                    qk_post(p, Bp, 512, 1, (j, 128), B_kr, slice(128, 640))
                    if has_lo:
                        p, Bp = get_pp()
                        for k in range(8):
                            S.op("pe", I("matmul", p[:, 0:128], lhsT=w[:, k, :], rhs=hTh[:, k, :], start=(k == 0), stop=(k == 7)), [Bw, B_hTh], [Bp])
                        qk_post(p, Bp, 128, 1, (j, 0), B_kr, slice(0, 128))
                for j in range(4):
                    p, Bp = proj_fm(l, w2fm_b, base["zC"] + j)
                    silu_from_psum(p, Bp, zs[:, 8 + j, :], B_zs[8 + j])
                for sl_i in ([0] if has_lo else []) + [1, 2, 3, 4]:
                    p, Bp = get_pp()
                    for k in range(8):
                        if sl_i == 0:
                            S.op("pe", I("matmul", p[:, 0:128], lhsT=hTh[:, k, :], rhs=wtm[:, k, 1024:1152], start=(k == 0), stop=(k == 7)),
                                 [B_hTh, B_wtm], [Bp])
                        else:
                            tsl = slice((sl_i - 1) * 128, sl_i * 128)
                            S.op("pe", I("matmul", p[:, 0:128], lhsT=hT[:, k, tsl], rhs=wtm[:, k, 1024:1152], start=(k == 0), stop=(k == 7)),
                                 [B_hT, B_wtm], [Bp])
                    pv = p[:, 0:128].rearrange("p (h d) -> p h d", h=2)
                    S.op("act", I("activation", out=vaug[:, sl_i, :, 0:64], in_=pv, func=AF.Copy), [Bp], [B_vaug])
                    S.op("dve", I("tensor_copy", out=vaug[:, sl_i, :, 128:192], in_=pv), [Bp], [B_vaug])
                kc_stage = int(os.environ.get("KC_STAGE", "9"))
                if top and kc_stage >= 2:
                    S.op("dve", I("tensor_copy", out=ccs[0:64, 512:768].rearrange("p (h t) -> p h t", h=2), in_=kz[0:64, 0, :, 512:640]), [B_kr], [B_ccs])
                    S.op("dve", I("tensor_copy", out=ccs[64:128, 512:768].rearrange("p (h t) -> p h t", h=2), in_=kz[64:128, 1, :, 512:640]), [B_kr], [B_ccs])
                    S.op("dve", I("tensor_copy", out=ccs[:, 768:896].rearrange("p (h d) -> p h d", h=2), in_=vaug[:, 4, :, 0:64]), [B_vaug], [B_ccs])
                    if not cfg.do_a:
                        S.op("dve", I("memset", ccs[:, 0:512], 0.0), [], [B_ccs])
                    else:
                        S.op("dve", I("tensor_copy", out=ccs[:, 0:512], in_=St[:].rearrange("p h v -> p (h v)")), [B_S], [B_ccs])
                    dma("pool", cc_in[l].ap(), ccs, [B_ccs], [DRAM_cc], sl_cc)
                    o = S.op("pool", I("collective_compute", "AllGather", ALU.bypass, replica_groups=[[0, 1], [2, 3], [4, 5], [6, 7]],
                                                                  ins=[cc_in[l].ap().opt()], outs=[cc_out[l].ap().opt()]), [DRAM_cc], [DRAM_cc])
                    o.signal = True
                    dma("pool", ccg, cc_out[l].ap().rearrange("(r p) n -> p r n", p=128), [DRAM_cc], [B_ccg], sl_cc)
                    S.op("dve", I("tensor_scalar", out=ccp, in0=ccg[:, 0, :], scalar1=selt[:, 0:1], scalar2=None, op0=ALU.mult), [B_ccg, B_const], [B_ccp])
                    S.op("dve", I("scalar_tensor_tensor", out=ccp, in0=ccg[:, 1, :], scalar=selt[:, 1:2], in1=ccp, op0=ALU.mult, op1=ALU.add),
                         [B_ccg, B_const, B_ccp], [B_ccp])
                    S.op("dve", I("tensor_copy", out=kz[0:64, 0, :, 640:768], in_=ccp[0:64, 512:768].rearrange("p (h t) -> p h t", h=2)), [B_ccp], [B_kr])
                    S.op("dve", I("tensor_copy", out=kz[64:128, 1, :, 640:768], in_=ccp[64:128, 512:768].rearrange("p (h t) -> p h t", h=2)), [B_ccp], [B_kr])
                    S.op("dve", I("tensor_copy", out=vaug[:, 5, :, 0:64], in_=ccp[:, 768:896].rearrange("p (h d) -> p h d", h=2)), [B_ccp], [B_vaug])
                    S.op("dve", I("tensor_copy", out=vaug[:, 5, :, 128:192], in_=ccp[:, 768:896].rearrange("p (h d) -> p h d", h=2)), [B_ccp], [B_vaug])
                    if cfg.do_a:
                        S.op("dve", I("tensor_copy", out=St[:].rearrange("p h v -> p (h v)"), in_=ccp[:, 0:512]), [B_ccp], [B_S])
                for sb_i in ((4, 3, 2, 1) if kc_stage >= 3 else ()):
                    jglob = m * 4 + sb_i - 1
                    tsl = slice((sb_i - 1) * 128, sb_i * 128)
                    kbs = []
                    if jglob > 0:
                        kbs.append((sb_i - 1, 0))
                    kbs.append((sb_i, None))
                    kbs.append((sb_i + 1, 2 if (top and sb_i == 4) else 1))
                    for h in range(2):
                        pssv = [pbr[i][:].rearrange("p (e n) -> p e n", e=2) for i in range(3)]
                        for ki, (slk, mk) in enumerate(kbs):
                            for e_ in range(2):
                                rows = slice(e_ * 64, (e_ + 1) * 64)
                                S.op("pe", I("matmul",
                                    pssv[ki][:, e_, :].rearrange("p (c t) -> p c t", c=2), lhsT=kz[:, e_, h, slk * 128:(slk + 1) * 128],
                                    rhs=qr[:, 2 * h:2 * h + 2, tsl], start=True, stop=True),
                                    [B_kr] + B_qr, [B_pbr[ki]])
                            S.op("act", I("activation", out=pt[:, ki, :, :], in_=pssv[ki], func=AF.Exp, scale=0.125), [B_pbr[ki]], [B_ptk[ki]])
                            if mk is not None and kc_stage >= 4:
                                ptv = pt[:, ki, :, :].rearrange("p e (c t) -> p (e c) t", c=2)
                                S.op("dve" if mk == 0 else "pool", I("tensor_tensor", out=ptv, in0=ptv, in1=amask[:, mk:mk + 1, :].to_broadcast([128, 4, 128]), op=ALU.mult),
                                     [B_ptk[ki], B_const], [B_ptk[ki]])
                        pso = pmx[:].rearrange("p a b -> p (a b)").rearrange("p (e n) -> p e n", e=2)
                        for e_ in (range(2) if kc_stage >= 5 else ()):
                            for ki, (slk, mk) in enumerate(kbs):
                                S.op("pe", I("matmul", pso[:, e_, :], lhsT=vaug[:, slk, h, e_ * 64:e_ * 64 + 128], rhs=pt[:, ki, e_, :],
                                                                                       start=(ki == 0), stop=(ki == len(kbs) - 1)), [B_vaug, B_ptk[ki]], [B_pmx])
                        for e_ in (range(2) if kc_stage >= 6 else ()):
                            nr = slice(0, 64) if e_ == 0 else slice(64, 128)
                            dr = slice(64, 128) if e_ == 0 else slice(0, 64)
                            for c in range(2):
                                head = 2 * (2 * h + c) + e_
                                S.op("dve", I("tensor_scalar",
                                    out=dtmp[nr, c * 128:(c + 1) * 128], in0=pso[dr, e_, c * 128:(c + 1) * 128], scalar1=esink[dr, l, head:head + 1], scalar2=None, op0=ALU.add),
                                    [B_pmx, B_const], [B_dtmp])
                        S.op("act", I("activation", out=dtmp[:], in_=dtmp[:], func=AF.Ln), [B_dtmp], [B_dtmp])
                        S.op("act", I("activation", out=dtmp[:], in_=dtmp[:], func=AF.Exp, scale=-1.0), [B_dtmp], [B_dtmp])
                        for e_ in (range(2) if kc_stage >= 6 else ()):
                            nr = slice(0, 64) if e_ == 0 else slice(64, 128)
                            S.op("dve", I("tensor_tensor", out=dtmp[nr, :], in0=pso[nr, e_, :], in1=dtmp[nr, :], op=ALU.mult), [B_pmx, B_dtmp], [B_dtmp])
                        S.op("pool", I("tensor_tensor", out=yc[:, 2 * h:2 * h + 2, tsl], in0=dtmp[:].rearrange("p (c t) -> p c t", c=2),
                                       in1=zs[:, 8 + 2 * h:8 + 2 * h + 2, tsl], op=ALU.mult),
                             [B_dtmp] + B_zs[8:12], [B_yc])
            if cfg.do_a:
                pp_mode[0] = "wide"
                hgrn_q_and_v(l, w2fm_b, base["qA"])
                hgrn_gates(l, 1, w2fm_b, base["a2"])
                for j in range(4):
                    p, Bp = proj_fm(l, w2fm_b, base["zA"] + j)
                    silu_from_psum(p, Bp, zs[:, j, :], B_zs[j])
                for bi in (3, 2, 1, 0):
                    hgrn_vtok(bi)
                    hgrn_block(l, 1, bi, m)
            pp_mode[0] = "narrow"
            scal = {0: 0.25, 1: 0.125, 2: 0.25}
            for dc in range(8):
                dsl = slice(dc * 128, (dc + 1) * 128)
                acc, Bacc = get_tmp()
                first = True
                wi = _wbrc_i[0] % 2
                _wbrc_i[0] += 1
                dma("sp", wbrc[wi][:], wbr_b[l, dc], [B_wcast[l]], [B_wbrc[wi]], sl_wbrc[wi])
                for bi, (on, ysrc, By) in enumerate(((cfg.do_a, ya, B_ya), (cfg.do_b, yb, B_yb), (cfg.do_c, yc, B_yc))):
                    if not on:
                        continue
                    for k in range(4):
                        S.op("pe", I("matmul", pbr[bi][:], lhsT=wbrc[wi][:, bi, k, :], rhs=ysrc[:, k, :],
                                                                                     start=(k == 0), stop=(k == 3)), [B_wbrc[wi], By], [B_pbr[bi]])
                    pg, Bpg = proj_fm(l, w2fm_b, base[("gA", "gB", "gC")[bi]] + dc)
                    gtile, Bg = get_tmp()
                    S.op("act", tanh_gate(gtile[:], pg[:], 0.5), [Bpg], [Bg])
                    g = gtile[:]
                    if first:
                        S.op("dve", I("scalar_tensor_tensor", out=acc[:], in0=g, scalar=1.0, in1=pbr[bi][:], op0=ALU.add, op1=ALU.mult),
                             [Bg, B_pbr[bi]], [Bacc])
                        S.op("pool", I("tensor_scalar", out=acc[:], in0=acc[:], scalar1=scal[bi], scalar2=None, op0=ALU.mult), [Bacc], [Bacc])
                        first = False
                    else:
                        t2, B2 = get_tmp()
                        S.op("dve", I("scalar_tensor_tensor", out=t2[:], in0=g, scalar=1.0, in1=pbr[bi][:], op0=ALU.add, op1=ALU.mult),
                             [Bg, B_pbr[bi]], [B2])
                        S.op("dve", I("scalar_tensor_tensor", out=acc[:], in0=t2[:], scalar=scal[bi], in1=acc[:], op0=ALU.mult, op1=ALU.add),
                             [B2, Bacc], [Bacc])
                S.op("act", I("activation", out=mg[:, dc, :], in_=acc[:], func=AF.Copy), [Bacc], [B_mg[dc]])
            for ec in range(8):
                esl = slice(ec * 128, (ec + 1) * 128)
                w, Bw = load_w_chunk(l, wo_b, ec)
                p, Bp = get_pp()
                for k in range(8):
                    S.op("pe", I("matmul", p[:], lhsT=w[:, k, :], rhs=mg[:, k, :], start=(k == 0), stop=(k == 7)),
                         [Bw, B_mg[k]], [Bp])
                S.op("dve", I("tensor_tensor", out=xo[:, ec, :], in0=p[:], in1=xt[:, ec, :], op=ALU.add), [Bp, B_xt], [B_xo])
            dma("sp", xview(out_d, m), xo[:], [B_xo], [DRAM_x[m]], sl_xo)

        import os
        stop = os.environ.get("KSTOP", "")
        for l in range(depth):
            S.epoch = l
            if stop == "cast":
                for m in range(n_mt):
                    dma("sp", xt[:], xview(xsrc(l), m), [DRAM_x[m]] + B_wcast, [B_xt], sl_xt)
                    dma("sp", xview(out_d, m), xt[:], [B_xt], [DRAM_x[m]], sl_xo)
                continue
            layer_weights(l)
            if stop == "lw":
                for m in range(n_mt):
                    dma("sp", xt[:], xview(xsrc(l), m), [DRAM_x[m]] + B_wcast + [B_wtm, B_lw], [B_xt], sl_xt)
                    dma("sp", xview(out_d, m), xt[:], [B_xt], [DRAM_x[m]], sl_xo)
                continue
            if cfg.do_a:
                S.op("dve", I("memset", St[:], 0.0), [], [B_S])
                for m in range(n_mt):
                    sweep1_mt(l, m)
            for m in reversed(range(n_mt)):
                sweep2_mt(l, m)
        S.epoch = depth
        S.op("sp", I("nop"), reads=[DRAM_x[m] for m in range(n_mt)], writes=[])

        with nc.Block() as block:
            S.finalize(engsems, block)
        build_nc.last_stats = S.stats
    return nc


def run(inputs, cfg, trace=False):
    nc = build_nc(cfg)
    in_maps = [prep_core_inputs(inputs, c, cfg) for c in range(NCORES)]
    res = run_bass_kernel_spmd(nc, in_maps, core_ids=list(range(NCORES)), trace=trace)
    T = cfg.T
    L = 2 * T
    B = NCORES // 2
    out = np.empty((B, L, D), np.float32)
    for c in range(NCORES):
        o = np.asarray(res.results[c]["out"]).T
        if c % 2 == 0:
            out[c // 2, :T] = o
        else:
            out[c // 2, T:] = o[::-1]
    return out, res


def kernel(**inputs):
    cfg = Cfg()
    out, _ = run(inputs, cfg)
    return out
```

```python
import numpy as np
import concourse.bass as bass
import concourse.mybir as mybir
from concourse.bass_utils import run_bass_kernel_spmd

F32 = mybir.dt.float32
BF16 = mybir.dt.bfloat16
ALU = mybir.AluOpType
AF = mybir.ActivationFunctionType

D = 1024
DEPTH = 4
EPS = 1e-6
NCORES = 8
SAME_ENGINE_SYNC = True


def I(name, *args, **kw):
    return lambda e: getattr(e, name)(*args, **kw)


class Buf:
    __slots__ = ("name", "last_w", "readers")

    def __init__(self, name):
        self.name = name
        self.last_w = None
        self.readers = []


class Slot:
    def __init__(self, sem, name):
        self.sem = sem
        self.count = 0
        self.token = Buf("slot_" + name)


class Op:
    __slots__ = ("eng", "fn", "deps", "signal", "val", "sem", "slot", "idx", "epoch", "raw")


class Sched:
    ENGS = ("pe", "act", "dve", "pool", "sp")

    def __init__(self, nc, n_epochs):
        self.nc = nc
        self.ops = {e: [] for e in self.ENGS}
        self.epoch = 0
        self.n_epochs = n_epochs
        self.engsem = {}

    def op(self, eng, fn, reads=(), writes=(), slot=None):
        o = Op()
        o.eng = eng
        o.fn = fn
        o.signal = False
        o.val = None
        o.sem = None
        o.slot = slot
        o.epoch = self.epoch
        deps = []
        raw = set()
        writes = list(writes)
        if slot is not None:
            writes.append(slot.token)
        for b in reads:
            if b.last_w is not None:
                deps.append(b.last_w)
                raw.add(id(b.last_w))
        for b in writes:
            if b.last_w is not None:
                deps.append(b.last_w)
            deps.extend(b.readers)
        seen = set()
        dd = []
        for d in deps:
            if id(d) not in seen and d is not o:
                seen.add(id(d))
                dd.append(d)
        o.deps = dd
        o.raw = raw
        for b in reads:
            b.readers.append(o)
        for b in writes:
            b.last_w = o
            b.readers = []
        if slot is not None:
            slot.count += 1
            o.sem = slot.sem
            o.val = 16 * slot.count
        o.idx = len(self.ops[eng])
        self.ops[eng].append(o)
        return o

    def _needs_wait(self, cons, prod):
        if prod.slot is not None:
            return True
        if prod.eng == cons.eng and cons.slot is None:
            if prod.eng == "pe":
                return False
            return SAME_ENGINE_SYNC and (id(prod) in cons.raw)
        return True

    def finalize(self, sems, block):
        for e in self.ENGS:
            for o in self.ops[e]:
                for d in o.deps:
                    if self._needs_wait(o, d):
                        d.signal = True
        for e in self.ENGS:
            cnt = {}
            for o in self.ops[e]:
                if o.slot is not None:
                    pass
                elif o.signal:
                    cnt[o.epoch] = cnt.get(o.epoch, 0) + 1
                    o.sem = sems[(e, o.epoch)]
                    o.val = cnt[o.epoch]
        self.stats = {e: len(self.ops[e]) for e in self.ENGS}

        def emit(e, eng):
            waited = {}
            nwaits = 0
            for o in self.ops[e]:
                for d in o.deps:
                    if not self._needs_wait(o, d):
                        continue
                    key = id(d.sem)
                    if waited.get(key, 0) >= d.val:
                        continue
                    eng.wait_ge(d.sem, d.val)
                    nwaits += 1
                    waited[key] = d.val
                ins = o.fn(eng)
                if o.slot is not None:
                    ins.then_inc(o.sem, 16)
                elif o.signal:
                    ins.then_inc(o.sem, 1)
            self.stats[e + "_waits"] = nwaits

        @block.tensor
        def _(eng):
            emit("pe", eng)

        @block.scalar
        def _(eng):
            emit("act", eng)

        @block.vector
        def _(eng):
            emit("dve", eng)

        @block.gpsimd
        def _(eng):
            emit("pool", eng)

        @block.sync
        def _(eng):
            emit("sp", eng)


OFF = dict(qA=0, fAf=512, fAb=1024, iA=1536, zA=2048, uB=2560, vB=3072, zB=3584, qC=4096, kC=4608,
           vC=4736, zC=4864, gA=5376, gB=6400, gC=7424)

FM2 = (["a2"] * 4 + ["qA"] * 4 + ["zA"] * 4 + ["uB"] * 4 + ["zB"] * 4 + ["qC"] * 4 + ["kC"] * 2 + ["zC"] * 4
       + ["gA"] * 8 + ["gB"] * 8 + ["gC"] * 8)
FM1 = ["a1"] * 4 + ["qA"] * 4


def _fm_cols(kind_list, odd):
    out = []
    cnt = {}
    for kind in kind_list:
        j = cnt.get(kind, 0)
        cnt[kind] = j + 1
        if kind == "a1":
            base = OFF["fAb"] if odd else OFF["fAf"]
            cols = np.arange(base + j * 128, base + (j + 1) * 128)
        elif kind == "a2":
            base = OFF["fAf"] if odd else OFF["fAb"]
            cols = np.arange(base + j * 128, base + (j + 1) * 128)
        elif kind == "kC":
            c = np.arange(OFF["kC"] + j * 64, OFF["kC"] + (j + 1) * 64)
            cols = np.concatenate([c, c])
        else:
            base = OFF[kind]
            cols = np.arange(base + j * 128, base + (j + 1) * 128)
        out.append(cols)
    return out


def _tm_cols(sweep):
    if sweep == 1:
        return np.arange(OFF["iA"], OFF["iA"] + 512)
    return np.concatenate([np.arange(OFF["iA"], OFF["iA"] + 512), np.arange(OFF["vB"], OFF["vB"] + 512),
                           np.arange(OFF["vC"], OFF["vC"] + 128)])


NF1 = len(FM1)
NF2 = len(FM2)
TM1 = 512
TM2 = 1152


class Cfg:
    def __init__(self, n_mt=8, depth=DEPTH, do_a=True, do_b=True, do_c=True):
        self.n_mt = n_mt
        self.T = n_mt * 512
        self.NB = n_mt * 4
        self.depth = depth
        self.do_a = do_a
        self.do_b = do_b
        self.do_c = do_c


def prep_core_inputs(inp, core, cfg):
    T = cfg.T
    L = 2 * T
    b = core // 2
    odd = core % 2
    depth = cfg.depth
    f32 = np.float32
    pos = (np.arange(T) if not odd else (L - 1 - np.arange(T))).astype(np.int64)
    m = {}
    x = np.asarray(inp["x"])[b]
    m["xT"] = np.ascontiguousarray(x[pos, :].T).astype(f32)
    w_in = np.asarray(inp["w_in"])
    w1fm = np.empty((depth, NF1, 128, 8, 128), f32)
    w2fm = np.empty((depth, NF2, 128, 8, 128), f32)
    w1tm = np.empty((depth, 128, 8, TM1), f32)
    w2tm = np.empty((depth, 128, 8, TM2), f32)
    c1 = _fm_cols(FM1, odd)
    c2 = _fm_cols(FM2, odd)
    for l in range(depth):
        wl = w_in[l].reshape(8, 128, -1)
        for j, cols in enumerate(c1):
            w1fm[l, j] = wl[:, :, cols].transpose(1, 0, 2)
        for j, cols in enumerate(c2):
            w2fm[l, j] = wl[:, :, cols].transpose(1, 0, 2)
        w1tm[l] = wl[:, :, _tm_cols(1)].transpose(1, 0, 2)
        w2tm[l] = wl[:, :, _tm_cols(2)].transpose(1, 0, 2)
    m["w1fm"] = w1fm
    m["w2fm"] = w2fm
    m["w1tm"] = w1tm
    m["w2tm"] = w2tm
    wbr = np.empty((depth, 8, 128, 3, 4, 128), f32)
    for bi, key in enumerate(("w_branch_a", "w_branch_b", "w_branch_c")):
        w = np.asarray(inp[key])[:depth].reshape(depth, 4, 128, 8, 128)
        wbr[:, :, :, bi] = w.transpose(0, 3, 2, 1, 4)
    m["wbr"] = wbr
    w = np.asarray(inp["w_out"])[:depth].reshape(depth, 8, 128, 8, 128)
    m["wo"] = np.ascontiguousarray(w.transpose(0, 3, 2, 1, 4)).astype(f32)
    ng = np.asarray(inp["norm_gain"])[:depth]
    m["ngain"] = np.ascontiguousarray(ng.reshape(depth, 8, 128).transpose(2, 0, 1)).astype(f32)
    lb = np.asarray(inp["lb_logits"]).reshape(DEPTH, 2, 4, 128)
    if odd:
        lb = lb[:, ::-1]
    m["lbl"] = np.ascontiguousarray(lb.transpose(3, 0, 1, 2)).astype(f32)
    hg = np.asarray(inp["hg_norm_gain"])[:depth]
    m["hgain"] = np.ascontiguousarray(hg.transpose(2, 0, 1)).astype(f32)
    m["lng"] = np.ascontiguousarray(np.broadcast_to(np.asarray(inp["sg_ln_gain"])[:depth][None], (128, depth, 512))).astype(f32)
    m["lnb"] = np.ascontiguousarray(np.broadcast_to(np.asarray(inp["sg_ln_bias"])[:depth][None], (128, depth, 512))).astype(f32)
    ws = np.asarray(inp["w_spatial"])[:depth]
    bs = np.asarray(inp["b_spatial"])[:depth]
    if odd:
        ws = ws[:, :, ::-1, ::-1]
        bs = bs[:, :, ::-1]
    m["wsT"] = np.ascontiguousarray(ws.transpose(3, 0, 1, 2)).astype(f32)
    m["bsp"] = np.ascontiguousarray(np.broadcast_to(bs[None], (128, depth, 4, 128))).astype(f32)
    qg = np.asarray(inp["q_norm_gain"])[:depth]
    kg = np.asarray(inp["k_norm_gain"])[:depth]
    m["qkg"] = np.ascontiguousarray(np.stack([np.concatenate([qg, qg], 1), np.concatenate([kg, kg], 1)], 1).transpose(2, 0, 1)).astype(f32)
    m["sink"] = np.ascontiguousarray(np.broadcast_to(np.asarray(inp["sink_logits"])[:depth][None], (128, depth, 8))).astype(f32)
    half = 32
    inv_freq = (10000.0 ** (-np.arange(half, dtype=np.float32) / half)).astype(np.float32)
    ang = pos.astype(np.float32)[None, :] * inv_freq[:, None]
    cos = np.cos(ang).astype(f32)
    sin = np.sin(ang).astype(f32)
    m["cosT"] = np.ascontiguousarray(np.concatenate([cos, cos, cos, cos], 0))
    m["sinT"] = np.ascontiguousarray(np.concatenate([sin, sin, sin, sin], 0))
    ident = np.eye(128, dtype=f32)
    m["c_ident"] = ident
    bd = np.zeros((128, 128), f32)
    bd[:64, :64] = 1
    bd[64:, 64:] = 1
    m["c_bd"] = bd
    rot = np.zeros((128, 128), f32)
    for hb in (0, 64):
        for d in range(32):
            rot[hb + d + 32, hb + d] = -1.0
            rot[hb + d, hb + d + 32] = 1.0
    m["c_rot"] = rot
    j = np.arange(128)[:, None]
    i = np.arange(128)[None, :]
    masks = np.stack([(j >= i), (j <= i), (j + i >= 127)], 1).astype(f32)
    m["c_amask"] = np.ascontiguousarray(masks)
    s = np.arange(64)[:, None]
    t = np.arange(64)[None, :]
    h1 = (s <= t).astype(f32)
    h2 = (s >= t).astype(f32)
    m["c_hmask"] = np.ascontiguousarray(np.stack([np.concatenate([h1, h1], 0), np.concatenate([h2, h2], 0)], 1))
    cm = np.ones((128, 512), f32)
    cm[:, ::64] = 0.0
    m["c_cmask"] = cm
    ss_ = np.arange(128)[:, None]
    tt_ = np.arange(128)[None, :]
    same = (ss_ // 64) == (tt_ // 64)
    m["c_hmask2"] = np.ascontiguousarray(np.stack([(same & (ss_ <= tt_)), (same & (ss_ >= tt_))], 1).astype(f32))
    sel = np.zeros((128, 2), f32)
    sel[:, 1 - odd] = 1.0
    m["sel"] = sel
    return m


def build_nc(cfg):
    nc = bass.Bass("TRN2", target_bir_lowering=False)
    T = cfg.T
    depth = cfg.depth
    n_mt = cfg.n_mt

    def din(name, shape, dt=F32):
        return nc.dram_tensor(name, list(shape), dt, kind="ExternalInput")

    xT_d = din("xT", [D, T])
    w1fm_d = din("w1fm", [depth, NF1, 128, 8, 128])
    w2fm_d = din("w2fm", [depth, NF2, 128, 8, 128])
    w1tm_d = din("w1tm", [depth, 128, 8, TM1])
    w2tm_d = din("w2tm", [depth, 128, 8, TM2])
    wbr_d = din("wbr", [depth, 8, 128, 3, 4, 128])
    wo_d = din("wo", [depth, 8, 128, 8, 128])
    ngain_d = din("ngain", [128, depth, 8])
    lbl_d = din("lbl", [128, DEPTH, 2, 4])
    hgain_d = din("hgain", [128, depth, 4])
    lng_d = din("lng", [128, depth, 512])
    lnb_d = din("lnb", [128, depth, 512])
    wsT_d = din("wsT", [128, depth, 4, 128])
    bsp_d = din("bsp", [128, depth, 4, 128])
    qkg_d = din("qkg", [128, depth, 2])
    sink_d = din("sink", [128, depth, 8])
    cosT_d = din("cosT", [128, T])
    sinT_d = din("sinT", [128, T])
    c_ident_d = din("c_ident", [128, 128])
    c_bd_d = din("c_bd", [128, 128])
    c_rot_d = din("c_rot", [128, 128])
    c_amask_d = din("c_amask", [128, 3, 128])
    c_hmask_d = din("c_hmask", [128, 2, 64])
    sel_d = din("sel", [128, 2])
    c_cmask_d = din("c_cmask", [128, 512])
    c_hmask2_d = din("c_hmask2", [128, 2, 128])
    out_d = nc.dram_tensor("out", [D, T], F32, kind="ExternalOutput")

    w1fm_b = nc.dram_tensor("w1fm_b", [depth, NF1, 128, 8, 128], BF16)
    w2fm_b = nc.dram_tensor("w2fm_b", [depth, NF2, 128, 8, 128], BF16)
    w1tm_b = nc.dram_tensor("w1tm_b", [depth, 128, 8, TM1], BF16)
    w2tm_b = nc.dram_tensor("w2tm_b", [depth, 128, 8, TM2], BF16)
    wbr_b = nc.dram_tensor("wbr_b", [depth, 8, 128, 3, 4, 128], BF16)
    wo_b = nc.dram_tensor("wo_b", [depth, 8, 128, 8, 128], BF16)
    o1_d = nc.dram_tensor("o1_spill", [128, cfg.NB, 4, 128], F32)
    CCW = 512 + 256 + 128
    cc_in = [nc.dram_tensor(f"cc_in{l}", [128, CCW], F32) for l in range(depth)]
    cc_out = [nc.dram_tensor(f"cc_out{l}", [256, CCW], F32) for l in range(depth)]

    from contextlib import ExitStack
    es = ExitStack()
    with es:
        S = Sched(nc, depth + 1)

        def sb(name, shape, dt=F32):
            return es.enter_context(nc.sbuf_tensor(name, list(shape), dt))

        def ps(name, shape, dt=F32):
            return es.enter_context(nc.psum_tensor(name, list(shape), dt))

        def sem(name):
            return es.enter_context(nc.semaphore(name))

        engsems = {(e, ep): sem(f"s_{e}_{ep}") for e in ("pe", "act", "dve", "pool") for ep in range(depth + 1)}
        _slot_n = [0]

        def slot(name):
            _slot_n[0] += 1
            return Slot(sem(f"d_{name}_{_slot_n[0]}"), name)

        ident_b = sb("ident_b", [128, 128], BF16)
        ones_b = sb("ones_b", [128, 128], BF16)
        bd_b = sb("bd_b", [128, 128], BF16)
        rot_b = sb("rot_b", [128, 128], BF16)
        amask = sb("amask", [128, 3, 128], BF16)
        hmask = sb("hmask", [128, 2, 64], BF16)
        selt = sb("selt", [128, 2])
        ngain = sb("ngain_s", [128, depth, 8])
        lbl = sb("lbl_s", [128, DEPTH, 2, 4])
        hgain = sb("hgain_s", [128, depth, 4])
        qkg = sb("qkg_s", [128, depth, 2])
        sinkt = sb("sink_s", [128, depth, 8])
        esink = sb("esink", [128, depth, 8])
        lbc1 = sb("lbc1", [128, DEPTH, 2, 4])
        lbc0 = sb("lbc0", [128, DEPTH, 2, 4])
        B_const = Buf("const")
        sl_c = slot("const")

        def dma(eng, out, in_, reads, writes, sl):
            return S.op(eng, I("dma_start", out=out, in_=in_), reads=reads, writes=writes, slot=sl)

        for dst, src in ((ident_b, c_ident_d), (bd_b, c_bd_d), (rot_b, c_rot_d), (amask, c_amask_d), (hmask, c_hmask_d)):
            dma("pool", dst[:], src.ap(), [], [B_const], sl_c)
        for dst, src in ((selt, sel_d), (ngain, ngain_d), (lbl, lbl_d), (hgain, hgain_d), (qkg, qkg_d), (sinkt, sink_d)):
            dma("sp", dst[:], src.ap(), [], [B_const], sl_c)
        S.op("dve", I("memset", ones_b[:], 1.0), [], [B_const])
        S.op("act", I("activation", out=esink[:], in_=sinkt[:], func=AF.Exp), [B_const], [B_const])
        lbe = sb("lbe", [128, DEPTH, 8])
        lbs = sb("lbs", [128, 8])
        lbv = lbl[:].rearrange("p l a b -> p l (a b)")
        S.op("act", I("activation", out=lbe[:], in_=lbv, func=AF.Exp), [B_const], [B_const])
        S.op("dve", I("tensor_tensor", out=lbs[:], in0=lbe[:, 0, :], in1=lbe[:, 1, :], op=ALU.add), [B_const], [B_const])
        S.op("dve", I("tensor_tensor", out=lbs[:], in0=lbs[:], in1=lbe[:, 2, :], op=ALU.add), [B_const], [B_const])
        S.op("dve", I("tensor_tensor", out=lbs[:], in0=lbs[:], in1=lbe[:, 3, :], op=ALU.add), [B_const], [B_const])
        S.op("dve", I("reciprocal", out=lbs[:], in_=lbs[:]), [B_const], [B_const])
        c1v = lbc1[:].rearrange("p l a b -> p l (a b)")
        c0v = lbc0[:].rearrange("p l a b -> p l (a b)")
        S.op("dve", I("memset", c0v[:, 0, :], 0.0), [B_const], [B_const])
        for l in range(1, DEPTH):
            S.op("dve", I("tensor_tensor", out=c1v[:, l, :], in0=lbe[:, l, :], in1=lbs[:], op=ALU.mult), [B_const], [B_const])
            S.op("dve", I("tensor_tensor", out=c0v[:, l, :], in0=c0v[:, l - 1, :], in1=c1v[:, l, :], op=ALU.add), [B_const], [B_const])
        S.op("dve", I("tensor_scalar", out=lbc1[:], in0=lbc0[:], scalar1=-0.5, scalar2=0.5, op0=ALU.mult, op1=ALU.add), [B_const], [B_const])
        S.op("dve", I("tensor_scalar", out=lbc0[:], in0=lbc0[:], scalar1=0.5, scalar2=0.5, op0=ALU.mult, op1=ALU.add), [B_const], [B_const])

        B_wcast = [Buf(f"wcast{l}") for l in range(depth)]
        sl_wc = [slot(f"wc{i}") for i in range(4)]
        _wc_i = [0]

        def wcast(l, dst, src):
            sl = sl_wc[_wc_i[0] % 4]
            _wc_i[0] += 1
            S.op("pool", I("dma_start", out=dst, in_=src), reads=[], writes=[B_wcast[l]], slot=sl)

        import os
        for l in range(depth if not os.environ.get("KSKIPCAST") else 0):
            def v4(t, j0, j1):
                return t[l, j0:j1].rearrange("a p k n -> (a p) (k n)")

            def v3(t):
                return t[l].rearrange("p k n -> p (k n)")

            for j in range(0, NF1, 4):
                wcast(l, v4(w1fm_b, j, j + 4), v4(w1fm_d, j, j + 4))
            wcast(l, v3(w1tm_b), v3(w1tm_d))
            for j in range(0, NF2, 6):
                wcast(l, v4(w2fm_b, j, j + 6), v4(w2fm_d, j, j + 6))
            wcast(l, v3(w2tm_b), v3(w2tm_d))
            wcast(l, wbr_b[l].rearrange("a p b k n -> (a p) (b k n)"), wbr_d[l].rearrange("a p b k n -> (a p) (b k n)"))
            wcast(l, v4(wo_b, 0, 8), v4(wo_d, 0, 8))

        NWB = 4
        wbuf = [sb(f"wbuf{i}", [128, 8, 128], BF16) for i in range(NWB)]
        B_wbuf = [Buf(f"wbuf{i}") for i in range(NWB)]
        sl_wbuf = [slot(f"wb{i}") for i in range(NWB)]
        _wb_i = [0]
        wtm = sb("wtm", [128, 8, TM2], BF16)
        B_wtm = Buf("wtm")
        sl_wtm = slot("wtm")
        wbrc = [sb(f"wbrc{i}", [128, 3, 4, 128], BF16) for i in range(2)]
        B_wbrc = [Buf(f"wbrc{i}") for i in range(2)]
        sl_wbrc = [slot(f"wbrc{i}") for i in range(2)]
        _wbrc_i = [0]
        sl_wl = slot("wl")
        lng = sb("lng_s", [128, 512])
        lnb = sb("lnb_s", [128, 512])
        wsT = sb("wsT_s", [128, 4, 128], BF16)
        bsp = sb("bsp_s", [128, 4, 128])
        B_lw = Buf("layerw")

        xt = sb("xt", [128, 8, 512])
        B_xt = Buf("xt")
        sl_xt = slot("xt")
        hT = sb("hT", [128, 8, 512], BF16)
        B_hT = Buf("hT")
        rstd = sb("rstd", [128, 512])
        B_rstd = Buf("rstd")
        rtmp = sb("rtmp", [128, 512])
        B_rtmp = Buf("rtmp")
        epsb = sb("epsb", [128, 1])
        S.op("dve", I("memset", epsb[:], EPS), [], [B_const])
        mhalf = sb("mhalf", [128, 512])
        S.op("pool", I("memset", mhalf[:], -0.5), [], [B_const])

        zs = sb("zs", [128, 12, 512], BF16)
        B_zs = [Buf(f"zs{i}") for i in range(12)]
        ub = sb("ub", [128, 4, 512], BF16)
        B_ub = [Buf(f"ub{i}") for i in range(4)]
        yb = sb("yb", [128, 4, 512], BF16)
        B_yb = Buf("yb")
        ya = sb("ya", [128, 4, 512], BF16)
        B_ya = Buf("ya")
        yc = sb("yc", [128, 4, 512], BF16)
        B_yc = Buf("yc")
        mg = sb("mg", [128, 8, 512], BF16)
        B_mg = [Buf(f"mg{i}") for i in range(8)]
        sq = mg
        xo = sb("xo", [128, 8, 512])
        B_xo = Buf("xo")
        sl_xo = slot("xo")
        ya_scr = sb("qkscr", [128, 1024], BF16)
        B_ya_scr = Buf("qkscr")
        NTMP = 8
        tmpf = [sb(f"tmpf{i}", [128, 512]) for i in range(NTMP)]
        B_tmpf = [Buf(f"tmpf{i}") for i in range(NTMP)]
        _tf_i = [0]

        def get_tmp():
            i = _tf_i[0] % NTMP
            _tf_i[0] += 1
            return tmpf[i], B_tmpf[i]

        vn = sb("vn", [128, 512], BF16)
        B_vn = Buf("vn")
        bnst = sb("bnst", [128, 6])
        bnag = sb("bnag", [128, 2])
        B_bn = Buf("bn")
        mhalf1 = sb("mhalf1", [128, 1])
        S.op("pool", I("memset", mhalf1[:], -0.5), [], [B_const])

        NPP = 2
        pp = [ps(f"pp{i}", [128, 512]) for i in range(NPP)]
        B_pp = [Buf(f"pp{i}") for i in range(NPP)]
        _pp_i = [0]
        pp_mode = ["narrow"]

        def get_pp():
            if pp_mode[0] == "wide":
                banks = [(pp[0], B_pp[0]), (pp[1], B_pp[1]), (pbr[0], B_pbr[0]), (pbr[1], B_pbr[1]), (pbr[2], B_pbr[2]), (pmx_flat, B_pmx)]
            elif pp_mode[0] == "nopmx":
                banks = [(pp[0], B_pp[0]), (pp[1], B_pp[1]), (pbr[0], B_pbr[0]), (pbr[1], B_pbr[1]), (pbr[2], B_pbr[2])]
            else:
                banks = [(pp[0], B_pp[0]), (pp[1], B_pp[1])]
            i = _pp_i[0] % len(banks)
            _pp_i[0] += 1
            return banks[i]

        pst = ps("pst", [128, 512])
        B_pst = Buf("pst")
        pmx = ps("pmx", [128, 4, 128])
        B_pmx = Buf("pmx")
        pbr = [ps(f"pbr{i}", [128, 512]) for i in range(3)]
        B_pbr = [Buf(f"pbr{i}") for i in range(3)]

        class _Flat:
            def __getitem__(self, idx):
                return pmx[:].rearrange("p a b -> p (a b)")[idx]
        pmx_flat = _Flat()

        DRAM_x = [Buf(f"dram_x{m}") for m in range(n_mt)]
        qr = sb("qr", [128, 4, 512], BF16)
        B_qr = [Buf(f"qr{i}") for i in range(4)]
        kz = sb("kz", [128, 2, 2, 768], BF16)
        B_kr = Buf("kr")
        S.op("pool", I("memset", kz[:], 0.0), [], [B_kr])
        vaug = sb("vaug", [128, 6, 2, 192], BF16)
        B_vaug = Buf("vaug")
        S.op("dve", I("memset", vaug[:, :, :, 64:128], 1.0), [], [B_vaug])
        pt = sb("pt", [128, 3, 2, 256], BF16)
        B_pt = Buf("pt")
        B_ptk = [Buf(f"pt{i}") for i in range(3)]
        cs = sb("cs", [128, 2, 640])
        B_cs = Buf("cs")
        sl_cs = slot("cs")
        xh = sb("xh", [128, 8, 128])
        B_xh = Buf("xh")
        sl_xh = slot("xh")
        hTh = sb("hTh", [128, 8, 128], BF16)
        B_hTh = Buf("hTh")
        sqh = sb("sqh", [128, 8, 128], BF16)
        B_sqh = Buf("sqh")
        mone = sb("mone", [128, 256])
        S.op("pool", I("memset", mone[:], -1.0), [], [B_const])
        dtmp = sb("dtmp", [128, 256])
        B_dtmp = Buf("dtmp")
        xo_flat = xo[:].rearrange("p c t -> p (c t)")
        ccg = xo_flat[:, 0:2 * CCW].rearrange("p (r n) -> p r n", r=2)
        ccs = xo_flat[:, 2048:2048 + CCW]
        ccp = xo_flat[:, 3072:3072 + CCW]
        B_ccs = B_ccg = B_ccp = B_xo
        sl_cc = slot("cc")
        DRAM_cc = Buf("dram_cc")
        ccsem = sem("ccsem")
        cmask = sb("cmask", [128, 512])
        hmask2 = sb("hmask2", [128, 2, 128], BF16)
        dma("sp", cmask[:], c_cmask_d.ap(), [], [B_const], sl_c)
        dma("pool", hmask2[:], c_hmask2_d.ap(), [], [B_const], sl_c)
        qraw = sb("qraw", [128, 4, 512])
        B_qraw = Buf("qraw")
        qtT = sb("qtT", [128, 4, 512], BF16)
        B_qtT = Buf("qtT")
        ktT = sb("ktT", [128, 4, 512], BF16)
        B_ktT = Buf("ktT")
        ktA = sb("ktA", [128, 4, 128], BF16)
        ktB = sb("ktB", [128, 4, 128], BF16)
        B_kt = Buf("kt")
        S.op("pool", I("memset", ktA[:], 0.0), [], [B_kt])
        S.op("pool", I("memset", ktB[:], 0.0), [], [B_kt])
        vtok = sb("vtok", [128, 4, 4, 128], BF16)
        B_vtok = [Buf(f"vtok{i}") for i in range(4)]
        sc = sb("sc", [128, 4, 3, 8])
        B_sc = Buf("sc")
        sc8 = sb("sc8", [128, 8])
        B_sc8 = Buf("sc8")
        St = sb("St", [128, 4, 128])
        B_S = Buf("S")
        Sp = sb("Sp", [128, 4, 128], BF16)
        B_Sp = Buf("Sp")
        ATs = sb("ATs", [128, 4, 128], BF16)
        B_ATs = Buf("ATs")
        o1s = sb("o1s", [128, 4, 128])
        B_o1s = Buf("o1s")
        sl_o1 = slot("o1")
        DRAM_o1 = [Buf(f"dram_o1_{i}") for i in range(cfg.NB)]
        ptr = ps("ptr", [128, 4, 128], BF16)
        B_ptr = Buf("ptr")

        def hgrn_gates(l, dirn, srcw, base_a):
            for h in range(4):
                p, Bp = proj_fm(l, srcw, base_a + h)
                tA, BA = get_tmp()
                tK, BK = get_tmp()
                tB, BB = get_tmp()
                tD, BD = get_tmp()
                tE, BE = get_tmp()
                v = lambda t: t[:].rearrange("p (c t) -> p c t", t=64)
                S.op("act", I("activation", out=tA[:], in_=p[:], func=AF.Tanh, scale=0.5), [Bp], [BA])
                S.op("dve", I("tensor_scalar", out=tA[:], in0=tA[:], scalar1=lbc1[:, l, dirn, h:h + 1], scalar2=lbc0[:, l, dirn, h:h + 1], op0=ALU.mult, op1=ALU.add),
                     [BA, B_const], [BA])
                S.op("pool", I("tensor_scalar", out=tK[:], in0=tA[:], scalar1=-1.0, scalar2=1.0, op0=ALU.mult, op1=ALU.add), [BA], [BK])
                S.op("act", I("activation", out=tA[:], in_=tA[:], func=AF.Ln), [BA, BK], [BA])
                S.op("dve", I("tensor_tensor_scan", out=tB[:], data0=cmask[:], data1=tA[:], initial=0.0, op0=ALU.mult, op1=ALU.add), [BA, B_const], [BB])
                if dirn == 0:
                    S.op("dve", I("tensor_tensor", out=v(tD), in0=v(tB), in1=v(tB)[:, :, 31:32].to_broadcast([128, 8, 64]), op=ALU.subtract), [BB], [BD])
                    S.op("act", I("activation", out=sc[:, h, 0, :], in_=v(tB)[:, :, 31], func=AF.Exp), [BB], [B_sc])
                    S.op("act", I("activation", out=sc[:, h, 1, :], in_=v(tB)[:, :, 63], func=AF.Exp), [BB], [B_sc])
                    S.op("act", I("activation", out=sc[:, h, 2, :], in_=v(tD)[:, :, 63], func=AF.Exp), [BD], [B_sc])
                else:
                    S.op("dve", I("tensor_tensor", out=tA[:], in0=tB[:], in1=tA[:], op=ALU.subtract), [BB, BA], [BA])
                    S.op("dve", I("tensor_tensor", out=v(tD), in0=v(tA)[:, :, 32:33].to_broadcast([128, 8, 64]), in1=v(tA), op=ALU.subtract), [BA], [BD])
                    S.op("dve", I("tensor_tensor", out=sc8[:], in0=v(tB)[:, :, 63], in1=v(tA)[:, :, 32], op=ALU.subtract), [BB, BA], [B_sc8])
                    S.op("act", I("activation", out=sc[:, h, 0, :], in_=sc8[:], func=AF.Exp), [B_sc8], [B_sc])
                    S.op("act", I("activation", out=sc[:, h, 1, :], in_=v(tB)[:, :, 63], func=AF.Exp), [BB], [B_sc])
                    S.op("act", I("activation", out=sc[:, h, 2, :], in_=v(tD)[:, :, 0], func=AF.Exp), [BD], [B_sc])
                S.op("act", I("activation", out=tE[:], in_=tD[:], func=AF.Exp), [BD], [BE])
                S.op("dve", I("tensor_tensor", out=qtT[:, h, :], in0=qraw[:, h, :], in1=tE[:], op=ALU.mult), [B_qraw, BE], [B_qtT])
                S.op("act", I("activation", out=tB[:], in_=tD[:], func=AF.Exp, scale=-1.0), [BD, B_sc, B_sc8], [BB])
                S.op("pool", I("tensor_tensor", out=ktT[:, h, :], in0=tK[:], in1=tB[:], op=ALU.mult), [BK, BB], [B_ktT])

        def hgrn_q_and_v(l, srcw, base_q):
            for h in range(4):
                p, Bp = proj_fm(l, srcw, base_q + h)
                S.op("act", I("activation", out=qraw[:, h, :], in_=p[:], func=AF.Copy), [Bp], [B_qraw])

        def hgrn_vtok(bi):
            tsl = slice(bi * 128, (bi + 1) * 128)
            p, Bp = get_pp()
            for k in range(8):
                S.op("pe", I("matmul", p[:], lhsT=hT[:, k, tsl], rhs=wtm[:, k, 0:512], start=(k == 0), stop=(k == 7)), [B_hT, B_wtm], [Bp])
            S.op("act", I("activation", out=vtok[:, bi, :, :], in_=p[:].rearrange("p (h d) -> p h d", h=4), func=AF.Copy), [Bp], [B_vtok[bi]])

        def hgrn_block(l, dirn, bi, m):
            tsl = slice(bi * 128, (bi + 1) * 128)
            psc, B_psc = pbr[0][:].rearrange("p (h t) -> p h t", h=4), B_pbr[0]
            po, B_po = pbr[1][:].rearrange("p (h t) -> p h t", h=4), B_pbr[1]
            pob, B_pob = pbr[2][:].rearrange("p (h t) -> p h t", h=4), B_pbr[2]
            pS, B_pS = pmx[:], B_pmx
            for h in range(4):
                S.op("pe", I("transpose", out=ptr[:, h, :], in_=ktT[:, h, tsl], identity=ident_b[:]), [B_ktT, B_const], [B_ptr])
            S.op("dve", I("tensor_copy", out=ktA[0:64, :, :], in_=ptr[0:64, :, :]), [B_ptr], [B_kt])
            S.op("act", I("activation", out=ktB[64:128, :, :], in_=ptr[64:128, :, :], func=AF.Copy), [B_ptr], [B_kt])
            for h in range(4):
                S.op("pe", I("matmul", psc[:, h, :], lhsT=ktT[:, h, tsl], rhs=qtT[:, h, tsl], start=True, stop=True), [B_ktT, B_qtT], [B_psc])
            S.op("dve", I("tensor_tensor", out=ATs[:], in0=psc, in1=hmask2[:, dirn:dirn + 1, :].to_broadcast([128, 4, 128]), op=ALU.mult), [B_psc, B_const], [B_ATs])
            for h in range(4):
                S.op("pe", I("matmul", po[:, h, :], lhsT=vtok[:, bi, h, :], rhs=ATs[:, h, :], start=True, stop=True), [B_vtok[bi], B_ATs], [B_po])
            chunks = [(0, ktA), (1, ktB)] if dirn == 0 else [(1, ktB), (0, ktA)]
            for ci, (c, kt) in enumerate(chunks):
                gc = bi * 2 + c
                csl = slice(bi * 128 + c * 64, bi * 128 + (c + 1) * 64)
                for h in range(4):
                    S.op("dve", I("tensor_scalar", out=Sp[:, h, :], in0=St[:, h, :], scalar1=sc[:, h, 0, gc:gc + 1], scalar2=None, op0=ALU.mult), [B_S, B_sc], [B_Sp])
                for h in range(4):
                    S.op("pe", I("matmul", pob[:, h, c * 64:(c + 1) * 64], lhsT=Sp[:, h, :], rhs=qtT[:, h, csl], start=True, stop=True), [B_Sp, B_qtT], [B_pob])
                for h in range(4):
                    S.op("pe", I("matmul", pS[:, h, :], lhsT=kt[:, h, :], rhs=vtok[:, bi, h, :], start=True, stop=True), [B_kt, B_vtok[bi]], [B_pS])
                for h in range(4):
                    S.op("dve", I("tensor_scalar", out=St[:, h, :], in0=St[:, h, :], scalar1=sc[:, h, 1, gc:gc + 1], scalar2=None, op0=ALU.mult), [B_S, B_sc], [B_S])
                    S.op("dve", I("scalar_tensor_tensor", out=St[:, h, :], in0=pS[:, h, :], scalar=sc[:, h, 2, gc:gc + 1], in1=St[:, h, :], op0=ALU.mult, op1=ALU.add),
                         [B_pS, B_sc, B_S], [B_S])
            gblk = m * 4 + bi
            if dirn == 0:
                S.op("act", I("activation", out=o1s[:], in_=po, func=AF.Copy), [B_po], [B_o1s])
                S.op("dve", I("tensor_tensor", out=o1s[:], in0=o1s[:], in1=pob, op=ALU.add), [B_o1s, B_pob], [B_o1s])
                dma("sp", o1_d[:, gblk], o1s[:], [B_o1s], [DRAM_o1[gblk]], sl_o1)
            else:
                dma("sp", o1s[:], o1_d[:, gblk], [DRAM_o1[gblk]], [B_o1s], sl_o1)
                osum, Bos = get_tmp()
                rt, Brt = get_tmp()
                osv = osum[:].rearrange("p (h t) -> p h t", h=4)
                S.op("dve", I("tensor_tensor", out=osv, in0=po, in1=o1s[:], op=ALU.add), [B_po, B_o1s], [Bos])
                S.op("dve", I("tensor_tensor", out=osv, in0=osv, in1=pob, op=ALU.add), [Bos, B_pob], [Bos])
                S.op("act", I("activation", out=ya_scr[:, 0:512], in_=osum[:], func=AF.Square), [Bos], [B_ya_scr])
                S.op("pe", I("matmul", pst[:], lhsT=ones_b[:], rhs=ya_scr[:, 0:512], start=True, stop=True), [B_ya_scr, B_const], [B_pst])
                S.op("act", I("activation", out=rt[:], in_=pst[:], func=AF.Ln, scale=1.0 / 128, bias=epsb[:]), [B_pst, B_const], [Brt])
                S.op("act", I("activation", out=rt[:], in_=rt[:], func=AF.Exp, scale=-0.5), [Brt], [Brt])
                S.op("dve", I("tensor_tensor", out=osum[:], in0=osum[:], in1=rt[:], op=ALU.mult), [Bos, Brt], [Bos])
                for h in range(4):
                    S.op("dve", I("scalar_tensor_tensor", out=ya[:, h, tsl], in0=osv[:, h, :], scalar=hgain[:, l, h:h + 1], in1=zs[:, h, tsl], op0=ALU.mult, op1=ALU.mult),
                         [Bos, B_const, B_zs[h]], [B_ya])

        def sweep1_mt(l, m):
            norm_mt(l, m)
            pp_mode[0] = "wide"
            hgrn_q_and_v(l, w1fm_b, 4)
            hgrn_gates(l, 0, w1fm_b, 0)
            for bi in range(4):
                hgrn_vtok(bi)
                hgrn_block(l, 0, bi, m)
        sl_out = slot("out")

        def xsrc(l):
            return xT_d if l == 0 else out_d

        xview = lambda t, m: t.ap().rearrange("(c p) t -> p c t", p=128)[:, :, m * 512:(m + 1) * 512]

        def load_w_chunk(l, src_b, j):
            i = _wb_i[0] % NWB
            _wb_i[0] += 1
            dma("sp", wbuf[i][:], src_b[l, j], [B_wcast[l]], [B_wbuf[i]], sl_wbuf[i])
            return wbuf[i], B_wbuf[i]

        def tanh_gate(dst, src, scale):
            return I("activation", out=dst, in_=src, func=AF.Tanh, scale=scale)

        def norm_mt(l, m):
            dma("sp", xt[:], xview(xsrc(l), m), [DRAM_x[m]], [B_xt], sl_xt)
            for c in range(8):
                S.op("act", I("activation", out=sq[:, c, :], in_=xt[:, c, :], func=AF.Square), [B_xt], [B_mg[c]])
            for c in range(8):
                S.op("pe", I("matmul", pst[:], lhsT=ones_b[:], rhs=sq[:, c, :], start=(c == 0), stop=(c == 7)),
                     [B_mg[c], B_const], [B_pst])
            S.op("act", I("activation", out=rtmp[:], in_=pst[:], func=AF.Ln, scale=1.0 / D, bias=epsb[:]), [B_pst, B_const], [B_rtmp])
            S.op("act", I("activation", out=rstd[:], in_=rtmp[:], func=AF.Exp, scale=-0.5), [B_rtmp], [B_rstd])
            for c in range(8):
                S.op("dve", I("scalar_tensor_tensor", out=hT[:, c, :], in0=xt[:, c, :], scalar=ngain[:, l, c:c + 1],
                                                                   in1=rstd[:], op0=ALU.mult, op1=ALU.mult),
                     [B_xt, B_rstd, B_const], [B_hT])

        def proj_fm(l, src_b, j):
            w, Bw = load_w_chunk(l, src_b, j)
            p, Bp = get_pp()
            for k in range(8):
                S.op("pe", I("matmul", p[:], lhsT=w[:, k, :], rhs=hT[:, k, :], start=(k == 0), stop=(k == 7)),
                     [Bw, B_hT], [Bp])
            return p, Bp

        def layer_weights(l):
            dma("sp", wtm[:], w2tm_b[l], [B_wcast[l]], [B_wtm], sl_wtm)
            dma("sp", lng[:], lng_d[:, l, :], [], [B_lw], sl_wl)
            dma("sp", lnb[:], lnb_d[:, l, :], [], [B_lw], sl_wl)
            dma("sp", bsp[:], bsp_d[:, l], [], [B_lw], sl_wl)
            dma("pool", wsT[:], wsT_d[:, l], [], [B_lw], sl_wl)

        GELU_C = 0.7978845608028654

        def gelu_from_psum(p, Bp, dst, Bdst, eng2="pool"):
            t1, B1 = get_tmp()
            t2, B2 = get_tmp()
            S.op("act", I("activation", out=t1[:], in_=p[:], func=AF.Square), [Bp], [B1])
            S.op("dve", I("tensor_scalar", out=t1[:], in0=t1[:], scalar1=0.044715, scalar2=1.0, op0=ALU.mult, op1=ALU.add), [B1], [B1])
            S.op("dve", I("tensor_tensor", out=t2[:], in0=t1[:], in1=p[:], op=ALU.mult), [B1, Bp], [B2])
            S.op("act", I("activation", out=t1[:], in_=t2[:], func=AF.Tanh, scale=GELU_C), [B2], [B1])
            S.op("dve", I("scalar_tensor_tensor", out=dst, in0=t1[:], scalar=1.0, in1=p[:], op0=ALU.add, op1=ALU.mult), [B1, Bp], [Bdst])

        def silu_from_psum(p, Bp, dst, Bdst):
            t1, B1 = get_tmp()
            S.op("act", I("activation", out=t1[:], in_=p[:], func=AF.Tanh, scale=0.5), [Bp], [B1])
            S.op("dve", I("scalar_tensor_tensor", out=dst, in0=t1[:], scalar=1.0, in1=p[:], op0=ALU.add, op1=ALU.mult), [B1, Bp], [Bdst])

        def sweep2_mt(l, m):
            norm_mt(l, m)
            base = {}
            cnt = 0
            for kind in FM2:
                base.setdefault(kind, cnt)
                cnt += 1
            if cfg.do_b:
                pp_mode[0] = "wide"
                for j in range(4):
                    p, Bp = proj_fm(l, w2fm_b, base["uB"] + j)
                    gelu_from_psum(p, Bp, ub[:, j, :], B_ub[j])
                for j in range(4):
                    p, Bp = proj_fm(l, w2fm_b, base["zB"] + j)
                    silu_from_psum(p, Bp, zs[:, 4 + j, :], B_zs[4 + j])
                pp_mode[0] = "nopmx"
                for blk in range(4):
                    tsl = slice(blk * 128, (blk + 1) * 128)
                    p, Bp = get_pp()
                    for k in range(8):
                        S.op("pe", I("matmul", p[:], lhsT=hT[:, k, tsl], rhs=wtm[:, k, 512:1024],
                                                                          start=(k == 0), stop=(k == 7)), [B_hT, B_wtm], [Bp])
                    vt, B_vt = get_tmp()
                    vt2, B_vt2 = get_tmp()
                    gelu_from_psum(p, Bp, vt[:], B_vt)
                    S.op("dve", I("bn_stats", out=bnst[:], in_=vt[:]), [B_vt], [B_bn])
                    S.op("dve", I("bn_aggr", out=bnag[:], in_=bnst[:]), [B_bn], [B_bn])
                    S.op("dve", I("tensor_scalar", out=bnag[:, 1:2], in0=bnag[:, 1:2], scalar1=0.25, scalar2=EPS, op0=ALU.mult, op1=ALU.add),
                         [B_bn], [B_bn])
                    S.op("pool", I("tensor_tensor", out=bnag[:, 1:2], in0=bnag[:, 1:2], in1=mhalf1[:], op=ALU.pow), [B_bn, B_const], [B_bn])
                    S.op("dve", I("tensor_scalar", out=bnag[:, 1:2], in0=bnag[:, 1:2], scalar1=0.5, scalar2=None, op0=ALU.mult), [B_bn], [B_bn])
                    S.op("dve", I("tensor_scalar", out=vt2[:], in0=vt[:], scalar1=bnag[:, 0:1], scalar2=bnag[:, 1:2],
                                                          op0=ALU.subtract, op1=ALU.mult), [B_vt, B_bn], [B_vt2])
                    S.op("dve", I("tensor_tensor", out=vt2[:], in0=vt2[:], in1=lng[:], op=ALU.mult), [B_vt2, B_lw], [B_vt2])
                    S.op("dve", I("tensor_tensor", out=vn[:], in0=vt2[:], in1=lnb[:], op=ALU.add), [B_vt2, B_lw], [B_vn])
                    for g in range(4):
                        S.op("pe", I("matmul", pmx[:, g, :], lhsT=vn[:, g * 128:(g + 1) * 128], rhs=wsT[:, g, :], start=True, stop=True),
                             [B_vn, B_lw], [B_pmx])
                    t1, B1 = get_tmp()
                    t1v = t1[:].rearrange("p (g t) -> p g t", g=4)
                    S.op("dve", I("tensor_tensor", out=t1v, in0=pmx[:], in1=bsp[:], op=ALU.add), [B_pmx, B_lw], [B1])
                    S.op("pool", I("tensor_tensor", out=t1v, in0=t1v, in1=ub[:, :, tsl], op=ALU.mult), [B1] + B_ub, [B1])
                    S.op("dve", I("tensor_tensor", out=yb[:, :, tsl], in0=t1v, in1=zs[:, 4:8, tsl], op=ALU.mult),
                         [B1] + B_zs[4:8], [B_yb])
            if cfg.do_c:
                pp_mode[0] = "wide"
                top = (m == n_mt - 1)
                has_lo = (m > 0)
                t0 = m * 512 - 128
                if has_lo:
                    dma("sp", cs[:, 0, :], cosT_d[:, t0:t0 + 640], [], [B_cs], sl_cs)
                    dma("sp", cs[:, 1, :], sinT_d[:, t0:t0 + 640], [], [B_cs], sl_cs)
                    dma("sp", xh[:], xsrc(l).ap().rearrange("(c p) t -> p c t", p=128)[:, :, t0:t0 + 128], [DRAM_x[m - 1]], [B_xh], sl_xh)
                    for c in range(8):
                        S.op("act", I("activation", out=sqh[:, c, :], in_=xh[:, c, :], func=AF.Square), [B_xh], [B_sqh])
                    for c in range(8):
                        S.op("pe", I("matmul", pst[:, 0:128], lhsT=ones_b[:], rhs=sqh[:, c, :], start=(c == 0), stop=(c == 7)),
                             [B_sqh, B_const], [B_pst])
                    S.op("act", I("activation", out=rtmp[:, 0:128], in_=pst[:, 0:128], func=AF.Ln, scale=1.0 / D, bias=epsb[:]), [B_pst, B_const], [B_rtmp])
                    S.op("act", I("activation", out=rtmp[:, 128:256], in_=rtmp[:, 0:128], func=AF.Exp, scale=-0.5), [B_rtmp], [B_rtmp])
                    for c in range(8):
                        S.op("dve", I("scalar_tensor_tensor", out=hTh[:, c, :], in0=xh[:, c, :], scalar=ngain[:, l, c:c + 1],
                                                                           in1=rtmp[:, 128:256], op0=ALU.mult, op1=ALU.mult),
                             [B_xh, B_rtmp, B_const], [B_hTh])
                else:
                    dma("sp", cs[:, 0, 128:640], cosT_d[:, 0:512], [], [B_cs], sl_cs)
                    dma("sp", cs[:, 1, 128:640], sinT_d[:, 0:512], [], [B_cs], sl_cs)
                if not top:
                    S.op("pool", I("tensor_copy", out=kz[:, :, :, 640:768], in_=kz[:, :, :, 128:256]), [B_kr], [B_kr])
                    S.op("pool", I("tensor_copy", out=vaug[:, 5, :, :], in_=vaug[:, 1, :, :]), [B_vaug], [B_vaug])

                def qk_post(p, Bp, n, which, dst, Bdst, csl):
                    t1, B1 = get_tmp()
                    t2, B2 = get_tmp()
                    t3, B3 = get_tmp()
                    sqb, Bsqb = ya_scr, B_ya_scr
                    S.op("act", I("activation", out=sqb[:, 0:n], in_=p[:, 0:n], func=AF.Square), [Bp], [Bsqb])
                    S.op("dve", I("tensor_scalar", out=sqb[:, 512:512 + n], in0=p[:, 0:n], scalar1=qkg[:, l, which:which + 1], scalar2=None, op0=ALU.mult),
                         [Bp, B_const], [Bsqb])
                    S.op("pe", I("matmul", pst[:, 0:n], lhsT=bd_b[:], rhs=sqb[:, 0:n], start=True, stop=True), [Bsqb, B_const], [B_pst])
                    pr, Bpr = get_pp()
                    S.op("pe", I("matmul", pr[:, 0:n], lhsT=rot_b[:], rhs=sqb[:, 512:512 + n], start=True, stop=True), [Bsqb, B_const], [Bpr])
                    S.op("act", I("activation", out=t1[:, 0:n], in_=pst[:, 0:n], func=AF.Ln, scale=1.0 / 64, bias=epsb[:]), [B_pst, B_const], [B1])
                    S.op("act", I("activation", out=t1[:, 0:n], in_=t1[:, 0:n], func=AF.Exp, scale=-0.5), [B1], [B1])
                    S.op("pool", I("tensor_tensor", out=t2[:, 0:n], in0=sqb[:, 512:512 + n], in1=cs[:, 0, csl], op=ALU.mult), [Bsqb, B_cs], [B2])
                    S.op("dve", I("tensor_tensor", out=t3[:, 0:n], in0=pr[:, 0:n], in1=cs[:, 1, csl], op=ALU.mult), [Bpr, B_cs], [B3])
                    S.op("dve", I("tensor_tensor", out=t2[:, 0:n], in0=t2[:, 0:n], in1=t3[:, 0:n], op=ALU.add), [B2, B3], [B2])
                    if which == 0:
                        S.op("dve", I("tensor_tensor", out=dst, in0=t2[:, 0:n], in1=t1[:, 0:n], op=ALU.mult), [B2, B1], [Bdst])
                    else:
                        jh, c0 = dst
                        S.op("dve", I("tensor_tensor", out=kz[0:64, 0, jh, c0:c0 + n], in0=t2[0:64, 0:n], in1=t1[0:64, 0:n], op=ALU.mult), [B2, B1], [Bdst])
                        S.op("dve", I("tensor_tensor", out=kz[64:128, 1, jh, c0:c0 + n], in0=t2[64:128, 0:n], in1=t1[64:128, 0:n], op=ALU.mult), [B2, B1], [Bdst])

                for j in range(4):
                    p, Bp = proj_fm(l, w2fm_b, base["qC"] + j)
                    qk_post(p, Bp, 512, 0, qr[:, j, :], B_qr[j], slice(128, 640))
                for j in range(2):
                    w, Bw = load_w_chunk(l, w2fm_b, base["kC"] + j)
                    p, Bp = get_pp()
                    for k in range(8):
                        S.op("pe", I("matmul", p[:], lhsT=w[:, k, :], rhs=hT[:, k, :], start=(k == 0), stop=(k == 7)), [Bw, B_hT], [Bp])
                    qk_post(p, Bp, 512, 1, (j, 128), B_kr, slice(128, 640))
                    if has_lo:
                        p, Bp = get_pp()
                        for k in range(8):
                            S.op("pe", I("matmul", p[:, 0:128], lhsT=w[:, k, :], rhs=hTh[:, k, :], start=(k == 0), stop=(k == 7)), [Bw, B_hTh], [Bp])
                        qk_post(p, Bp, 128, 1, (j, 0), B_kr, slice(0, 128))
                for j in range(4):
                    p, Bp = proj_fm(l, w2fm_b, base["zC"] + j)
                    silu_from_psum(p, Bp, zs[:, 8 + j, :], B_zs[8 + j])
                for sl_i in ([0] if has_lo else []) + [1, 2, 3, 4]:
                    p, Bp = get_pp()
                    for k in range(8):
                        if sl_i == 0:
                            S.op("pe", I("matmul", p[:, 0:128], lhsT=hTh[:, k, :], rhs=wtm[:, k, 1024:1152], start=(k == 0), stop=(k == 7)),
                                 [B_hTh, B_wtm], [Bp])
                        else:
                            tsl = slice((sl_i - 1) * 128, sl_i * 128)
                            S.op("pe", I("matmul", p[:, 0:128], lhsT=hT[:, k, tsl], rhs=wtm[:, k, 1024:1152], start=(k == 0), stop=(k == 7)),
                                 [B_hT, B_wtm], [Bp])
                    pv = p[:, 0:128].rearrange("p (h d) -> p h d", h=2)
                    S.op("act", I("activation", out=vaug[:, sl_i, :, 0:64], in_=pv, func=AF.Copy), [Bp], [B_vaug])
                    S.op("dve", I("tensor_copy", out=vaug[:, sl_i, :, 128:192], in_=pv), [Bp], [B_vaug])
                kc_stage = int(os.environ.get("KC_STAGE", "9"))
                if top and kc_stage >= 2:
                    S.op("dve", I("tensor_copy", out=ccs[0:64, 512:768].rearrange("p (h t) -> p h t", h=2), in_=kz[0:64, 0, :, 512:640]), [B_kr], [B_ccs])
                    S.op("dve", I("tensor_copy", out=ccs[64:128, 512:768].rearrange("p (h t) -> p h t", h=2), in_=kz[64:128, 1, :, 512:640]), [B_kr], [B_ccs])
                    S.op("dve", I("tensor_copy", out=ccs[:, 768:896].rearrange("p (h d) -> p h d", h=2), in_=vaug[:, 4, :, 0:64]), [B_vaug], [B_ccs])
                    if not cfg.do_a:
                        S.op("dve", I("memset", ccs[:, 0:512], 0.0), [], [B_ccs])
                    else:
                        S.op("dve", I("tensor_copy", out=ccs[:, 0:512], in_=St[:].rearrange("p h v -> p (h v)")), [B_S], [B_ccs])
                    dma("pool", cc_in[l].ap(), ccs, [B_ccs], [DRAM_cc], sl_cc)
                    o = S.op("pool", I("collective_compute", "AllGather", ALU.bypass, replica_groups=[[0, 1], [2, 3], [4, 5], [6, 7]],
                                                                  ins=[cc_in[l].ap().opt()], outs=[cc_out[l].ap().opt()]), [DRAM_cc], [DRAM_cc])
                    o.signal = True
                    dma("pool", ccg, cc_out[l].ap().rearrange("(r p) n -> p r n", p=128), [DRAM_cc], [B_ccg], sl_cc)
                    S.op("dve", I("tensor_scalar", out=ccp, in0=ccg[:, 0, :], scalar1=selt[:, 0:1], scalar2=None, op0=ALU.mult), [B_ccg, B_const], [B_ccp])
                    S.op("dve", I("scalar_tensor_tensor", out=ccp, in0=ccg[:, 1, :], scalar=selt[:, 1:2], in1=ccp, op0=ALU.mult, op1=ALU.add),
                         [B_ccg, B_const, B_ccp], [B_ccp])
                    S.op("dve", I("tensor_copy", out=kz[0:64, 0, :, 640:768], in_=ccp[0:64, 512:768].rearrange("p (h t) -> p h t", h=2)), [B_ccp], [B_kr])
                    S.op("dve", I("tensor_copy", out=kz[64:128, 1, :, 640:768], in_=ccp[64:128, 512:768].rearrange("p (h t) -> p h t", h=2)), [B_ccp], [B_kr])
                    S.op("dve", I("tensor_copy", out=vaug[:, 5, :, 0:64], in_=ccp[:, 768:896].rearrange("p (h d) -> p h d", h=2)), [B_ccp], [B_vaug])
                    S.op("dve", I("tensor_copy", out=vaug[:, 5, :, 128:192], in_=ccp[:, 768:896].rearrange("p (h d) -> p h d", h=2)), [B_ccp], [B_vaug])
                    if cfg.do_a:
                        S.op("dve", I("tensor_copy", out=St[:].rearrange("p h v -> p (h v)"), in_=ccp[:, 0:512]), [B_ccp], [B_S])
                for sb_i in ((4, 3, 2, 1) if kc_stage >= 3 else ()):
                    jglob = m * 4 + sb_i - 1
                    tsl = slice((sb_i - 1) * 128, sb_i * 128)
                    kbs = []
                    if jglob > 0:
                        kbs.append((sb_i - 1, 0))
                    kbs.append((sb_i, None))
                    kbs.append((sb_i + 1, 2 if (top and sb_i == 4) else 1))
                    for h in range(2):
                        pssv = [pbr[i][:].rearrange("p (e n) -> p e n", e=2) for i in range(3)]
                        for ki, (slk, mk) in enumerate(kbs):
                            for e_ in range(2):
                                rows = slice(e_ * 64, (e_ + 1) * 64)
                                S.op("pe", I("matmul",
                                    pssv[ki][:, e_, :].rearrange("p (c t) -> p c t", c=2), lhsT=kz[:, e_, h, slk * 128:(slk + 1) * 128],
                                    rhs=qr[:, 2 * h:2 * h + 2, tsl], start=True, stop=True),
                                    [B_kr] + B_qr, [B_pbr[ki]])
                            S.op("act", I("activation", out=pt[:, ki, :, :], in_=pssv[ki], func=AF.Exp, scale=0.125), [B_pbr[ki]], [B_ptk[ki]])
                            if mk is not None and kc_stage >= 4:
                                ptv = pt[:, ki, :, :].rearrange("p e (c t) -> p (e c) t", c=2)
                                S.op("dve" if mk == 0 else "pool", I("tensor_tensor", out=ptv, in0=ptv, in1=amask[:, mk:mk + 1, :].to_broadcast([128, 4, 128]), op=ALU.mult),
                                     [B_ptk[ki], B_const], [B_ptk[ki]])
                        pso = pmx[:].rearrange("p a b -> p (a b)").rearrange("p (e n) -> p e n", e=2)
                        for e_ in (range(2) if kc_stage >= 5 else ()):
                            for ki, (slk, mk) in enumerate(kbs):
                                S.op("pe", I("matmul", pso[:, e_, :], lhsT=vaug[:, slk, h, e_ * 64:e_ * 64 + 128], rhs=pt[:, ki, e_, :],
                                                                                       start=(ki == 0), stop=(ki == len(kbs) - 1)), [B_vaug, B_ptk[ki]], [B_pmx])
                        for e_ in (range(2) if kc_stage >= 6 else ()):
                            nr = slice(0, 64) if e_ == 0 else slice(64, 128)
                            dr = slice(64, 128) if e_ == 0 else slice(0, 64)
                            for c in range(2):
                                head = 2 * (2 * h + c) + e_
                                S.op("dve", I("tensor_scalar",
                                    out=dtmp[nr, c * 128:(c + 1) * 128], in0=pso[dr, e_, c * 128:(c + 1) * 128], scalar1=esink[dr, l, head:head + 1], scalar2=None, op0=ALU.add),
                                    [B_pmx, B_const], [B_dtmp])
                        S.op("act", I("activation", out=dtmp[:], in_=dtmp[:], func=AF.Ln), [B_dtmp], [B_dtmp])
                        S.op("act", I("activation", out=dtmp[:], in_=dtmp[:], func=AF.Exp, scale=-1.0), [B_dtmp], [B_dtmp])
                        for e_ in (range(2) if kc_stage >= 6 else ()):
                            nr = slice(0, 64) if e_ == 0 else slice(64, 128)
                            S.op("dve", I("tensor_tensor", out=dtmp[nr, :], in0=pso[nr, e_, :], in1=dtmp[nr, :], op=ALU.mult), [B_pmx, B_dtmp], [B_dtmp])
                        S.op("pool", I("tensor_tensor", out=yc[:, 2 * h:2 * h + 2, tsl], in0=dtmp[:].rearrange("p (c t) -> p c t", c=2),
                                       in1=zs[:, 8 + 2 * h:8 + 2 * h + 2, tsl], op=ALU.mult),
                             [B_dtmp] + B_zs[8:12], [B_yc])
            if cfg.do_a:
                pp_mode[0] = "wide"
                hgrn_q_and_v(l, w2fm_b, base["qA"])
                hgrn_gates(l, 1, w2fm_b, base["a2"])
                for j in range(4):
                    p, Bp = proj_fm(l, w2fm_b, base["zA"] + j)
                    silu_from_psum(p, Bp, zs[:, j, :], B_zs[j])
                for bi in (3, 2, 1, 0):
                    hgrn_vtok(bi)
                    hgrn_block(l, 1, bi, m)
            pp_mode[0] = "narrow"
            scal = {0: 1.0, 1: 0.5, 2: 1.0}
            for dc in range(8):
                dsl = slice(dc * 128, (dc + 1) * 128)
                acc, Bacc = get_tmp()
                first = True
                wi = _wbrc_i[0] % 2
                _wbrc_i[0] += 1
                dma("sp", wbrc[wi][:], wbr_b[l, dc], [B_wcast[l]], [B_wbrc[wi]], sl_wbrc[wi])
                for bi, (on, ysrc, By) in enumerate(((cfg.do_a, ya, B_ya), (cfg.do_b, yb, B_yb), (cfg.do_c, yc, B_yc))):
                    if not on:
                        continue
                    for k in range(4):
                        S.op("pe", I("matmul", pbr[bi][:], lhsT=wbrc[wi][:, bi, k, :], rhs=ysrc[:, k, :],
                                                                                     start=(k == 0), stop=(k == 3)), [B_wbrc[wi], By], [B_pbr[bi]])
                    pg, Bpg = proj_fm(l, w2fm_b, base[("gA", "gB", "gC")[bi]] + dc)
                    gtile, Bg = get_tmp()
                    S.op("act", tanh_gate(gtile[:], pg[:], 0.5), [Bpg], [Bg])
                    g = gtile[:]
                    if first:
                        S.op("dve", I("scalar_tensor_tensor", out=acc[:], in0=g, scalar=1.0, in1=pbr[bi][:], op0=ALU.add, op1=ALU.mult),
                             [Bg, B_pbr[bi]], [Bacc])
                        if scal[bi] != 1.0:
                            S.op("dve", I("tensor_scalar", out=acc[:], in0=acc[:], scalar1=scal[bi], scalar2=None, op0=ALU.mult), [Bacc], [Bacc])
                        first = False
                    else:
                        t2, B2 = get_tmp()
                        S.op("dve", I("scalar_tensor_tensor", out=t2[:], in0=g, scalar=1.0, in1=pbr[bi][:], op0=ALU.add, op1=ALU.mult),
                             [Bg, B_pbr[bi]], [B2])
                        S.op("dve", I("scalar_tensor_tensor", out=acc[:], in0=t2[:], scalar=scal[bi], in1=acc[:], op0=ALU.mult, op1=ALU.add),
                             [B2, Bacc], [Bacc])
                S.op("act", I("activation", out=mg[:, dc, :], in_=acc[:], func=AF.Copy), [Bacc], [B_mg[dc]])
            for ec in range(8):
                esl = slice(ec * 128, (ec + 1) * 128)
                w, Bw = load_w_chunk(l, wo_b, ec)
                p, Bp = get_pp()
                for k in range(8):
                    S.op("pe", I("matmul", p[:], lhsT=w[:, k, :], rhs=mg[:, k, :], start=(k == 0), stop=(k == 7)),
                         [Bw, B_mg[k]], [Bp])
                S.op("dve", I("scalar_tensor_tensor", out=xo[:, ec, :], in0=p[:], scalar=0.25, in1=xt[:, ec, :], op0=ALU.mult, op1=ALU.add), [Bp, B_xt], [B_xo])
            dma("sp", xview(out_d, m), xo[:], [B_xo], [DRAM_x[m]], sl_xo)

        import os
        stop = os.environ.get("KSTOP", "")
        for l in range(depth):
            S.epoch = l
            if stop == "cast":
                for m in range(n_mt):
                    dma("sp", xt[:], xview(xsrc(l), m), [DRAM_x[m]] + B_wcast, [B_xt], sl_xt)
                    dma("sp", xview(out_d, m), xt[:], [B_xt], [DRAM_x[m]], sl_xo)
                continue
            layer_weights(l)
            if stop == "lw":
                for m in range(n_mt):
                    dma("sp", xt[:], xview(xsrc(l), m), [DRAM_x[m]] + B_wcast + [B_wtm, B_lw], [B_xt], sl_xt)
                    dma("sp", xview(out_d, m), xt[:], [B_xt], [DRAM_x[m]], sl_xo)
                continue
            if cfg.do_a:
                S.op("dve", I("memset", St[:], 0.0), [], [B_S])
                for m in range(n_mt):
                    sweep1_mt(l, m)
            for m in reversed(range(n_mt)):
                sweep2_mt(l, m)
        S.epoch = depth
        S.op("sp", I("nop"), reads=[DRAM_x[m] for m in range(n_mt)], writes=[])

        with nc.Block() as block:
            S.finalize(engsems, block)
        build_nc.last_stats = S.stats
    return nc


def run(inputs, cfg, trace=False):
    nc = build_nc(cfg)
    in_maps = [prep_core_inputs(inputs, c, cfg) for c in range(NCORES)]
    res = run_bass_kernel_spmd(nc, in_maps, core_ids=list(range(NCORES)), trace=trace)
    T = cfg.T
    L = 2 * T
    B = NCORES // 2
    out = np.empty((B, L, D), np.float32)
    for c in range(NCORES):
        o = np.asarray(res.results[c]["out"]).T
        if c % 2 == 0:
            out[c // 2, :T] = o
        else:
            out[c // 2, T:] = o[::-1]
    return out, res


def kernel(**inputs):
    cfg = Cfg()
    out, _ = run(inputs, cfg)
    return out
```

```python
import numpy as np
import concourse.bass as bass
import concourse.mybir as mybir
from concourse.bass_utils import run_bass_kernel_spmd

F32 = mybir.dt.float32
BF16 = mybir.dt.bfloat16
ALU = mybir.AluOpType
AF = mybir.ActivationFunctionType

D = 1024
DEPTH = 4
EPS = 1e-6
NCORES = 8
SAME_ENGINE_SYNC = True


def I(name, *args, **kw):
    return lambda e: getattr(e, name)(*args, **kw)


class Buf:
    __slots__ = ("name", "last_w", "readers")

    def __init__(self, name):
        self.name = name
        self.last_w = None
        self.readers = []


class Slot:
    def __init__(self, sem, name):
        self.sem = sem
        self.count = 0
        self.token = Buf("slot_" + name)


class Op:
    __slots__ = ("eng", "fn", "deps", "signal", "val", "sem", "slot", "idx", "epoch", "raw")


class Sched:
    ENGS = ("pe", "act", "dve", "pool", "sp")

    def __init__(self, nc, n_epochs):
        self.nc = nc
        self.ops = {e: [] for e in self.ENGS}
        self.epoch = 0
        self.n_epochs = n_epochs
        self.engsem = {}

    def op(self, eng, fn, reads=(), writes=(), slot=None):
        o = Op()
        o.eng = eng
        o.fn = fn
        o.signal = False
        o.val = None
        o.sem = None
        o.slot = slot
        o.epoch = self.epoch
        deps = []
        raw = set()
        writes = list(writes)
        if slot is not None:
            writes.append(slot.token)
        for b in reads:
            if b.last_w is not None:
                deps.append(b.last_w)
                raw.add(id(b.last_w))
        for b in writes:
            if b.last_w is not None:
                deps.append(b.last_w)
            deps.extend(b.readers)
        seen = set()
        dd = []
        for d in deps:
            if id(d) not in seen and d is not o:
                seen.add(id(d))
                dd.append(d)
        o.deps = dd
        o.raw = raw
        for b in reads:
            b.readers.append(o)
        for b in writes:
            b.last_w = o
            b.readers = []
        if slot is not None:
            slot.count += 1
            o.sem = slot.sem
            o.val = 16 * slot.count
        o.idx = len(self.ops[eng])
        self.ops[eng].append(o)
        return o

    def _needs_wait(self, cons, prod):
        if prod.slot is not None:
            return True
        if prod.eng == cons.eng and cons.slot is None:
            if prod.eng == "pe":
                return False
            return SAME_ENGINE_SYNC and (id(prod) in cons.raw)
        return True

    def finalize(self, sems, block):
        for e in self.ENGS:
            for o in self.ops[e]:
                for d in o.deps:
                    if self._needs_wait(o, d):
                        d.signal = True
        for e in self.ENGS:
            cnt = {}
            for o in self.ops[e]:
                if o.slot is not None:
                    pass
                elif o.signal:
                    cnt[o.epoch] = cnt.get(o.epoch, 0) + 1
                    o.sem = sems[(e, o.epoch)]
                    o.val = cnt[o.epoch]
        self.stats = {e: len(self.ops[e]) for e in self.ENGS}

        def emit(e, eng):
            waited = {}
            nwaits = 0
            for o in self.ops[e]:
                for d in o.deps:
                    if not self._needs_wait(o, d):
                        continue
                    key = id(d.sem)
                    if waited.get(key, 0) >= d.val:
                        continue
                    eng.wait_ge(d.sem, d.val)
                    nwaits += 1
                    waited[key] = d.val
                ins = o.fn(eng)
                if o.slot is not None:
                    ins.then_inc(o.sem, 16)
                elif o.signal:
                    ins.then_inc(o.sem, 1)
            self.stats[e + "_waits"] = nwaits

        @block.tensor
        def _(eng):
            emit("pe", eng)

        @block.scalar
        def _(eng):
            emit("act", eng)

        @block.vector
        def _(eng):
            emit("dve", eng)

        @block.gpsimd
        def _(eng):
            emit("pool", eng)

        @block.sync
        def _(eng):
            emit("sp", eng)


OFF = dict(qA=0, fAf=512, fAb=1024, iA=1536, zA=2048, uB=2560, vB=3072, zB=3584, qC=4096, kC=4608,
           vC=4736, zC=4864, gA=5376, gB=6400, gC=7424)

FM2 = (["a2"] * 4 + ["qA"] * 4 + ["zA"] * 4 + ["uB"] * 4 + ["zB"] * 4 + ["qC"] * 4 + ["kC"] * 2 + ["zC"] * 4
       + ["gA"] * 8 + ["gB"] * 8 + ["gC"] * 8)
FM1 = ["a1"] * 4 + ["qA"] * 4


def _fm_cols(kind_list, odd):
    out = []
    cnt = {}
    for kind in kind_list:
        j = cnt.get(kind, 0)
        cnt[kind] = j + 1
        if kind == "a1":
            base = OFF["fAb"] if odd else OFF["fAf"]
            cols = np.arange(base + j * 128, base + (j + 1) * 128)
        elif kind == "a2":
            base = OFF["fAf"] if odd else OFF["fAb"]
            cols = np.arange(base + j * 128, base + (j + 1) * 128)
        elif kind == "kC":
            c = np.arange(OFF["kC"] + j * 64, OFF["kC"] + (j + 1) * 64)
            cols = np.concatenate([c, c])
        else:
            base = OFF[kind]
            cols = np.arange(base + j * 128, base + (j + 1) * 128)
        out.append(cols)
    return out


def _tm_cols(sweep):
    if sweep == 1:
        return np.arange(OFF["iA"], OFF["iA"] + 512)
    return np.concatenate([np.arange(OFF["iA"], OFF["iA"] + 512), np.arange(OFF["vB"], OFF["vB"] + 512),
                           np.arange(OFF["vC"], OFF["vC"] + 128)])


NF1 = len(FM1)
NF2 = len(FM2)
TM1 = 512
TM2 = 1152


class Cfg:
    def __init__(self, n_mt=8, depth=DEPTH, do_a=True, do_b=True, do_c=True):
        self.n_mt = n_mt
        self.T = n_mt * 512
        self.NB = n_mt * 4
        self.depth = depth
        self.do_a = do_a
        self.do_b = do_b
        self.do_c = do_c


def prep_core_inputs(inp, core, cfg):
    T = cfg.T
    L = 2 * T
    b = core // 2
    odd = core % 2
    depth = cfg.depth
    f32 = np.float32
    pos = (np.arange(T) if not odd else (L - 1 - np.arange(T))).astype(np.int64)
    m = {}
    x = np.asarray(inp["x"])[b]
    m["xT"] = np.ascontiguousarray(x[pos, :].T).astype(f32)
    w_in = np.asarray(inp["w_in"])
    w1fm = np.empty((depth, NF1, 128, 8, 128), f32)
    w2fm = np.empty((depth, NF2, 128, 8, 128), f32)
    w1tm = np.empty((depth, 128, 8, TM1), f32)
    w2tm = np.empty((depth, 128, 8, TM2), f32)
    c1 = _fm_cols(FM1, odd)
    c2 = _fm_cols(FM2, odd)
    for l in range(depth):
        wl = w_in[l].reshape(8, 128, -1)
        for j, cols in enumerate(c1):
            w1fm[l, j] = wl[:, :, cols].transpose(1, 0, 2)
        for j, cols in enumerate(c2):
            w2fm[l, j] = wl[:, :, cols].transpose(1, 0, 2)
        w1tm[l] = wl[:, :, _tm_cols(1)].transpose(1, 0, 2)
        w2tm[l] = wl[:, :, _tm_cols(2)].transpose(1, 0, 2)
    m["w1fm"] = w1fm
    m["w2fm"] = w2fm
    m["w1tm"] = w1tm
    m["w2tm"] = w2tm
    wbr = np.empty((depth, 8, 128, 3, 4, 128), f32)
    for bi, key in enumerate(("w_branch_a", "w_branch_b", "w_branch_c")):
        w = np.asarray(inp[key])[:depth].reshape(depth, 4, 128, 8, 128)
        wbr[:, :, :, bi] = w.transpose(0, 3, 2, 1, 4)
    m["wbr"] = wbr
    w = np.asarray(inp["w_out"])[:depth].reshape(depth, 8, 128, 8, 128)
    m["wo"] = np.ascontiguousarray(w.transpose(0, 3, 2, 1, 4)).astype(f32)
    ng = np.asarray(inp["norm_gain"])[:depth]
    m["ngain"] = np.ascontiguousarray(ng.reshape(depth, 8, 128).transpose(2, 0, 1)).astype(f32)
    lb = np.asarray(inp["lb_logits"]).reshape(DEPTH, 2, 4, 128)
    if odd:
        lb = lb[:, ::-1]
    m["lbl"] = np.ascontiguousarray(lb.transpose(3, 0, 1, 2)).astype(f32)
    hg = np.asarray(inp["hg_norm_gain"])[:depth]
    m["hgain"] = np.ascontiguousarray(hg.transpose(2, 0, 1)).astype(f32)
    m["lng"] = np.ascontiguousarray(np.broadcast_to(np.asarray(inp["sg_ln_gain"])[:depth][None], (128, depth, 512))).astype(f32)
    m["lnb"] = np.ascontiguousarray(np.broadcast_to(np.asarray(inp["sg_ln_bias"])[:depth][None], (128, depth, 512))).astype(f32)
    ws = np.asarray(inp["w_spatial"])[:depth]
    bs = np.asarray(inp["b_spatial"])[:depth]
    if odd:
        ws = ws[:, :, ::-1, ::-1]
        bs = bs[:, :, ::-1]
    m["wsT"] = np.ascontiguousarray(ws.transpose(3, 0, 1, 2)).astype(f32)
    m["bsp"] = np.ascontiguousarray(np.broadcast_to(bs[None], (128, depth, 4, 128))).astype(f32)
    qg = np.asarray(inp["q_norm_gain"])[:depth]
    kg = np.asarray(inp["k_norm_gain"])[:depth]
    m["qkg"] = np.ascontiguousarray(np.stack([np.concatenate([qg, qg], 1), np.concatenate([kg, kg], 1)], 1).transpose(2, 0, 1)).astype(f32)
    m["sink"] = np.ascontiguousarray(np.broadcast_to(np.asarray(inp["sink_logits"])[:depth][None], (128, depth, 8))).astype(f32)
    half = 32
    inv_freq = (10000.0 ** (-np.arange(half, dtype=np.float32) / half)).astype(np.float32)
    ang = pos.astype(np.float32)[None, :] * inv_freq[:, None]
    cos = np.cos(ang).astype(f32)
    sin = np.sin(ang).astype(f32)
    m["cosT"] = np.ascontiguousarray(np.concatenate([cos, cos, cos, cos], 0))
    m["sinT"] = np.ascontiguousarray(np.concatenate([sin, sin, sin, sin], 0))
    ident = np.eye(128, dtype=f32)
    m["c_ident"] = ident
    bd = np.zeros((128, 128), f32)
    bd[:64, :64] = 1
    bd[64:, 64:] = 1
    m["c_bd"] = bd
    rot = np.zeros((128, 128), f32)
    for hb in (0, 64):
        for d in range(32):
            rot[hb + d + 32, hb + d] = -1.0
            rot[hb + d, hb + d + 32] = 1.0
    m["c_rot"] = rot
    j = np.arange(128)[:, None]
    i = np.arange(128)[None, :]
    masks = np.stack([(j >= i), (j <= i), (j + i >= 127)], 1).astype(f32)
    m["c_amask"] = np.ascontiguousarray(masks)
    s = np.arange(64)[:, None]
    t = np.arange(64)[None, :]
    h1 = (s <= t).astype(f32)
    h2 = (s >= t).astype(f32)
    m["c_hmask"] = np.ascontiguousarray(np.stack([np.concatenate([h1, h1], 0), np.concatenate([h2, h2], 0)], 1))
    cm = np.ones((128, 512), f32)
    cm[:, ::64] = 0.0
    m["c_cmask"] = cm
    ss_ = np.arange(128)[:, None]
    tt_ = np.arange(128)[None, :]
    same = (ss_ // 64) == (tt_ // 64)
    m["c_hmask2"] = np.ascontiguousarray(np.stack([(same & (ss_ <= tt_)), (same & (ss_ >= tt_))], 1).astype(f32))
    sel = np.zeros((128, 2), f32)
    sel[:, 1 - odd] = 1.0
    m["sel"] = sel
    return m


def build_nc(cfg):
    nc = bass.Bass("TRN2", target_bir_lowering=False)
    T = cfg.T
    depth = cfg.depth
    n_mt = cfg.n_mt

    def din(name, shape, dt=F32):
        return nc.dram_tensor(name, list(shape), dt, kind="ExternalInput")

    xT_d = din("xT", [D, T])
    w1fm_d = din("w1fm", [depth, NF1, 128, 8, 128])
    w2fm_d = din("w2fm", [depth, NF2, 128, 8, 128])
    w1tm_d = din("w1tm", [depth, 128, 8, TM1])
    w2tm_d = din("w2tm", [depth, 128, 8, TM2])
    wbr_d = din("wbr", [depth, 8, 128, 3, 4, 128])
    wo_d = din("wo", [depth, 8, 128, 8, 128])
    ngain_d = din("ngain", [128, depth, 8])
    lbl_d = din("lbl", [128, DEPTH, 2, 4])
    hgain_d = din("hgain", [128, depth, 4])
    lng_d = din("lng", [128, depth, 512])
    lnb_d = din("lnb", [128, depth, 512])
    wsT_d = din("wsT", [128, depth, 4, 128])
    bsp_d = din("bsp", [128, depth, 4, 128])
    qkg_d = din("qkg", [128, depth, 2])
    sink_d = din("sink", [128, depth, 8])
    cosT_d = din("cosT", [128, T])
    sinT_d = din("sinT", [128, T])
    c_ident_d = din("c_ident", [128, 128])
    c_bd_d = din("c_bd", [128, 128])
    c_rot_d = din("c_rot", [128, 128])
    c_amask_d = din("c_amask", [128, 3, 128])
    c_hmask_d = din("c_hmask", [128, 2, 64])
    sel_d = din("sel", [128, 2])
    c_cmask_d = din("c_cmask", [128, 512])
    c_hmask2_d = din("c_hmask2", [128, 2, 128])
    out_d = nc.dram_tensor("out", [D, T], F32, kind="ExternalOutput")

    w1fm_b = nc.dram_tensor("w1fm_b", [depth, NF1, 128, 8, 128], BF16)
    w2fm_b = nc.dram_tensor("w2fm_b", [depth, NF2, 128, 8, 128], BF16)
    w1tm_b = nc.dram_tensor("w1tm_b", [depth, 128, 8, TM1], BF16)
    w2tm_b = nc.dram_tensor("w2tm_b", [depth, 128, 8, TM2], BF16)
    wbr_b = nc.dram_tensor("wbr_b", [depth, 8, 128, 3, 4, 128], BF16)
    wo_b = nc.dram_tensor("wo_b", [depth, 8, 128, 8, 128], BF16)
    o1_d = nc.dram_tensor("o1_spill", [128, cfg.NB, 4, 128], F32)
    CCW = 512 + 256 + 128
    cc_in = [nc.dram_tensor(f"cc_in{l}", [128, CCW], F32) for l in range(depth)]
    cc_out = [nc.dram_tensor(f"cc_out{l}", [256, CCW], F32) for l in range(depth)]

    from contextlib import ExitStack
    es = ExitStack()
    with es:
        S = Sched(nc, depth + 1)

        def sb(name, shape, dt=F32):
            return es.enter_context(nc.sbuf_tensor(name, list(shape), dt))

        def ps(name, shape, dt=F32):
            return es.enter_context(nc.psum_tensor(name, list(shape), dt))

        def sem(name):
            return es.enter_context(nc.semaphore(name))

        engsems = {(e, ep): sem(f"s_{e}_{ep}") for e in ("pe", "act", "dve", "pool") for ep in range(depth + 1)}
        _slot_n = [0]

        def slot(name):
            _slot_n[0] += 1
            return Slot(sem(f"d_{name}_{_slot_n[0]}"), name)

        ident_b = sb("ident_b", [128, 128], BF16)
        ones_b = sb("ones_b", [128, 128], BF16)
        bd_b = sb("bd_b", [128, 128], BF16)
        rot_b = sb("rot_b", [128, 128], BF16)
        amask = sb("amask", [128, 3, 128], BF16)
        hmask = sb("hmask", [128, 2, 64], BF16)
        selt = sb("selt", [128, 2])
        ngain = sb("ngain_s", [128, depth, 8])
        lbl = sb("lbl_s", [128, DEPTH, 2, 4])
        hgain = sb("hgain_s", [128, depth, 4])
        qkg = sb("qkg_s", [128, depth, 2])
        sinkt = sb("sink_s", [128, depth, 8])
        esink = sb("esink", [128, depth, 8])
        lbc1 = sb("lbc1", [128, DEPTH, 2, 4])
        lbc0 = sb("lbc0", [128, DEPTH, 2, 4])
        B_const = Buf("const")
        sl_c = slot("const")

        def dma(eng, out, in_, reads, writes, sl):
            return S.op(eng, I("dma_start", out=out, in_=in_), reads=reads, writes=writes, slot=sl)

        for dst, src in ((ident_b, c_ident_d), (bd_b, c_bd_d), (rot_b, c_rot_d), (amask, c_amask_d), (hmask, c_hmask_d)):
            dma("pool", dst[:], src.ap(), [], [B_const], sl_c)
        for dst, src in ((selt, sel_d), (ngain, ngain_d), (lbl, lbl_d), (hgain, hgain_d), (qkg, qkg_d), (sinkt, sink_d)):
            dma("sp", dst[:], src.ap(), [], [B_const], sl_c)
        S.op("dve", I("memset", ones_b[:], 1.0), [], [B_const])
        S.op("act", I("activation", out=esink[:], in_=sinkt[:], func=AF.Exp), [B_const], [B_const])
        lbe = sb("lbe", [128, DEPTH, 8])
        lbs = sb("lbs", [128, 8])
        lbv = lbl[:].rearrange("p l a b -> p l (a b)")
        S.op("act", I("activation", out=lbe[:], in_=lbv, func=AF.Exp), [B_const], [B_const])
        S.op("dve", I("tensor_tensor", out=lbs[:], in0=lbe[:, 0, :], in1=lbe[:, 1, :], op=ALU.add), [B_const], [B_const])
        S.op("dve", I("tensor_tensor", out=lbs[:], in0=lbs[:], in1=lbe[:, 2, :], op=ALU.add), [B_const], [B_const])
        S.op("dve", I("tensor_tensor", out=lbs[:], in0=lbs[:], in1=lbe[:, 3, :], op=ALU.add), [B_const], [B_const])
        S.op("dve", I("reciprocal", out=lbs[:], in_=lbs[:]), [B_const], [B_const])
        c1v = lbc1[:].rearrange("p l a b -> p l (a b)")
        c0v = lbc0[:].rearrange("p l a b -> p l (a b)")
        S.op("dve", I("memset", c0v[:, 0, :], 0.0), [B_const], [B_const])
        for l in range(1, DEPTH):
            S.op("dve", I("tensor_tensor", out=c1v[:, l, :], in0=lbe[:, l, :], in1=lbs[:], op=ALU.mult), [B_const], [B_const])
            S.op("dve", I("tensor_tensor", out=c0v[:, l, :], in0=c0v[:, l - 1, :], in1=c1v[:, l, :], op=ALU.add), [B_const], [B_const])
        S.op("dve", I("tensor_scalar", out=lbc1[:], in0=lbc0[:], scalar1=-0.5, scalar2=0.5, op0=ALU.mult, op1=ALU.add), [B_const], [B_const])
        S.op("dve", I("tensor_scalar", out=lbc0[:], in0=lbc0[:], scalar1=0.5, scalar2=0.5, op0=ALU.mult, op1=ALU.add), [B_const], [B_const])

        B_wcast = [Buf(f"wcast{l}") for l in range(depth)]
        sl_wc = [slot(f"wc{i}") for i in range(4)]
        _wc_i = [0]

        def wcast(l, dst, src):
            sl = sl_wc[_wc_i[0] % 4]
            _wc_i[0] += 1
            S.op("pool", I("dma_start", out=dst, in_=src), reads=[], writes=[B_wcast[l]], slot=sl)

        import os
        for l in range(depth if not os.environ.get("KSKIPCAST") else 0):
            def v4(t, j0, j1):
                return t[l, j0:j1].rearrange("a p k n -> (a p) (k n)")

            def v3(t):
                return t[l].rearrange("p k n -> p (k n)")

            for j in range(0, NF1, 4):
                wcast(l, v4(w1fm_b, j, j + 4), v4(w1fm_d, j, j + 4))
            wcast(l, v3(w1tm_b), v3(w1tm_d))
            for j in range(0, NF2, 6):
                wcast(l, v4(w2fm_b, j, j + 6), v4(w2fm_d, j, j + 6))
            wcast(l, v3(w2tm_b), v3(w2tm_d))
            wcast(l, wbr_b[l].rearrange("a p b k n -> (a p) (b k n)"), wbr_d[l].rearrange("a p b k n -> (a p) (b k n)"))
            wcast(l, v4(wo_b, 0, 8), v4(wo_d, 0, 8))

        NWB = 4
        wbuf = [sb(f"wbuf{i}", [128, 8, 128], BF16) for i in range(NWB)]
        B_wbuf = [Buf(f"wbuf{i}") for i in range(NWB)]
        sl_wbuf = [slot(f"wb{i}") for i in range(NWB)]
        _wb_i = [0]
        wtm = sb("wtm", [128, 8, TM2], BF16)
        B_wtm = Buf("wtm")
        sl_wtm = slot("wtm")
        wbrc = [sb(f"wbrc{i}", [128, 3, 4, 128], BF16) for i in range(2)]
        B_wbrc = [Buf(f"wbrc{i}") for i in range(2)]
        sl_wbrc = [slot(f"wbrc{i}") for i in range(2)]
        _wbrc_i = [0]
        sl_wl = slot("wl")
        lng = sb("lng_s", [128, 512])
        lnb = sb("lnb_s", [128, 512])
        wsT = sb("wsT_s", [128, 4, 128], BF16)
        bsp = sb("bsp_s", [128, 4, 128])
        B_lw = Buf("layerw")

        xt = sb("xt", [128, 8, 512])
        B_xt = Buf("xt")
        sl_xt = slot("xt")
        hT = sb("hT", [128, 8, 512], BF16)
        B_hT = Buf("hT")
        rstd = sb("rstd", [128, 512])
        B_rstd = Buf("rstd")
        rtmp = sb("rtmp", [128, 512])
        B_rtmp = Buf("rtmp")
        epsb = sb("epsb", [128, 1])
        S.op("dve", I("memset", epsb[:], EPS), [], [B_const])

        zs = sb("zs", [128, 12, 512], BF16)
        B_zs = [Buf(f"zs{i}") for i in range(12)]
        ub = sb("ub", [128, 4, 512], BF16)
        B_ub = [Buf(f"ub{i}") for i in range(4)]
        yb = sb("yb", [128, 4, 512], BF16)
        B_yb = Buf("yb")
        ya = sb("ya", [128, 4, 512], BF16)
        B_ya = Buf("ya")
        yc = sb("yc", [128, 4, 512], BF16)
        B_yc = Buf("yc")
        mg = sb("mg", [128, 8, 512], BF16)
        B_mg = [Buf(f"mg{i}") for i in range(8)]
        sq = mg
        xo = sb("xo", [128, 8, 512])
        B_xo = Buf("xo")
        sl_xo = slot("xo")
        ya_scr = sb("qkscr", [128, 1024], BF16)
        B_ya_scr = Buf("qkscr")
        NTMP = 8
        tmpf = [sb(f"tmpf{i}", [128, 512]) for i in range(NTMP)]
        B_tmpf = [Buf(f"tmpf{i}") for i in range(NTMP)]
        _tf_i = [0]

        def get_tmp():
            i = _tf_i[0] % NTMP
            _tf_i[0] += 1
            return tmpf[i], B_tmpf[i]

        vn = sb("vn", [128, 512], BF16)
        B_vn = Buf("vn")
        bnst = sb("bnst", [128, 6])
        bnag = sb("bnag", [128, 2])
        B_bn = Buf("bn")
        mhalf1 = sb("mhalf1", [128, 1])
        S.op("pool", I("memset", mhalf1[:], -0.5), [], [B_const])

        NPP = 2
        pp = [ps(f"pp{i}", [128, 512]) for i in range(NPP)]
        B_pp = [Buf(f"pp{i}") for i in range(NPP)]
        _pp_i = [0]
        pp_mode = ["narrow"]

        def get_pp():
            if pp_mode[0] == "wide":
                banks = [(pp[0], B_pp[0]), (pp[1], B_pp[1]), (pbr[0], B_pbr[0]), (pbr[1], B_pbr[1]), (pbr[2], B_pbr[2]), (pmx_flat, B_pmx)]
            elif pp_mode[0] == "nopmx":
                banks = [(pp[0], B_pp[0]), (pp[1], B_pp[1]), (pbr[0], B_pbr[0]), (pbr[1], B_pbr[1]), (pbr[2], B_pbr[2])]
            else:
                banks = [(pp[0], B_pp[0]), (pp[1], B_pp[1])]
            i = _pp_i[0] % len(banks)
            _pp_i[0] += 1
            return banks[i]

        pst = ps("pst", [128, 512])
        B_pst = Buf("pst")
        pmx = ps("pmx", [128, 4, 128])
        B_pmx = Buf("pmx")
        pbr = [ps(f"pbr{i}", [128, 512]) for i in range(3)]
        B_pbr = [Buf(f"pbr{i}") for i in range(3)]

        class _Flat:
            def __getitem__(self, idx):
                return pmx[:].rearrange("p a b -> p (a b)")[idx]
        pmx_flat = _Flat()

        DRAM_x = [Buf(f"dram_x{m}") for m in range(n_mt)]
        qr = sb("qr", [128, 4, 512], BF16)
        B_qr = [Buf(f"qr{i}") for i in range(4)]
        kz = sb("kz", [128, 2, 2, 768], BF16)
        B_kr = Buf("kr")
        S.op("pool", I("memset", kz[:], 0.0), [], [B_kr])
        vaug = sb("vaug", [128, 6, 2, 192], BF16)
        B_vaug = Buf("vaug")
        S.op("dve", I("memset", vaug[:, :, :, 64:128], 1.0), [], [B_vaug])
        pt = sb("pt", [128, 3, 2, 256], BF16)
        B_pt = Buf("pt")
        B_ptk = [Buf(f"pt{i}") for i in range(3)]
        cs = sb("cs", [128, 2, 640])
        B_cs = Buf("cs")
        sl_cs = slot("cs")
        xh = sb("xh", [128, 8, 128])
        B_xh = Buf("xh")
        sl_xh = slot("xh")
        hTh = sb("hTh", [128, 8, 128], BF16)
        B_hTh = Buf("hTh")
        sqh = sb("sqh", [128, 8, 128], BF16)
        B_sqh = Buf("sqh")
        dtmp = sb("dtmp", [128, 256])
        B_dtmp = Buf("dtmp")
        xo_flat = xo[:].rearrange("p c t -> p (c t)")
        ccg = xo_flat[:, 0:2 * CCW].rearrange("p (r n) -> p r n", r=2)
        ccs = xo_flat[:, 2048:2048 + CCW]
        ccp = xo_flat[:, 3072:3072 + CCW]
        B_ccs = B_ccg = B_ccp = B_xo
        sl_cc = slot("cc")
        DRAM_cc = Buf("dram_cc")
        ccsem = sem("ccsem")
        cmask = sb("cmask", [128, 512])
        hmask2 = sb("hmask2", [128, 2, 128], BF16)
        dma("sp", cmask[:], c_cmask_d.ap(), [], [B_const], sl_c)
        dma("pool", hmask2[:], c_hmask2_d.ap(), [], [B_const], sl_c)
        qraw = sb("qraw", [128, 4, 512])
        B_qraw = Buf("qraw")
        qtT = sb("qtT", [128, 4, 512], BF16)
        B_qtT = Buf("qtT")
        ktT = sb("ktT", [128, 4, 512], BF16)
        B_ktT = Buf("ktT")
        ktA = sb("ktA", [128, 4, 128], BF16)
        ktB = sb("ktB", [128, 4, 128], BF16)
        B_kt = Buf("kt")
        S.op("pool", I("memset", ktA[:], 0.0), [], [B_kt])
        S.op("pool", I("memset", ktB[:], 0.0), [], [B_kt])
        vtok = sb("vtok", [128, 4, 4, 128], BF16)
        B_vtok = [Buf(f"vtok{i}") for i in range(4)]
        sc = sb("sc", [128, 4, 3, 8])
        B_sc = Buf("sc")
        sc8 = sb("sc8", [128, 8])
        B_sc8 = Buf("sc8")
        St = sb("St", [128, 4, 128])
        B_S = Buf("S")
        Stmp = sb("Stmp", [128, 4, 128])
        B_Stmp = Buf("Stmp")
        Sp = sb("Sp", [128, 4, 128], BF16)
        B_Sp = Buf("Sp")
        ATs = sb("ATs", [128, 4, 128], BF16)
        B_ATs = Buf("ATs")
        o1s = sb("o1s", [128, 4, 128])
        B_o1s = Buf("o1s")
        sl_o1 = slot("o1")
        DRAM_o1 = [Buf(f"dram_o1_{i}") for i in range(cfg.NB)]
        ptr = ps("ptr", [128, 4, 128], BF16)
        B_ptr = Buf("ptr")

        def hgrn_gates(l, dirn, srcw, base_a):
            for h in range(4):
                p, Bp = proj_fm(l, srcw, base_a + h)
                tA, BA = get_tmp()
                tK, BK = get_tmp()
                tB, BB = get_tmp()
                tD, BD = get_tmp()
                tE, BE = get_tmp()
                v = lambda t: t[:].rearrange("p (c t) -> p c t", t=64)
                S.op("act", I("activation", out=tA[:], in_=p[:], func=AF.Tanh, scale=0.5), [Bp], [BA])
                S.op("dve", I("tensor_scalar", out=tA[:], in0=tA[:], scalar1=lbc1[:, l, dirn, h:h + 1], scalar2=lbc0[:, l, dirn, h:h + 1], op0=ALU.mult, op1=ALU.add),
                     [BA, B_const], [BA])
                S.op("pool", I("tensor_scalar", out=tK[:], in0=tA[:], scalar1=-1.0, scalar2=1.0, op0=ALU.mult, op1=ALU.add), [BA], [BK])
                S.op("act", I("activation", out=tA[:], in_=tA[:], func=AF.Ln), [BA, BK], [BA])
                S.op("dve", I("tensor_tensor_scan", out=tB[:], data0=cmask[:], data1=tA[:], initial=0.0, op0=ALU.mult, op1=ALU.add), [BA, B_const], [BB])
                if dirn == 0:
                    S.op("dve", I("tensor_tensor", out=v(tD), in0=v(tB), in1=v(tB)[:, :, 31:32].to_broadcast([128, 8, 64]), op=ALU.subtract), [BB], [BD])
                    S.op("act", I("activation", out=sc[:, h, 0, :], in_=v(tB)[:, :, 31], func=AF.Exp), [BB], [B_sc])
                    S.op("act", I("activation", out=sc[:, h, 1, :], in_=v(tB)[:, :, 63], func=AF.Exp), [BB], [B_sc])
                    S.op("act", I("activation", out=sc[:, h, 2, :], in_=v(tD)[:, :, 63], func=AF.Exp), [BD], [B_sc])
                else:
                    S.op("dve", I("tensor_tensor", out=tA[:], in0=tB[:], in1=tA[:], op=ALU.subtract), [BB, BA], [BA])
                    S.op("dve", I("tensor_tensor", out=v(tD), in0=v(tA)[:, :, 32:33].to_broadcast([128, 8, 64]), in1=v(tA), op=ALU.subtract), [BA], [BD])
                    S.op("dve", I("tensor_tensor", out=sc8[:], in0=v(tB)[:, :, 63], in1=v(tA)[:, :, 32], op=ALU.subtract), [BB, BA], [B_sc8])
                    S.op("act", I("activation", out=sc[:, h, 0, :], in_=sc8[:], func=AF.Exp), [B_sc8], [B_sc])
                    S.op("act", I("activation", out=sc[:, h, 1, :], in_=v(tB)[:, :, 63], func=AF.Exp), [BB], [B_sc])
                    S.op("act", I("activation", out=sc[:, h, 2, :], in_=v(tD)[:, :, 0], func=AF.Exp), [BD], [B_sc])
                S.op("act", I("activation", out=tE[:], in_=tD[:], func=AF.Exp), [BD], [BE])
                S.op("dve", I("tensor_tensor", out=qtT[:, h, :], in0=qraw[:, h, :], in1=tE[:], op=ALU.mult), [B_qraw, BE], [B_qtT])
                S.op("act", I("activation", out=tB[:], in_=tD[:], func=AF.Exp, scale=-1.0), [BD, B_sc, B_sc8], [BB])
                S.op("pool", I("tensor_tensor", out=ktT[:, h, :], in0=tK[:], in1=tB[:], op=ALU.mult), [BK, BB], [B_ktT])

        def hgrn_q_and_v(l, srcw, base_q):
            for h in range(4):
                p, Bp = proj_fm(l, srcw, base_q + h)
                S.op("act", I("activation", out=qraw[:, h, :], in_=p[:], func=AF.Copy), [Bp], [B_qraw])

        def hgrn_vtok(bi):
            tsl = slice(bi * 128, (bi + 1) * 128)
            p, Bp = get_pp()
            for k in range(8):
                S.op("pe", I("matmul", p[:], lhsT=hT[:, k, tsl], rhs=wtm[:, k, 0:512], start=(k == 0), stop=(k == 7)), [B_hT, B_wtm], [Bp])
            S.op("act", I("activation", out=vtok[:, bi, :, :], in_=p[:].rearrange("p (h d) -> p h d", h=4), func=AF.Copy), [Bp], [B_vtok[bi]])

        def hgrn_block(l, dirn, bi, m):
            tsl = slice(bi * 128, (bi + 1) * 128)
            psc, B_psc = pbr[0][:].rearrange("p (h t) -> p h t", h=4), B_pbr[0]
            po, B_po = pbr[1][:].rearrange("p (h t) -> p h t", h=4), B_pbr[1]
            pob, B_pob = pbr[2][:].rearrange("p (h t) -> p h t", h=4), B_pbr[2]
            pS, B_pS = pmx[:], B_pmx
            for h in range(4):
                S.op("pe", I("transpose", out=ptr[:, h, :], in_=ktT[:, h, tsl], identity=ident_b[:]), [B_ktT, B_const], [B_ptr])
            S.op("dve", I("tensor_copy", out=ktA[0:64, :, :], in_=ptr[0:64, :, :]), [B_ptr], [B_kt])
            S.op("act", I("activation", out=ktB[64:128, :, :], in_=ptr[64:128, :, :], func=AF.Copy), [B_ptr], [B_kt])
            for h in range(4):
                S.op("pe", I("matmul", psc[:, h, :], lhsT=ktT[:, h, tsl], rhs=qtT[:, h, tsl], start=True, stop=True), [B_ktT, B_qtT], [B_psc])
            S.op("dve", I("tensor_tensor", out=ATs[:], in0=psc, in1=hmask2[:, dirn:dirn + 1, :].to_broadcast([128, 4, 128]), op=ALU.mult), [B_psc, B_const], [B_ATs])
            for h in range(4):
                S.op("pe", I("matmul", po[:, h, :], lhsT=vtok[:, bi, h, :], rhs=ATs[:, h, :], start=True, stop=True), [B_vtok[bi], B_ATs], [B_po])
            chunks = [(0, ktA), (1, ktB)] if dirn == 0 else [(1, ktB), (0, ktA)]
            for ci, (c, kt) in enumerate(chunks):
                gc = bi * 2 + c
                csl = slice(bi * 128 + c * 64, bi * 128 + (c + 1) * 64)
                bc = lambda k: sc[:, :, k, gc:gc + 1].to_broadcast([128, 4, 128])
                S.op("dve", I("tensor_tensor", out=Sp[:], in0=St[:], in1=bc(0), op=ALU.mult), [B_S, B_sc], [B_Sp])
                S.op("dve", I("tensor_tensor", out=St[:], in0=St[:], in1=bc(1), op=ALU.mult), [B_S, B_sc], [B_S])
                for h in range(4):
                    S.op("pe", I("matmul", pob[:, h, c * 64:(c + 1) * 64], lhsT=Sp[:, h, :], rhs=qtT[:, h, csl], start=True, stop=True), [B_Sp, B_qtT], [B_pob])
                for h in range(4):
                    S.op("pe", I("matmul", pS[:, h, :], lhsT=kt[:, h, :], rhs=vtok[:, bi, h, :], start=True, stop=True), [B_kt, B_vtok[bi]], [B_pS])
                S.op("dve", I("tensor_tensor", out=Stmp[:], in0=pS, in1=bc(2), op=ALU.mult), [B_pS, B_sc], [B_Stmp])
                S.op("dve", I("tensor_tensor", out=St[:], in0=St[:], in1=Stmp[:], op=ALU.add), [B_S, B_Stmp], [B_S])
            gblk = m * 4 + bi
            if dirn == 0:
                S.op("act", I("activation", out=o1s[:], in_=po, func=AF.Copy), [B_po], [B_o1s])
                S.op("dve", I("tensor_tensor", out=o1s[:], in0=o1s[:], in1=pob, op=ALU.add), [B_o1s, B_pob], [B_o1s])
                dma("sp", o1_d[:, gblk], o1s[:], [B_o1s], [DRAM_o1[gblk]], sl_o1)
            else:
                dma("sp", o1s[:], o1_d[:, gblk], [DRAM_o1[gblk]], [B_o1s], sl_o1)
                osum, Bos = get_tmp()
                rt, Brt = get_tmp()
                osv = osum[:].rearrange("p (h t) -> p h t", h=4)
                S.op("dve", I("tensor_tensor", out=osv, in0=po, in1=o1s[:], op=ALU.add), [B_po, B_o1s], [Bos])
                S.op("dve", I("tensor_tensor", out=osv, in0=osv, in1=pob, op=ALU.add), [Bos, B_pob], [Bos])
                S.op("act", I("activation", out=ya_scr[:, 0:512], in_=osum[:], func=AF.Square), [Bos], [B_ya_scr])
                S.op("pe", I("matmul", pst[:], lhsT=ones_b[:], rhs=ya_scr[:, 0:512], start=True, stop=True), [B_ya_scr, B_const], [B_pst])
                S.op("act", I("activation", out=rt[:], in_=pst[:], func=AF.Ln, scale=1.0 / 128, bias=epsb[:]), [B_pst, B_const], [Brt])
                S.op("act", I("activation", out=rt[:], in_=rt[:], func=AF.Exp, scale=-0.5), [Brt], [Brt])
                S.op("dve", I("tensor_tensor", out=osum[:], in0=osum[:], in1=rt[:], op=ALU.mult), [Bos, Brt], [Bos])
                for h in range(4):
                    S.op("dve", I("scalar_tensor_tensor", out=ya[:, h, tsl], in0=osv[:, h, :], scalar=hgain[:, l, h:h + 1], in1=zs[:, h, tsl], op0=ALU.mult, op1=ALU.mult),
                         [Bos, B_const, B_zs[h]], [B_ya])

        def sweep1_mt(l, m):
            norm_mt(l, m)
            pp_mode[0] = "wide"
            hgrn_q_and_v(l, w1fm_b, 4)
            hgrn_gates(l, 0, w1fm_b, 0)
            for bi in range(4):
                hgrn_vtok(bi)
                hgrn_block(l, 0, bi, m)
        sl_out = slot("out")

        def xsrc(l):
            return xT_d if l == 0 else out_d

        xview = lambda t, m: t.ap().rearrange("(c p) t -> p c t", p=128)[:, :, m * 512:(m + 1) * 512]

        def load_w_chunk(l, src_b, j):
            i = _wb_i[0] % NWB
            _wb_i[0] += 1
            dma("sp", wbuf[i][:], src_b[l, j], [B_wcast[l]], [B_wbuf[i]], sl_wbuf[i])
            return wbuf[i], B_wbuf[i]

        def tanh_gate(dst, src, scale):
            return I("activation", out=dst, in_=src, func=AF.Tanh, scale=scale)

        def norm_mt(l, m):
            dma("sp", xt[:], xview(xsrc(l), m), [DRAM_x[m]], [B_xt], sl_xt)
            for c in range(8):
                S.op("act", I("activation", out=sq[:, c, :], in_=xt[:, c, :], func=AF.Square), [B_xt], [B_mg[c]])
            for c in range(8):
                S.op("pe", I("matmul", pst[:], lhsT=ones_b[:], rhs=sq[:, c, :], start=(c == 0), stop=(c == 7)),
                     [B_mg[c], B_const], [B_pst])
            S.op("act", I("activation", out=rtmp[:], in_=pst[:], func=AF.Ln, scale=1.0 / D, bias=epsb[:]), [B_pst, B_const], [B_rtmp])
            S.op("act", I("activation", out=rstd[:], in_=rtmp[:], func=AF.Exp, scale=-0.5), [B_rtmp], [B_rstd])
            for c in range(8):
                S.op("dve", I("scalar_tensor_tensor", out=hT[:, c, :], in0=xt[:, c, :], scalar=ngain[:, l, c:c + 1],
                                                                   in1=rstd[:], op0=ALU.mult, op1=ALU.mult),
                     [B_xt, B_rstd, B_const], [B_hT])

        def proj_fm(l, src_b, j):
            w, Bw = load_w_chunk(l, src_b, j)
            p, Bp = get_pp()
            for k in range(8):
                S.op("pe", I("matmul", p[:], lhsT=w[:, k, :], rhs=hT[:, k, :], start=(k == 0), stop=(k == 7)),
                     [Bw, B_hT], [Bp])
            return p, Bp

        def layer_weights(l):
            dma("sp", wtm[:], w2tm_b[l], [B_wcast[l]], [B_wtm], sl_wtm)
            dma("sp", lng[:], lng_d[:, l, :], [], [B_lw], sl_wl)
            dma("sp", lnb[:], lnb_d[:, l, :], [], [B_lw], sl_wl)
            dma("sp", bsp[:], bsp_d[:, l], [], [B_lw], sl_wl)
            dma("pool", wsT[:], wsT_d[:, l], [], [B_lw], sl_wl)

        GELU_C = 0.7978845608028654

        def gelu_from_psum(p, Bp, dst, Bdst, eng2="pool"):
            t1, B1 = get_tmp()
            t2, B2 = get_tmp()
            S.op("act", I("activation", out=t1[:], in_=p[:], func=AF.Square), [Bp], [B1])
            S.op("dve", I("tensor_scalar", out=t1[:], in0=t1[:], scalar1=0.044715, scalar2=1.0, op0=ALU.mult, op1=ALU.add), [B1], [B1])
            S.op("dve", I("tensor_tensor", out=t2[:], in0=t1[:], in1=p[:], op=ALU.mult), [B1, Bp], [B2])
            S.op("act", I("activation", out=t1[:], in_=t2[:], func=AF.Tanh, scale=GELU_C), [B2], [B1])
            S.op("dve", I("scalar_tensor_tensor", out=dst, in0=t1[:], scalar=1.0, in1=p[:], op0=ALU.add, op1=ALU.mult), [B1, Bp], [Bdst])

        def silu_from_psum(p, Bp, dst, Bdst):
            t1, B1 = get_tmp()
            S.op("act", I("activation", out=t1[:], in_=p[:], func=AF.Tanh, scale=0.5), [Bp], [B1])
            S.op("dve", I("scalar_tensor_tensor", out=dst, in0=t1[:], scalar=1.0, in1=p[:], op0=ALU.add, op1=ALU.mult), [B1, Bp], [Bdst])

        def sweep2_mt(l, m):
            norm_mt(l, m)
            base = {}
            cnt = 0
            for kind in FM2:
                base.setdefault(kind, cnt)
                cnt += 1
            if cfg.do_b:
                pp_mode[0] = "wide"
                for j in range(4):
                    p, Bp = proj_fm(l, w2fm_b, base["uB"] + j)
                    gelu_from_psum(p, Bp, ub[:, j, :], B_ub[j])
                for j in range(4):
                    p, Bp = proj_fm(l, w2fm_b, base["zB"] + j)
                    silu_from_psum(p, Bp, zs[:, 4 + j, :], B_zs[4 + j])
                pp_mode[0] = "nopmx"
                for blk in range(4):
                    tsl = slice(blk * 128, (blk + 1) * 128)
                    p, Bp = get_pp()
                    for k in range(8):
                        S.op("pe", I("matmul", p[:], lhsT=hT[:, k, tsl], rhs=wtm[:, k, 512:1024],
                                                                          start=(k == 0), stop=(k == 7)), [B_hT, B_wtm], [Bp])
                    vt, B_vt = get_tmp()
                    vt2, B_vt2 = get_tmp()
                    gelu_from_psum(p, Bp, vt[:], B_vt)
                    S.op("dve", I("bn_stats", out=bnst[:], in_=vt[:]), [B_vt], [B_bn])
                    S.op("dve", I("bn_aggr", out=bnag[:], in_=bnst[:]), [B_bn], [B_bn])
                    S.op("dve", I("tensor_scalar", out=bnag[:, 1:2], in0=bnag[:, 1:2], scalar1=0.25, scalar2=EPS, op0=ALU.mult, op1=ALU.add),
                         [B_bn], [B_bn])
                    S.op("pool", I("tensor_tensor", out=bnag[:, 1:2], in0=bnag[:, 1:2], in1=mhalf1[:], op=ALU.pow), [B_bn, B_const], [B_bn])
                    S.op("dve", I("tensor_scalar", out=bnag[:, 1:2], in0=bnag[:, 1:2], scalar1=0.5, scalar2=None, op0=ALU.mult), [B_bn], [B_bn])
                    S.op("dve", I("tensor_scalar", out=vt2[:], in0=vt[:], scalar1=bnag[:, 0:1], scalar2=bnag[:, 1:2],
                                                          op0=ALU.subtract, op1=ALU.mult), [B_vt, B_bn], [B_vt2])
                    S.op("dve", I("tensor_tensor", out=vt2[:], in0=vt2[:], in1=lng[:], op=ALU.mult), [B_vt2, B_lw], [B_vt2])
                    S.op("dve", I("tensor_tensor", out=vn[:], in0=vt2[:], in1=lnb[:], op=ALU.add), [B_vt2, B_lw], [B_vn])
                    for g in range(4):
                        S.op("pe", I("matmul", pmx[:, g, :], lhsT=vn[:, g * 128:(g + 1) * 128], rhs=wsT[:, g, :], start=True, stop=True),
                             [B_vn, B_lw], [B_pmx])
                    t1, B1 = get_tmp()
                    t1v = t1[:].rearrange("p (g t) -> p g t", g=4)
                    S.op("dve", I("tensor_tensor", out=t1v, in0=pmx[:], in1=bsp[:], op=ALU.add), [B_pmx, B_lw], [B1])
                    S.op("pool", I("tensor_tensor", out=t1v, in0=t1v, in1=ub[:, :, tsl], op=ALU.mult), [B1] + B_ub, [B1])
                    S.op("dve", I("tensor_tensor", out=yb[:, :, tsl], in0=t1v, in1=zs[:, 4:8, tsl], op=ALU.mult),
                         [B1] + B_zs[4:8], [B_yb])
            if cfg.do_c:
                pp_mode[0] = "wide"
                top = (m == n_mt - 1)
                has_lo = (m > 0)
                t0 = m * 512 - 128
                if has_lo:
                    dma("sp", cs[:, 0, :], cosT_d[:, t0:t0 + 640], [], [B_cs], sl_cs)
                    dma("sp", cs[:, 1, :], sinT_d[:, t0:t0 + 640], [], [B_cs], sl_cs)
                    dma("sp", xh[:], xsrc(l).ap().rearrange("(c p) t -> p c t", p=128)[:, :, t0:t0 + 128], [DRAM_x[m - 1]], [B_xh], sl_xh)
                    for c in range(8):
                        S.op("act", I("activation", out=sqh[:, c, :], in_=xh[:, c, :], func=AF.Square), [B_xh], [B_sqh])
                    for c in range(8):
                        S.op("pe", I("matmul", pst[:, 0:128], lhsT=ones_b[:], rhs=sqh[:, c, :], start=(c == 0), stop=(c == 7)),
                             [B_sqh, B_const], [B_pst])
                    S.op("act", I("activation", out=rtmp[:, 0:128], in_=pst[:, 0:128], func=AF.Ln, scale=1.0 / D, bias=epsb[:]), [B_pst, B_const], [B_rtmp])
                    S.op("act", I("activation", out=rtmp[:, 128:256], in_=rtmp[:, 0:128], func=AF.Exp, scale=-0.5), [B_rtmp], [B_rtmp])
                    for c in range(8):
                        S.op("dve", I("scalar_tensor_tensor", out=hTh[:, c, :], in0=xh[:, c, :], scalar=ngain[:, l, c:c + 1],
                                                                           in1=rtmp[:, 128:256], op0=ALU.mult, op1=ALU.mult),
                             [B_xh, B_rtmp, B_const], [B_hTh])
                else:
                    dma("sp", cs[:, 0, 128:640], cosT_d[:, 0:512], [], [B_cs], sl_cs)
                    dma("sp", cs[:, 1, 128:640], sinT_d[:, 0:512], [], [B_cs], sl_cs)
                if not top:
                    S.op("pool", I("tensor_copy", out=kz[:, :, :, 640:768], in_=kz[:, :, :, 128:256]), [B_kr], [B_kr])
                    S.op("pool", I("tensor_copy", out=vaug[:, 5, :, :], in_=vaug[:, 1, :, :]), [B_vaug], [B_vaug])

                def qk_post(p, Bp, n, which, dst, Bdst, csl):
                    t1, B1 = get_tmp()
                    t2, B2 = get_tmp()
                    t3, B3 = get_tmp()
                    sqb, Bsqb = ya_scr, B_ya_scr
                    S.op("act", I("activation", out=sqb[:, 0:n], in_=p[:, 0:n], func=AF.Square), [Bp], [Bsqb])
                    S.op("dve", I("tensor_scalar", out=sqb[:, 512:512 + n], in0=p[:, 0:n], scalar1=qkg[:, l, which:which + 1], scalar2=None, op0=ALU.mult),
                         [Bp, B_const], [Bsqb])
                    S.op("pe", I("matmul", pst[:, 0:n], lhsT=bd_b[:], rhs=sqb[:, 0:n], start=True, stop=True), [Bsqb, B_const], [B_pst])
                    pr, Bpr = get_pp()
                    S.op("pe", I("matmul", pr[:, 0:n], lhsT=rot_b[:], rhs=sqb[:, 512:512 + n], start=True, stop=True), [Bsqb, B_const], [Bpr])
                    S.op("act", I("activation", out=t1[:, 0:n], in_=pst[:, 0:n], func=AF.Ln, scale=1.0 / 64, bias=epsb[:]), [B_pst, B_const], [B1])
                    S.op("act", I("activation", out=t1[:, 0:n], in_=t1[:, 0:n], func=AF.Exp, scale=-0.5), [B1], [B1])
                    S.op("pool", I("tensor_tensor", out=t2[:, 0:n], in0=sqb[:, 512:512 + n], in1=cs[:, 0, csl], op=ALU.mult), [Bsqb, B_cs], [B2])
                    S.op("dve", I("tensor_tensor", out=t3[:, 0:n], in0=pr[:, 0:n], in1=cs[:, 1, csl], op=ALU.mult), [Bpr, B_cs], [B3])
                    S.op("dve", I("tensor_tensor", out=t2[:, 0:n], in0=t2[:, 0:n], in1=t3[:, 0:n], op=ALU.add), [B2, B3], [B2])
                    if which == 0:
                        S.op("dve", I("tensor_tensor", out=dst, in0=t2[:, 0:n], in1=t1[:, 0:n], op=ALU.mult), [B2, B1], [Bdst])
                    else:
                        jh, c0 = dst
                        S.op("dve", I("tensor_tensor", out=kz[0:64, 0, jh, c0:c0 + n], in0=t2[0:64, 0:n], in1=t1[0:64, 0:n], op=ALU.mult), [B2, B1], [Bdst])
                        S.op("dve", I("tensor_tensor", out=kz[64:128, 1, jh, c0:c0 + n], in0=t2[64:128, 0:n], in1=t1[64:128, 0:n], op=ALU.mult), [B2, B1], [Bdst])

                for j in range(4):
                    p, Bp = proj_fm(l, w2fm_b, base["qC"] + j)
                    qk_post(p, Bp, 512, 0, qr[:, j, :], B_qr[j], slice(128, 640))
                for j in range(2):
                    w, Bw = load_w_chunk(l, w2fm_b, base["kC"] + j)
                    p, Bp = get_pp()
                    for k in range(8):
                        S.op("pe", I("matmul", p[:], lhsT=w[:, k, :], rhs=hT[:, k, :], start=(k == 0), stop=(k == 7)), [Bw, B_hT], [Bp])
                    qk_post(p, Bp, 512, 1, (j, 128), B_kr, slice(128, 640))
                    if has_lo:
                        p, Bp = get_pp()
                        for k in range(8):
                            S.op("pe", I("matmul", p[:, 0:128], lhsT=w[:, k, :], rhs=hTh[:, k, :], start=(k == 0), stop=(k == 7)), [Bw, B_hTh], [Bp])
                        qk_post(p, Bp, 128, 1, (j, 0), B_kr, slice(0, 128))
                for j in range(4):
                    p, Bp = proj_fm(l, w2fm_b, base["zC"] + j)
                    silu_from_psum(p, Bp, zs[:, 8 + j, :], B_zs[8 + j])
                for sl_i in ([0] if has_lo else []) + [1, 2, 3, 4]:
                    p, Bp = get_pp()
                    for k in range(8):
                        if sl_i == 0:
                            S.op("pe", I("matmul", p[:, 0:128], lhsT=hTh[:, k, :], rhs=wtm[:, k, 1024:1152], start=(k == 0), stop=(k == 7)),
                                 [B_hTh, B_wtm], [Bp])
                        else:
                            tsl = slice((sl_i - 1) * 128, sl_i * 128)
                            S.op("pe", I("matmul", p[:, 0:128], lhsT=hT[:, k, tsl], rhs=wtm[:, k, 1024:1152], start=(k == 0), stop=(k == 7)),
                                 [B_hT, B_wtm], [Bp])
                    pv = p[:, 0:128].rearrange("p (h d) -> p h d", h=2)
                    S.op("act", I("activation", out=vaug[:, sl_i, :, 0:64], in_=pv, func=AF.Copy), [Bp], [B_vaug])
                    S.op("dve", I("tensor_copy", out=vaug[:, sl_i, :, 128:192], in_=pv), [Bp], [B_vaug])
                kc_stage = int(os.environ.get("KC_STAGE", "9"))
                if top and kc_stage >= 2:
                    S.op("dve", I("tensor_copy", out=ccs[0:64, 512:768].rearrange("p (h t) -> p h t", h=2), in_=kz[0:64, 0, :, 512:640]), [B_kr], [B_ccs])
                    S.op("dve", I("tensor_copy", out=ccs[64:128, 512:768].rearrange("p (h t) -> p h t", h=2), in_=kz[64:128, 1, :, 512:640]), [B_kr], [B_ccs])
                    S.op("dve", I("tensor_copy", out=ccs[:, 768:896].rearrange("p (h d) -> p h d", h=2), in_=vaug[:, 4, :, 0:64]), [B_vaug], [B_ccs])
                    if not cfg.do_a:
                        S.op("dve", I("memset", ccs[:, 0:512], 0.0), [], [B_ccs])
                    else:
                        S.op("dve", I("tensor_copy", out=ccs[:, 0:512], in_=St[:].rearrange("p h v -> p (h v)")), [B_S], [B_ccs])
                    dma("pool", cc_in[l].ap(), ccs, [B_ccs], [DRAM_cc], sl_cc)
                    o = S.op("pool", I("collective_compute", "AllGather", ALU.bypass, replica_groups=[[0, 1], [2, 3], [4, 5], [6, 7]],
                                                                  ins=[cc_in[l].ap().opt()], outs=[cc_out[l].ap().opt()]), [DRAM_cc], [DRAM_cc])
                    o.signal = True
                    dma("pool", ccg, cc_out[l].ap().rearrange("(r p) n -> p r n", p=128), [DRAM_cc], [B_ccg], sl_cc)
                    S.op("dve", I("tensor_scalar", out=ccp, in0=ccg[:, 0, :], scalar1=selt[:, 0:1], scalar2=None, op0=ALU.mult), [B_ccg, B_const], [B_ccp])
                    S.op("dve", I("scalar_tensor_tensor", out=ccp, in0=ccg[:, 1, :], scalar=selt[:, 1:2], in1=ccp, op0=ALU.mult, op1=ALU.add),
                         [B_ccg, B_const, B_ccp], [B_ccp])
                    S.op("dve", I("tensor_copy", out=kz[0:64, 0, :, 640:768], in_=ccp[0:64, 512:768].rearrange("p (h t) -> p h t", h=2)), [B_ccp], [B_kr])
                    S.op("dve", I("tensor_copy", out=kz[64:128, 1, :, 640:768], in_=ccp[64:128, 512:768].rearrange("p (h t) -> p h t", h=2)), [B_ccp], [B_kr])
                    S.op("dve", I("tensor_copy", out=vaug[:, 5, :, 0:64], in_=ccp[:, 768:896].rearrange("p (h d) -> p h d", h=2)), [B_ccp], [B_vaug])
                    S.op("dve", I("tensor_copy", out=vaug[:, 5, :, 128:192], in_=ccp[:, 768:896].rearrange("p (h d) -> p h d", h=2)), [B_ccp], [B_vaug])
                    if cfg.do_a:
                        S.op("dve", I("tensor_copy", out=St[:].rearrange("p h v -> p (h v)"), in_=ccp[:, 0:512]), [B_ccp], [B_S])
                for sb_i in ((4, 3, 2, 1) if kc_stage >= 3 else ()):
                    jglob = m * 4 + sb_i - 1
                    tsl = slice((sb_i - 1) * 128, sb_i * 128)
                    kbs = []
                    if jglob > 0:
                        kbs.append((sb_i - 1, 0))
                    kbs.append((sb_i, None))
                    kbs.append((sb_i + 1, 2 if (top and sb_i == 4) else 1))
                    for h in range(2):
                        pssv = [pbr[i][:].rearrange("p (e n) -> p e n", e=2) for i in range(3)]
                        for ki, (slk, mk) in enumerate(kbs):
                            for e_ in range(2):
                                rows = slice(e_ * 64, (e_ + 1) * 64)
                                S.op("pe", I("matmul",
                                    pssv[ki][:, e_, :].rearrange("p (c t) -> p c t", c=2), lhsT=kz[:, e_, h, slk * 128:(slk + 1) * 128],
                                    rhs=qr[:, 2 * h:2 * h + 2, tsl], start=True, stop=True),
                                    [B_kr] + B_qr, [B_pbr[ki]])
                            S.op("act", I("activation", out=pt[:, ki, :, :], in_=pssv[ki], func=AF.Exp, scale=0.125), [B_pbr[ki]], [B_ptk[ki]])
                            if mk is not None and kc_stage >= 4:
                                ptv = pt[:, ki, :, :].rearrange("p e (c t) -> p (e c) t", c=2)
                                S.op("dve" if mk == 0 else "pool", I("tensor_tensor", out=ptv, in0=ptv, in1=amask[:, mk:mk + 1, :].to_broadcast([128, 4, 128]), op=ALU.mult),
                                     [B_ptk[ki], B_const], [B_ptk[ki]])
                        pso = pmx[:].rearrange("p a b -> p (a b)").rearrange("p (e n) -> p e n", e=2)
                        for e_ in (range(2) if kc_stage >= 5 else ()):
                            for ki, (slk, mk) in enumerate(kbs):
                                S.op("pe", I("matmul", pso[:, e_, :], lhsT=vaug[:, slk, h, e_ * 64:e_ * 64 + 128], rhs=pt[:, ki, e_, :],
                                                                                       start=(ki == 0), stop=(ki == len(kbs) - 1)), [B_vaug, B_ptk[ki]], [B_pmx])
                        for e_ in (range(2) if kc_stage >= 6 else ()):
                            nr = slice(0, 64) if e_ == 0 else slice(64, 128)
                            dr = slice(64, 128) if e_ == 0 else slice(0, 64)
                            for c in range(2):
                                head = 2 * (2 * h + c) + e_
                                S.op("dve", I("tensor_scalar",
                                    out=dtmp[nr, c * 128:(c + 1) * 128], in0=pso[dr, e_, c * 128:(c + 1) * 128], scalar1=esink[dr, l, head:head + 1], scalar2=None, op0=ALU.add),
                                    [B_pmx, B_const], [B_dtmp])
                        S.op("act", I("activation", out=dtmp[:], in_=dtmp[:], func=AF.Ln), [B_dtmp], [B_dtmp])
                        S.op("act", I("activation", out=dtmp[:], in_=dtmp[:], func=AF.Exp, scale=-1.0), [B_dtmp], [B_dtmp])
                        for e_ in (range(2) if kc_stage >= 6 else ()):
                            nr = slice(0, 64) if e_ == 0 else slice(64, 128)
                            S.op("dve", I("tensor_tensor", out=dtmp[nr, :], in0=pso[nr, e_, :], in1=dtmp[nr, :], op=ALU.mult), [B_pmx, B_dtmp], [B_dtmp])
                        S.op("pool", I("tensor_tensor", out=yc[:, 2 * h:2 * h + 2, tsl], in0=dtmp[:].rearrange("p (c t) -> p c t", c=2),
                                       in1=zs[:, 8 + 2 * h:8 + 2 * h + 2, tsl], op=ALU.mult),
                             [B_dtmp] + B_zs[8:12], [B_yc])
            if cfg.do_a:
                pp_mode[0] = "wide"
                hgrn_q_and_v(l, w2fm_b, base["qA"])
                hgrn_gates(l, 1, w2fm_b, base["a2"])
                for j in range(4):
                    p, Bp = proj_fm(l, w2fm_b, base["zA"] + j)
                    silu_from_psum(p, Bp, zs[:, j, :], B_zs[j])
                for bi in (3, 2, 1, 0):
                    hgrn_vtok(bi)
                    hgrn_block(l, 1, bi, m)
            pp_mode[0] = "narrow"
            scal = {0: 1.0, 1: 0.5, 2: 1.0}
            for dc in range(8):
                dsl = slice(dc * 128, (dc + 1) * 128)
                acc, Bacc = get_tmp()
                first = True
                wi = _wbrc_i[0] % 2
                _wbrc_i[0] += 1
                dma("sp", wbrc[wi][:], wbr_b[l, dc], [B_wcast[l]], [B_wbrc[wi]], sl_wbrc[wi])
                for bi, (on, ysrc, By) in enumerate(((cfg.do_a, ya, B_ya), (cfg.do_b, yb, B_yb), (cfg.do_c, yc, B_yc))):
                    if not on:
                        continue
                    for k in range(4):
                        S.op("pe", I("matmul", pbr[bi][:], lhsT=wbrc[wi][:, bi, k, :], rhs=ysrc[:, k, :],
                                                                                     start=(k == 0), stop=(k == 3)), [B_wbrc[wi], By], [B_pbr[bi]])
                    pg, Bpg = proj_fm(l, w2fm_b, base[("gA", "gB", "gC")[bi]] + dc)
                    gtile, Bg = get_tmp()
                    S.op("act", tanh_gate(gtile[:], pg[:], 0.5), [Bpg], [Bg])
                    g = gtile[:]
                    if first:
                        S.op("dve", I("scalar_tensor_tensor", out=acc[:], in0=g, scalar=1.0, in1=pbr[bi][:], op0=ALU.add, op1=ALU.mult),
                             [Bg, B_pbr[bi]], [Bacc])
                        if scal[bi] != 1.0:
                            S.op("dve", I("tensor_scalar", out=acc[:], in0=acc[:], scalar1=scal[bi], scalar2=None, op0=ALU.mult), [Bacc], [Bacc])
                        first = False
                    else:
                        t2, B2 = get_tmp()
                        S.op("dve", I("scalar_tensor_tensor", out=t2[:], in0=g, scalar=1.0, in1=pbr[bi][:], op0=ALU.add, op1=ALU.mult),
                             [Bg, B_pbr[bi]], [B2])
                        S.op("dve", I("scalar_tensor_tensor", out=acc[:], in0=t2[:], scalar=scal[bi], in1=acc[:], op0=ALU.mult, op1=ALU.add),
                             [B2, Bacc], [Bacc])
                S.op("act", I("activation", out=mg[:, dc, :], in_=acc[:], func=AF.Copy), [Bacc], [B_mg[dc]])
            for ec in range(8):
                esl = slice(ec * 128, (ec + 1) * 128)
                w, Bw = load_w_chunk(l, wo_b, ec)
                p, Bp = get_pp()
                for k in range(8):
                    S.op("pe", I("matmul", p[:], lhsT=w[:, k, :], rhs=mg[:, k, :], start=(k == 0), stop=(k == 7)),
                         [Bw, B_mg[k]], [Bp])
                S.op("dve", I("scalar_tensor_tensor", out=xo[:, ec, :], in0=p[:], scalar=0.25, in1=xt[:, ec, :], op0=ALU.mult, op1=ALU.add), [Bp, B_xt], [B_xo])
            dma("sp", xview(out_d, m), xo[:], [B_xo], [DRAM_x[m]], sl_xo)

        import os
        stop = os.environ.get("KSTOP", "")
        for l in range(depth):
            S.epoch = l
            if stop == "cast":
                for m in range(n_mt):
                    dma("sp", xt[:], xview(xsrc(l), m), [DRAM_x[m]] + B_wcast, [B_xt], sl_xt)
                    dma("sp", xview(out_d, m), xt[:], [B_xt], [DRAM_x[m]], sl_xo)
                continue
            layer_weights(l)
            if stop == "lw":
                for m in range(n_mt):
                    dma("sp", xt[:], xview(xsrc(l), m), [DRAM_x[m]] + B_wcast + [B_wtm, B_lw], [B_xt], sl_xt)
                    dma("sp", xview(out_d, m), xt[:], [B_xt], [DRAM_x[m]], sl_xo)
                continue
            if cfg.do_a:
                S.op("dve", I("memset", St[:], 0.0), [], [B_S])
                for m in range(n_mt):
                    sweep1_mt(l, m)
            for m in reversed(range(n_mt)):
                sweep2_mt(l, m)
        S.epoch = depth
        S.op("sp", I("nop"), reads=[DRAM_x[m] for m in range(n_mt)], writes=[])

        with nc.Block() as block:
            S.finalize(engsems, block)
        build_nc.last_stats = S.stats
    return nc


def run(inputs, cfg, trace=False):
    nc = build_nc(cfg)
    in_maps = [prep_core_inputs(inputs, c, cfg) for c in range(NCORES)]
    res = run_bass_kernel_spmd(nc, in_maps, core_ids=list(range(NCORES)), trace=trace)
    T = cfg.T
    L = 2 * T
    B = NCORES // 2
    out = np.empty((B, L, D), np.float32)
    for c in range(NCORES):
        o = np.asarray(res.results[c]["out"]).T
        if c % 2 == 0:
            out[c // 2, :T] = o
        else:
            out[c // 2, T:] = o[::-1]
    return out, res


def kernel(**inputs):
    cfg = Cfg()
    out, _ = run(inputs, cfg)
    return out
```

```python
import numpy as np
import concourse.bass as bass
import concourse.mybir as mybir
from concourse.bass_utils import run_bass_kernel_spmd

F32 = mybir.dt.float32
BF16 = mybir.dt.bfloat16
ALU = mybir.AluOpType
AF = mybir.ActivationFunctionType

D = 1024
DEPTH = 4
EPS = 1e-6
NCORES = 8
SAME_ENGINE_SYNC = True


def I(name, *args, **kw):
    return lambda e: getattr(e, name)(*args, **kw)


class Buf:
    __slots__ = ("name", "last_w", "readers")

    def __init__(self, name):
        self.name = name
        self.last_w = None
        self.readers = []


class Slot:
    def __init__(self, sem, name):
        self.sem = sem
        self.count = 0
        self.token = Buf("slot_" + name)


class Op:
    __slots__ = ("eng", "fn", "deps", "signal", "val", "sem", "slot", "idx", "epoch", "raw")


class Sched:
    ENGS = ("pe", "act", "dve", "pool", "sp")

    def __init__(self, nc, n_epochs):
        self.nc = nc
        self.ops = {e: [] for e in self.ENGS}
        self.epoch = 0
        self.n_epochs = n_epochs
        self.engsem = {}

    def op(self, eng, fn, reads=(), writes=(), slot=None):
        o = Op()
        o.eng = eng
        o.fn = fn
        o.signal = False
        o.val = None
        o.sem = None
        o.slot = slot
        o.epoch = self.epoch
        deps = []
        raw = set()
        writes = list(writes)
        if slot is not None:
            writes.append(slot.token)
        for b in reads:
            if b.last_w is not None:
                deps.append(b.last_w)
                raw.add(id(b.last_w))
        for b in writes:
            if b.last_w is not None:
                deps.append(b.last_w)
            deps.extend(b.readers)
        seen = set()
        dd = []
        for d in deps:
            if id(d) not in seen and d is not o:
                seen.add(id(d))
                dd.append(d)
        o.deps = dd
        o.raw = raw
        for b in reads:
            b.readers.append(o)
        for b in writes:
            b.last_w = o
            b.readers = []
        if slot is not None:
            slot.count += 1
            o.sem = slot.sem
            o.val = 16 * slot.count
        o.idx = len(self.ops[eng])
        self.ops[eng].append(o)
        return o

    def _needs_wait(self, cons, prod):
        if prod.slot is not None:
            return True
        if prod.eng == cons.eng and cons.slot is None:
            if prod.eng == "pe":
                return False
            return SAME_ENGINE_SYNC and (id(prod) in cons.raw)
        return True

    def finalize(self, sems, block):
        for e in self.ENGS:
            for o in self.ops[e]:
                for d in o.deps:
                    if self._needs_wait(o, d):
                        d.signal = True
        for e in self.ENGS:
            cnt = {}
            for o in self.ops[e]:
                if o.slot is not None:
                    pass
                elif o.signal:
                    cnt[o.epoch] = cnt.get(o.epoch, 0) + 1
                    o.sem = sems[(e, o.epoch)]
                    o.val = cnt[o.epoch]
        self.stats = {e: len(self.ops[e]) for e in self.ENGS}

        def emit(e, eng):
            waited = {}
            nwaits = 0
            for o in self.ops[e]:
                for d in o.deps:
                    if not self._needs_wait(o, d):
                        continue
                    key = id(d.sem)
                    if waited.get(key, 0) >= d.val:
                        continue
                    eng.wait_ge(d.sem, d.val)
                    nwaits += 1
                    waited[key] = d.val
                ins = o.fn(eng)
                if o.slot is not None:
                    ins.then_inc(o.sem, 16)
                elif o.signal:
                    ins.then_inc(o.sem, 1)
            self.stats[e + "_waits"] = nwaits

        @block.tensor
        def _(eng):
            emit("pe", eng)

        @block.scalar
        def _(eng):
            emit("act", eng)

        @block.vector
        def _(eng):
            emit("dve", eng)

        @block.gpsimd
        def _(eng):
            emit("pool", eng)

        @block.sync
        def _(eng):
            emit("sp", eng)


OFF = dict(qA=0, fAf=512, fAb=1024, iA=1536, zA=2048, uB=2560, vB=3072, zB=3584, qC=4096, kC=4608,
           vC=4736, zC=4864, gA=5376, gB=6400, gC=7424)

FM2 = (["a2"] * 4 + ["qA"] * 4 + ["zA"] * 4 + ["uB"] * 4 + ["zB"] * 4 + ["qC"] * 4 + ["kC"] * 2 + ["zC"] * 4
       + ["gA"] * 8 + ["gB"] * 8 + ["gC"] * 8)
FM1 = ["a1"] * 4 + ["qA"] * 4


def _fm_cols(kind_list, odd):
    out = []
    cnt = {}
    for kind in kind_list:
        j = cnt.get(kind, 0)
        cnt[kind] = j + 1
        if kind == "a1":
            base = OFF["fAb"] if odd else OFF["fAf"]
            cols = np.arange(base + j * 128, base + (j + 1) * 128)
        elif kind == "a2":
            base = OFF["fAf"] if odd else OFF["fAb"]
            cols = np.arange(base + j * 128, base + (j + 1) * 128)
        elif kind == "kC":
            c = np.arange(OFF["kC"] + j * 64, OFF["kC"] + (j + 1) * 64)
            cols = np.concatenate([c, c])
        else:
            base = OFF[kind]
            cols = np.arange(base + j * 128, base + (j + 1) * 128)
        out.append(cols)
    return out


def _tm_cols(sweep):
    if sweep == 1:
        return np.arange(OFF["iA"], OFF["iA"] + 512)
    return np.concatenate([np.arange(OFF["iA"], OFF["iA"] + 512), np.arange(OFF["vB"], OFF["vB"] + 512),
                           np.arange(OFF["vC"], OFF["vC"] + 128)])


NF1 = len(FM1)
NF2 = len(FM2)
TM1 = 512
TM2 = 1152


class Cfg:
    def __init__(self, n_mt=8, depth=DEPTH, do_a=True, do_b=True, do_c=True):
        self.n_mt = n_mt
        self.T = n_mt * 512
        self.NB = n_mt * 4
        self.depth = depth
        self.do_a = do_a
        self.do_b = do_b
        self.do_c = do_c


def prep_core_inputs(inp, core, cfg):
    T = cfg.T
    L = 2 * T
    b = core // 2
    odd = core % 2
    depth = cfg.depth
    f32 = np.float32
    pos = (np.arange(T) if not odd else (L - 1 - np.arange(T))).astype(np.int64)
    m = {}
    x = np.asarray(inp["x"])[b]
    m["xT"] = np.ascontiguousarray(x[pos, :].T).astype(f32)
    w_in = np.asarray(inp["w_in"])
    w1fm = np.empty((depth, NF1, 128, 8, 128), f32)
    w2fm = np.empty((depth, NF2, 128, 8, 128), f32)
    w1tm = np.empty((depth, 128, 8, TM1), f32)
    w2tm = np.empty((depth, 128, 8, TM2), f32)
    c1 = _fm_cols(FM1, odd)
    c2 = _fm_cols(FM2, odd)
    for l in range(depth):
        wl = w_in[l].reshape(8, 128, -1)
        for j, cols in enumerate(c1):
            w1fm[l, j] = wl[:, :, cols].transpose(1, 0, 2)
        for j, cols in enumerate(c2):
            w2fm[l, j] = wl[:, :, cols].transpose(1, 0, 2)
        w1tm[l] = wl[:, :, _tm_cols(1)].transpose(1, 0, 2)
        w2tm[l] = wl[:, :, _tm_cols(2)].transpose(1, 0, 2)
    m["w1fm"] = w1fm
    m["w2fm"] = w2fm
    m["w1tm"] = w1tm
    m["w2tm"] = w2tm
    wbr = np.empty((depth, 8, 128, 3, 4, 128), f32)
    for bi, key in enumerate(("w_branch_a", "w_branch_b", "w_branch_c")):
        w = np.asarray(inp[key])[:depth].reshape(depth, 4, 128, 8, 128)
        wbr[:, :, :, bi] = w.transpose(0, 3, 2, 1, 4)
    m["wbr"] = wbr
    w = np.asarray(inp["w_out"])[:depth].reshape(depth, 8, 128, 8, 128)
    m["wo"] = np.ascontiguousarray(w.transpose(0, 3, 2, 1, 4)).astype(f32)
    ng = np.asarray(inp["norm_gain"])[:depth]
    m["ngain"] = np.ascontiguousarray(ng.reshape(depth, 8, 128).transpose(2, 0, 1)).astype(f32)
    lb = np.asarray(inp["lb_logits"]).reshape(DEPTH, 2, 4, 128)
    if odd:
        lb = lb[:, ::-1]
    m["lbl"] = np.ascontiguousarray(lb.transpose(3, 0, 1, 2)).astype(f32)
    hg = np.asarray(inp["hg_norm_gain"])[:depth]
    m["hgain"] = np.ascontiguousarray(hg.transpose(2, 0, 1)).astype(f32)
    m["lng"] = np.ascontiguousarray(np.broadcast_to(np.asarray(inp["sg_ln_gain"])[:depth][None], (128, depth, 512))).astype(f32)
    m["lnb"] = np.ascontiguousarray(np.broadcast_to(np.asarray(inp["sg_ln_bias"])[:depth][None], (128, depth, 512))).astype(f32)
    ws = np.asarray(inp["w_spatial"])[:depth]
    bs = np.asarray(inp["b_spatial"])[:depth]
    if odd:
        ws = ws[:, :, ::-1, ::-1]
        bs = bs[:, :, ::-1]
    m["wsT"] = np.ascontiguousarray(ws.transpose(3, 0, 1, 2)).astype(f32)
    m["bsp"] = np.ascontiguousarray(np.broadcast_to(bs[None], (128, depth, 4, 128))).astype(f32)
    qg = np.asarray(inp["q_norm_gain"])[:depth]
    kg = np.asarray(inp["k_norm_gain"])[:depth]
    m["qkg"] = np.ascontiguousarray(np.stack([np.concatenate([qg, qg], 1), np.concatenate([kg, kg], 1)], 1).transpose(2, 0, 1)).astype(f32)
    m["sink"] = np.ascontiguousarray(np.broadcast_to(np.asarray(inp["sink_logits"])[:depth][None], (128, depth, 8))).astype(f32)
    half = 32
    inv_freq = (10000.0 ** (-np.arange(half, dtype=np.float32) / half)).astype(np.float32)
    ang = pos.astype(np.float32)[None, :] * inv_freq[:, None]
    cos = np.cos(ang).astype(f32)
    sin = np.sin(ang).astype(f32)
    m["cosT"] = np.ascontiguousarray(np.concatenate([cos, cos, cos, cos], 0))
    m["sinT"] = np.ascontiguousarray(np.concatenate([sin, sin, sin, sin], 0))
    ident = np.eye(128, dtype=f32)
    m["c_ident"] = ident
    bd = np.zeros((128, 128), f32)
    bd[:64, :64] = 1
    bd[64:, 64:] = 1
    m["c_bd"] = bd
    rot = np.zeros((128, 128), f32)
    for hb in (0, 64):
        for d in range(32):
            rot[hb + d + 32, hb + d] = -1.0
            rot[hb + d, hb + d + 32] = 1.0
    m["c_rot"] = rot
    j = np.arange(128)[:, None]
    i = np.arange(128)[None, :]
    masks = np.stack([(j >= i), (j <= i), (j + i >= 127)], 1).astype(f32)
    m["c_amask"] = np.ascontiguousarray(masks)
    s = np.arange(64)[:, None]
    t = np.arange(64)[None, :]
    h1 = (s <= t).astype(f32)
    h2 = (s >= t).astype(f32)
    m["c_hmask"] = np.ascontiguousarray(np.stack([np.concatenate([h1, h1], 0), np.concatenate([h2, h2], 0)], 1))
    cm = np.ones((128, 512), f32)
    cm[:, ::64] = 0.0
    m["c_cmask"] = cm
    ss_ = np.arange(128)[:, None]
    tt_ = np.arange(128)[None, :]
    same = (ss_ // 64) == (tt_ // 64)
    m["c_hmask2"] = np.ascontiguousarray(np.stack([(same & (ss_ <= tt_)), (same & (ss_ >= tt_))], 1).astype(f32))
    sel = np.zeros((128, 2), f32)
    sel[:, 1 - odd] = 1.0
    m["sel"] = sel
    return m


def build_nc(cfg):
    nc = bass.Bass("TRN2", target_bir_lowering=False)
    T = cfg.T
    depth = cfg.depth
    n_mt = cfg.n_mt

    def din(name, shape, dt=F32):
        return nc.dram_tensor(name, list(shape), dt, kind="ExternalInput")

    xT_d = din("xT", [D, T])
    w1fm_d = din("w1fm", [depth, NF1, 128, 8, 128])
    w2fm_d = din("w2fm", [depth, NF2, 128, 8, 128])
    w1tm_d = din("w1tm", [depth, 128, 8, TM1])
    w2tm_d = din("w2tm", [depth, 128, 8, TM2])
    wbr_d = din("wbr", [depth, 8, 128, 3, 4, 128])
    wo_d = din("wo", [depth, 8, 128, 8, 128])
    ngain_d = din("ngain", [128, depth, 8])
    lbl_d = din("lbl", [128, DEPTH, 2, 4])
    hgain_d = din("hgain", [128, depth, 4])
    lng_d = din("lng", [128, depth, 512])
    lnb_d = din("lnb", [128, depth, 512])
    wsT_d = din("wsT", [128, depth, 4, 128])
    bsp_d = din("bsp", [128, depth, 4, 128])
    qkg_d = din("qkg", [128, depth, 2])
    sink_d = din("sink", [128, depth, 8])
    cosT_d = din("cosT", [128, T])
    sinT_d = din("sinT", [128, T])
    c_ident_d = din("c_ident", [128, 128])
    c_bd_d = din("c_bd", [128, 128])
    c_rot_d = din("c_rot", [128, 128])
    c_amask_d = din("c_amask", [128, 3, 128])
    c_hmask_d = din("c_hmask", [128, 2, 64])
    sel_d = din("sel", [128, 2])
    c_cmask_d = din("c_cmask", [128, 512])
    c_hmask2_d = din("c_hmask2", [128, 2, 128])
    out_d = nc.dram_tensor("out", [D, T], F32, kind="ExternalOutput")

    w1fm_b = nc.dram_tensor("w1fm_b", [depth, NF1, 128, 8, 128], BF16)
    w2fm_b = nc.dram_tensor("w2fm_b", [depth, NF2, 128, 8, 128], BF16)
    w1tm_b = nc.dram_tensor("w1tm_b", [depth, 128, 8, TM1], BF16)
    w2tm_b = nc.dram_tensor("w2tm_b", [depth, 128, 8, TM2], BF16)
    wbr_b = nc.dram_tensor("wbr_b", [depth, 8, 128, 3, 4, 128], BF16)
    wo_b = nc.dram_tensor("wo_b", [depth, 8, 128, 8, 128], BF16)
    o1_d = nc.dram_tensor("o1_spill", [128, cfg.NB, 4, 128], F32)
    CCW = 512 + 256 + 128
    cc_in = [nc.dram_tensor(f"cc_in{l}", [128, CCW], F32) for l in range(depth)]
    cc_out = [nc.dram_tensor(f"cc_out{l}", [256, CCW], F32) for l in range(depth)]

    from contextlib import ExitStack
    es = ExitStack()
    with es:
        S = Sched(nc, depth + 1)

        def sb(name, shape, dt=F32):
            return es.enter_context(nc.sbuf_tensor(name, list(shape), dt))

        def ps(name, shape, dt=F32):
            return es.enter_context(nc.psum_tensor(name, list(shape), dt))

        def sem(name):
            return es.enter_context(nc.semaphore(name))

        engsems = {(e, ep): sem(f"s_{e}_{ep}") for e in ("pe", "act", "dve", "pool") for ep in range(depth + 1)}
        _slot_n = [0]

        def slot(name):
            _slot_n[0] += 1
            return Slot(sem(f"d_{name}_{_slot_n[0]}"), name)

        ident_b = sb("ident_b", [128, 128], BF16)
        ones_b = sb("ones_b", [128, 128], BF16)
        bd_b = sb("bd_b", [128, 128], BF16)
        rot_b = sb("rot_b", [128, 128], BF16)
        amask = sb("amask", [128, 3, 128], BF16)
        hmask = sb("hmask", [128, 2, 64], BF16)
        selt = sb("selt", [128, 2])
        ngain = sb("ngain_s", [128, depth, 8])
        lbl = sb("lbl_s", [128, DEPTH, 2, 4])
        hgain = sb("hgain_s", [128, depth, 4])
        qkg = sb("qkg_s", [128, depth, 2])
        sinkt = sb("sink_s", [128, depth, 8])
        esink = sb("esink", [128, depth, 8])
        lbc1 = sb("lbc1", [128, DEPTH, 2, 4])
        lbc0 = sb("lbc0", [128, DEPTH, 2, 4])
        B_const = Buf("const")
        sl_c = slot("const")

        def dma(eng, out, in_, reads, writes, sl):
            return S.op(eng, I("dma_start", out=out, in_=in_), reads=reads, writes=writes, slot=sl)

        for dst, src in ((ident_b, c_ident_d), (bd_b, c_bd_d), (rot_b, c_rot_d), (amask, c_amask_d), (hmask, c_hmask_d)):
            dma("pool", dst[:], src.ap(), [], [B_const], sl_c)
        for dst, src in ((selt, sel_d), (ngain, ngain_d), (lbl, lbl_d), (hgain, hgain_d), (qkg, qkg_d), (sinkt, sink_d)):
            dma("sp", dst[:], src.ap(), [], [B_const], sl_c)
        S.op("dve", I("memset", ones_b[:], 1.0), [], [B_const])
        S.op("act", I("activation", out=esink[:], in_=sinkt[:], func=AF.Exp), [B_const], [B_const])
        lbe = sb("lbe", [128, DEPTH, 8])
        lbs = sb("lbs", [128, 8])
        lbv = lbl[:].rearrange("p l a b -> p l (a b)")
        S.op("act", I("activation", out=lbe[:], in_=lbv, func=AF.Exp), [B_const], [B_const])
        S.op("dve", I("tensor_tensor", out=lbs[:], in0=lbe[:, 0, :], in1=lbe[:, 1, :], op=ALU.add), [B_const], [B_const])
        S.op("dve", I("tensor_tensor", out=lbs[:], in0=lbs[:], in1=lbe[:, 2, :], op=ALU.add), [B_const], [B_const])
        S.op("dve", I("tensor_tensor", out=lbs[:], in0=lbs[:], in1=lbe[:, 3, :], op=ALU.add), [B_const], [B_const])
        S.op("dve", I("reciprocal", out=lbs[:], in_=lbs[:]), [B_const], [B_const])
        c1v = lbc1[:].rearrange("p l a b -> p l (a b)")
        c0v = lbc0[:].rearrange("p l a b -> p l (a b)")
        S.op("dve", I("memset", c0v[:, 0, :], 0.0), [B_const], [B_const])
        for l in range(1, DEPTH):
            S.op("dve", I("tensor_tensor", out=c1v[:, l, :], in0=lbe[:, l, :], in1=lbs[:], op=ALU.mult), [B_const], [B_const])
            S.op("dve", I("tensor_tensor", out=c0v[:, l, :], in0=c0v[:, l - 1, :], in1=c1v[:, l, :], op=ALU.add), [B_const], [B_const])
        S.op("dve", I("tensor_scalar", out=lbc1[:], in0=lbc0[:], scalar1=-0.5, scalar2=0.5, op0=ALU.mult, op1=ALU.add), [B_const], [B_const])
        S.op("dve", I("tensor_scalar", out=lbc0[:], in0=lbc0[:], scalar1=0.5, scalar2=0.5, op0=ALU.mult, op1=ALU.add), [B_const], [B_const])

        B_wcast = [Buf(f"wcast{l}") for l in range(depth)]
        sl_wc = [slot(f"wc{i}") for i in range(4)]
        _wc_i = [0]

        def wcast(l, dst, src):
            sl = sl_wc[_wc_i[0] % 4]
            _wc_i[0] += 1
            S.op("pool", I("dma_start", out=dst, in_=src), reads=[], writes=[B_wcast[l]], slot=sl)

        import os
        for l in range(depth if not os.environ.get("KSKIPCAST") else 0):
            def v4(t, j0, j1):
                return t[l, j0:j1].rearrange("a p k n -> (a p) (k n)")

            def v3(t):
                return t[l].rearrange("p k n -> p (k n)")

            for j in range(0, NF1, 4):
                wcast(l, v4(w1fm_b, j, j + 4), v4(w1fm_d, j, j + 4))
            wcast(l, v3(w1tm_b), v3(w1tm_d))
            for j in range(0, NF2, 6):
                wcast(l, v4(w2fm_b, j, j + 6), v4(w2fm_d, j, j + 6))
            wcast(l, v3(w2tm_b), v3(w2tm_d))
            wcast(l, wbr_b[l].rearrange("a p b k n -> (a p) (b k n)"), wbr_d[l].rearrange("a p b k n -> (a p) (b k n)"))
            wcast(l, v4(wo_b, 0, 8), v4(wo_d, 0, 8))

        NWB = 4
        wbuf = [sb(f"wbuf{i}", [128, 8, 128], BF16) for i in range(NWB)]
        B_wbuf = [Buf(f"wbuf{i}") for i in range(NWB)]
        sl_wbuf = [slot(f"wb{i}") for i in range(NWB)]
        _wb_i = [0]
        wtm = sb("wtm", [128, 8, TM2], BF16)
        B_wtm = Buf("wtm")
        sl_wtm = slot("wtm")
        wbrc = [sb(f"wbrc{i}", [128, 3, 4, 128], BF16) for i in range(2)]
        B_wbrc = [Buf(f"wbrc{i}") for i in range(2)]
        sl_wbrc = [slot(f"wbrc{i}") for i in range(2)]
        _wbrc_i = [0]
        sl_wl = slot("wl")
        lng = sb("lng_s", [128, 512])
        lnb = sb("lnb_s", [128, 512])
        wsT = sb("wsT_s", [128, 4, 128], BF16)
        bsp = sb("bsp_s", [128, 4, 128])
        B_lw = Buf("layerw")

        xbufs = [sb("xb0", [128, 8, 512]), sb("xb1", [128, 8, 512])]
        B_xbufs = [Buf("xb0"), Buf("xb1")]

        class XB:
            cur = 0
            prefetched = False

        sl_xt = slot("xt")
        hT = sb("hT", [128, 8, 512], BF16)
        B_hT = Buf("hT")
        rstd = sb("rstd", [128, 512])
        B_rstd = Buf("rstd")
        rtmp = sb("rtmp", [128, 512])
        B_rtmp = Buf("rtmp")
        epsb = sb("epsb", [128, 1])
        S.op("dve", I("memset", epsb[:], EPS), [], [B_const])

        zs = sb("zs", [128, 12, 512], BF16)
        B_zs = [Buf(f"zs{i}") for i in range(12)]
        ub = sb("ub", [128, 4, 512], BF16)
        B_ub = [Buf(f"ub{i}") for i in range(4)]
        yb = sb("yb", [128, 4, 512], BF16)
        B_yb = Buf("yb")
        ya = sb("ya", [128, 4, 512], BF16)
        B_ya = Buf("ya")
        yc = sb("yc", [128, 4, 512], BF16)
        B_yc = Buf("yc")
        mg = sb("mg", [128, 8, 512], BF16)
        B_mg = [Buf(f"mg{i}") for i in range(8)]
        sq = mg
        sl_xo = slot("xo")
        ya_scr = sb("qkscr", [128, 1024], BF16)
        B_ya_scr = Buf("qkscr")
        NTMP = 8
        tmpf = [sb(f"tmpf{i}", [128, 512]) for i in range(NTMP)]
        B_tmpf = [Buf(f"tmpf{i}") for i in range(NTMP)]
        _tf_i = [0]

        def get_tmp():
            i = _tf_i[0] % NTMP
            _tf_i[0] += 1
            return tmpf[i], B_tmpf[i]

        vn = sb("vn", [128, 512], BF16)
        B_vn = Buf("vn")
        bnst = sb("bnst", [128, 6])
        bnag = sb("bnag", [128, 2])
        B_bn = Buf("bn")
        mhalf1 = sb("mhalf1", [128, 1])
        S.op("pool", I("memset", mhalf1[:], -0.5), [], [B_const])

        NPP = 2
        pp = [ps(f"pp{i}", [128, 512]) for i in range(NPP)]
        B_pp = [Buf(f"pp{i}") for i in range(NPP)]
        _pp_i = [0]
        pp_mode = ["narrow"]

        def get_pp():
            if pp_mode[0] == "wide":
                banks = [(pp[0], B_pp[0]), (pp[1], B_pp[1]), (pbr[0], B_pbr[0]), (pbr[1], B_pbr[1]), (pbr[2], B_pbr[2]), (pmx_flat, B_pmx)]
            elif pp_mode[0] == "nopmx":
                banks = [(pp[0], B_pp[0]), (pp[1], B_pp[1]), (pbr[0], B_pbr[0]), (pbr[1], B_pbr[1]), (pbr[2], B_pbr[2])]
            else:
                banks = [(pp[0], B_pp[0]), (pp[1], B_pp[1])]
            i = _pp_i[0] % len(banks)
            _pp_i[0] += 1
            return banks[i]

        pst = ps("pst", [128, 512])
        B_pst = Buf("pst")
        pmx = ps("pmx", [128, 4, 128])
        B_pmx = Buf("pmx")
        pbr = [ps(f"pbr{i}", [128, 512]) for i in range(3)]
        B_pbr = [Buf(f"pbr{i}") for i in range(3)]

        class _Flat:
            def __getitem__(self, idx):
                return pmx[:].rearrange("p a b -> p (a b)")[idx]
        pmx_flat = _Flat()

        DRAM_x = [Buf(f"dram_x{m}") for m in range(n_mt)]
        qr = sb("qr", [128, 4, 512], BF16)
        B_qr = [Buf(f"qr{i}") for i in range(4)]
        kz = sb("kz", [128, 2, 2, 768], BF16)
        B_kr = Buf("kr")
        S.op("pool", I("memset", kz[:], 0.0), [], [B_kr])
        vaug = sb("vaug", [128, 6, 2, 192], BF16)
        B_vaug = Buf("vaug")
        S.op("dve", I("memset", vaug[:, :, :, 64:128], 1.0), [], [B_vaug])
        pt = sb("pt", [128, 3, 2, 256], BF16)
        B_pt = Buf("pt")
        B_ptk = [Buf(f"pt{i}") for i in range(3)]
        cs = sb("cs", [128, 2, 640])
        B_cs = Buf("cs")
        sl_cs = slot("cs")
        xh = sb("xh", [128, 8, 128])
        B_xh = Buf("xh")
        sl_xh = slot("xh")
        hTh = sb("hTh", [128, 8, 128], BF16)
        B_hTh = Buf("hTh")
        sqh = sb("sqh", [128, 8, 128], BF16)
        B_sqh = Buf("sqh")
        dtmp = sb("dtmp", [128, 256])
        B_dtmp = Buf("dtmp")
        def cc_views():
            oth = xbufs[1 - XB.cur]
            flat = oth[:].rearrange("p c t -> p (c t)")
            return (flat[:, 0:2 * CCW].rearrange("p (r n) -> p r n", r=2), flat[:, 2048:2048 + CCW], flat[:, 3072:3072 + CCW], B_xbufs[1 - XB.cur])
        sl_cc = slot("cc")
        DRAM_cc = Buf("dram_cc")
        ccsem = sem("ccsem")
        cmask = sb("cmask", [128, 512])
        hmask2 = sb("hmask2", [128, 2, 128], BF16)
        dma("sp", cmask[:], c_cmask_d.ap(), [], [B_const], sl_c)
        dma("pool", hmask2[:], c_hmask2_d.ap(), [], [B_const], sl_c)
        qraw = sb("qraw", [128, 4, 512])
        B_qraw = Buf("qraw")
        qtT = sb("qtT", [128, 4, 512], BF16)
        B_qtT = Buf("qtT")
        ktT = sb("ktT", [128, 4, 512], BF16)
        B_ktT = Buf("ktT")
        ktA = sb("ktA", [128, 4, 128], BF16)
        ktB = sb("ktB", [128, 4, 128], BF16)
        B_kt = Buf("kt")
        S.op("pool", I("memset", ktA[:], 0.0), [], [B_kt])
        S.op("pool", I("memset", ktB[:], 0.0), [], [B_kt])
        vtok = sb("vtok", [128, 4, 4, 128], BF16)
        B_vtok = [Buf(f"vtok{i}") for i in range(4)]
        sc = sb("sc", [128, 4, 3, 8])
        B_sc = Buf("sc")
        sc8 = sb("sc8", [128, 8])
        B_sc8 = Buf("sc8")
        St = sb("St", [128, 4, 128])
        B_S = Buf("S")
        Stmp = sb("Stmp", [128, 4, 128])
        B_Stmp = Buf("Stmp")
        Sp = sb("Sp", [128, 4, 128], BF16)
        B_Sp = Buf("Sp")
        ATs = sb("ATs", [128, 4, 128], BF16)
        B_ATs = Buf("ATs")
        o1s = sb("o1s", [128, 4, 128])
        B_o1s = Buf("o1s")
        sl_o1 = slot("o1")
        DRAM_o1 = [Buf(f"dram_o1_{i}") for i in range(cfg.NB)]
        ptr = ps("ptr", [128, 4, 128], BF16)
        B_ptr = Buf("ptr")

        def hgrn_gates(l, dirn, srcw, base_a):
            for h in range(4):
                p, Bp = proj_fm(l, srcw, base_a + h)
                tA, BA = get_tmp()
                tK, BK = get_tmp()
                tB, BB = get_tmp()
                tD, BD = get_tmp()
                tE, BE = get_tmp()
                v = lambda t: t[:].rearrange("p (c t) -> p c t", t=64)
                S.op("act", I("activation", out=tA[:], in_=p[:], func=AF.Tanh, scale=0.5), [Bp], [BA])
                S.op("dve", I("tensor_scalar", out=tA[:], in0=tA[:], scalar1=lbc1[:, l, dirn, h:h + 1], scalar2=lbc0[:, l, dirn, h:h + 1], op0=ALU.mult, op1=ALU.add),
                     [BA, B_const], [BA])
                S.op("pool", I("tensor_scalar", out=tK[:], in0=tA[:], scalar1=-1.0, scalar2=1.0, op0=ALU.mult, op1=ALU.add), [BA], [BK])
                S.op("act", I("activation", out=tA[:], in_=tA[:], func=AF.Ln), [BA, BK], [BA])
                S.op("dve", I("tensor_tensor_scan", out=tB[:], data0=cmask[:], data1=tA[:], initial=0.0, op0=ALU.mult, op1=ALU.add), [BA, B_const], [BB])
                if dirn == 0:
                    S.op("dve", I("tensor_tensor", out=v(tD), in0=v(tB), in1=v(tB)[:, :, 31:32].to_broadcast([128, 8, 64]), op=ALU.subtract), [BB], [BD])
                    S.op("act", I("activation", out=sc[:, h, 0, :], in_=v(tB)[:, :, 31], func=AF.Exp), [BB], [B_sc])
                    S.op("act", I("activation", out=sc[:, h, 1, :], in_=v(tB)[:, :, 63], func=AF.Exp), [BB], [B_sc])
                    S.op("act", I("activation", out=sc[:, h, 2, :], in_=v(tD)[:, :, 63], func=AF.Exp), [BD], [B_sc])
                else:
                    S.op("dve", I("tensor_tensor", out=tA[:], in0=tB[:], in1=tA[:], op=ALU.subtract), [BB, BA], [BA])
                    S.op("dve", I("tensor_tensor", out=v(tD), in0=v(tA)[:, :, 32:33].to_broadcast([128, 8, 64]), in1=v(tA), op=ALU.subtract), [BA], [BD])
                    S.op("dve", I("tensor_tensor", out=sc8[:], in0=v(tB)[:, :, 63], in1=v(tA)[:, :, 32], op=ALU.subtract), [BB, BA], [B_sc8])
                    S.op("act", I("activation", out=sc[:, h, 0, :], in_=sc8[:], func=AF.Exp), [B_sc8], [B_sc])
                    S.op("act", I("activation", out=sc[:, h, 1, :], in_=v(tB)[:, :, 63], func=AF.Exp), [BB], [B_sc])
                    S.op("act", I("activation", out=sc[:, h, 2, :], in_=v(tD)[:, :, 0], func=AF.Exp), [BD], [B_sc])
                S.op("act", I("activation", out=tE[:], in_=tD[:], func=AF.Exp), [BD], [BE])
                S.op("dve", I("tensor_tensor", out=qtT[:, h, :], in0=qraw[:, h, :], in1=tE[:], op=ALU.mult), [B_qraw, BE], [B_qtT])
                S.op("act", I("activation", out=tB[:], in_=tD[:], func=AF.Exp, scale=-1.0), [BD, B_sc, B_sc8], [BB])
                S.op("pool", I("tensor_tensor", out=ktT[:, h, :], in0=tK[:], in1=tB[:], op=ALU.mult), [BK, BB], [B_ktT])

        def hgrn_q_and_v(l, srcw, base_q):
            for h in range(4):
                p, Bp = proj_fm(l, srcw, base_q + h)
                S.op("act", I("activation", out=qraw[:, h, :], in_=p[:], func=AF.Copy), [Bp], [B_qraw])

        def hgrn_vtok(bi):
            tsl = slice(bi * 128, (bi + 1) * 128)
            p, Bp = get_pp()
            for k in range(8):
                S.op("pe", I("matmul", p[:], lhsT=hT[:, k, tsl], rhs=wtm[:, k, 0:512], start=(k == 0), stop=(k == 7)), [B_hT, B_wtm], [Bp])
            S.op("act", I("activation", out=vtok[:, bi, :, :], in_=p[:].rearrange("p (h d) -> p h d", h=4), func=AF.Copy), [Bp], [B_vtok[bi]])

        def hgrn_block(l, dirn, bi, m):
            tsl = slice(bi * 128, (bi + 1) * 128)
            psc, B_psc = pbr[0][:].rearrange("p (h t) -> p h t", h=4), B_pbr[0]
            po, B_po = pbr[1][:].rearrange("p (h t) -> p h t", h=4), B_pbr[1]
            pob, B_pob = pbr[2][:].rearrange("p (h t) -> p h t", h=4), B_pbr[2]
            pS, B_pS = pmx[:], B_pmx
            for h in range(4):
                S.op("pe", I("transpose", out=ptr[:, h, :], in_=ktT[:, h, tsl], identity=ident_b[:]), [B_ktT, B_const], [B_ptr])
            S.op("dve", I("tensor_copy", out=ktA[0:64, :, :], in_=ptr[0:64, :, :]), [B_ptr], [B_kt])
            S.op("act", I("activation", out=ktB[64:128, :, :], in_=ptr[64:128, :, :], func=AF.Copy), [B_ptr], [B_kt])
            for h in range(4):
                S.op("pe", I("matmul", psc[:, h, :], lhsT=ktT[:, h, tsl], rhs=qtT[:, h, tsl], start=True, stop=True), [B_ktT, B_qtT], [B_psc])
            S.op("dve", I("tensor_tensor", out=ATs[:], in0=psc, in1=hmask2[:, dirn:dirn + 1, :].to_broadcast([128, 4, 128]), op=ALU.mult), [B_psc, B_const], [B_ATs])
            for h in range(4):
                S.op("pe", I("matmul", po[:, h, :], lhsT=vtok[:, bi, h, :], rhs=ATs[:, h, :], start=True, stop=True), [B_vtok[bi], B_ATs], [B_po])
            chunks = [(0, ktA), (1, ktB)] if dirn == 0 else [(1, ktB), (0, ktA)]
            for ci, (c, kt) in enumerate(chunks):
                gc = bi * 2 + c
                csl = slice(bi * 128 + c * 64, bi * 128 + (c + 1) * 64)
                bc = lambda k: sc[:, :, k, gc:gc + 1].to_broadcast([128, 4, 128])
                S.op("dve", I("tensor_tensor", out=Sp[:], in0=St[:], in1=bc(0), op=ALU.mult), [B_S, B_sc], [B_Sp])
                S.op("dve", I("tensor_tensor", out=St[:], in0=St[:], in1=bc(1), op=ALU.mult), [B_S, B_sc], [B_S])
                for h in range(4):
                    S.op("pe", I("matmul", pob[:, h, c * 64:(c + 1) * 64], lhsT=Sp[:, h, :], rhs=qtT[:, h, csl], start=True, stop=True), [B_Sp, B_qtT], [B_pob])
                for h in range(4):
                    S.op("pe", I("matmul", pS[:, h, :], lhsT=kt[:, h, :], rhs=vtok[:, bi, h, :], start=True, stop=True), [B_kt, B_vtok[bi]], [B_pS])
                S.op("dve", I("tensor_tensor", out=Stmp[:], in0=pS, in1=bc(2), op=ALU.mult), [B_pS, B_sc], [B_Stmp])
                S.op("dve", I("tensor_tensor", out=St[:], in0=St[:], in1=Stmp[:], op=ALU.add), [B_S, B_Stmp], [B_S])
            gblk = m * 4 + bi
            if dirn == 0:
                S.op("act", I("activation", out=o1s[:], in_=po, func=AF.Copy), [B_po], [B_o1s])
                S.op("dve", I("tensor_tensor", out=o1s[:], in0=o1s[:], in1=pob, op=ALU.add), [B_o1s, B_pob], [B_o1s])
                dma("sp", o1_d[:, gblk], o1s[:], [B_o1s], [DRAM_o1[gblk]], sl_o1)
            else:
                dma("sp", o1s[:], o1_d[:, gblk], [DRAM_o1[gblk]], [B_o1s], sl_o1)
                osum, Bos = get_tmp()
                rt, Brt = get_tmp()
                osv = osum[:].rearrange("p (h t) -> p h t", h=4)
                S.op("dve", I("tensor_tensor", out=osv, in0=po, in1=o1s[:], op=ALU.add), [B_po, B_o1s], [Bos])
                S.op("dve", I("tensor_tensor", out=osv, in0=osv, in1=pob, op=ALU.add), [Bos, B_pob], [Bos])
                S.op("act", I("activation", out=ya_scr[:, 0:512], in_=osum[:], func=AF.Square), [Bos], [B_ya_scr])
                S.op("pe", I("matmul", pst[:], lhsT=ones_b[:], rhs=ya_scr[:, 0:512], start=True, stop=True), [B_ya_scr, B_const], [B_pst])
                S.op("act", I("activation", out=rt[:], in_=pst[:], func=AF.Ln, scale=1.0 / 128, bias=epsb[:]), [B_pst, B_const], [Brt])
                S.op("act", I("activation", out=rt[:], in_=rt[:], func=AF.Exp, scale=-0.5), [Brt], [Brt])
                S.op("dve", I("tensor_tensor", out=osum[:], in0=osum[:], in1=rt[:], op=ALU.mult), [Bos, Brt], [Bos])
                for h in range(4):
                    S.op("dve", I("scalar_tensor_tensor", out=ya[:, h, tsl], in0=osv[:, h, :], scalar=hgain[:, l, h:h + 1], in1=zs[:, h, tsl], op0=ALU.mult, op1=ALU.mult),
                         [Bos, B_const, B_zs[h]], [B_ya])

        def sweep1_mt(l, m, hook=None):
            norm_mt(l, m)
            pp_mode[0] = "wide"
            hgrn_q_and_v(l, w1fm_b, 4)
            hgrn_gates(l, 0, w1fm_b, 0)
            for bi in range(4):
                hgrn_vtok(bi)
                hgrn_block(l, 0, bi, m)
        sl_out = slot("out")

        def xsrc(l):
            return xT_d if l == 0 else out_d

        xview = lambda t, m: t.ap().rearrange("(c p) t -> p c t", p=128)[:, :, m * 512:(m + 1) * 512]

        def load_w_chunk(l, src_b, j):
            i = _wb_i[0] % NWB
            _wb_i[0] += 1
            dma("sp", wbuf[i][:], src_b[l, j], [B_wcast[l]], [B_wbuf[i]], sl_wbuf[i])
            return wbuf[i], B_wbuf[i]

        def tanh_gate(dst, src, scale):
            return I("activation", out=dst, in_=src, func=AF.Tanh, scale=scale)

        def load_x(l, m, which):
            dma("sp", xbufs[which][:], xview(xsrc(l), m), [DRAM_x[m]], [B_xbufs[which]], sl_xt)

        def norm_mt(l, m):
            xt, B_xt = xbufs[XB.cur], B_xbufs[XB.cur]
            if not XB.prefetched:
                load_x(l, m, XB.cur)
            XB.prefetched = False
            for c in range(8):
                S.op("act", I("activation", out=sq[:, c, :], in_=xt[:, c, :], func=AF.Square), [B_xt], [B_mg[c]])
            for c in range(8):
                S.op("pe", I("matmul", pst[:], lhsT=ones_b[:], rhs=sq[:, c, :], start=(c == 0), stop=(c == 7)),
                     [B_mg[c], B_const], [B_pst])
            S.op("act", I("activation", out=rtmp[:], in_=pst[:], func=AF.Ln, scale=1.0 / D, bias=epsb[:]), [B_pst, B_const], [B_rtmp])
            S.op("act", I("activation", out=rstd[:], in_=rtmp[:], func=AF.Exp, scale=-0.5), [B_rtmp], [B_rstd])
            for c in range(8):
                S.op("dve", I("scalar_tensor_tensor", out=hT[:, c, :], in0=xt[:, c, :], scalar=ngain[:, l, c:c + 1],
                                                                   in1=rstd[:], op0=ALU.mult, op1=ALU.mult),
                     [B_xt, B_rstd, B_const], [B_hT])

        def proj_fm(l, src_b, j):
            w, Bw = load_w_chunk(l, src_b, j)
            p, Bp = get_pp()
            for k in range(8):
                S.op("pe", I("matmul", p[:], lhsT=w[:, k, :], rhs=hT[:, k, :], start=(k == 0), stop=(k == 7)),
                     [Bw, B_hT], [Bp])
            return p, Bp

        def layer_weights(l):
            dma("sp", wtm[:], w2tm_b[l], [B_wcast[l]], [B_wtm], sl_wtm)
            dma("sp", lng[:], lng_d[:, l, :], [], [B_lw], sl_wl)
            dma("sp", lnb[:], lnb_d[:, l, :], [], [B_lw], sl_wl)
            dma("sp", bsp[:], bsp_d[:, l], [], [B_lw], sl_wl)
            dma("pool", wsT[:], wsT_d[:, l], [], [B_lw], sl_wl)

        GELU_C = 0.7978845608028654

        def gelu_from_psum(p, Bp, dst, Bdst, eng2="pool"):
            t1, B1 = get_tmp()
            t2, B2 = get_tmp()
            S.op("act", I("activation", out=t1[:], in_=p[:], func=AF.Square), [Bp], [B1])
            S.op("dve", I("tensor_scalar", out=t1[:], in0=t1[:], scalar1=0.044715, scalar2=1.0, op0=ALU.mult, op1=ALU.add), [B1], [B1])
            S.op("dve", I("tensor_tensor", out=t2[:], in0=t1[:], in1=p[:], op=ALU.mult), [B1, Bp], [B2])
            S.op("act", I("activation", out=t1[:], in_=t2[:], func=AF.Tanh, scale=GELU_C), [B2], [B1])
            S.op("dve", I("scalar_tensor_tensor", out=dst, in0=t1[:], scalar=1.0, in1=p[:], op0=ALU.add, op1=ALU.mult), [B1, Bp], [Bdst])

        def silu_from_psum(p, Bp, dst, Bdst):
            t1, B1 = get_tmp()
            S.op("act", I("activation", out=t1[:], in_=p[:], func=AF.Tanh, scale=0.5), [Bp], [B1])
            S.op("dve", I("scalar_tensor_tensor", out=dst, in0=t1[:], scalar=1.0, in1=p[:], op0=ALU.add, op1=ALU.mult), [B1, Bp], [Bdst])

        def sweep2_mt(l, m, hook=None):
            norm_mt(l, m)
            xt, B_xt = xbufs[XB.cur], B_xbufs[XB.cur]
            base = {}
            cnt = 0
            for kind in FM2:
                base.setdefault(kind, cnt)
                cnt += 1
            if cfg.do_b:
                pp_mode[0] = "wide"
                for j in range(4):
                    p, Bp = proj_fm(l, w2fm_b, base["uB"] + j)
                    gelu_from_psum(p, Bp, ub[:, j, :], B_ub[j])
                for j in range(4):
                    p, Bp = proj_fm(l, w2fm_b, base["zB"] + j)
                    silu_from_psum(p, Bp, zs[:, 4 + j, :], B_zs[4 + j])
                pp_mode[0] = "nopmx"
                for blk in range(4):
                    tsl = slice(blk * 128, (blk + 1) * 128)
                    p, Bp = get_pp()
                    for k in range(8):
                        S.op("pe", I("matmul", p[:], lhsT=hT[:, k, tsl], rhs=wtm[:, k, 512:1024],
                                                                          start=(k == 0), stop=(k == 7)), [B_hT, B_wtm], [Bp])
                    vt, B_vt = get_tmp()
                    vt2, B_vt2 = get_tmp()
                    gelu_from_psum(p, Bp, vt[:], B_vt)
                    S.op("dve", I("bn_stats", out=bnst[:], in_=vt[:]), [B_vt], [B_bn])
                    S.op("dve", I("bn_aggr", out=bnag[:], in_=bnst[:]), [B_bn], [B_bn])
                    S.op("dve", I("tensor_scalar", out=bnag[:, 1:2], in0=bnag[:, 1:2], scalar1=0.25, scalar2=EPS, op0=ALU.mult, op1=ALU.add),
                         [B_bn], [B_bn])
                    S.op("pool", I("tensor_tensor", out=bnag[:, 1:2], in0=bnag[:, 1:2], in1=mhalf1[:], op=ALU.pow), [B_bn, B_const], [B_bn])
                    S.op("dve", I("tensor_scalar", out=bnag[:, 1:2], in0=bnag[:, 1:2], scalar1=0.5, scalar2=None, op0=ALU.mult), [B_bn], [B_bn])
                    S.op("dve", I("tensor_scalar", out=vt2[:], in0=vt[:], scalar1=bnag[:, 0:1], scalar2=bnag[:, 1:2],
                                                          op0=ALU.subtract, op1=ALU.mult), [B_vt, B_bn], [B_vt2])
                    S.op("dve", I("tensor_tensor", out=vt2[:], in0=vt2[:], in1=lng[:], op=ALU.mult), [B_vt2, B_lw], [B_vt2])
                    S.op("dve", I("tensor_tensor", out=vn[:], in0=vt2[:], in1=lnb[:], op=ALU.add), [B_vt2, B_lw], [B_vn])
                    for g in range(4):
                        S.op("pe", I("matmul", pmx[:, g, :], lhsT=vn[:, g * 128:(g + 1) * 128], rhs=wsT[:, g, :], start=True, stop=True),
                             [B_vn, B_lw], [B_pmx])
                    t1, B1 = get_tmp()
                    t1v = t1[:].rearrange("p (g t) -> p g t", g=4)
                    S.op("dve", I("tensor_tensor", out=t1v, in0=pmx[:], in1=bsp[:], op=ALU.add), [B_pmx, B_lw], [B1])
                    S.op("pool", I("tensor_tensor", out=t1v, in0=t1v, in1=ub[:, :, tsl], op=ALU.mult), [B1] + B_ub, [B1])
                    S.op("dve", I("tensor_tensor", out=yb[:, :, tsl], in0=t1v, in1=zs[:, 4:8, tsl], op=ALU.mult),
                         [B1] + B_zs[4:8], [B_yb])
            if cfg.do_c:
                pp_mode[0] = "wide"
                top = (m == n_mt - 1)
                has_lo = (m > 0)
                t0 = m * 512 - 128
                if has_lo:
                    dma("sp", cs[:, 0, :], cosT_d[:, t0:t0 + 640], [], [B_cs], sl_cs)
                    dma("sp", cs[:, 1, :], sinT_d[:, t0:t0 + 640], [], [B_cs], sl_cs)
                    dma("sp", xh[:], xsrc(l).ap().rearrange("(c p) t -> p c t", p=128)[:, :, t0:t0 + 128], [DRAM_x[m - 1]], [B_xh], sl_xh)
                    for c in range(8):
                        S.op("act", I("activation", out=sqh[:, c, :], in_=xh[:, c, :], func=AF.Square), [B_xh], [B_sqh])
                    for c in range(8):
                        S.op("pe", I("matmul", pst[:, 0:128], lhsT=ones_b[:], rhs=sqh[:, c, :], start=(c == 0), stop=(c == 7)),
                             [B_sqh, B_const], [B_pst])
                    S.op("act", I("activation", out=rtmp[:, 0:128], in_=pst[:, 0:128], func=AF.Ln, scale=1.0 / D, bias=epsb[:]), [B_pst, B_const], [B_rtmp])
                    S.op("act", I("activation", out=rtmp[:, 128:256], in_=rtmp[:, 0:128], func=AF.Exp, scale=-0.5), [B_rtmp], [B_rtmp])
                    for c in range(8):
                        S.op("dve", I("scalar_tensor_tensor", out=hTh[:, c, :], in0=xh[:, c, :], scalar=ngain[:, l, c:c + 1],
                                                                           in1=rtmp[:, 128:256], op0=ALU.mult, op1=ALU.mult),
                             [B_xh, B_rtmp, B_const], [B_hTh])
                else:
                    dma("sp", cs[:, 0, 128:640], cosT_d[:, 0:512], [], [B_cs], sl_cs)
                    dma("sp", cs[:, 1, 128:640], sinT_d[:, 0:512], [], [B_cs], sl_cs)
                if not top:
                    S.op("pool", I("tensor_copy", out=kz[:, :, :, 640:768], in_=kz[:, :, :, 128:256]), [B_kr], [B_kr])
                    S.op("pool", I("tensor_copy", out=vaug[:, 5, :, :], in_=vaug[:, 1, :, :]), [B_vaug], [B_vaug])

                def qk_post(p, Bp, n, which, dst, Bdst, csl):
                    t1, B1 = get_tmp()
                    t2, B2 = get_tmp()
                    t3, B3 = get_tmp()
                    sqb, Bsqb = ya_scr, B_ya_scr
                    S.op("act", I("activation", out=sqb[:, 0:n], in_=p[:, 0:n], func=AF.Square), [Bp], [Bsqb])
                    S.op("dve", I("tensor_scalar", out=sqb[:, 512:512 + n], in0=p[:, 0:n], scalar1=qkg[:, l, which:which + 1], scalar2=None, op0=ALU.mult),
                         [Bp, B_const], [Bsqb])
                    S.op("pe", I("matmul", pst[:, 0:n], lhsT=bd_b[:], rhs=sqb[:, 0:n], start=True, stop=True), [Bsqb, B_const], [B_pst])
                    pr, Bpr = get_pp()
                    S.op("pe", I("matmul", pr[:, 0:n], lhsT=rot_b[:], rhs=sqb[:, 512:512 + n], start=True, stop=True), [Bsqb, B_const], [Bpr])
                    S.op("act", I("activation", out=t1[:, 0:n], in_=pst[:, 0:n], func=AF.Ln, scale=1.0 / 64, bias=epsb[:]), [B_pst, B_const], [B1])
                    S.op("act", I("activation", out=t1[:, 0:n], in_=t1[:, 0:n], func=AF.Exp, scale=-0.5), [B1], [B1])
                    S.op("pool", I("tensor_tensor", out=t2[:, 0:n], in0=sqb[:, 512:512 + n], in1=cs[:, 0, csl], op=ALU.mult), [Bsqb, B_cs], [B2])
                    S.op("dve", I("tensor_tensor", out=t3[:, 0:n], in0=pr[:, 0:n], in1=cs[:, 1, csl], op=ALU.mult), [Bpr, B_cs], [B3])
                    S.op("dve", I("tensor_tensor", out=t2[:, 0:n], in0=t2[:, 0:n], in1=t3[:, 0:n], op=ALU.add), [B2, B3], [B2])
                    if which == 0:
                        S.op("dve", I("tensor_tensor", out=dst, in0=t2[:, 0:n], in1=t1[:, 0:n], op=ALU.mult), [B2, B1], [Bdst])
                    else:
                        jh, c0 = dst
                        S.op("dve", I("tensor_tensor", out=kz[0:64, 0, jh, c0:c0 + n], in0=t2[0:64, 0:n], in1=t1[0:64, 0:n], op=ALU.mult), [B2, B1], [Bdst])
                        S.op("dve", I("tensor_tensor", out=kz[64:128, 1, jh, c0:c0 + n], in0=t2[64:128, 0:n], in1=t1[64:128, 0:n], op=ALU.mult), [B2, B1], [Bdst])

                for j in range(4):
                    p, Bp = proj_fm(l, w2fm_b, base["qC"] + j)
                    qk_post(p, Bp, 512, 0, qr[:, j, :], B_qr[j], slice(128, 640))
                for j in range(2):
                    w, Bw = load_w_chunk(l, w2fm_b, base["kC"] + j)
                    p, Bp = get_pp()
                    for k in range(8):
                        S.op("pe", I("matmul", p[:], lhsT=w[:, k, :], rhs=hT[:, k, :], start=(k == 0), stop=(k == 7)), [Bw, B_hT], [Bp])
                    qk_post(p, Bp, 512, 1, (j, 128), B_kr, slice(128, 640))
                    if has_lo:
                        p, Bp = get_pp()
                        for k in range(8):
                            S.op("pe", I("matmul", p[:, 0:128], lhsT=w[:, k, :], rhs=hTh[:, k, :], start=(k == 0), stop=(k == 7)), [Bw, B_hTh], [Bp])
                        qk_post(p, Bp, 128, 1, (j, 0), B_kr, slice(0, 128))
                for j in range(4):
                    p, Bp = proj_fm(l, w2fm_b, base["zC"] + j)
                    silu_from_psum(p, Bp, zs[:, 8 + j, :], B_zs[8 + j])
                for sl_i in ([0] if has_lo else []) + [1, 2, 3, 4]:
                    p, Bp = get_pp()
                    for k in range(8):
                        if sl_i == 0:
                            S.op("pe", I("matmul", p[:, 0:128], lhsT=hTh[:, k, :], rhs=wtm[:, k, 1024:1152], start=(k == 0), stop=(k == 7)),
                                 [B_hTh, B_wtm], [Bp])
                        else:
                            tsl = slice((sl_i - 1) * 128, sl_i * 128)
                            S.op("pe", I("matmul", p[:, 0:128], lhsT=hT[:, k, tsl], rhs=wtm[:, k, 1024:1152], start=(k == 0), stop=(k == 7)),
                                 [B_hT, B_wtm], [Bp])
                    pv = p[:, 0:128].rearrange("p (h d) -> p h d", h=2)
                    S.op("act", I("activation", out=vaug[:, sl_i, :, 0:64], in_=pv, func=AF.Copy), [Bp], [B_vaug])
                    S.op("dve", I("tensor_copy", out=vaug[:, sl_i, :, 128:192], in_=pv), [Bp], [B_vaug])
                kc_stage = int(os.environ.get("KC_STAGE", "9"))
                if top and kc_stage >= 2:
                    ccg, ccs, ccp, B_cc = cc_views()
                    B_ccs = B_ccg = B_ccp = B_cc
                    S.op("dve", I("tensor_copy", out=ccs[0:64, 512:768].rearrange("p (h t) -> p h t", h=2), in_=kz[0:64, 0, :, 512:640]), [B_kr], [B_ccs])
                    S.op("dve", I("tensor_copy", out=ccs[64:128, 512:768].rearrange("p (h t) -> p h t", h=2), in_=kz[64:128, 1, :, 512:640]), [B_kr], [B_ccs])
                    S.op("dve", I("tensor_copy", out=ccs[:, 768:896].rearrange("p (h d) -> p h d", h=2), in_=vaug[:, 4, :, 0:64]), [B_vaug], [B_ccs])
                    if not cfg.do_a:
                        S.op("dve", I("memset", ccs[:, 0:512], 0.0), [], [B_ccs])
                    else:
                        S.op("dve", I("tensor_copy", out=ccs[:, 0:512], in_=St[:].rearrange("p h v -> p (h v)")), [B_S], [B_ccs])
                    dma("pool", cc_in[l].ap(), ccs, [B_ccs], [DRAM_cc], sl_cc)
                    o = S.op("pool", I("collective_compute", "AllGather", ALU.bypass, replica_groups=[[0, 1], [2, 3], [4, 5], [6, 7]],
                                                                  ins=[cc_in[l].ap().opt()], outs=[cc_out[l].ap().opt()]), [DRAM_cc], [DRAM_cc])
                    o.signal = True
                    dma("pool", ccg, cc_out[l].ap().rearrange("(r p) n -> p r n", p=128), [DRAM_cc], [B_ccg], sl_cc)
                    S.op("dve", I("tensor_scalar", out=ccp, in0=ccg[:, 0, :], scalar1=selt[:, 0:1], scalar2=None, op0=ALU.mult), [B_ccg, B_const], [B_ccp])
                    S.op("dve", I("scalar_tensor_tensor", out=ccp, in0=ccg[:, 1, :], scalar=selt[:, 1:2], in1=ccp, op0=ALU.mult, op1=ALU.add),
                         [B_ccg, B_const, B_ccp], [B_ccp])
                    S.op("dve", I("tensor_copy", out=kz[0:64, 0, :, 640:768], in_=ccp[0:64, 512:768].rearrange("p (h t) -> p h t", h=2)), [B_ccp], [B_kr])
                    S.op("dve", I("tensor_copy", out=kz[64:128, 1, :, 640:768], in_=ccp[64:128, 512:768].rearrange("p (h t) -> p h t", h=2)), [B_ccp], [B_kr])
                    S.op("dve", I("tensor_copy", out=vaug[:, 5, :, 0:64], in_=ccp[:, 768:896].rearrange("p (h d) -> p h d", h=2)), [B_ccp], [B_vaug])
                    S.op("dve", I("tensor_copy", out=vaug[:, 5, :, 128:192], in_=ccp[:, 768:896].rearrange("p (h d) -> p h d", h=2)), [B_ccp], [B_vaug])
                    if cfg.do_a:
                        S.op("dve", I("tensor_copy", out=St[:].rearrange("p h v -> p (h v)"), in_=ccp[:, 0:512]), [B_ccp], [B_S])
                for sb_i in ((4, 3, 2, 1) if kc_stage >= 3 else ()):
                    jglob = m * 4 + sb_i - 1
                    tsl = slice((sb_i - 1) * 128, sb_i * 128)
                    kbs = []
                    if jglob > 0:
                        kbs.append((sb_i - 1, 0))
                    kbs.append((sb_i, None))
                    kbs.append((sb_i + 1, 2 if (top and sb_i == 4) else 1))
                    for h in range(2):
                        pssv = [pbr[i][:].rearrange("p (e n) -> p e n", e=2) for i in range(3)]
                        for ki, (slk, mk) in enumerate(kbs):
                            for e_ in range(2):
                                rows = slice(e_ * 64, (e_ + 1) * 64)
                                S.op("pe", I("matmul",
                                    pssv[ki][:, e_, :].rearrange("p (c t) -> p c t", c=2), lhsT=kz[:, e_, h, slk * 128:(slk + 1) * 128],
                                    rhs=qr[:, 2 * h:2 * h + 2, tsl], start=True, stop=True),
                                    [B_kr] + B_qr, [B_pbr[ki]])
                            S.op("act", I("activation", out=pt[:, ki, :, :], in_=pssv[ki], func=AF.Exp, scale=0.125), [B_pbr[ki]], [B_ptk[ki]])
                            if mk is not None and kc_stage >= 4:
                                ptv = pt[:, ki, :, :].rearrange("p e (c t) -> p (e c) t", c=2)
                                S.op("dve" if mk == 0 else "pool", I("tensor_tensor", out=ptv, in0=ptv, in1=amask[:, mk:mk + 1, :].to_broadcast([128, 4, 128]), op=ALU.mult),
                                     [B_ptk[ki], B_const], [B_ptk[ki]])
                        pso = pmx[:].rearrange("p a b -> p (a b)").rearrange("p (e n) -> p e n", e=2)
                        for e_ in (range(2) if kc_stage >= 5 else ()):
                            for ki, (slk, mk) in enumerate(kbs):
                                S.op("pe", I("matmul", pso[:, e_, :], lhsT=vaug[:, slk, h, e_ * 64:e_ * 64 + 128], rhs=pt[:, ki, e_, :],
                                                                                       start=(ki == 0), stop=(ki == len(kbs) - 1)), [B_vaug, B_ptk[ki]], [B_pmx])
                        for e_ in (range(2) if kc_stage >= 6 else ()):
                            nr = slice(0, 64) if e_ == 0 else slice(64, 128)
                            dr = slice(64, 128) if e_ == 0 else slice(0, 64)
                            for c in range(2):
                                head = 2 * (2 * h + c) + e_
                                S.op("dve", I("tensor_scalar",
                                    out=dtmp[nr, c * 128:(c + 1) * 128], in0=pso[dr, e_, c * 128:(c + 1) * 128], scalar1=esink[dr, l, head:head + 1], scalar2=None, op0=ALU.add),
                                    [B_pmx, B_const], [B_dtmp])
                        S.op("act", I("activation", out=dtmp[:], in_=dtmp[:], func=AF.Ln), [B_dtmp], [B_dtmp])
                        S.op("act", I("activation", out=dtmp[:], in_=dtmp[:], func=AF.Exp, scale=-1.0), [B_dtmp], [B_dtmp])
                        for e_ in (range(2) if kc_stage >= 6 else ()):
                            nr = slice(0, 64) if e_ == 0 else slice(64, 128)
                            S.op("dve", I("tensor_tensor", out=dtmp[nr, :], in0=pso[nr, e_, :], in1=dtmp[nr, :], op=ALU.mult), [B_pmx, B_dtmp], [B_dtmp])
                        S.op("pool", I("tensor_tensor", out=yc[:, 2 * h:2 * h + 2, tsl], in0=dtmp[:].rearrange("p (c t) -> p c t", c=2),
                                       in1=zs[:, 8 + 2 * h:8 + 2 * h + 2, tsl], op=ALU.mult),
                             [B_dtmp] + B_zs[8:12], [B_yc])
            if cfg.do_a:
                pp_mode[0] = "wide"
                hgrn_q_and_v(l, w2fm_b, base["qA"])
                hgrn_gates(l, 1, w2fm_b, base["a2"])
                for j in range(4):
                    p, Bp = proj_fm(l, w2fm_b, base["zA"] + j)
                    silu_from_psum(p, Bp, zs[:, j, :], B_zs[j])
                for bi in (3, 2, 1, 0):
                    hgrn_vtok(bi)
                    hgrn_block(l, 1, bi, m)
            if hook is not None:
                hook()
            pp_mode[0] = "narrow"
            scal = {0: 1.0, 1: 0.5, 2: 1.0}
            for dc in range(8):
                dsl = slice(dc * 128, (dc + 1) * 128)
                acc, Bacc = get_tmp()
                first = True
                wi = _wbrc_i[0] % 2
                _wbrc_i[0] += 1
                dma("sp", wbrc[wi][:], wbr_b[l, dc], [B_wcast[l]], [B_wbrc[wi]], sl_wbrc[wi])
                for bi, (on, ysrc, By) in enumerate(((cfg.do_a, ya, B_ya), (cfg.do_b, yb, B_yb), (cfg.do_c, yc, B_yc))):
                    if not on:
                        continue
                    for k in range(4):
                        S.op("pe", I("matmul", pbr[bi][:], lhsT=wbrc[wi][:, bi, k, :], rhs=ysrc[:, k, :],
                                                                                     start=(k == 0), stop=(k == 3)), [B_wbrc[wi], By], [B_pbr[bi]])
                    pg, Bpg = proj_fm(l, w2fm_b, base[("gA", "gB", "gC")[bi]] + dc)
                    gtile, Bg = get_tmp()
                    S.op("act", tanh_gate(gtile[:], pg[:], 0.5), [Bpg], [Bg])
                    g = gtile[:]
                    if first:
                        S.op("dve", I("scalar_tensor_tensor", out=acc[:], in0=g, scalar=1.0, in1=pbr[bi][:], op0=ALU.add, op1=ALU.mult),
                             [Bg, B_pbr[bi]], [Bacc])
                        if scal[bi] != 1.0:
                            S.op("dve", I("tensor_scalar", out=acc[:], in0=acc[:], scalar1=scal[bi], scalar2=None, op0=ALU.mult), [Bacc], [Bacc])
                        first = False
                    else:
                        t2, B2 = get_tmp()
                        S.op("dve", I("scalar_tensor_tensor", out=t2[:], in0=g, scalar=1.0, in1=pbr[bi][:], op0=ALU.add, op1=ALU.mult),
                             [Bg, B_pbr[bi]], [B2])
                        S.op("dve", I("scalar_tensor_tensor", out=acc[:], in0=t2[:], scalar=scal[bi], in1=acc[:], op0=ALU.mult, op1=ALU.add),
                             [B2, Bacc], [Bacc])
                S.op("act", I("activation", out=mg[:, dc, :], in_=acc[:], func=AF.Copy), [Bacc], [B_mg[dc]])
            for ec in range(8):
                esl = slice(ec * 128, (ec + 1) * 128)
                w, Bw = load_w_chunk(l, wo_b, ec)
                p, Bp = get_pp()
                for k in range(8):
                    S.op("pe", I("matmul", p[:], lhsT=w[:, k, :], rhs=mg[:, k, :], start=(k == 0), stop=(k == 7)),
                         [Bw, B_mg[k]], [Bp])
                S.op("dve", I("scalar_tensor_tensor", out=xt[:, ec, :], in0=p[:], scalar=0.25, in1=xt[:, ec, :], op0=ALU.mult, op1=ALU.add), [Bp, B_xt], [B_xt])
            dma("sp", xview(out_d, m), xt[:], [B_xt], [DRAM_x[m]], sl_xo)

        import os
        stop = os.environ.get("KSTOP", "")
        for l in range(depth):
            S.epoch = l
            if stop == "cast":
                for m in range(n_mt):
                    dma("sp", xbufs[0][:], xview(xsrc(l), m), [DRAM_x[m]] + B_wcast, [B_xbufs[0]], sl_xt)
                    dma("sp", xview(out_d, m), xbufs[0][:], [B_xbufs[0]], [DRAM_x[m]], sl_xo)
                continue
            layer_weights(l)
            if stop == "lw":
                for m in range(n_mt):
                    dma("sp", xbufs[0][:], xview(xsrc(l), m), [DRAM_x[m]] + B_wcast + [B_wtm, B_lw], [B_xbufs[0]], sl_xt)
                    dma("sp", xview(out_d, m), xbufs[0][:], [B_xbufs[0]], [DRAM_x[m]], sl_xo)
                continue
            steps = ([(1, m) for m in range(n_mt)] if cfg.do_a else []) + [(2, m) for m in reversed(range(n_mt))]
            if cfg.do_a:
                S.op("dve", I("memset", St[:], 0.0), [], [B_S])
            for si, (sw, m) in enumerate(steps):
                hook = None
                if si + 1 < len(steps):
                    nm = steps[si + 1][1]

                    def hook(nm=nm, l=l):
                        load_x(l, nm, 1 - XB.cur)
                        XB.prefetched = True
                (sweep1_mt if sw == 1 else sweep2_mt)(l, m, hook)
                XB.cur = 1 - XB.cur
        S.epoch = depth
        S.op("sp", I("nop"), reads=[DRAM_x[m] for m in range(n_mt)], writes=[])

        with nc.Block() as block:
            S.finalize(engsems, block)
        build_nc.last_stats = S.stats
    return nc


def run(inputs, cfg, trace=False):
    nc = build_nc(cfg)
    in_maps = [prep_core_inputs(inputs, c, cfg) for c in range(NCORES)]
    res = run_bass_kernel_spmd(nc, in_maps, core_ids=list(range(NCORES)), trace=trace)
    T = cfg.T
    L = 2 * T
    B = NCORES // 2
    out = np.empty((B, L, D), np.float32)
    for c in range(NCORES):
        o = np.asarray(res.results[c]["out"]).T
        if c % 2 == 0:
            out[c // 2, :T] = o
        else:
            out[c // 2, T:] = o[::-1]
    return out, res


def kernel(**inputs):
    cfg = Cfg()
    out, _ = run(inputs, cfg)
    return out
```
